# Optimizing a Trainium2 kernel written in Bass

```python
import math
import jax, jax.numpy as jnp
from jax import lax
import numpy as np

D_MODEL = 1024
BATCH = 8
SEQ = 4096
DEPTH = 2

GRID_W = 64
CTX_LEN = 256
N_MOD = 9
D_FF = 2816
ALPHA = (2 * DEPTH) ** 0.25
BETA = (8 * DEPTH) ** -0.25
EPS = 1e-6
A_HEADS = 6
A_KV_HEADS = 2
HEAD_DIM = 64
ROPE_THETA = 10000.0
Q_BLOCK = 128
HY_WIDTH = 256
HY_EMB = 33
HY_BANDS = (HY_EMB - 1) // 2
HY_ORDER = 64
HY_TARGET = 1e-2
HY_FAST = 0.3
HY_SLOW = 1.5
C_HEADS = 6
C_Q_LORA = 256
C_KV_LORA = 128
C_NOPE = 64
C_ROPE = 32
C_V = 64
A_WIDTH = A_HEADS * HEAD_DIM
C_WIDTH = C_HEADS * C_V
D_MIX = A_WIDTH + HY_WIDTH + C_WIDTH
A_COLS = (A_HEADS + 2 * A_KV_HEADS) * HEAD_DIM
B_COLS = 3 * HY_WIDTH
C_COLS = C_Q_LORA + C_KV_LORA + C_ROPE
P_IN = A_COLS + B_COLS + C_COLS

kernel_name = "hybrid_parallel_gqa_hyena_mla_block"


def layer_norm(x, g=None, b=None):
    xf = x.astype(jnp.float32)
    mu = jnp.mean(xf, -1, keepdims=True)
    var = jnp.mean(jnp.square(xf - mu), -1, keepdims=True)
    y = (xf - mu) * lax.rsqrt(var + EPS)
    if g is not None:
        y = y * g + b
    return y.astype(x.dtype)


def rms_norm(x, g):
    xf = x.astype(jnp.float32)
    y = xf * lax.rsqrt(jnp.mean(jnp.square(xf), -1, keepdims=True) + EPS) * g
    return y.astype(x.dtype)


def axial_rope(rows, rot_dim):
    row = jnp.repeat(jnp.arange(rows, dtype=jnp.float32), GRID_W)
    col = jnp.tile(jnp.arange(GRID_W, dtype=jnp.float32), rows)
    n_freq = rot_dim // 4
    inv = ROPE_THETA ** (-jnp.arange(n_freq, dtype=jnp.float32) / n_freq)
    ang = jnp.concatenate([row[:, None] * inv, col[:, None] * inv], -1)
    return jnp.cos(ang)[:, None, :], jnp.sin(ang)[:, None, :]


def apply_rope(x, cos, sin):
    xf = x.astype(jnp.float32)
    half = xf.shape[-1] // 2
    x1, x2 = xf[..., :half], xf[..., half:]
    return jnp.concatenate([x1 * cos - x2 * sin, x2 * cos + x1 * sin], -1).astype(x.dtype)


def block_attention(q, k, v, scale):
    b, n_q, n_heads, d = q.shape
    n_kv = k.shape[2]
    n_blk = n_q // Q_BLOCK
    qb = (q * scale).reshape(b, n_blk, Q_BLOCK, n_kv, n_heads // n_kv, d).transpose(1, 0, 2, 3, 4, 5)

    def one_block(q_blk):
        s = jnp.einsum('bqkgd,bskd->bkgqs', q_blk, k, preferred_element_type=jnp.float32)
        p = jax.nn.softmax(s, axis=-1).astype(v.dtype)
        return jnp.einsum('bkgqs,bske->bqkge', p, v)

    o = lax.map(one_block, qb)
    return o.transpose(1, 0, 2, 3, 4, 5).reshape(b, n_q, n_heads * v.shape[-1])


def gqa_qkv(pa, q_g, k_g, rope):
    b, n, _ = pa.shape
    q, k, v = jnp.split(pa, [A_WIDTH, A_WIDTH + A_KV_HEADS * HEAD_DIM], -1)
    q = rms_norm(q.reshape(b, n, A_HEADS, HEAD_DIM), q_g)
    k = rms_norm(k.reshape(b, n, A_KV_HEADS, HEAD_DIM), k_g)
    v = v.reshape(b, n, A_KV_HEADS, HEAD_DIM)
    if rope is not None:
        q = apply_rope(q, *rope)
        k = apply_rope(k, *rope)
    return q, k, v


def mla_qkv(pc, q_g, kv_g, w_uq, w_ukv, rope):
    b, n, _ = pc.shape
    c_q, c_kv, k_r = jnp.split(pc, [C_Q_LORA, C_Q_LORA + C_KV_LORA], -1)
    q = (rms_norm(c_q, q_g) @ w_uq).reshape(b, n, C_HEADS, C_NOPE + C_ROPE)
    kv = (rms_norm(c_kv, kv_g) @ w_ukv).reshape(b, n, C_HEADS, C_NOPE + C_V)
    q_nope, q_rope = q[..., :C_NOPE], q[..., C_NOPE:]
    k_nope, v = kv[..., :C_NOPE], kv[..., C_NOPE:]
    k_r = k_r[:, :, None, :]
    if rope is not None:
        q_rope = apply_rope(q_rope, *rope)
        k_r = apply_rope(k_r, *rope)
    q = jnp.concatenate([q_nope, q_rope], -1)
    k = jnp.concatenate([k_nope, jnp.broadcast_to(k_r, (b, n, C_HEADS, C_ROPE))], -1)
    return q, k, v


def hyena_filter(n, w1, b1, w2, b2, w3, b3, w4, freq):
    t = jnp.linspace(0.0, 1.0, n, dtype=jnp.float32)[:, None]
    w = 2.0 * math.pi * jnp.arange(n, dtype=jnp.float32)[:, None] / n
    f = jnp.linspace(1e-4, HY_BANDS - 1, HY_BANDS, dtype=jnp.float32)[None, :]
    z = jnp.concatenate([t, jnp.cos(f * w), -jnp.sin(f * w)], -1)
    hdn = jnp.sin(freq * (z @ w1 + b1))
    hdn = jnp.sin(freq * (hdn @ w2 + b2))
    hdn = jnp.sin(freq * (hdn @ w3 + b3))
    h = (hdn @ w4).astype(jnp.float32)
    deltas = jnp.abs(jnp.linspace(math.log(HY_TARGET) / HY_SLOW, math.log(HY_TARGET) / HY_FAST,
                                  HY_WIDTH, dtype=jnp.float32))
    h = h * jnp.exp(-t * jnp.tile(deltas, 2))
    h_fwd, h_bwd = h[:, :HY_WIDTH], h[:, HY_WIDTH:]
    filt = jnp.concatenate([h_fwd, jnp.zeros((1, HY_WIDTH), jnp.float32), h_bwd[:0:-1]], 0)
    return filt / jnp.sum(jnp.abs(filt), 0, keepdims=True)


def hyena_mix(p, conv_w, conv_b, w1, b1, w2, b2, w3, b3, w4, freq, d_skip):
    n = p.shape[1]
    pp = jnp.pad(p, ((0, 0), (1, 1), (0, 0)))
    p = pp[:, :-2] * conv_w[0] + pp[:, 1:-1] * conv_w[1] + pp[:, 2:] * conv_w[2] + conv_b
    v, x1, x0 = jnp.split(p, 3, -1)
    u = (v * x1).astype(jnp.float32)
    filt = hyena_filter(n, w1, b1, w2, b2, w3, b3, w4, freq)
    uf = jnp.fft.rfft(u, n=2 * n, axis=1)
    y = jnp.fft.irfft(uf * jnp.fft.rfft(filt, axis=0)[None], n=2 * n, axis=1)[:, :n]
    y = y + u * d_skip
    return (y * x0).astype(x0.dtype)


def modulate(x, shift, scale):
    return layer_norm(x) * (1.0 + scale) + shift


def swiglu(h, w_gu, w_down):
    g, u = jnp.split(h @ w_gu, 2, -1)
    return (jax.nn.silu(g) * u) @ w_down


def ffn_sublayer(x, m, w_gu, w_down, g, b):
    y = swiglu(modulate(x, m[:, 0], m[:, 1]), w_gu, w_down)
    return layer_norm(ALPHA * x + 0.5 * m[:, 2] * y, g, b)


def token_mix(h_lat, h_ctx, w_in, w_out, a_qn, a_kn, hy, mla, rope_a, rope_c, need_ctx):
    p_lat = h_lat @ w_in
    p_ctx = h_ctx @ w_in
    pa_l, pb_l, pc_l = jnp.split(p_lat, [A_COLS, A_COLS + B_COLS], -1)
    pa_c, pb_c, pc_c = jnp.split(p_ctx, [A_COLS, A_COLS + B_COLS], -1)
    qa_l, ka_l, va_l = gqa_qkv(pa_l, a_qn, a_kn, rope_a)
    qa_c, ka_c, va_c = gqa_qkv(pa_c, a_qn, a_kn, None)
    oa_l = block_attention(qa_l, jnp.concatenate([ka_c, ka_l], 1), jnp.concatenate([va_c, va_l], 1),
                           HEAD_DIM ** -0.5)
    qc_l, kc_l, vc_l = mla_qkv(pc_l, *mla, rope_c)
    qc_c, kc_c, vc_c = mla_qkv(pc_c, *mla, None)
    oc_l = block_attention(qc_l, jnp.concatenate([kc_c, kc_l], 1), jnp.concatenate([vc_c, vc_l], 1),
                           (C_NOPE + C_ROPE) ** -0.5)
    ob_l = hyena_mix(pb_l, *hy)
    out_lat = jnp.concatenate([oa_l, ob_l, oc_l], -1) @ w_out
    if not need_ctx:
        return out_lat, None
    oa_c = block_attention(qa_c, ka_c, va_c, HEAD_DIM ** -0.5)
    oc_c = block_attention(qc_c, kc_c, vc_c, (C_NOPE + C_ROPE) ** -0.5)
    ob_c = hyena_mix(pb_c, *hy)
    out_ctx = jnp.concatenate([oa_c, ob_c, oc_c], -1) @ w_out
    return out_lat, out_ctx


def setup_inputs(seed: int = 0) -> dict:
    key = jax.random.key(seed)
    keys = list(jax.random.split(key, 40))

    def nrm(i, shape, s):
        return jax.random.normal(keys[i], shape, jnp.float32) * s

    L = DEPTH
    return {
        "x": nrm(0, (BATCH, SEQ, D_MODEL), 1.0),
        "c": nrm(1, (BATCH, D_MODEL), 1.0),
        "ctx": nrm(2, (BATCH, CTX_LEN, D_MODEL), 1.0),
        "c_ctx": nrm(3, (D_MODEL,), 1.0),
        "ada_w": nrm(4, (L, D_MODEL, N_MOD * D_MODEL), 0.5 * D_MODEL ** -0.5),
        "ada_b": nrm(5, (L, N_MOD * D_MODEL), 0.01),
        "ffn1_w_gu": nrm(6, (L, D_MODEL, 2 * D_FF), D_MODEL ** -0.5),
        "ffn1_w_down": nrm(7, (L, D_FF, D_MODEL), BETA * D_FF ** -0.5),
        "ffn2_w_gu": nrm(8, (L, D_MODEL, 2 * D_FF), D_MODEL ** -0.5),
        "ffn2_w_down": nrm(9, (L, D_FF, D_MODEL), BETA * D_FF ** -0.5),
        "ln_g": 1.0 + nrm(10, (L, 3, D_MODEL), 0.01),
        "ln_b": nrm(11, (L, 3, D_MODEL), 0.01),
        "w_in": nrm(12, (L, D_MODEL, P_IN), D_MODEL ** -0.5),
        "w_out": nrm(13, (L, D_MIX, D_MODEL), BETA * D_MIX ** -0.5),
        "a_q_norm": 1.0 + nrm(14, (L, HEAD_DIM), 0.01),
        "a_k_norm": 1.0 + nrm(15, (L, HEAD_DIM), 0.01),
        "hy_conv_w": nrm(16, (L, 3, B_COLS), 3 ** -0.5),
        "hy_conv_b": nrm(17, (L, B_COLS), 0.01),
        "hy_f_w1": nrm(18, (L, HY_EMB, HY_ORDER), HY_EMB ** -0.5),
        "hy_f_b1": nrm(19, (L, HY_ORDER), 0.01),
        "hy_f_w2": nrm(20, (L, HY_ORDER, HY_ORDER), HY_ORDER ** -0.5),
        "hy_f_b2": nrm(21, (L, HY_ORDER), 0.01),
        "hy_f_w3": nrm(22, (L, HY_ORDER, HY_ORDER), HY_ORDER ** -0.5),
        "hy_f_b3": nrm(23, (L, HY_ORDER), 0.01),
        "hy_f_w4": nrm(24, (L, HY_ORDER, 2 * HY_WIDTH), HY_ORDER ** -0.5),
        "hy_f_freq": 1.0 + nrm(25, (L, HY_ORDER), 0.01),
        "hy_bias": nrm(26, (L, HY_WIDTH), 0.1),
        "mla_q_norm": 1.0 + nrm(27, (L, C_Q_LORA), 0.01),
        "mla_kv_norm": 1.0 + nrm(28, (L, C_KV_LORA), 0.01),
        "mla_w_uq": nrm(29, (L, C_Q_LORA, C_HEADS * (C_NOPE + C_ROPE)), C_Q_LORA ** -0.5),
        "mla_w_ukv": nrm(30, (L, C_KV_LORA, C_HEADS * (C_NOPE + C_V)), C_KV_LORA ** -0.5),
    }


def reference(x, c, ctx, c_ctx, ada_w, ada_b, ffn1_w_gu, ffn1_w_down, ffn2_w_gu, ffn2_w_down,
              ln_g, ln_b, w_in, w_out, a_q_norm, a_k_norm, hy_conv_w, hy_conv_b,
              hy_f_w1, hy_f_b1, hy_f_w2, hy_f_b2, hy_f_w3, hy_f_b3, hy_f_w4, hy_f_freq, hy_bias,
              mla_q_norm, mla_kv_norm, mla_w_uq, mla_w_ukv):
    b, n_lat, _ = x.shape
    rows = n_lat // GRID_W
    rope_a = axial_rope(rows, HEAD_DIM)
    rope_c = axial_rope(rows, C_ROPE)
    for l in range(DEPTH):
        need_ctx = l < DEPTH - 1
        m_lat = (jax.nn.silu(c) @ ada_w[l] + ada_b[l]).reshape(b, N_MOD, 1, D_MODEL)
        m_ctx = (jax.nn.silu(c_ctx) @ ada_w[l] + ada_b[l]).reshape(1, N_MOD, 1, D_MODEL)
        x = ffn_sublayer(x, m_lat[:, 0:3], ffn1_w_gu[l], ffn1_w_down[l], ln_g[l, 0], ln_b[l, 0])
        ctx = ffn_sublayer(ctx, m_ctx[:, 0:3], ffn1_w_gu[l], ffn1_w_down[l], ln_g[l, 0], ln_b[l, 0])
        hy = (hy_conv_w[l], hy_conv_b[l], hy_f_w1[l], hy_f_b1[l], hy_f_w2[l], hy_f_b2[l],
              hy_f_w3[l], hy_f_b3[l], hy_f_w4[l], hy_f_freq[l], hy_bias[l])
        mla = (mla_q_norm[l], mla_kv_norm[l], mla_w_uq[l], mla_w_ukv[l])
        mix_lat, mix_ctx = token_mix(modulate(x, m_lat[:, 3], m_lat[:, 4]),
                                     modulate(ctx, m_ctx[:, 3], m_ctx[:, 4]),
                                     w_in[l], w_out[l], a_q_norm[l], a_k_norm[l], hy, mla,
                                     rope_a, rope_c, need_ctx)
        x = layer_norm(ALPHA * x + m_lat[:, 5] * mix_lat, ln_g[l, 1], ln_b[l, 1])
        x = ffn_sublayer(x, m_lat[:, 6:9], ffn2_w_gu[l], ffn2_w_down[l], ln_g[l, 2], ln_b[l, 2])
        if need_ctx:
            ctx = layer_norm(ALPHA * ctx + m_ctx[:, 5] * mix_ctx, ln_g[l, 1], ln_b[l, 1])
            ctx = ffn_sublayer(ctx, m_ctx[:, 6:9], ffn2_w_gu[l], ffn2_w_down[l], ln_g[l, 2], ln_b[l, 2])
    return x
```

```python
import math
from contextlib import ExitStack
import numpy as np
import concourse.bass as bass
import concourse.mybir as mybir
from concourse.bass_utils import run_bass_kernel_spmd

F32 = mybir.dt.float32
BF16 = mybir.dt.bfloat16
AF = mybir.ActivationFunctionType
ALU = mybir.AluOpType

D = 1024
SEQ = 4096
CTX = 256
NT = SEQ + CTX
NCH = NT // 128
DEPTH = 2
DFF = 2816
NMOD = 9
ALPHA = (2 * DEPTH) ** 0.25
EPS = 1e-6
P_IN = 1824
USE_FFT = True


class Tl:
    def __init__(self, name=""):
        self.name = name
        self.w = None
        self.weng = None
        self.r = {}


class MK:
    def __init__(self, nc):
        self.nc = nc
        self.eng = {'pe': nc.tensor, 'act': nc.scalar, 'dve': nc.vector, 'pool': nc.gpsimd, 'sp': nc.sync}
        self.sem = {}
        self.cnt = {}
        self.seen = {e: {} for e in self.eng}
        self._stack = ExitStack()
        for e in self.eng:
            self.sem[e] = self._stack.enter_context(nc.semaphore("s_" + e))
            self.cnt[e] = 0
        self.dsems = []
        self.free_dsems = []

    def new_dsem(self):
        if self.free_dsems:
            return self.free_dsems.pop()
        s = self._stack.enter_context(self.nc.semaphore("d%d" % len(self.dsems)))
        d = [s, 0]
        self.dsems.append(d)
        return d

    def _need(self, e, waits, dep):
        if dep is None:
            return
        sem, val = dep
        k = id(sem)
        if self.seen[e].get(k, 0) >= val:
            return
        if k not in waits or waits[k][1] < val:
            waits[k] = (sem, val)

    def _dowaits(self, e, waits):
        E = self.eng[e]
        for k, (sem, val) in waits.items():
            E.wait_ge(sem, val)
            self.seen[e][k] = val

    def op(self, e, fn, r=(), w=()):
        waits = {}
        for t in r:
            self._need(e, waits, t.w)
        for t in w:
            if t.weng != e:
                self._need(e, waits, t.w)
            for re_, dep in t.r.items():
                if re_ != e:
                    self._need(e, waits, dep)
        self._dowaits(e, waits)
        ins = fn()
        self.cnt[e] += 1
        ins.then_inc(self.sem[e], 1)
        dep = (self.sem[e], self.cnt[e])
        for t in r:
            t.r[e] = dep
        for t in w:
            t.w = dep
            t.weng = e
            t.r = {}
        return ins

    def dma(self, q, out, in_, r=(), w=(), dsem=None):
        waits = {}
        for t in r:
            self._need(q, waits, t.w)
        for t in w:
            self._need(q, waits, t.w)
            for re_, dep in t.r.items():
                self._need(q, waits, dep)
        self._dowaits(q, waits)
        ins = self.eng[q].dma_start(out=out, in_=in_)
        dsem[1] += 16
        ins.then_inc(dsem[0], 16)
        dep = (dsem[0], dsem[1])
        for t in r:
            t.r['dma%d' % id(dsem)] = dep
        for t in w:
            t.w = dep
            t.weng = 'dma'
            t.r = {}
        return ins

    def barrier(self):
        waits = {}
        for e in self.eng:
            if e != 'sp' and self.cnt[e] > 0:
                self._need('sp', waits, (self.sem[e], self.cnt[e]))
        for d in self.dsems:
            if d[1] > 0:
                self._need('sp', waits, (d[0], d[1]))
        self._dowaits('sp', waits)
        self.cnt['sp'] += 1
        self.nc.sync.sem_inc(self.sem['sp'], 1)
        for e in self.eng:
            if e != 'sp':
                self.eng[e].wait_ge(self.sem['sp'], self.cnt['sp'])
                self.seen[e][id(self.sem['sp'])] = self.cnt['sp']
                for f in self.eng:
                    if f != 'sp':
                        self.seen[e][id(self.sem[f])] = self.cnt[f]
                for d in self.dsems:
                    self.seen[e][id(d[0])] = d[1]

    def recycle(self):
        self.free_dsems = list(self.dsems)

    def close(self):
        self._stack.close()


class Ctx:
    pass


DECLARED = set()


_UID = [0]


def sb(nc, st, name, shape, dt):
    _UID[0] += 1
    return st.enter_context(nc.sbuf_tensor("sb%d_%s" % (_UID[0], name), shape, dt))


def ps(nc, st, name, shape, dt):
    _UID[0] += 1
    return st.enter_context(nc.psum_tensor("ps%d_%s" % (_UID[0], name), shape, dt))


def phase_ada(K, l):
    nc, mk = K.nc, K.mk
    with ExitStack() as st:
        cc = sb(nc, st, "ada_cc", [128, 8, 2], F32)
        sg = sb(nc, st, "ada_sg", [128, 8, 2], F32)
        wbuf = [sb(nc, st, "ada_w%d" % i, [128, 8, 512], F32) for i in range(4)]
        bia = sb(nc, st, "ada_b", [2, 9216], F32)
        mrow = sb(nc, st, "ada_m", [2, 9216], F32)
        pm = [ps(nc, st, "ada_p%d" % i, [2, 512], F32) for i in range(2)]
        t_cc, t_sg, t_b, t_m = Tl(), Tl(), Tl(), Tl()
        t_w = [Tl() for _ in range(4)]
        t_p = [Tl(), Tl()]
        d0 = mk.new_dsem()
        dw = [mk.new_dsem() for _ in range(4)]
        mk.dma('sp', cc[:], K.cc_in.rearrange("p (kc g) -> p kc g", g=2), w=[t_cc], dsem=d0)
        mk.dma('sp', bia[:], K.ada_b[l].partition_broadcast(2), w=[t_b], dsem=d0)
        mk.op('act', lambda: nc.scalar.activation(out=sg[:], in_=cc[:], func=AF.Silu), r=[t_cc], w=[t_sg])
        for nb in range(18):
            i = nb % 2
            wi = nb % 4
            mk.dma('sp', wbuf[wi][:], K.ada_w[l][:, nb * 512:(nb + 1) * 512].rearrange("(kc p) n -> p kc n", p=128),
                   w=[t_w[wi]], dsem=dw[wi])
            for kc in range(8):
                mk.op('pe', lambda kc=kc, i=i, wi=wi: nc.tensor.matmul(pm[i][:], lhsT=sg[:, kc, :], rhs=wbuf[wi][:, kc, :],
                                                                      start=(kc == 0), stop=(kc == 7)),
                      r=[t_sg, t_w[wi]], w=[t_p[i]])
            mk.op('dve', lambda nb=nb, i=i: nc.vector.tensor_add(out=mrow[:, nb * 512:(nb + 1) * 512], in0=pm[i][:],
                                                                 in1=bia[:, nb * 512:(nb + 1) * 512]),
                  r=[t_p[i], t_b], w=[t_m])
        mk.dma('sp', K.mod[l], mrow[:], r=[t_m], w=[K.t_mod[l]], dsem=d0)
        mk.barrier()


def ln_stats(K, x_ap, t_x, st6, mv, rstd, t_st):
    nc, mk = K.nc, K.mk
    for c in range(2):
        mk.op('dve', lambda c=c: nc.vector.bn_stats(out=st6[:, c, :], in_=x_ap[:, c * 512:(c + 1) * 512]), r=[t_x], w=[t_st])
    mk.op('dve', lambda: nc.vector.bn_aggr(out=mv[:], in_=st6[:]), r=[t_st], w=[t_st])
    mk.op('act', lambda: nc.scalar.activation(out=rstd[:], in_=mv[:, 1:2], func=AF.Sqrt, bias=K.eps_t[:, 0:1], scale=1.0), r=[t_st], w=[t_st])
    mk.op('dve', lambda: nc.vector.reciprocal(out=rstd[:], in_=rstd[:]), r=[t_st], w=[t_st])


def load_bcast(K, dst_ap, src_row_ap, t, dsem):
    K.mk.dma('sp', dst_ap, src_row_ap.partition_broadcast(128), w=[t], dsem=dsem)


def ffn_load_weights(K, st, w_gu, w_down):
    nc, mk = K.nc, K.mk
    W = Ctx()
    W.wgu = sb(nc, st, "f_wgu", [128, 8, 2 * DFF], BF16)
    W.wdn = sb(nc, st, "f_wdn", [128, 22, D], BF16)
    W.t_wgu, W.t_wdn = Tl(), Tl()
    dws = mk.new_dsem()
    for kc in range(8):
        for hh in range(2):
            mk.dma('pool', W.wgu[:, kc, hh * DFF:(hh + 1) * DFF], w_gu[kc * 128:(kc + 1) * 128, hh * DFF:(hh + 1) * DFF], w=[W.t_wgu], dsem=dws)
    for fc in range(22):
        mk.dma('pool', W.wdn[:, fc, :], w_down[fc * 128:(fc + 1) * 128, :], w=[W.t_wdn], dsem=dws)
    return W


def phase_ffn(K, l, which, Xin, t_xin, Xout, t_xout, w_gu, w_down, first_chunk=0, W=None):
    nc, mk = K.nc, K.mk
    mbase = 0 if which == 0 else 6
    lni = 0 if which == 0 else 2
    with ExitStack() as st:
        if W is None:
            W = ffn_load_weights(K, st, w_gu, w_down)
        wgu, wdn, t_wgu, t_wdn = W.wgu, W.wdn, W.t_wgu, W.t_wdn
        modv = sb(nc, st, "f_mod", [128, 5, D], F32)
        xc = [sb(nc, st, "f_xc%d" % i, [128, D], F32) for i in range(6)]
        hT = sb(nc, st, "f_hT", [128, 8, 256], BF16)
        actT = sb(nc, st, "f_actT", [128, 22, 256], BF16)
        tt = sb(nc, st, "f_tt", [128, D], F32)
        hn2 = [sb(nc, st, "f_hn%d" % i, [128, D], BF16) for i in range(2)]
        st6j = [sb(nc, st, "f_st6j%d" % i, [128, 2, 6], F32) for i in range(2)]
        mvj = [sb(nc, st, "f_mvj%d" % i, [128, 2], F32) for i in range(2)]
        rstdj = [sb(nc, st, "f_rstdj%d" % i, [128, 1], F32) for i in range(2)]
        sgt = [sb(nc, st, "f_sg%d" % i, [128, 256], F32) for i in range(2)]
        st6b = sb(nc, st, "f_st6b", [128, 2, 6], F32)
        mvb = sb(nc, st, "f_mvb", [128, 2], F32)
        rstdb = sb(nc, st, "f_rstdb", [128, 1], F32)
        p_tr = ps(nc, st, "f_ptr", [128, 8, 128], BF16)
        p_up = [ps(nc, st, "f_pup%d" % i, [128, 2, 256], F32) for i in range(3)]
        p_dn = [ps(nc, st, "f_pdn%d" % i, [128, 512], F32) for i in range(4)]
        t_mod = Tl()
        t_xc = [Tl() for _ in range(6)]
        t_hT, t_act, t_tt, t_stb, t_ptr = Tl(), Tl(), Tl(), Tl(), Tl()
        t_hn2 = [Tl(), Tl()]
        t_stj = [Tl(), Tl()]
        t_sg = [Tl(), Tl()]
        t_pup = [Tl() for _ in range(3)]
        t_pdn = [Tl() for _ in range(4)]
        dmod = mk.new_dsem()
        dx = [mk.new_dsem() for _ in range(6)]
        dout = mk.new_dsem()
        nblk = NCH // 2
        state = {"upi": 0}
        blks = list(range(first_chunk // 2, nblk))

        def xidx(blk, j):
            return (blk % 3) * 2 + j

        def load_mod(g):
            mrow = K.mod[l][g]
            for i, src in enumerate([mrow[(mbase + 0) * D:(mbase + 1) * D], mrow[(mbase + 1) * D:(mbase + 2) * D],
                                     mrow[(mbase + 2) * D:(mbase + 3) * D], K.ln_g[l, lni], K.ln_b[l, lni]]):
                mk.dma('sp', modv[:, i, :], src.partition_broadcast(128), r=[K.t_mod[l]], w=[t_mod], dsem=dmod)
            mk.op('pool', lambda: nc.gpsimd.tensor_scalar_add(out=modv[:, 1, :], in0=modv[:, 1, :], scalar1=1.0), r=[t_mod], w=[t_mod])
            mk.op('pool', lambda: nc.gpsimd.tensor_scalar_mul(out=modv[:, 2, :], in0=modv[:, 2, :], scalar1=0.5), r=[t_mod], w=[t_mod])

        def prep_load(blk):
            for j in range(2):
                ch = blk * 2 + j
                xi = xidx(blk, j)
                mk.dma('sp', xc[xi][:], Xin[ch * 128:(ch + 1) * 128, :], r=[t_xin], w=[t_xc[xi]], dsem=dx[xi])

        def prep_ln(blk):
            for j in range(2):
                xi = xidx(blk, j)
                ln_stats(K, xc[xi], t_xc[xi], st6j[j], mvj[j], rstdj[j], t_stj[j])
            for j in range(2):
                xi = xidx(blk, j)
                mk.op('dve', lambda xi=xi, j=j: nc.vector.scalar_tensor_tensor(out=tt[:], in0=xc[xi][:], scalar=mvj[j][:, 0:1], in1=modv[:, 1, :],
                                                                               op0=ALU.subtract, op1=ALU.mult), r=[t_xc[xi], t_stj[j], t_mod], w=[t_tt])
                mk.op('dve', lambda j=j: nc.vector.scalar_tensor_tensor(out=hn2[j][:], in0=tt[:], scalar=rstdj[j][:, 0:1], in1=modv[:, 0, :],
                                                                        op0=ALU.mult, op1=ALU.add), r=[t_tt, t_stj[j], t_mod], w=[t_hn2[j]])

        def prep_b(blk):
            for j in range(2):
                for kc in range(8):
                    mk.op('pe', lambda kc=kc, j=j: nc.tensor.transpose(out=p_tr[:, kc, :], in_=hn2[j][:, kc * 128:(kc + 1) * 128], identity=K.identb[:]),
                          r=[t_hn2[j], K.t_const], w=[t_ptr])
                mk.op('act', lambda j=j: nc.scalar.copy(out=hT[:, :, j * 128:(j + 1) * 128], in_=p_tr[:]), r=[t_ptr], w=[t_hT])

        def up(blk):
            for fc in range(22):
                pi = state["upi"] % 3
                si = state["upi"] % 2
                state["upi"] += 1
                for hh in range(2):
                    for kc in range(8):
                        mk.op('pe', lambda kc=kc, hh=hh, fc=fc, pi=pi: nc.tensor.matmul(
                            p_up[pi][:, hh, :], lhsT=wgu[:, kc, hh * DFF + fc * 128: hh * DFF + (fc + 1) * 128], rhs=hT[:, kc, :],
                            start=(kc == 0), stop=(kc == 7)), r=[t_wgu, t_hT], w=[t_pup[pi]])
                mk.op('act', lambda pi=pi, si=si: nc.scalar.activation(out=sgt[si][:], in_=p_up[pi][:, 0, :], func=AF.Silu), r=[t_pup[pi]], w=[t_sg[si]])
                mk.op('dve', lambda pi=pi, si=si, fc=fc: nc.vector.tensor_mul(out=actT[:, fc, :], in0=sgt[si][:], in1=p_up[pi][:, 1, :]),
                      r=[t_sg[si], t_pup[pi]], w=[t_act])

        def down(blk):
            for j in range(2):
                ch = blk * 2 + j
                xi = xidx(blk, j)
                for hf in range(2):
                    pd = j * 2 + hf
                    for fc in range(22):
                        mk.op('pe', lambda fc=fc, j=j, hf=hf, pd=pd: nc.tensor.matmul(
                            p_dn[pd][:], lhsT=actT[:, fc, j * 128:(j + 1) * 128], rhs=wdn[:, fc, hf * 512:(hf + 1) * 512],
                            start=(fc == 0), stop=(fc == 21)), r=[t_act, t_wdn], w=[t_pdn[pd]])
                    mk.op('dve', lambda hf=hf, pd=pd: nc.vector.tensor_mul(out=tt[:, hf * 512:(hf + 1) * 512], in0=p_dn[pd][:],
                                                                           in1=modv[:, 2, hf * 512:(hf + 1) * 512]), r=[t_pdn[pd], t_mod], w=[t_tt])
                mk.op('dve', lambda xi=xi: nc.vector.scalar_tensor_tensor(out=xc[xi][:], in0=xc[xi][:], scalar=ALPHA, in1=tt[:],
                                                                          op0=ALU.mult, op1=ALU.add), r=[t_tt, t_xc[xi]], w=[t_xc[xi]])
                ln_stats(K, xc[xi], t_xc[xi], st6b, mvb, rstdb, t_stb)
                mk.op('dve', lambda xi=xi: nc.vector.scalar_tensor_tensor(out=xc[xi][:], in0=xc[xi][:], scalar=mvb[:, 0:1], in1=modv[:, 3, :],
                                                                          op0=ALU.subtract, op1=ALU.mult), r=[t_xc[xi], t_stb, t_mod], w=[t_xc[xi]])
                mk.op('dve', lambda xi=xi: nc.vector.scalar_tensor_tensor(out=xc[xi][:], in0=xc[xi][:], scalar=rstdb[:, 0:1], in1=modv[:, 4, :],
                                                                          op0=ALU.mult, op1=ALU.add), r=[t_xc[xi], t_stb, t_mod], w=[t_xc[xi]])
                mk.dma('pool', Xout[ch * 128:(ch + 1) * 128, :], xc[xi][:], r=[t_xc[xi]], w=[t_xout], dsem=dout)

        if blks and blks[0] == 0:
            load_mod(1)
            prep_load(0)
            prep_ln(0)
            prep_b(0)
            up(0)
            down(0)
            blks = blks[1:]
        if blks:
            load_mod(0)
            prep_load(blks[0])
            if len(blks) > 1:
                prep_load(blks[1])
            prep_ln(blks[0])
            prep_b(blks[0])
            for bi_, blk in enumerate(blks):
                nxt = blks[bi_ + 1] if bi_ + 1 < len(blks) else None
                nx2 = blks[bi_ + 2] if bi_ + 2 < len(blks) else None
                if nxt is not None:
                    prep_ln(nxt)
                up(blk)
                if nxt is not None:
                    prep_b(nxt)
                if nx2 is not None:
                    prep_load(nx2)
                down(blk)
        mk.barrier()


def phase_mixin(K, l, Xin, t_xin):
    nc, mk = K.nc, K.mk
    S = K.S
    with ExitStack() as st:
        win = sb(nc, st, "m_win", [128, 8, P_IN], BF16)
        wuq = sb(nc, st, "m_wuq", [128, 2, 576], BF16)
        wukv = sb(nc, st, "m_wukv", [128, 768], BF16)
        modv = sb(nc, st, "m_mod", [128, 2, D], F32)
        gA = sb(nc, st, "m_gA", [128, 8, 64], F32)
        gq = sb(nc, st, "m_gq", [128, 256], F32)
        gkv = sb(nc, st, "m_gkv", [128, 128], F32)
        ropeA_l = [sb(nc, st, "m_ropeA%d" % i, [128, 2, 32], F32) for i in range(2)]
        ropeC_l = [sb(nc, st, "m_ropeC%d" % i, [128, 2, 16], F32) for i in range(2)]
        t_rope_l = [Tl(), Tl()]
        xc = [sb(nc, st, "m_xc%d" % i, [128, D], F32) for i in range(2)]
        xn = sb(nc, st, "m_xn", [128, D], F32)
        hn = sb(nc, st, "m_hn", [128, D], BF16)
        hT = sb(nc, st, "m_hT", [128, 8, 128], BF16)
        ptA = sb(nc, st, "m_ptA", [128, 8, 64], F32)
        ptC = sb(nc, st, "m_ptC", [128, 416], F32)
        sq = sb(nc, st, "m_sq", [128, 512], F32)
        ss = sb(nc, st, "m_ss", [128, 16], F32)
        r1 = sb(nc, st, "m_r1", [128, 8, 32], F32)
        r2 = sb(nc, st, "m_r2", [128, 8, 32], F32)
        qkb = sb(nc, st, "m_qkb", [128, 8, 64], BF16)
        vab = sb(nc, st, "m_vab", [128, 128], BF16)
        qkT = sb(nc, st, "m_qkT", [128, 4, 128], BF16)
        pbs = sb(nc, st, "m_pbs", [128, 6, 128], F32)
        cn = sb(nc, st, "m_cn", [128, 384], BF16)
        cT = sb(nc, st, "m_cT", [128, 3, 128], BF16)
        qc = sb(nc, st, "m_qc", [128, 6, 96], F32)
        kvc = sb(nc, st, "m_kvc", [128, 6, 128], F32)
        kr = sb(nc, st, "m_kr", [128, 32], F32)
        qcb = sb(nc, st, "m_qcb", [128, 6, 96], BF16)
        kcb = sb(nc, st, "m_kcb", [128, 6, 96], BF16)
        vcb = sb(nc, st, "m_vcb", [128, 6, 64], BF16)
        qcT = sb(nc, st, "m_qcT", [128, 6, 128], BF16)
        kcT = sb(nc, st, "m_kcT", [128, 6, 128], BF16)
        st6 = sb(nc, st, "m_st6", [128, 2, 6], F32)
        mv = sb(nc, st, "m_mv", [128, 2], F32)
        rstd = sb(nc, st, "m_rstd", [128, 1], F32)
        B0 = ps(nc, st, "m_B0", [128, 8, 128], BF16)
        B1 = ps(nc, st, "m_B1", [128, 512], F32)
        B2 = ps(nc, st, "m_B2", [128, 512], F32)
        B3 = ps(nc, st, "m_B3", [128, 512], F32)
        B4 = ps(nc, st, "m_B4", [128, 4, 128], F32)
        B5 = ps(nc, st, "m_B5", [128, 4, 128], F32)
        B6 = ps(nc, st, "m_B6", [128, 8, 128], BF16)
        B7 = ps(nc, st, "m_B7", [128, 512], F32)
        tB = [Tl() for _ in range(8)]
        t_w, t_mod, t_g = Tl(), Tl(), Tl()
        t_xc = [Tl(), Tl()]
        (t_xn, t_hn, t_hT, t_ptA, t_ptC, t_sq, t_ss, t_r, t_qkb, t_vab, t_qkT, t_pbs, t_cn, t_cT, t_qc, t_kvc,
         t_kr, t_qcb, t_kcb, t_vcb, t_qcT, t_kcT, t_st) = [Tl() for _ in range(23)]
        dws, dmod, dg, dout = [mk.new_dsem() for _ in range(4)]
        drope_l = [mk.new_dsem(), mk.new_dsem()]
        dx = [mk.new_dsem(), mk.new_dsem()]

        def issue_loads(ch):
            xi = ch % 2
            mk.dma('sp', xc[xi][:], Xin[ch * 128:(ch + 1) * 128, :], r=[t_xin], w=[t_xc[xi]], dsem=dx[xi])
            if ch >= 2:
                lt = slice((ch - 2) * 128, (ch - 1) * 128)
                mk.dma('sp', ropeA_l[xi][:], K.ropeA_in[lt, :].rearrange("p (a b) -> p a b", a=2), w=[t_rope_l[xi]], dsem=drope_l[xi])
                mk.dma('sp', ropeC_l[xi][:], K.ropeC_in[lt, :].rearrange("p (a b) -> p a b", a=2), w=[t_rope_l[xi]], dsem=drope_l[xi])
        for kc in range(8):
            for two in range(2):
                mk.dma('pool', win[:, kc, 0:384].rearrange("p (h two d) -> p h two d", h=3, two=2)[:, :, two, :],
                       K.w_in[l][kc * 128:(kc + 1) * 128, two * 192:(two + 1) * 192].rearrange("p (h d) -> p h d", h=3), w=[t_w], dsem=dws)
            mk.dma('pool', win[:, kc, 384:P_IN], K.w_in[l][kc * 128:(kc + 1) * 128, 384:P_IN], w=[t_w], dsem=dws)
        for kc in range(2):
            mk.dma('pool', wuq[:, kc, :], K.w_uq[l][kc * 128:(kc + 1) * 128, :], w=[t_w], dsem=dws)
        mk.dma('pool', wukv[:], K.w_ukv[l][:, :], w=[t_w], dsem=dws)
        for h in range(6):
            mk.dma('sp', gA[:, h, :], K.a_qn[l].partition_broadcast(128), w=[t_g], dsem=dg)
        for h in range(2):
            mk.dma('sp', gA[:, 6 + h, :], K.a_kn[l].partition_broadcast(128), w=[t_g], dsem=dg)
        mk.dma('sp', gq[:], K.c_qn[l].partition_broadcast(128), w=[t_g], dsem=dg)
        mk.dma('sp', gkv[:], K.c_kvn[l].partition_broadcast(128), w=[t_g], dsem=dg)
        mk.op('pool', lambda: nc.gpsimd.tensor_scalar_mul(out=gA[:, 0:6, :], in0=gA[:, 0:6, :], scalar1=0.125), r=[t_g], w=[t_g])
        cur_g = None
        for ch in range(NCH):
            g = 1 if ch < 2 else 0
            lat = (g == 0)
            if g != cur_g:
                cur_g = g
                mrow = K.mod[l][g]
                for i in range(2):
                    mk.dma('sp', modv[:, i, :], mrow[(3 + i) * D:(4 + i) * D].partition_broadcast(128), r=[K.t_mod[l]], w=[t_mod], dsem=dmod)
                mk.op('pool', lambda: nc.gpsimd.tensor_scalar_add(out=modv[:, 1, :], in0=modv[:, 1, :], scalar1=1.0), r=[t_mod], w=[t_mod])
            xi = ch % 2
            tok = slice(ch * 128, (ch + 1) * 128)
            if ch == 0:
                issue_loads(0)
            if ch + 1 < NCH:
                issue_loads(ch + 1)
            ropeA, ropeC, t_rope = ropeA_l[xi], ropeC_l[xi], t_rope_l[xi]
            ln_stats(K, xc[xi], t_xc[xi], st6, mv, rstd, t_st)
            mk.op('dve', lambda xi=xi: nc.vector.scalar_tensor_tensor(out=xn[:], in0=xc[xi][:], scalar=mv[:, 0:1], in1=modv[:, 1, :],
                                                                      op0=ALU.subtract, op1=ALU.mult), r=[t_xc[xi], t_st, t_mod], w=[t_xn])
            mk.op('dve', lambda: nc.vector.scalar_tensor_tensor(out=hn[:], in0=xn[:], scalar=rstd[:, 0:1], in1=modv[:, 0, :],
                                                                op0=ALU.mult, op1=ALU.add), r=[t_xn, t_st, t_mod], w=[t_hn])
            for kc in range(8):
                mk.op('pe', lambda kc=kc: nc.tensor.transpose(out=B0[:, kc, :], in_=hn[:, kc * 128:(kc + 1) * 128], identity=K.identb[:]),
                      r=[t_hn, K.t_const], w=[tB[0]])
            mk.op('act', lambda: nc.scalar.copy(out=hT[:], in_=B0[:]), r=[tB[0]], w=[t_hT])
            for (Bk, tb, c0, c1) in ((B1, tB[1], 0, 512), (B3, tB[3], 512, 640), (B2, tB[2], 1408, 1824)):
                for kc in range(8):
                    mk.op('pe', lambda kc=kc, Bk=Bk, c0=c0, c1=c1: nc.tensor.matmul(Bk[:, 0:c1 - c0], lhsT=hT[:, kc, :], rhs=win[:, kc, c0:c1],
                                                                                   start=(kc == 0), stop=(kc == 7)), r=[t_hT, t_w], w=[tb])
            for c in range(6):
                Bk, tb, ci = (B4, tB[4], c) if c < 4 else (B5, tB[5], c - 4)
                for kc in range(8):
                    mk.op('pe', lambda kc=kc, Bk=Bk, ci=ci, c=c: nc.tensor.matmul(Bk[:, ci, :], lhsT=win[:, kc, 640 + c * 128:640 + (c + 1) * 128],
                                                                                 rhs=hT[:, kc, :], start=(kc == 0), stop=(kc == 7)), r=[t_hT, t_w], w=[tb])
            mk.op('act', lambda: nc.scalar.copy(out=ptA[:].rearrange("p h d -> p (h d)"), in_=B1[:]), r=[tB[1]], w=[t_ptA])
            mk.op('act', lambda: nc.scalar.copy(out=ptC[:], in_=B2[:, 0:416]), r=[tB[2]], w=[t_ptC])
            mk.op('act', lambda: nc.scalar.copy(out=vab[:], in_=B3[:, 0:128]), r=[tB[3]], w=[t_vab])
            mk.op('act', lambda: nc.scalar.copy(out=pbs[:, 0:4, :], in_=B4[:]), r=[tB[4]], w=[t_pbs])
            mk.op('act', lambda: nc.scalar.copy(out=pbs[:, 4:6, :], in_=B5[:, 0:2, :]), r=[tB[5]], w=[t_pbs])
            mk.dma('sp', S.PBT[l].rearrange("(c p) t -> p c t", p=128)[:, :, tok], pbs[:], r=[t_pbs], w=[S.t_PBT], dsem=dout)
            mk.dma('sp', S.Va[l][tok, :], vab[:], r=[t_vab], w=[S.t_A], dsem=dout)
            mk.op('dve', lambda: nc.vector.tensor_mul(out=sq[:], in0=ptA[:].rearrange("p h d -> p (h d)"), in1=ptA[:].rearrange("p h d -> p (h d)")),
                  r=[t_ptA], w=[t_sq])
            mk.op('dve', lambda: nc.vector.reduce_sum(out=ss[:, 0:8], in_=sq[:].rearrange("p (h d) -> p h d", d=64), axis=mybir.AxisListType.X),
                  r=[t_sq], w=[t_ss])
            mk.op('act', lambda: nc.scalar.activation(out=ss[:, 0:8], in_=ss[:, 0:8], func=AF.Sqrt, bias=K.eps_t[:, 0:1], scale=1.0 / 64), r=[t_ss], w=[t_ss])
            mk.op('dve', lambda: nc.vector.reciprocal(out=ss[:, 0:8], in_=ss[:, 0:8]), r=[t_ss], w=[t_ss])
            mk.op('dve', lambda: nc.vector.tensor_mul(out=ptA[:], in0=ptA[:], in1=ss[:, 0:8].unsqueeze(2).to_broadcast([128, 8, 64])), r=[t_ptA, t_ss], w=[t_ptA])
            if lat:
                mk.op('pool', lambda: nc.gpsimd.tensor_mul(out=ptA[:], in0=ptA[:], in1=gA[:]), r=[t_ptA, t_g], w=[t_ptA])
                cosb = ropeA[:, 0:1, :].to_broadcast([128, 8, 32])
                sinb = ropeA[:, 1:2, :].to_broadcast([128, 8, 32])
                mk.op('dve', lambda: nc.vector.tensor_mul(out=r1[:], in0=ptA[:, :, 0:32], in1=cosb), r=[t_ptA, t_rope], w=[t_r])
                mk.op('dve', lambda: nc.vector.tensor_mul(out=r2[:], in0=ptA[:, :, 32:64], in1=sinb), r=[t_ptA, t_rope], w=[t_r])
                mk.op('dve', lambda: nc.vector.tensor_sub(out=qkb[:, :, 0:32], in0=r1[:], in1=r2[:]), r=[t_r], w=[t_qkb])
                mk.op('dve', lambda: nc.vector.tensor_mul(out=r1[:], in0=ptA[:, :, 32:64], in1=cosb), r=[t_ptA, t_rope], w=[t_r])
                mk.op('dve', lambda: nc.vector.tensor_mul(out=r2[:], in0=ptA[:, :, 0:32], in1=sinb), r=[t_ptA, t_rope], w=[t_r])
                mk.op('dve', lambda: nc.vector.tensor_add(out=qkb[:, :, 32:64], in0=r1[:], in1=r2[:]), r=[t_r], w=[t_qkb])
            else:
                mk.op('pool', lambda: nc.gpsimd.tensor_mul(out=qkb[:], in0=ptA[:], in1=gA[:]), r=[t_ptA, t_g], w=[t_qkb])
            for pr in range(3):
                mk.op('pe', lambda pr=pr: nc.tensor.transpose(out=B6[:, pr, :], in_=qkb[:, 2 * pr:2 * pr + 2, :].rearrange("p a d -> p (a d)"),
                                                              identity=K.identb[:]), r=[t_qkb, K.t_const], w=[tB[6]])
            mk.op('pe', lambda: nc.tensor.transpose(out=B6[:, 3, :], in_=qkb[:, 6:8, :].rearrange("p a d -> p (a d)"), identity=K.identb[:]),
                  r=[t_qkb, K.t_const], w=[tB[6]])
            mk.op('act', lambda: nc.scalar.copy(out=qkT[:], in_=B6[:, 0:4, :]), r=[tB[6]], w=[t_qkT])
            for pr in range(3):
                mk.dma('sp', S.QTa[l][pr][:, tok], qkT[0:64, pr, :], r=[t_qkT], w=[S.t_A], dsem=dout)
                mk.dma('sp', S.QTa[l][pr + 3][:, tok], qkT[64:128, pr, :], r=[t_qkT], w=[S.t_A], dsem=dout)
            mk.dma('sp', S.KTa[l].rearrange("h d t -> (h d) t")[:, tok], qkT[:, 3, :], r=[t_qkT], w=[S.t_A], dsem=dout)
            mk.op('dve', lambda: nc.vector.tensor_mul(out=sq[:, 0:384], in0=ptC[:, 0:384], in1=ptC[:, 0:384]), r=[t_ptC], w=[t_sq])
            mk.op('dve', lambda: nc.vector.reduce_sum(out=ss[:, 8:9], in_=sq[:, 0:256], axis=mybir.AxisListType.X), r=[t_sq], w=[t_ss])
            mk.op('dve', lambda: nc.vector.reduce_sum(out=ss[:, 9:10], in_=sq[:, 256:384], axis=mybir.AxisListType.X), r=[t_sq], w=[t_ss])
            mk.op('act', lambda: nc.scalar.activation(out=ss[:, 8:9], in_=ss[:, 8:9], func=AF.Sqrt, bias=K.eps_t[:, 0:1], scale=1.0 / 256), r=[t_ss], w=[t_ss])
            mk.op('act', lambda: nc.scalar.activation(out=ss[:, 9:10], in_=ss[:, 9:10], func=AF.Sqrt, bias=K.eps_t[:, 0:1], scale=1.0 / 128), r=[t_ss], w=[t_ss])
            mk.op('dve', lambda: nc.vector.reciprocal(out=ss[:, 8:10], in_=ss[:, 8:10]), r=[t_ss], w=[t_ss])
            mk.op('dve', lambda: nc.vector.scalar_tensor_tensor(out=cn[:, 0:256], in0=ptC[:, 0:256], scalar=ss[:, 8:9], in1=gq[:], op0=ALU.mult, op1=ALU.mult),
                  r=[t_ptC, t_ss, t_g], w=[t_cn])
            mk.op('dve', lambda: nc.vector.scalar_tensor_tensor(out=cn[:, 256:384], in0=ptC[:, 256:384], scalar=ss[:, 9:10], in1=gkv[:], op0=ALU.mult, op1=ALU.mult),
                  r=[t_ptC, t_ss, t_g], w=[t_cn])
            for c in range(3):
                mk.op('pe', lambda c=c: nc.tensor.transpose(out=B6[:, 4 + c, :], in_=cn[:, c * 128:(c + 1) * 128], identity=K.identb[:]),
                      r=[t_cn, K.t_const], w=[tB[6]])
            mk.op('act', lambda: nc.scalar.copy(out=cT[:], in_=B6[:, 4:7, :]), r=[tB[6]], w=[t_cT])
            for kc in range(2):
                mk.op('pe', lambda kc=kc: nc.tensor.matmul(B7[:, 0:480], lhsT=cT[:, kc, :], rhs=wuq[:, kc, 0:480], start=(kc == 0), stop=(kc == 1)),
                      r=[t_cT, t_w], w=[tB[7]])
            for kc in range(2):
                mk.op('pe', lambda kc=kc: nc.tensor.matmul(B3[:, 128:224], lhsT=cT[:, kc, :], rhs=wuq[:, kc, 480:576], start=(kc == 0), stop=(kc == 1)),
                      r=[t_cT, t_w], w=[tB[3]])
            mk.op('pe', lambda: nc.tensor.matmul(B1[:], lhsT=cT[:, 2, :], rhs=wukv[:, 0:512], start=True, stop=True), r=[t_cT, t_w], w=[tB[1]])
            mk.op('pe', lambda: nc.tensor.matmul(B2[:, 0:256], lhsT=cT[:, 2, :], rhs=wukv[:, 512:768], start=True, stop=True), r=[t_cT, t_w], w=[tB[2]])
            qs = 96.0 ** -0.5
            mk.op('act', lambda: nc.scalar.mul(out=qc[:, 0:5, :].rearrange("p h d -> p (h d)"), in_=B7[:, 0:480], mul=qs), r=[tB[7]], w=[t_qc])
            mk.op('act', lambda: nc.scalar.mul(out=qc[:, 5, :], in_=B3[:, 128:224], mul=qs), r=[tB[3]], w=[t_qc])
            mk.op('act', lambda: nc.scalar.copy(out=kvc[:, 0:4, :].rearrange("p h d -> p (h d)"), in_=B1[:]), r=[tB[1]], w=[t_kvc])
            mk.op('act', lambda: nc.scalar.copy(out=kvc[:, 4:6, :].rearrange("p h d -> p (h d)"), in_=B2[:, 0:256]), r=[tB[2]], w=[t_kvc])
            mk.op('pool', lambda: nc.gpsimd.tensor_copy(out=qcb[:, :, 0:64], in_=qc[:, :, 0:64]), r=[t_qc], w=[t_qcb])
            mk.op('pool', lambda: nc.gpsimd.tensor_copy(out=kcb[:, :, 0:64], in_=kvc[:, :, 0:64]), r=[t_kvc], w=[t_kcb])
            mk.op('pool', lambda: nc.gpsimd.tensor_copy(out=vcb[:], in_=kvc[:, :, 64:128]), r=[t_kvc], w=[t_vcb])
            if lat:
                cosb = ropeC[:, 0:1, :].to_broadcast([128, 6, 16])
                sinb = ropeC[:, 1:2, :].to_broadcast([128, 6, 16])
                mk.op('dve', lambda: nc.vector.tensor_mul(out=r1[:, 0:6, 0:16], in0=qc[:, :, 64:80], in1=cosb), r=[t_qc, t_rope], w=[t_r])
                mk.op('dve', lambda: nc.vector.tensor_mul(out=r2[:, 0:6, 0:16], in0=qc[:, :, 80:96], in1=sinb), r=[t_qc, t_rope], w=[t_r])
                mk.op('dve', lambda: nc.vector.tensor_sub(out=qcb[:, :, 64:80], in0=r1[:, 0:6, 0:16], in1=r2[:, 0:6, 0:16]), r=[t_r], w=[t_qcb])
                mk.op('dve', lambda: nc.vector.tensor_mul(out=r1[:, 0:6, 0:16], in0=qc[:, :, 80:96], in1=cosb), r=[t_qc, t_rope], w=[t_r])
                mk.op('dve', lambda: nc.vector.tensor_mul(out=r2[:, 0:6, 0:16], in0=qc[:, :, 64:80], in1=sinb), r=[t_qc, t_rope], w=[t_r])
                mk.op('dve', lambda: nc.vector.tensor_add(out=qcb[:, :, 80:96], in0=r1[:, 0:6, 0:16], in1=r2[:, 0:6, 0:16]), r=[t_r], w=[t_qcb])
                mk.op('dve', lambda: nc.vector.tensor_mul(out=r1[:, 0, 0:16], in0=ptC[:, 384:400], in1=ropeC[:, 0, :]), r=[t_ptC, t_rope], w=[t_r])
                mk.op('dve', lambda: nc.vector.tensor_mul(out=r2[:, 0, 0:16], in0=ptC[:, 400:416], in1=ropeC[:, 1, :]), r=[t_ptC, t_rope], w=[t_r])
                mk.op('dve', lambda: nc.vector.tensor_sub(out=kr[:, 0:16], in0=r1[:, 0, 0:16], in1=r2[:, 0, 0:16]), r=[t_r], w=[t_kr])
                mk.op('dve', lambda: nc.vector.tensor_mul(out=r1[:, 0, 0:16], in0=ptC[:, 400:416], in1=ropeC[:, 0, :]), r=[t_ptC, t_rope], w=[t_r])
                mk.op('dve', lambda: nc.vector.tensor_mul(out=r2[:, 0, 0:16], in0=ptC[:, 384:400], in1=ropeC[:, 1, :]), r=[t_ptC, t_rope], w=[t_r])
                mk.op('dve', lambda: nc.vector.tensor_add(out=kr[:, 16:32], in0=r1[:, 0, 0:16], in1=r2[:, 0, 0:16]), r=[t_r], w=[t_kr])
            else:
                mk.op('pool', lambda: nc.gpsimd.tensor_copy(out=qcb[:, :, 64:96], in_=qc[:, :, 64:96]), r=[t_qc], w=[t_qcb])
                mk.op('pool', lambda: nc.gpsimd.tensor_copy(out=kr[:], in_=ptC[:, 384:416]), r=[t_ptC], w=[t_kr])
            mk.op('pool', lambda: nc.gpsimd.tensor_copy(out=kcb[:, :, 64:96], in_=kr[:].unsqueeze(1).to_broadcast([128, 6, 32])), r=[t_kr], w=[t_kcb])
            mk.dma('sp', S.Vc[l][tok, :], vcb[:].rearrange("p h d -> p (h d)"), r=[t_vcb], w=[S.t_C], dsem=dout)
            for h in range(6):
                mk.op('pe', lambda h=h: nc.tensor.transpose(out=B0[0:96, h, :], in_=qcb[:, h, :], identity=K.identb[:]), r=[t_qcb, K.t_const], w=[tB[0]])
            mk.op('act', lambda: nc.scalar.copy(out=qcT[0:96, :, :], in_=B0[0:96, 0:6, :]), r=[tB[0]], w=[t_qcT])
            for h in range(6):
                mk.op('pe', lambda h=h: nc.tensor.transpose(out=B6[0:96, h, :], in_=kcb[:, h, :], identity=K.identb[:]), r=[t_kcb, K.t_const], w=[tB[6]])
            mk.op('act', lambda: nc.scalar.copy(out=kcT[0:96, :, :], in_=B6[0:96, 0:6, :]), r=[tB[6]], w=[t_kcT])
            mk.dma('sp', S.QTc[l].rearrange("h d t -> d h t")[:, :, tok], qcT[0:96, :, :], r=[t_qcT], w=[S.t_C], dsem=dout)
            mk.dma('sp', S.KTc[l].rearrange("h d t -> d h t")[:, :, tok], kcT[0:96, :, :], r=[t_kcT], w=[S.t_C], dsem=dout)
        mk.barrier()


def phase_attn(K, l, need_ctx):
    nc, mk = K.nc, K.mk
    S = K.S
    heads = []
    for h in range(6):
        heads.append((64, S.KTa[l][h // 3], S.Va[l][:, (h // 3) * 64:(h // 3 + 1) * 64], S.QTa[l][h], h * 64, S.t_A))
    for h in range(6):
        heads.append((96, S.KTc[l][h], S.Vc[l][:, h * 64:(h + 1) * 64], S.QTc[l][h], 640 + h * 64, S.t_C))
    with ExitStack() as st:
        KT = [sb(nc, st, "a_KT%d" % i, [128, NT], BF16) for i in range(2)]
        QT = [sb(nc, st, "a_QT%d" % i, [128, NT], BF16) for i in range(2)]
        Vx = [sb(nc, st, "a_V%d" % i, [128, NCH, 128], BF16) for i in range(2)]
        PT = [sb(nc, st, "a_PT%d" % i, [128, 512], BF16) for i in range(4)]
        rden = sb(nc, st, "a_rden", [64, 512], F32)
        ob = [sb(nc, st, "a_ob%d" % i, [64, 512], BF16) for i in range(2)]
        ST = [ps(nc, st, "a_ST%d" % i, [128, 512], F32) for i in range(6)]
        OD = [ps(nc, st, "a_OD%d" % i, [128, 512], F32) for i in range(2)]
        t_KV = [Tl(), Tl()]
        t_rden = Tl()
        t_PT = [Tl() for _ in range(4)]
        t_ob = [Tl(), Tl()]
        t_ST = [Tl() for _ in range(6)]
        t_OD = [Tl(), Tl()]
        dl = [mk.new_dsem(), mk.new_dsem()]
        dout = mk.new_dsem()
        for i in range(2):
            mk.op('pool', lambda i=i: nc.gpsimd.memset(Vx[i][:, :, 64:128], 1.0), w=[t_KV[i]])
            mk.op('pool', lambda i=i: nc.gpsimd.memset(KT[i][:], 0.0), w=[t_KV[i]])
            mk.op('pool', lambda i=i: nc.gpsimd.memset(QT[i][:], 0.0), w=[t_KV[i]])
        si = 0
        pi = 0
        bi = 0
        for hi, (dk, KTd, Vd, QTd, row0, t_src) in enumerate(heads):
            b = hi % 2
            mk.dma('sp', KT[b][0:dk, :], KTd, r=[t_src], w=[t_KV[b]], dsem=dl[b])
            mk.dma('sp', QT[b][0:dk, :], QTd, r=[t_src], w=[t_KV[b]], dsem=dl[b])
            mk.dma('sp', Vx[b][:, :, 0:64], Vd.rearrange("(c p) d -> p c d", p=128), r=[t_src], w=[t_KV[b]], dsem=dl[b])
            blocks = [(CTX + qb * 512, 512, 0, NCH) for qb in range(8)]
            if need_ctx:
                blocks.append((0, 256, 0, 2))
            for (q0, qn, k0, k1) in blocks:
                o = bi % 2
                bi += 1
                LOOK = 3
                slots = {}

                def issue_s(kc):
                    nonlocal si, pi
                    s_ = si % 6
                    si += 1
                    p_ = pi % 4
                    pi += 1
                    slots[kc] = (s_, p_)
                    mk.op('pe', lambda kc=kc, s_=s_, b=b, dk=dk, q0=q0, qn=qn: nc.tensor.matmul(
                        ST[s_][:, 0:qn], lhsT=KT[b][:, kc * 128:(kc + 1) * 128], rhs=QT[b][:, q0:q0 + qn], start=True, stop=True),
                        r=[t_KV[b]], w=[t_ST[s_]])
                    mk.op('act', lambda s_=s_, p_=p_, qn=qn: nc.scalar.activation(out=PT[p_][:, 0:qn], in_=ST[s_][:, 0:qn], func=AF.Exp),
                          r=[t_ST[s_]], w=[t_PT[p_]])

                for kc in range(k0, min(k1, k0 + LOOK)):
                    issue_s(kc)
                for kc in range(k0, k1):
                    if kc + LOOK < k1:
                        issue_s(kc + LOOK)
                    s_, p_ = slots.pop(kc)
                    mk.op('pe', lambda kc=kc, p_=p_, o=o, b=b, qn=qn, k0=k0, k1=k1: nc.tensor.matmul(
                        OD[o][:, 0:qn], lhsT=Vx[b][:, kc, :], rhs=PT[p_][:, 0:qn], start=(kc == k0), stop=(kc == k1 - 1)),
                        r=[t_KV[b], t_PT[p_]], w=[t_OD[o]])
                mk.op('dve', lambda o=o, qn=qn: nc.vector.reciprocal(out=rden[:, 0:qn], in_=OD[o][64:128, 0:qn]), r=[t_OD[o]], w=[t_rden])
                mk.op('dve', lambda o=o, qn=qn: nc.vector.tensor_mul(out=ob[o][:, 0:qn], in0=OD[o][0:64, 0:qn], in1=rden[:, 0:qn]),
                      r=[t_OD[o], t_rden], w=[t_ob[o]])
                mk.dma('sp', S.CT[l][row0:row0 + 64, q0:q0 + qn], ob[o][:, 0:qn], r=[t_ob[o]], w=[S.t_CT], dsem=dout)
        mk.barrier()


def hyena_filter_dev(K, l, st, n, zT_d, tcol0, hT, rl1):
    nc, mk = K.nc, K.mk
    TWO_PI = 2.0 * math.pi
    with ExitStack() as fs:
        zT = sb(nc, fs, "h_zT", [33, n], F32)
        tb = sb(nc, fs, "h_tb", [128, n], F32)
        w1 = sb(nc, fs, "h_w1", [33, 64], F32)
        w23 = sb(nc, fs, "h_w23", [64, 2, 64], F32)
        w4 = sb(nc, fs, "h_w4", [64, 512], F32)
        a_all = [sb(nc, fs, "h_a%d" % i, [64, 512], F32) for i in range(4)]
        wn_all = [sb(nc, fs, "h_wn%d" % i, [128, 512], F32) for i in range(2)]
        ki_all = [sb(nc, fs, "h_ki%d" % i, [64, 512], mybir.dt.int32) for i in range(2)]
        kf_all = [sb(nc, fs, "h_kf%d" % i, [64, 512], F32) for i in range(2)]
        t_ki_all, t_kf_all = [Tl(), Tl()], [Tl(), Tl()]
        t_a_all = [Tl() for _ in range(4)]
        t_wn_all = [Tl(), Tl()]
        fb = sb(nc, fs, "h_fb", [64, 3], F32)
        l1 = sb(nc, fs, "h_l1", [128, 4], F32)
        pf_all = [ps(nc, fs, "h_pf%d" % i, [64, 512], F32) for i in range(2)]
        p4 = [ps(nc, fs, "h_p4%d" % i, [128, 512], F32) for i in range(2)]
        t_in, t_fb, t_l1 = Tl(), Tl(), Tl()
        t_pf_all = [Tl(), Tl()]
        t_p4 = [Tl(), Tl()]
        d = mk.new_dsem()
        mk.dma('sp', zT[:], zT_d, w=[t_in], dsem=d)
        mk.dma('sp', tb[:], K.hy_tdel[:, tcol0:tcol0 + n], w=[t_in], dsem=d)
        mk.dma('sp', w1[:], K.hy_w1[l], w=[t_in], dsem=d)
        mk.dma('sp', w23[:, 0, :], K.hy_w2[l], w=[t_in], dsem=d)
        mk.dma('sp', w23[:, 1, :], K.hy_w3[l], w=[t_in], dsem=d)
        mk.dma('sp', w4[:], K.hy_w4[l], w=[t_in], dsem=d)
        hs = K.hs
        mk.op('dve', lambda: nc.vector.tensor_mul(out=fb[:], in0=hs[0:64, 27:30], in1=hs[0:64, 26:27].to_broadcast([64, 3])), r=[K.t_hs], w=[t_fb])
        nb = (n + 511) // 512
        for blk in range(nb):
            c0 = blk * 512
            cn_ = min(512, n - c0)
            ai = 0
            bb = blk % 2
            a = a_all[bb * 2:bb * 2 + 2]
            t_a = t_a_all[bb * 2:bb * 2 + 2]
            ki, kf, t_ki, t_kf = ki_all[bb], kf_all[bb], t_ki_all[bb], t_kf_all[bb]
            pf = pf_all[bb * 2:bb * 2 + 2] if len(pf_all) == 4 else pf_all
            t_pf = t_pf_all[bb * 2:bb * 2 + 2] if len(pf_all) == 4 else t_pf_all
            for layer in range(3):
                pfi = layer % 2
                if layer == 0:
                    mk.op('pe', lambda pfi=pfi, c0=c0, cn_=cn_: nc.tensor.matmul(pf[pfi][:, 0:cn_], lhsT=w1[:], rhs=zT[:, c0:c0 + cn_], start=True, stop=True),
                          r=[t_in], w=[t_pf[pfi]])
                else:
                    mk.op('pe', lambda pfi=pfi, layer=layer, cn_=cn_, ai=ai: nc.tensor.matmul(pf[pfi][:, 0:cn_], lhsT=w23[:, layer - 1, :], rhs=a[ai][:, 0:cn_],
                                                                                       start=True, stop=True), r=[t_in, t_a[ai]], w=[t_pf[pfi]])
                    ai = 1 - ai
                mk.op('dve', lambda pfi=pfi, ai=ai, layer=layer, cn_=cn_: nc.vector.tensor_scalar(
                    out=a[ai][:, 0:cn_], in0=pf[pfi][:, 0:cn_], scalar1=hs[0:64, 26:27], scalar2=fb[:, layer:layer + 1], op0=ALU.mult, op1=ALU.add),
                    r=[t_pf[pfi], t_fb, K.t_hs], w=[t_a[ai]])
                mk.op('dve', lambda ai=ai, cn_=cn_: nc.vector.tensor_scalar(out=ki[:, 0:cn_], in0=a[ai][:, 0:cn_], scalar1=1.0 / TWO_PI, scalar2=None, op0=ALU.mult),
                      r=[t_a[ai]], w=[t_ki])
                mk.op('dve', lambda cn_=cn_: nc.vector.tensor_copy(out=kf[:, 0:cn_], in_=ki[:, 0:cn_]), r=[t_ki], w=[t_kf])
                mk.op('dve', lambda ai=ai, cn_=cn_: nc.vector.scalar_tensor_tensor(out=a[ai][:, 0:cn_], in0=kf[:, 0:cn_], scalar=-TWO_PI, in1=a[ai][:, 0:cn_],
                                                                                  op0=ALU.mult, op1=ALU.add), r=[t_kf, t_a[ai]], w=[t_a[ai]])
                mk.op('act', lambda ai=ai, cn_=cn_: nc.scalar.activation(out=kf[:, 0:cn_], in_=a[ai][:, 0:cn_], func=AF.Sign), r=[t_a[ai]], w=[t_kf])
                mk.op('dve', lambda ai=ai, cn_=cn_: nc.vector.scalar_tensor_tensor(out=a[ai][:, 0:cn_], in0=kf[:, 0:cn_], scalar=-math.pi, in1=a[ai][:, 0:cn_],
                                                                                  op0=ALU.mult, op1=ALU.add), r=[t_kf, t_a[ai]], w=[t_a[ai]])
                mk.op('act', lambda ai=ai, cn_=cn_: nc.scalar.activation(out=a[ai][:, 0:cn_], in_=a[ai][:, 0:cn_], func=AF.Sin, scale=-1.0),
                      r=[t_a[ai]], w=[t_a[ai]])
            for c in range(4):
                pi_ = c % 2
                mk.op('pe', lambda c=c, pi_=pi_, ai=ai, cn_=cn_: nc.tensor.matmul(p4[pi_][:, 0:cn_], lhsT=w4[:, c * 128:(c + 1) * 128], rhs=a[ai][:, 0:cn_], start=True, stop=True),
                      r=[t_in, t_a[ai]], w=[t_p4[pi_]])
                wn, t_wn = wn_all[c % 2], t_wn_all[c % 2]
                mk.op('act', lambda c=c, c0=c0, cn_=cn_, wn=wn: nc.scalar.activation(out=wn[:, 0:cn_], in_=tb[:, c0:c0 + cn_], func=AF.Exp, scale=K.negdel[:, c:c + 1]),
                      r=[t_in, K.t_const], w=[t_wn])
                mk.op('dve', lambda c=c, pi_=pi_, c0=c0, cn_=cn_, wn=wn: nc.vector.tensor_mul(out=hT[:, c, c0:c0 + cn_], in0=p4[pi_][:, 0:cn_], in1=wn[:, 0:cn_]),
                      r=[t_p4[pi_], t_wn], w=[K.t_hT])
        for c in range(4):
            lo = 0 if c < 2 else 1
            mk.op('dve', lambda c=c, lo=lo: nc.vector.tensor_reduce(out=l1[:, c:c + 1], in_=hT[:, c, lo:n], axis=mybir.AxisListType.X, op=ALU.add,
                                                                    apply_absolute_value=True), r=[K.t_hT], w=[t_l1])
        mk.op('dve', lambda: nc.vector.tensor_add(out=rl1[:], in0=l1[:, 0:2], in1=l1[:, 2:4]), r=[t_l1], w=[K.t_rl1])
        mk.op('dve', lambda: nc.vector.reciprocal(out=rl1[:], in_=rl1[:]), r=[K.t_rl1], w=[K.t_rl1])
        mk.barrier()


def hyena_conv_dev(K, l, st, n, off, hT, rl1):
    nc, mk = K.nc, K.mk
    S = K.S
    hs = K.hs
    with ExitStack() as cs:
        pp = sb(nc, cs, "h_pp", [128, n + 2], F32)
        u = sb(nc, cs, "h_u", [128, n], F32)
        x1 = sb(nc, cs, "h_x1", [128, n], F32)
        x0 = sb(nc, cs, "h_x0", [128, n], F32)
        ya = [sb(nc, cs, "h_y%d" % i, [128, n], F32) for i in range(2)]
        ob = sb(nc, cs, "h_ob", [128, n], BF16)
        t_pp, t_u, t_x1, t_x0, t_ob = Tl(), Tl(), Tl(), Tl(), Tl()
        t_y = [Tl(), Tl()]
        d = mk.new_dsem()
        dout = mk.new_dsem()
        mk.op('pool', lambda: nc.gpsimd.memset(pp[:, 0:1], 0.0), w=[t_pp])
        mk.op('pool', lambda: nc.gpsimd.memset(pp[:, n + 1:n + 2], 0.0), w=[t_pp])
        for cc in range(2):
            for which, dst, t_dst in ((0, u, t_u), (1, x1, t_x1), (2, x0, t_x0)):
                chn = which * 2 + cc
                mk.dma('sp', pp[:, 1:n + 1], S.PBT[l][chn * 128:(chn + 1) * 128, off:off + n], r=[S.t_PBT], w=[t_pp], dsem=d)
                mk.op('dve', lambda chn=chn, dst=dst: nc.vector.tensor_scalar(out=dst[:], in0=pp[:, 0:n], scalar1=hs[:, chn:chn + 1], scalar2=hs[:, 18 + chn:19 + chn],
                                                                            op0=ALU.mult, op1=ALU.add), r=[t_pp, K.t_hs], w=[t_dst])
                for k in (1, 2):
                    mk.op('dve', lambda chn=chn, dst=dst, k=k: nc.vector.scalar_tensor_tensor(out=dst[:], in0=pp[:, k:n + k], scalar=hs[:, k * 6 + chn:k * 6 + chn + 1],
                                                                                              in1=dst[:], op0=ALU.mult, op1=ALU.add), r=[t_pp, K.t_hs, t_dst], w=[t_dst])
            mk.op('dve', lambda: nc.vector.tensor_mul(out=u[:], in0=u[:], in1=x1[:]), r=[t_u, t_x1], w=[t_u])
            mk.op('pool', lambda: nc.gpsimd.memset(ya[0][:], 0.0), w=[t_y[0]])
            mk.op('pool', lambda: nc.gpsimd.memset(ya[1][:], 0.0), w=[t_y[1]])
            k = 0
            for lag in range(n):
                i = k % 2
                k += 1
                mk.op('dve', lambda lag=lag, i=i, cc=cc: nc.vector.scalar_tensor_tensor(out=ya[i][:, lag:n], in0=u[:, 0:n - lag], scalar=hT[:, cc, lag:lag + 1],
                                                                                        in1=ya[i][:, lag:n], op0=ALU.mult, op1=ALU.add), r=[t_u, K.t_hT, t_y[i]], w=[t_y[i]])
            for lag in range(1, n):
                i = k % 2
                k += 1
                mk.op('dve', lambda lag=lag, i=i, cc=cc: nc.vector.scalar_tensor_tensor(out=ya[i][:, 0:n - lag], in0=u[:, lag:n], scalar=hT[:, 2 + cc, lag:lag + 1],
                                                                                        in1=ya[i][:, 0:n - lag], op0=ALU.mult, op1=ALU.add), r=[t_u, K.t_hT, t_y[i]], w=[t_y[i]])
            mk.op('dve', lambda: nc.vector.tensor_add(out=ya[0][:], in0=ya[0][:], in1=ya[1][:]), r=[t_y[0], t_y[1]], w=[t_y[0]])
            mk.op('dve', lambda cc=cc: nc.vector.tensor_scalar_mul(out=ya[0][:], in0=ya[0][:], scalar1=rl1[:, cc:cc + 1]), r=[t_y[0], K.t_rl1], w=[t_y[0]])
            mk.op('dve', lambda cc=cc: nc.vector.scalar_tensor_tensor(out=ya[0][:], in0=u[:], scalar=hs[:, 24 + cc:25 + cc], in1=ya[0][:], op0=ALU.mult, op1=ALU.add),
                  r=[t_u, t_y[0], K.t_hs], w=[t_y[0]])
            mk.op('dve', lambda: nc.vector.tensor_mul(out=ob[:], in0=ya[0][:], in1=x0[:]), r=[t_y[0], t_x0], w=[t_ob])
            mk.dma('sp', S.CT[l][384 + cc * 128:384 + (cc + 1) * 128, off:off + n], ob[:], r=[t_ob], w=[S.t_CT], dsem=dout)
        mk.barrier()


def fft_setup(K, st):
    nc, mk = K.nc, K.mk
    F = Ctx()
    F.F1 = sb(nc, st, "x_F1", [64, 256], BF16)
    F.GB = [sb(nc, st, "x_GB%d" % i, [128, 24, 128], BF16) for i in range(2)]
    F.Us = sb(nc, st, "x_Us", [64, 128, 64], BF16)
    F.AC = sb(nc, st, "x_AC", [128, 16384], BF16)
    F.stg = sb(nc, st, "x_stg", [128, SEQ], BF16)
    F.t_stg = Tl()
    F.P1 = [ps(nc, st, "x_P1%d" % i, [128, 2, 256], F32) for i in range(2)]
    F.P2 = [ps(nc, st, "x_P2%d" % i, [128, 4, 2, 64], F32) for i in range(2)]
    F.t_tab, F.t_Us, F.t_AC, F.t_Y = Tl(), Tl(), Tl(), Tl()
    F.t_GB = [Tl(), Tl()]
    F.t_P1 = [Tl(), Tl()]
    F.t_P2 = [Tl(), Tl()]
    F.d_tab = mk.new_dsem()
    F.d_gb = [mk.new_dsem(), mk.new_dsem()]
    F.d_us = mk.new_dsem()
    F.d_x = [mk.new_dsem() for _ in range(4)]
    F.d_st = mk.new_dsem()
    F.gbi = 0
    F.xi = 0
    F.p2i = 0
    mk.dma('pool', F.F1[:], K.fft_F1[:, :], w=[F.t_tab], dsem=F.d_tab)
    for i in range(2):
        mk.op('pool', lambda i=i: nc.gpsimd.memset(F.GB[i][:], 0.0), w=[F.t_GB[i]])
    return F


def fft_alloc_work(K, F, st, extra_p2):
    nc = K.nc
    F.xs = [sb(nc, st, "x_xs%d" % i, [128, 4, 2, 64], F32) for i in range(4)]
    F.hfs = [sb(nc, st, "x_hf%d" % i, [128, 4, 2, 64], F32) for i in range(4)]
    F.tmp = [sb(nc, st, "x_tmp%d" % i, [128, 4, 64], F32) for i in range(4)]
    F.t_xs = [Tl() for _ in range(4)]
    F.t_hfs = [Tl() for _ in range(4)]
    F.t_tmp = [Tl() for _ in range(4)]
    F.P2 = F.P2[:2] + [ps(nc, st, "x_P2x%d" % i, [128, 4, 2, 64], F32) for i in range(extra_p2)]
    F.t_P2 = F.t_P2[:2] + [Tl() for _ in range(extra_p2)]


def fft_setup_inv(K, F, st):
    nc, mk = K.nc, K.mk
    F.W3 = sb(nc, st, "x_W3", [128, 512], BF16)
    F.F4 = sb(nc, st, "x_F4", [128, 64, 2, 64], BF16)
    F.Ysb = sb(nc, st, "x_Y", [128, 2, 64, 128], BF16)
    mk.dma('pool', F.W3[:], K.fft_W3[:, :], w=[F.t_tab], dsem=F.d_tab)
    mk.dma('pool', F.F4[:].rearrange("p a b c -> p (a b c)"), K.fft_F4[:, :], w=[F.t_tab], dsem=F.d_tab)


def fft_fwd(K, F, src, t_src, consumer):
    nc, mk = K.nc, K.mk
    A = F.AC[:].rearrange("p (ri k1 pr) -> p ri k1 pr", ri=2, k1=128)
    srcv = src.rearrange("c (s1 s2) -> s1 c s2", s2=64)
    for cb in range(16):
        mk.dma('sp', F.Us[:, cb * 8:(cb + 1) * 8, :], srcv[:, cb * 8:(cb + 1) * 8, :], r=[t_src], w=[F.t_Us], dsem=F.d_us)
    for pp2 in range(32):
        b = pp2 % 2
        for j in range(2):
            pr = pp2 * 2 + j
            mk.op('pe', lambda b=b, j=j, pr=pr: nc.tensor.matmul(F.P1[b][:, j, :], lhsT=F.Us[:, 2 * pr:2 * pr + 2, :].rearrange("p c s -> p (c s)"),
                                                                rhs=F.F1[:], start=True, stop=True), r=[F.t_Us, F.t_tab], w=[F.t_P1[b]])
        eng = 'act' if pp2 % 2 == 0 else 'dve'
        outv = A[:, :, :, 2 * pp2:2 * pp2 + 2].rearrange("p ri k1 pr -> p pr ri k1")
        inv = F.P1[b][:].rearrange("p pr (ri k1) -> p pr ri k1", ri=2)
        if eng == 'act':
            mk.op('act', lambda outv=outv, inv=inv: nc.scalar.copy(out=outv, in_=inv), r=[F.t_P1[b]], w=[F.t_AC])
        else:
            mk.op('dve', lambda outv=outv, inv=inv: nc.vector.tensor_copy(out=outv, in_=inv), r=[F.t_P1[b]], w=[F.t_AC])
    for kc in range(16):
        gb = F.gbi % 2
        F.gbi += 1
        if not K.gb_cached[kc]:
            srcg = K.fft_G[:, kc * 8:(kc + 1) * 8, :, :].rearrange("p a b c -> p (a b) c")
            mk.dma('pool', F.GB[gb][0:64, :, 0:64], srcg, w=[F.t_GB[gb]], dsem=F.d_gb[gb])
            mk.dma('pool', F.GB[gb][64:128, :, 64:128], srcg, w=[F.t_GB[gb]], dsem=F.d_gb[gb])
            mk.dma('sp', K.S.GBd[kc], F.GB[gb][:], r=[F.t_GB[gb]], w=[K.S.t_GBd], dsem=F.d_st)
            K.gb_cached[kc] = True
        else:
            mk.dma('sp', F.GB[gb][:], K.S.GBd[kc], r=[K.S.t_GBd], w=[F.t_GB[gb]], dsem=F.d_gb[gb])
        for half in range(2):
            pb = F.p2i % len(F.P2)
            F.p2i += 1
            for jj in range(4):
                kk = half * 4 + jj
                k1 = kc * 8 + kk
                Gre, Gim, nGim = F.GB[gb][:, kk * 3 + 0, :], F.GB[gb][:, kk * 3 + 1, :], F.GB[gb][:, kk * 3 + 2, :]
                Are, Aim = A[:, 0, k1, :], A[:, 1, k1, :]
                for (ri, l0, r0, l1, r1) in ((0, Gre, Are, nGim, Aim), (1, Gim, Are, Gre, Aim)):
                    mk.op('pe', lambda pb=pb, jj=jj, ri=ri, l0=l0, r0=r0: nc.tensor.matmul(F.P2[pb][:, jj, ri, :], lhsT=l0, rhs=r0, start=True, stop=False),
                          r=[F.t_GB[gb], F.t_AC], w=[F.t_P2[pb]])
                    mk.op('pe', lambda pb=pb, jj=jj, ri=ri, l1=l1, r1=r1: nc.tensor.matmul(F.P2[pb][:, jj, ri, :], lhsT=l1, rhs=r1, start=False, stop=True),
                          r=[F.t_GB[gb], F.t_AC], w=[F.t_P2[pb]])
            consumer(kc * 8 + half * 4, F.P2[pb], F.t_P2[pb])


def fft_inv(K, F, ysb, t_ysb):
    nc, mk = K.nc, K.mk
    C = F.AC[:].rearrange("p (ri t2 c) -> p ri t2 c", ri=2, t2=64)
    P3, tP3 = F.P1, F.t_P1
    P4 = [F.P2[i][:].rearrange("p a b c -> p (a b) c") for i in range(2)]
    tP4 = F.t_P2[0:2]
    for pp2 in range(32):
        b = pp2 % 2
        for j in range(2):
            pr = pp2 * 2 + j
            mk.op('pe', lambda b=b, j=j, pr=pr: nc.tensor.matmul(P3[b][:, j, :], lhsT=F.Ysb[:, 0, pr, :], rhs=F.W3[:, 0:256], start=True, stop=False),
                  r=[F.t_Y, F.t_tab], w=[tP3[b]])
            mk.op('pe', lambda b=b, j=j, pr=pr: nc.tensor.matmul(P3[b][:, j, :], lhsT=F.Ysb[:, 1, pr, :], rhs=F.W3[:, 256:512], start=False, stop=True),
                  r=[F.t_Y, F.t_tab], w=[tP3[b]])
        for j in range(2):
            outv = C[:, :, :, 4 * pp2 + 2 * j:4 * pp2 + 2 * j + 2].rearrange("p ri t2 cc -> p cc ri t2")
            inv = P3[b][:, j, :].rearrange("p (cc ri t2) -> p cc ri t2", cc=2, ri=2)
            if j == 0:
                mk.op('act', lambda outv=outv, inv=inv: nc.scalar.copy(out=outv, in_=inv), r=[tP3[b]], w=[F.t_AC])
            else:
                mk.op('dve', lambda outv=outv, inv=inv: nc.vector.tensor_copy(out=outv, in_=inv), r=[tP3[b]], w=[F.t_AC])
    yv = ysb[:].rearrange("p (t1 t2) -> p t1 t2", t2=64)
    for t8 in range(8):
        b = t8 % 2
        for j in range(8):
            t2 = t8 * 8 + j
            mk.op('pe', lambda b=b, j=j, t2=t2: nc.tensor.matmul(P4[b][:, j, :], lhsT=C[:, 0, t2, :], rhs=F.F4[:, t2, 0, :], start=True, stop=False),
                  r=[F.t_AC, F.t_tab], w=[tP4[b]])
            mk.op('pe', lambda b=b, j=j, t2=t2: nc.tensor.matmul(P4[b][:, j, :], lhsT=C[:, 1, t2, :], rhs=F.F4[:, t2, 1, :], start=False, stop=True),
                  r=[F.t_AC, F.t_tab], w=[tP4[b]])
        outv = yv[:, :, t8 * 8:(t8 + 1) * 8].rearrange("p t1 t2 -> p t2 t1")
        if t8 % 2 == 0:
            mk.op('act', lambda outv=outv, b=b: nc.scalar.copy(out=outv, in_=P4[b]), r=[tP4[b]], w=[t_ysb])
        else:
            mk.op('dve', lambda outv=outv, b=b: nc.vector.tensor_copy(out=outv, in_=P4[b]), r=[tP4[b]], w=[t_ysb])


def hyena_lat_fft_filter(K, l, F, hT):
    nc, mk = K.nc, K.mk
    S = K.S
    n = SEQ
    stg, t_stg = F.stg, F.t_stg
    dst = mk.new_dsem()
    with ExitStack() as ws:
        fft_alloc_work(K, F, ws, 4)
        for c in range(4):
            mk.op('pool', lambda c=c: nc.gpsimd.tensor_copy(out=stg[:], in_=hT[:, c, :]), r=[K.t_hT], w=[t_stg])
            if c >= 2:
                mk.op('pool', lambda: nc.gpsimd.memset(stg[:, 0:1], 0.0), w=[t_stg])
            mk.dma('sp', S.HTd[c * 128:(c + 1) * 128, :], stg[:], r=[t_stg], w=[S.t_HTd], dsem=dst)
        for g in range(2):
            def cons_a(k1_0, P2, tP2, g=g):
                i = F.xi % 4
                F.xi += 1
                mk.op('act', lambda: nc.scalar.copy(out=F.xs[i][:], in_=P2[:]), r=[tP2], w=[F.t_xs[i]])
                mk.dma('pool', S.XF[g][:, k1_0:k1_0 + 4, :, :], F.xs[i][:], r=[F.t_xs[i]], w=[S.t_XF], dsem=F.d_st)

            def cons_b(k1_0, P2, tP2, g=g):
                i = F.xi % 4
                F.xi += 1
                mk.dma('sp', F.hfs[i][:], S.XF[g][:, k1_0:k1_0 + 4, :, :], r=[S.t_XF], w=[F.t_hfs[i]], dsem=F.d_x[i])
                mk.op('dve', lambda: nc.vector.tensor_add(out=F.xs[i][:, :, 0, :], in0=P2[:, :, 0, :], in1=F.hfs[i][:, :, 0, :]), r=[tP2, F.t_hfs[i]], w=[F.t_xs[i]])
                mk.op('dve', lambda: nc.vector.tensor_sub(out=F.xs[i][:, :, 1, :], in0=F.hfs[i][:, :, 1, :], in1=P2[:, :, 1, :]), r=[tP2, F.t_hfs[i]], w=[F.t_xs[i]])
                mk.dma('pool', S.HF[g][:, k1_0:k1_0 + 4, :, :], F.xs[i][:], r=[F.t_xs[i]], w=[S.t_HF], dsem=F.d_st)

            fft_fwd(K, F, S.HTd[g * 128:(g + 1) * 128, :], S.t_HTd, cons_a)
            fft_fwd(K, F, S.HTd[256 + g * 128:256 + (g + 1) * 128, :], S.t_HTd, cons_b)
        mk.barrier()


def hyena_lat_fft_u(K, l, F, fs, rl1):
    nc, mk = K.nc, K.mk
    S = K.S
    hs = K.hs
    n, off = SEQ, CTX
    stg, t_stg = F.stg, F.t_stg
    dst = mk.new_dsem()
    fft_alloc_work(K, F, fs, 4)
    if True:
        pp = sb(nc, fs, "h_pp", [128, n + 2], F32)
        u = sb(nc, fs, "h_u", [128, n], F32)
        x0 = sb(nc, fs, "h_x0", [128, n], F32)
        ysb = sb(nc, fs, "h_ysb", [128, n], F32)
        t_pp, t_u, t_x0, t_ysb = Tl(), Tl(), Tl(), Tl()
        d = mk.new_dsem()
        dout = mk.new_dsem()
        mk.op('pool', lambda: nc.gpsimd.memset(pp[:, 0:1], 0.0), w=[t_pp])
        mk.op('pool', lambda: nc.gpsimd.memset(pp[:, n + 1:n + 2], 0.0), w=[t_pp])
        Yv = F.Ysb
        for cc in range(2):
            for which, dstt, t_dst in ((0, u, t_u), (1, ysb, t_ysb), (2, x0, t_x0)):
                chn = which * 2 + cc
                mk.dma('sp', pp[:, 1:n + 1], S.PBT[l][chn * 128:(chn + 1) * 128, off:off + n], r=[S.t_PBT], w=[t_pp], dsem=d)
                mk.op('dve', lambda chn=chn, dstt=dstt: nc.vector.tensor_scalar(out=dstt[:], in0=pp[:, 0:n], scalar1=hs[:, chn:chn + 1], scalar2=hs[:, 18 + chn:19 + chn],
                                                                              op0=ALU.mult, op1=ALU.add), r=[t_pp, K.t_hs], w=[t_dst])
                for k in (1, 2):
                    mk.op('dve', lambda chn=chn, dstt=dstt, k=k: nc.vector.scalar_tensor_tensor(out=dstt[:], in0=pp[:, k:n + k], scalar=hs[:, k * 6 + chn:k * 6 + chn + 1],
                                                                                                in1=dstt[:], op0=ALU.mult, op1=ALU.add), r=[t_pp, K.t_hs, t_dst], w=[t_dst])
            mk.op('dve', lambda: nc.vector.tensor_mul(out=u[:], in0=u[:], in1=ysb[:]), r=[t_u, t_ysb], w=[t_u])
            mk.op('pool', lambda: nc.gpsimd.tensor_copy(out=stg[:], in_=u[:]), r=[t_u], w=[t_stg])
            mk.dma('sp', S.UTd[cc * 128:(cc + 1) * 128, :], stg[:], r=[t_stg], w=[S.t_UTd], dsem=dst)

            def cons_c(k1_0, P2, tP2, cc=cc):
                i = F.xi % 4
                F.xi += 1
                mk.dma('sp', F.hfs[i][:], S.HF[cc][:, k1_0:k1_0 + 4, :, :], r=[S.t_HF], w=[F.t_hfs[i]], dsem=F.d_x[i])
                mk.op('act', lambda: nc.scalar.copy(out=F.xs[i][:], in_=P2[:]), r=[tP2], w=[F.t_xs[i]])
                Xre, Xim = F.xs[i][:, :, 0, :], F.xs[i][:, :, 1, :]
                Hre, Him = F.hfs[i][:, :, 0, :], F.hfs[i][:, :, 1, :]
                ta, tb, tc_, td = F.tmp
                yre = Yv[:, 0, :, k1_0:k1_0 + 4].rearrange("p pr k -> p k pr")
                yim = Yv[:, 1, :, k1_0:k1_0 + 4].rearrange("p pr k -> p k pr")
                mk.op('dve', lambda: nc.vector.tensor_mul(out=ta[:], in0=Xre, in1=Hre), r=[F.t_xs[i], F.t_hfs[i]], w=[F.t_tmp[0]])
                mk.op('dve', lambda: nc.vector.tensor_mul(out=tb[:], in0=Xim, in1=Him), r=[F.t_xs[i], F.t_hfs[i]], w=[F.t_tmp[1]])
                mk.op('dve', lambda: nc.vector.tensor_sub(out=yre, in0=ta[:], in1=tb[:]), r=[F.t_tmp[0], F.t_tmp[1]], w=[F.t_Y])
                mk.op('pool', lambda: nc.gpsimd.tensor_mul(out=tc_[:], in0=Xre, in1=Him), r=[F.t_xs[i], F.t_hfs[i]], w=[F.t_tmp[2]])
                mk.op('pool', lambda: nc.gpsimd.tensor_mul(out=td[:], in0=Xim, in1=Hre), r=[F.t_xs[i], F.t_hfs[i]], w=[F.t_tmp[3]])
                mk.op('pool', lambda: nc.gpsimd.tensor_add(out=yim, in0=tc_[:], in1=td[:]), r=[F.t_tmp[2], F.t_tmp[3]], w=[F.t_Y])

            fft_fwd(K, F, S.UTd[cc * 128:(cc + 1) * 128, :], S.t_UTd, cons_c)
            fft_inv(K, F, ysb, t_ysb)
            mk.op('dve', lambda cc=cc: nc.vector.tensor_scalar_mul(out=ysb[:], in0=ysb[:], scalar1=rl1[:, cc:cc + 1]), r=[t_ysb, K.t_rl1], w=[t_ysb])
            mk.op('dve', lambda cc=cc: nc.vector.scalar_tensor_tensor(out=ysb[:], in0=u[:], scalar=hs[:, 24 + cc:25 + cc], in1=ysb[:], op0=ALU.mult, op1=ALU.add),
                  r=[t_u, t_ysb, K.t_hs], w=[t_ysb])
            mk.op('dve', lambda: nc.vector.tensor_mul(out=stg[:], in0=ysb[:], in1=x0[:]), r=[t_ysb, t_x0], w=[t_stg])
            mk.dma('sp', S.CT[l][384 + cc * 128:384 + (cc + 1) * 128, off:off + n], stg[:], r=[t_stg], w=[S.t_CT], dsem=dout)
        mk.barrier()


def phase_hyena(K, l, need_ctx):
    nc, mk = K.nc, K.mk
    with ExitStack() as st:
        K.hs = sb(nc, st, "h_hs", [128, 32], F32)
        K.negpi = sb(nc, st, "h_negpi", [128, 1], F32)
        K.negdel = sb(nc, st, "h_negdel", [128, 4], F32)
        rl1 = sb(nc, st, "h_rl1", [128, 2], F32)
        K.t_hs, K.t_hT, K.t_rl1 = Tl(), Tl(), Tl()
        d = mk.new_dsem()
        mk.dma('sp', K.hs[:], K.hy_small[l], w=[K.t_hs], dsem=d)
        mk.dma('sp', K.negdel[:], K.hy_tdel[:, 0:4], w=[K.t_const], dsem=d)
        mk.op('dve', lambda: nc.vector.memset(K.negpi[:], -math.pi), w=[K.t_const])
        mk.op('dve', lambda: nc.vector.tensor_scalar_mul(out=K.negdel[:], in0=K.negdel[:], scalar1=-1.0), r=[K.t_const], w=[K.t_const])
        seqs = [(SEQ, CTX, K.zT_lat, 4)]
        if need_ctx:
            seqs.append((CTX, 0, K.zT_ctx, 4 + SEQ))
        for (n, off, zT_d, tcol0) in seqs:
            if n == SEQ and USE_FFT:
                with ExitStack() as s1:
                    F = fft_setup(K, s1)
                    with ExitStack() as s2:
                        hT = sb(nc, s2, "h_hT", [128, 4, n], F32)
                        hyena_filter_dev(K, l, s2, n, zT_d, tcol0, hT, rl1)
                        hyena_lat_fft_filter(K, l, F, hT)
                    fft_setup_inv(K, F, s1)
                    hyena_lat_fft_u(K, l, F, s1, rl1)
            else:
                with ExitStack() as s2:
                    hT = sb(nc, s2, "h_hT", [128, 4, n], F32)
                    hyena_filter_dev(K, l, s2, n, zT_d, tcol0, hT, rl1)
                    hyena_conv_dev(K, l, s2, n, off, hT, rl1)
        mk.barrier()


def phase_mixout(K, l, Xres, t_xres, Xout, t_xout, first_chunk=0):
    nc, mk = K.nc, K.mk
    S = K.S
    with ExitStack() as st:
        wo = sb(nc, st, "o_wo", [128, 8, D], BF16)
        modv = sb(nc, st, "o_mod", [128, 3, D], F32)
        xc = [sb(nc, st, "o_xc%d" % i, [128, D], F32) for i in range(2)]
        cT = [sb(nc, st, "o_cT%d" % i, [128, 8, 128], BF16) for i in range(2)]
        tt = sb(nc, st, "o_tt", [128, D], F32)
        st6 = sb(nc, st, "o_st6", [128, 2, 6], F32)
        mv = sb(nc, st, "o_mv", [128, 2], F32)
        rstd = sb(nc, st, "o_rstd", [128, 1], F32)
        pm = [ps(nc, st, "o_pm%d" % i, [128, 512], F32) for i in range(4)]
        t_wo, t_mod, t_tt, t_st = Tl(), Tl(), Tl(), Tl()
        t_xc = [Tl(), Tl()]
        t_cT = [Tl(), Tl()]
        t_pm = [Tl() for _ in range(4)]
        dws, dmod, dout = mk.new_dsem(), mk.new_dsem(), mk.new_dsem()
        dx = [mk.new_dsem(), mk.new_dsem()]
        for kc in range(8):
            mk.dma('pool', wo[:, kc, :], K.w_out[l][kc * 128:(kc + 1) * 128, :], w=[t_wo], dsem=dws)
        cur_g = None
        for ch in range(first_chunk, NCH):
            g = 1 if ch < 2 else 0
            if g != cur_g:
                cur_g = g
                mrow = K.mod[l][g]
                for i, src in enumerate([mrow[5 * D:6 * D], K.ln_g[l, 1], K.ln_b[l, 1]]):
                    mk.dma('sp', modv[:, i, :], src.partition_broadcast(128), r=[K.t_mod[l]], w=[t_mod], dsem=dmod)
            xi = ch % 2
            tok = slice(ch * 128, (ch + 1) * 128)

            def issue_loads(c2):
                x2 = c2 % 2
                tk = slice(c2 * 128, (c2 + 1) * 128)
                mk.dma('sp', xc[x2][:], Xres[tk, :], r=[t_xres], w=[t_xc[x2]], dsem=dx[x2])
                mk.dma('sp', cT[x2][:], S.CT[l].rearrange("(c p) t -> p c t", p=128)[:, :, tk], r=[S.t_CT], w=[t_cT[x2]], dsem=dx[x2])

            if ch == first_chunk:
                issue_loads(ch)
            if ch + 1 < NCH:
                issue_loads(ch + 1)
            for hf in range(2):
                pi = (ch % 2) * 2 + hf
                for kc in range(8):
                    mk.op('pe', lambda kc=kc, hf=hf, pi=pi, xi=xi: nc.tensor.matmul(pm[pi][:], lhsT=cT[xi][:, kc, :], rhs=wo[:, kc, hf * 512:(hf + 1) * 512],
                                                                                   start=(kc == 0), stop=(kc == 7)), r=[t_cT[xi], t_wo], w=[t_pm[pi]])
                mk.op('dve', lambda hf=hf, pi=pi: nc.vector.tensor_mul(out=tt[:, hf * 512:(hf + 1) * 512], in0=pm[pi][:], in1=modv[:, 0, hf * 512:(hf + 1) * 512]),
                      r=[t_pm[pi], t_mod], w=[t_tt])
            mk.op('dve', lambda xi=xi: nc.vector.scalar_tensor_tensor(out=xc[xi][:], in0=xc[xi][:], scalar=ALPHA, in1=tt[:], op0=ALU.mult, op1=ALU.add),
                  r=[t_tt, t_xc[xi]], w=[t_xc[xi]])
            ln_stats(K, xc[xi], t_xc[xi], st6, mv, rstd, t_st)
            mk.op('dve', lambda xi=xi: nc.vector.scalar_tensor_tensor(out=xc[xi][:], in0=xc[xi][:], scalar=mv[:, 0:1], in1=modv[:, 1, :],
                                                                      op0=ALU.subtract, op1=ALU.mult), r=[t_xc[xi], t_st, t_mod], w=[t_xc[xi]])
            mk.op('dve', lambda xi=xi: nc.vector.scalar_tensor_tensor(out=xc[xi][:], in0=xc[xi][:], scalar=rstd[:, 0:1], in1=modv[:, 2, :],
                                                                      op0=ALU.mult, op1=ALU.add), r=[t_xc[xi], t_st, t_mod], w=[t_xc[xi]])
            mk.dma('sp', Xout[tok, :], xc[xi][:], r=[t_xc[xi]], w=[t_xout], dsem=dout)
        mk.barrier()


def build(stop_after=None, dbg=()):
    nc = bass.Bass("TRN2", target_bir_lowering=False)
    K = Ctx()
    K.nc = nc
    K.mk = MK(nc)
    mk = K.mk
    DECLARED.clear()

    def din(name, shape):
        DECLARED.add(name)
        return nc.dram_tensor(name, shape, F32, kind="ExternalInput").ap()

    K.xin = din("xin", [NT, D])
    K.cc_in = din("cc", [128, 16])
    K.ada_w = din("ada_w", [DEPTH, D, NMOD * D])
    K.ada_b = din("ada_b", [DEPTH, NMOD * D])
    K.w_gu = [din("ffn1_w_gu", [DEPTH, D, 2 * DFF]), din("ffn2_w_gu", [DEPTH, D, 2 * DFF])]
    K.w_dn = [din("ffn1_w_down", [DEPTH, DFF, D]), din("ffn2_w_down", [DEPTH, DFF, D])]
    K.ln_g = din("ln_g", [DEPTH, 3, D])
    K.ln_b = din("ln_b", [DEPTH, 3, D])
    K.w_in = din("w_in", [DEPTH, D, P_IN])
    K.w_out = din("w_out", [DEPTH, D, D])
    K.a_qn = din("a_q_norm", [DEPTH, 64])
    K.a_kn = din("a_k_norm", [DEPTH, 64])
    K.c_qn = din("mla_q_norm", [DEPTH, 256])
    K.c_kvn = din("mla_kv_norm", [DEPTH, 128])
    K.w_uq = din("mla_w_uq", [DEPTH, 256, 576])
    K.w_ukv = din("mla_w_ukv", [DEPTH, 128, 768])
    K.hy_conv_w = din("hy_conv_w", [DEPTH, 3, 768])
    K.hy_conv_b = din("hy_conv_b", [DEPTH, 768])
    K.hy_w1 = din("hy_f_w1", [DEPTH, 33, 64])
    K.hy_b1 = din("hy_f_b1", [DEPTH, 64])
    K.hy_w2 = din("hy_f_w2", [DEPTH, 64, 64])
    K.hy_b2 = din("hy_f_b2", [DEPTH, 64])
    K.hy_w3 = din("hy_f_w3", [DEPTH, 64, 64])
    K.hy_b3 = din("hy_f_b3", [DEPTH, 64])
    K.hy_w4 = din("hy_f_w4", [DEPTH, 64, 512])
    K.hy_freq = din("hy_f_freq", [DEPTH, 64])
    K.hy_bias = din("hy_bias", [DEPTH, 256])
    K.ident_in = din("ident", [128, 128])
    K.hy_small = din("hy_small", [DEPTH, 128, 32])
    K.ropeA_in = din("ropeA", [SEQ, 64])
    K.ropeC_in = din("ropeC", [SEQ, 32])
    K.zT_lat = din("zT_lat", [33, SEQ])
    K.zT_ctx = din("zT_ctx", [33, CTX])
    K.hy_tdel = din("hy_tdel", [128, 4 + SEQ + CTX])
    K.fft_F1 = din("fft_F1", [64, 256])
    K.fft_G = din("fft_G", [64, 128, 3, 64])
    K.fft_W3 = din("fft_W3", [128, 512])
    K.fft_F4 = din("fft_F4", [128, 8192])
    K.out = nc.dram_tensor("out", [SEQ, D], F32, kind="ExternalOutput").ap()
    K.mod = nc.dram_tensor("mod", [DEPTH, 2, NMOD * D], F32).ap()
    K.t_mod = [Tl(), Tl()]
    XA = nc.dram_tensor("XA", [NT, D], F32).ap()
    XB = nc.dram_tensor("XB", [NT, D], F32).ap()
    XC = nc.dram_tensor("XC", [NT, D], F32).ap()
    S = Ctx()
    K.S = S
    S.QTa = nc.dram_tensor("QTa", [DEPTH, 6, 64, NT], BF16).ap()
    S.KTa = nc.dram_tensor("KTa", [DEPTH, 2, 64, NT], BF16).ap()
    S.Va = nc.dram_tensor("Va", [DEPTH, NT, 128], BF16).ap()
    S.QTc = nc.dram_tensor("QTc", [DEPTH, 6, 96, NT], BF16).ap()
    S.KTc = nc.dram_tensor("KTc", [DEPTH, 6, 96, NT], BF16).ap()
    S.Vc = nc.dram_tensor("Vc", [DEPTH, NT, 384], BF16).ap()
    S.PBT = nc.dram_tensor("PBT", [DEPTH, 768, NT], F32).ap()
    S.CT = nc.dram_tensor("CT", [DEPTH, D, NT], BF16).ap()
    S.t_A, S.t_C, S.t_PBT, S.t_CT = Tl(), Tl(), Tl(), Tl()
    S.HTd = nc.dram_tensor("HTd", [512, SEQ], BF16).ap()
    S.UTd = nc.dram_tensor("UTd", [256, SEQ], BF16).ap()
    S.XF = nc.dram_tensor("XF", [2, 128, 128, 2, 64], F32).ap()
    S.HF = nc.dram_tensor("HF", [2, 128, 128, 2, 64], F32).ap()
    S.t_HTd, S.t_UTd, S.t_XF, S.t_HF = Tl(), Tl(), Tl(), Tl()
    S.GBd = nc.dram_tensor("GBd", [16, 128, 24, 128], BF16).ap()
    S.t_GBd = Tl()
    K.gb_cached = [False] * 16
    t_xin, t_XA, t_XB, t_XC, t_out = Tl(), Tl(), Tl(), Tl(), Tl()
    dbg_outs = []

    def finish():
        dd = mk.new_dsem()
        for name in dbg:
            src = {"XA": XA, "XB": XB, "XC": XC, "QTa": S.QTa, "KTa": S.KTa, "Va": S.Va, "QTc": S.QTc, "KTc": S.KTc, "Vc": S.Vc,
                   "PBT": S.PBT, "CT": S.CT, "mod": K.mod}[name]
            o = nc.dram_tensor("dbg_" + name, list(src.shape), src.dtype, kind="ExternalOutput").ap()
            mk.dma('sp', o, src, w=[t_out], dsem=dd)
        mk.barrier()
        return nc

    with ExitStack() as gst:
        K.identb = sb(nc, gst, "identb", [128, 128], BF16)
        identf = sb(nc, gst, "identf", [128, 128], F32)
        K.eps_t = sb(nc, gst, "eps_t", [128, 1], F32)
        K.t_const = Tl()
        d0 = mk.new_dsem()
        mk.dma('sp', identf[:], K.ident_in[:, :], w=[K.t_const], dsem=d0)
        mk.op('dve', lambda: nc.vector.tensor_copy(out=K.identb[:], in_=identf[:]), r=[K.t_const], w=[K.t_const])
        mk.op('dve', lambda: nc.vector.memset(K.eps_t[:], EPS), w=[K.t_const])
        mk.barrier()

        Xcur, t_cur = K.xin, t_xin
        for l in range(DEPTH):
            last = (l == DEPTH - 1)
            phase_ada(K, l)
            mk.recycle()
            phase_ffn(K, l, 0, Xcur, t_cur, XC, t_XC, K.w_gu[0][l], K.w_dn[0][l])
            mk.recycle()
            if stop_after == "ffn1_%d" % l:
                return finish()
            phase_mixin(K, l, XC, t_XC)
            mk.recycle()
            if stop_after == "mixin_%d" % l:
                return finish()
            phase_attn(K, l, need_ctx=not last)
            mk.recycle()
            if stop_after == "attn_%d" % l:
                return finish()
            phase_hyena(K, l, need_ctx=not last)
            mk.recycle()
            if stop_after == "hyena_%d" % l:
                return finish()
            with ExitStack() as wst:
                W2 = ffn_load_weights(K, wst, K.w_gu[1][l], K.w_dn[1][l])
                phase_mixout(K, l, XC, t_XC, XB, t_XB, first_chunk=(2 if last else 0))
                if stop_after == "mixout_%d" % l:
                    return finish()
                phase_ffn(K, l, 1, XB, t_XB, XA, t_XA, K.w_gu[1][l], K.w_dn[1][l], first_chunk=(2 if last else 0), W=W2)
            mk.recycle()
            Xcur, t_cur = XA, t_XA
        dd = mk.new_dsem()
        mk.dma('sp', K.out[:, :], XA[CTX:, :], r=[t_XA], w=[t_out], dsem=dd)
        return finish()


def rope_table(rot_dim):
    rows = SEQ // 64
    row = np.repeat(np.arange(rows, dtype=np.float32), 64)
    col = np.tile(np.arange(64, dtype=np.float32), rows)
    n_freq = rot_dim // 4
    inv = (np.float32(10000.0) ** (-np.arange(n_freq, dtype=np.float32) / np.float32(n_freq))).astype(np.float32)
    ang = np.concatenate([row[:, None] * inv, col[:, None] * inv], -1).astype(np.float32)
    return np.ascontiguousarray(np.concatenate([np.cos(ang), np.sin(ang)], -1).astype(np.float32))


def fft_tables():
    N = 8192
    s1 = np.arange(64)[:, None]; k1 = np.arange(128)[None, :]
    F1 = np.exp(-2j * np.pi * s1 * k1 / 128)
    F1cat = np.concatenate([F1.real, F1.imag], 1)
    s2 = np.arange(64)[:, None, None]; k1g = np.arange(128)[None, :, None]; k2 = np.arange(64)[None, None, :]
    G = np.exp(-2j * np.pi * (s2 * k1g / 8192 + s2 * k2 / 64))
    Gc = np.stack([G.real, G.imag, -G.imag], 2)
    k2_ = np.arange(64)[:, None]; t2 = np.arange(64)[None, :]
    W3 = np.exp(2j * np.pi * k2_ * t2 / 64)
    W3a = np.zeros((128, 2, 2, 64)); W3b = np.zeros((128, 2, 2, 64))
    for cc in range(2):
        W3a[cc * 64:(cc + 1) * 64, cc, 0] = W3.real; W3a[cc * 64:(cc + 1) * 64, cc, 1] = W3.imag
        W3b[cc * 64:(cc + 1) * 64, cc, 0] = -W3.imag; W3b[cc * 64:(cc + 1) * 64, cc, 1] = W3.real
    W3cat = np.concatenate([W3a.reshape(128, 256), W3b.reshape(128, 256)], 1)
    k1_ = np.arange(128)[:, None, None]; t2_ = np.arange(64)[None, :, None]; t1 = np.arange(64)[None, None, :]
    F4 = np.exp(2j * np.pi * (t1 * k1_ / 128 + t2_ * k1_ / 8192)) / N
    F4c = np.stack([F4.real, -F4.imag], 2).reshape(128, 8192)
    f = lambda a: np.ascontiguousarray(a.astype(np.float32))
    return f(F1cat), f(Gc), f(W3cat), f(F4c)


def hyena_z(n):
    t = np.linspace(0.0, 1.0, n, dtype=np.float32)[:, None]
    w = (np.float32(2.0 * math.pi) * np.arange(n, dtype=np.float32)[:, None] / np.float32(n)).astype(np.float32)
    f = np.linspace(1e-4, 15, 16, dtype=np.float32)[None, :]
    z = np.concatenate([t, np.cos(f * w), -np.sin(f * w)], -1).astype(np.float32)
    return np.ascontiguousarray(z.T), t[:, 0]


def make_in_maps(inputs):
    x = np.asarray(inputs["x"], dtype=np.float32)
    ctx = np.asarray(inputs["ctx"], dtype=np.float32)
    c = np.asarray(inputs["c"], dtype=np.float32)
    cctx = np.asarray(inputs["c_ctx"], dtype=np.float32)
    shared = {k: np.ascontiguousarray(np.asarray(inputs[k], dtype=np.float32)) for k in inputs if k not in ("x", "ctx", "c")}
    shared["ident"] = np.eye(128, dtype=np.float32)
    shared["ropeA"] = rope_table(64)
    shared["ropeC"] = rope_table(32)
    zl, tl = hyena_z(SEQ)
    zc, tc_ = hyena_z(CTX)
    shared["zT_lat"] = zl
    shared["zT_ctx"] = zc
    deltas = np.abs(np.linspace(math.log(1e-2) / 1.5, math.log(1e-2) / 0.3, 256, dtype=np.float32))
    deltas = np.tile(deltas, 2)
    td = np.zeros((128, 4 + SEQ + CTX), np.float32)
    td[:, 0:4] = deltas.reshape(4, 128).T
    td[:, 4:4 + SEQ] = tl[None, :]
    td[:, 4 + SEQ:] = tc_[None, :]
    shared["hy_tdel"] = td
    hsm = np.zeros((DEPTH, 128, 32), np.float32)
    for l in range(DEPTH):
        cw = np.asarray(inputs["hy_conv_w"][l], np.float32)
        hsm[l, :, 0:18] = cw.reshape(3, 6, 128).transpose(2, 0, 1).reshape(128, 18)
        hsm[l, :, 18:24] = np.asarray(inputs["hy_conv_b"][l], np.float32).reshape(6, 128).T
        hsm[l, :, 24:26] = np.asarray(inputs["hy_bias"][l], np.float32).reshape(2, 128).T
        hsm[l, 0:64, 26] = np.asarray(inputs["hy_f_freq"][l], np.float32)
        hsm[l, 0:64, 27] = np.asarray(inputs["hy_f_b1"][l], np.float32)
        hsm[l, 0:64, 28] = np.asarray(inputs["hy_f_b2"][l], np.float32)
        hsm[l, 0:64, 29] = np.asarray(inputs["hy_f_b3"][l], np.float32)
    shared["hy_small"] = hsm
    shared["fft_F1"], shared["fft_G"], shared["fft_W3"], shared["fft_F4"] = fft_tables()
    maps = []
    for b in range(8):
        m = dict(shared)
        m["xin"] = np.ascontiguousarray(np.concatenate([ctx[b], x[b]], axis=0))
        cc = np.stack([c[b], cctx], axis=-1).reshape(8, 128, 2).transpose(1, 0, 2).reshape(128, 16)
        m["cc"] = np.ascontiguousarray(cc)
        maps.append(m)
    return maps


def kernel(**inputs):
    nc = build()
    maps = make_in_maps(inputs)
    maps = [{k: v for k, v in m.items() if k in DECLARED} for m in maps]
    res = run_bass_kernel_spmd(nc, maps, core_ids=list(range(8)))
    return np.stack([np.asarray(r["out"]) for r in res.results], axis=0).astype(np.float32)
```

```python
import math
from contextlib import ExitStack
import numpy as np
import concourse.bass as bass
import concourse.mybir as mybir
from concourse.bass_utils import run_bass_kernel_spmd

F32 = mybir.dt.float32
BF16 = mybir.dt.bfloat16
AF = mybir.ActivationFunctionType
ALU = mybir.AluOpType

D = 1024
SEQ = 4096
CTX = 256
NT = SEQ + CTX
NCH = NT // 128
DEPTH = 2
DFF = 2816
NMOD = 9
ALPHA = (2 * DEPTH) ** 0.25
EPS = 1e-6
P_IN = 1824
USE_FFT = True


class Tl:
    def __init__(self, name=""):
        self.name = name
        self.w = None
        self.weng = None
        self.r = {}


class MK:
    def __init__(self, nc):
        self.nc = nc
        self.eng = {'pe': nc.tensor, 'act': nc.scalar, 'dve': nc.vector, 'pool': nc.gpsimd, 'sp': nc.sync}
        self.sem = {}
        self.cnt = {}
        self.seen = {e: {} for e in self.eng}
        self._stack = ExitStack()
        for e in self.eng:
            self.sem[e] = self._stack.enter_context(nc.semaphore("s_" + e))
            self.cnt[e] = 0
        self.dsems = []
        self.free_dsems = []

    def new_dsem(self):
        if self.free_dsems:
            return self.free_dsems.pop()
        s = self._stack.enter_context(self.nc.semaphore("d%d" % len(self.dsems)))
        d = [s, 0]
        self.dsems.append(d)
        return d

    def _need(self, e, waits, dep):
        if dep is None:
            return
        sem, val = dep
        k = id(sem)
        if self.seen[e].get(k, 0) >= val:
            return
        if k not in waits or waits[k][1] < val:
            waits[k] = (sem, val)

    def _dowaits(self, e, waits):
        E = self.eng[e]
        for k, (sem, val) in waits.items():
            E.wait_ge(sem, val)
            self.seen[e][k] = val

    def op(self, e, fn, r=(), w=()):
        waits = {}
        for t in r:
            self._need(e, waits, t.w)
        for t in w:
            if t.weng != e:
                self._need(e, waits, t.w)
            for re_, dep in t.r.items():
                if re_ != e:
                    self._need(e, waits, dep)
        self._dowaits(e, waits)
        ins = fn()
        self.cnt[e] += 1
        ins.then_inc(self.sem[e], 1)
        dep = (self.sem[e], self.cnt[e])
        for t in r:
            t.r[e] = dep
        for t in w:
            t.w = dep
            t.weng = e
            t.r = {}
        return ins

    def dma(self, q, out, in_, r=(), w=(), dsem=None):
        waits = {}
        for t in r:
            self._need(q, waits, t.w)
        for t in w:
            self._need(q, waits, t.w)
            for re_, dep in t.r.items():
                self._need(q, waits, dep)
        self._dowaits(q, waits)
        ins = self.eng[q].dma_start(out=out, in_=in_)
        dsem[1] += 16
        ins.then_inc(dsem[0], 16)
        dep = (dsem[0], dsem[1])
        for t in r:
            t.r['dma%d' % id(dsem)] = dep
        for t in w:
            t.w = dep
            t.weng = 'dma'
            t.r = {}
        return ins

    def barrier(self):
        waits = {}
        for e in self.eng:
            if e != 'sp' and self.cnt[e] > 0:
                self._need('sp', waits, (self.sem[e], self.cnt[e]))
        for d in self.dsems:
            if d[1] > 0:
                self._need('sp', waits, (d[0], d[1]))
        self._dowaits('sp', waits)
        self.cnt['sp'] += 1
        self.nc.sync.sem_inc(self.sem['sp'], 1)
        for e in self.eng:
            if e != 'sp':
                self.eng[e].wait_ge(self.sem['sp'], self.cnt['sp'])
                self.seen[e][id(self.sem['sp'])] = self.cnt['sp']
                for f in self.eng:
                    if f != 'sp':
                        self.seen[e][id(self.sem[f])] = self.cnt[f]
                for d in self.dsems:
                    self.seen[e][id(d[0])] = d[1]

    def recycle(self):
        self.free_dsems = list(self.dsems)

    def close(self):
        self._stack.close()


class Ctx:
    pass


DECLARED = set()


_UID = [0]


def sb(nc, st, name, shape, dt):
    _UID[0] += 1
    return st.enter_context(nc.sbuf_tensor("sb%d_%s" % (_UID[0], name), shape, dt))


def ps(nc, st, name, shape, dt):
    _UID[0] += 1
    return st.enter_context(nc.psum_tensor("ps%d_%s" % (_UID[0], name), shape, dt))


def phase_ada(K, l):
    nc, mk = K.nc, K.mk
    with ExitStack() as st:
        cc = sb(nc, st, "ada_cc", [128, 8, 2], F32)
        sg = sb(nc, st, "ada_sg", [128, 8, 2], F32)
        wbuf = [sb(nc, st, "ada_w%d" % i, [128, 8, 512], F32) for i in range(4)]
        bia = sb(nc, st, "ada_b", [2, 9216], F32)
        mrow = sb(nc, st, "ada_m", [2, 9216], F32)
        pm = [ps(nc, st, "ada_p%d" % i, [2, 512], F32) for i in range(2)]
        t_cc, t_sg, t_b, t_m = Tl(), Tl(), Tl(), Tl()
        t_w = [Tl() for _ in range(4)]
        t_p = [Tl(), Tl()]
        d0 = mk.new_dsem()
        dw = [mk.new_dsem() for _ in range(4)]
        mk.dma('sp', cc[:], K.cc_in.rearrange("p (kc g) -> p kc g", g=2), w=[t_cc], dsem=d0)
        mk.dma('sp', bia[:], K.ada_b[l].partition_broadcast(2), w=[t_b], dsem=d0)
        mk.op('act', lambda: nc.scalar.activation(out=sg[:], in_=cc[:], func=AF.Silu), r=[t_cc], w=[t_sg])
        for nb in range(18):
            i = nb % 2
            wi = nb % 4
            mk.dma('sp', wbuf[wi][:], K.ada_w[l][:, nb * 512:(nb + 1) * 512].rearrange("(kc p) n -> p kc n", p=128),
                   w=[t_w[wi]], dsem=dw[wi])
            for kc in range(8):
                mk.op('pe', lambda kc=kc, i=i, wi=wi: nc.tensor.matmul(pm[i][:], lhsT=sg[:, kc, :], rhs=wbuf[wi][:, kc, :],
                                                                      start=(kc == 0), stop=(kc == 7)),
                      r=[t_sg, t_w[wi]], w=[t_p[i]])
            mk.op('dve', lambda nb=nb, i=i: nc.vector.tensor_add(out=mrow[:, nb * 512:(nb + 1) * 512], in0=pm[i][:],
                                                                 in1=bia[:, nb * 512:(nb + 1) * 512]),
                  r=[t_p[i], t_b], w=[t_m])
        mk.dma('sp', K.mod[l], mrow[:], r=[t_m], w=[K.t_mod[l]], dsem=d0)
        mk.barrier()


def ln_stats(K, x_ap, t_x, st6, mv, rstd, t_st):
    nc, mk = K.nc, K.mk
    for c in range(2):
        mk.op('dve', lambda c=c: nc.vector.bn_stats(out=st6[:, c, :], in_=x_ap[:, c * 512:(c + 1) * 512]), r=[t_x], w=[t_st])
    mk.op('dve', lambda: nc.vector.bn_aggr(out=mv[:], in_=st6[:]), r=[t_st], w=[t_st])
    mk.op('act', lambda: nc.scalar.activation(out=rstd[:], in_=mv[:, 1:2], func=AF.Sqrt, bias=K.eps_t[:, 0:1], scale=1.0), r=[t_st], w=[t_st])
    mk.op('dve', lambda: nc.vector.reciprocal(out=rstd[:], in_=rstd[:]), r=[t_st], w=[t_st])


def load_bcast(K, dst_ap, src_row_ap, t, dsem):
    K.mk.dma('sp', dst_ap, src_row_ap.partition_broadcast(128), w=[t], dsem=dsem)


def ffn_load_weights(K, st, w_gu, w_down):
    nc, mk = K.nc, K.mk
    W = Ctx()
    W.wgu = sb(nc, st, "f_wgu", [128, 8, 2 * DFF], BF16)
    W.wdn = sb(nc, st, "f_wdn", [128, 22, D], BF16)
    W.t_wgu, W.t_wdn = Tl(), Tl()
    dws = mk.new_dsem()
    for kc in range(8):
        for hh in range(2):
            mk.dma('pool', W.wgu[:, kc, hh * DFF:(hh + 1) * DFF], w_gu[kc * 128:(kc + 1) * 128, hh * DFF:(hh + 1) * DFF], w=[W.t_wgu], dsem=dws)
    for fc in range(22):
        mk.dma('pool', W.wdn[:, fc, :], w_down[fc * 128:(fc + 1) * 128, :], w=[W.t_wdn], dsem=dws)
    return W


def phase_ffn(K, l, which, Xin, t_xin, Xout, t_xout, w_gu, w_down, first_chunk=0, W=None):
    nc, mk = K.nc, K.mk
    mbase = 0 if which == 0 else 6
    lni = 0 if which == 0 else 2
    with ExitStack() as st:
        if W is None:
            W = ffn_load_weights(K, st, w_gu, w_down)
        wgu, wdn, t_wgu, t_wdn = W.wgu, W.wdn, W.t_wgu, W.t_wdn
        modv = sb(nc, st, "f_mod", [128, 5, D], F32)
        xc = [sb(nc, st, "f_xc%d" % i, [128, D], F32) for i in range(6)]
        hT = sb(nc, st, "f_hT", [128, 8, 256], BF16)
        actT = sb(nc, st, "f_actT", [128, 22, 256], BF16)
        tt = sb(nc, st, "f_tt", [128, D], F32)
        hn2 = [sb(nc, st, "f_hn%d" % i, [128, D], BF16) for i in range(2)]
        st6j = [sb(nc, st, "f_st6j%d" % i, [128, 2, 6], F32) for i in range(2)]
        mvj = [sb(nc, st, "f_mvj%d" % i, [128, 2], F32) for i in range(2)]
        rstdj = [sb(nc, st, "f_rstdj%d" % i, [128, 1], F32) for i in range(2)]
        sgt = [sb(nc, st, "f_sg%d" % i, [128, 256], F32) for i in range(2)]
        st6b = sb(nc, st, "f_st6b", [128, 2, 6], F32)
        mvb = sb(nc, st, "f_mvb", [128, 2], F32)
        rstdb = sb(nc, st, "f_rstdb", [128, 1], F32)
        p_tr = ps(nc, st, "f_ptr", [128, 8, 128], BF16)
        p_up = [ps(nc, st, "f_pup%d" % i, [128, 2, 256], F32) for i in range(3)]
        p_dn = [ps(nc, st, "f_pdn%d" % i, [128, 512], F32) for i in range(4)]
        t_mod = Tl()
        t_xc = [Tl() for _ in range(6)]
        t_hT, t_act, t_tt, t_stb, t_ptr = Tl(), Tl(), Tl(), Tl(), Tl()
        t_hn2 = [Tl(), Tl()]
        t_stj = [Tl(), Tl()]
        t_sg = [Tl(), Tl()]
        t_pup = [Tl() for _ in range(3)]
        t_pdn = [Tl() for _ in range(4)]
        dmod = mk.new_dsem()
        dx = [mk.new_dsem() for _ in range(6)]
        dout = mk.new_dsem()
        nblk = NCH // 2
        state = {"upi": 0}
        blks = list(range(first_chunk // 2, nblk))

        def xidx(blk, j):
            return (blk % 3) * 2 + j

        def load_mod(g):
            mrow = K.mod[l][g]
            for i, src in enumerate([mrow[(mbase + 0) * D:(mbase + 1) * D], mrow[(mbase + 1) * D:(mbase + 2) * D],
                                     mrow[(mbase + 2) * D:(mbase + 3) * D], K.ln_g[l, lni], K.ln_b[l, lni]]):
                mk.dma('sp', modv[:, i, :], src.partition_broadcast(128), r=[K.t_mod[l]], w=[t_mod], dsem=dmod)
            mk.op('pool', lambda: nc.gpsimd.tensor_scalar_add(out=modv[:, 1, :], in0=modv[:, 1, :], scalar1=1.0), r=[t_mod], w=[t_mod])
            mk.op('pool', lambda: nc.gpsimd.tensor_scalar_mul(out=modv[:, 2, :], in0=modv[:, 2, :], scalar1=0.5), r=[t_mod], w=[t_mod])

        def prep_load(blk):
            for j in range(2):
                ch = blk * 2 + j
                xi = xidx(blk, j)
                mk.dma('sp', xc[xi][:], Xin[ch * 128:(ch + 1) * 128, :], r=[t_xin], w=[t_xc[xi]], dsem=dx[xi])

        def prep_ln(blk):
            for j in range(2):
                xi = xidx(blk, j)
                ln_stats(K, xc[xi], t_xc[xi], st6j[j], mvj[j], rstdj[j], t_stj[j])
            for j in range(2):
                xi = xidx(blk, j)
                mk.op('dve', lambda xi=xi, j=j: nc.vector.scalar_tensor_tensor(out=tt[:], in0=xc[xi][:], scalar=mvj[j][:, 0:1], in1=modv[:, 1, :],
                                                                               op0=ALU.subtract, op1=ALU.mult), r=[t_xc[xi], t_stj[j], t_mod], w=[t_tt])
                mk.op('dve', lambda j=j: nc.vector.scalar_tensor_tensor(out=hn2[j][:], in0=tt[:], scalar=rstdj[j][:, 0:1], in1=modv[:, 0, :],
                                                                        op0=ALU.mult, op1=ALU.add), r=[t_tt, t_stj[j], t_mod], w=[t_hn2[j]])

        def prep_ln_gen(blk):
            for j in range(2):
                xi = xidx(blk, j)
                for c in range(2):
                    mk.op('dve', lambda c=c, xi=xi, j=j: nc.vector.bn_stats(out=st6j[j][:, c, :], in_=xc[xi][:, c * 512:(c + 1) * 512]), r=[t_xc[xi]], w=[t_stj[j]])
                    yield
                mk.op('dve', lambda j=j: nc.vector.bn_aggr(out=mvj[j][:], in_=st6j[j][:]), r=[t_stj[j]], w=[t_stj[j]])
                mk.op('act', lambda j=j: nc.scalar.activation(out=rstdj[j][:], in_=mvj[j][:, 1:2], func=AF.Sqrt, bias=K.eps_t[:, 0:1], scale=1.0), r=[t_stj[j]], w=[t_stj[j]])
                yield
                yield
                mk.op('dve', lambda j=j: nc.vector.reciprocal(out=rstdj[j][:], in_=rstdj[j][:]), r=[t_stj[j]], w=[t_stj[j]])
                yield
                mk.op('dve', lambda xi=xi, j=j: nc.vector.scalar_tensor_tensor(out=tt[:], in0=xc[xi][:], scalar=mvj[j][:, 0:1], in1=modv[:, 1, :],
                                                                               op0=ALU.subtract, op1=ALU.mult), r=[t_xc[xi], t_stj[j], t_mod], w=[t_tt])
                yield
                mk.op('dve', lambda j=j: nc.vector.scalar_tensor_tensor(out=hn2[j][:], in0=tt[:], scalar=rstdj[j][:, 0:1], in1=modv[:, 0, :],
                                                                        op0=ALU.mult, op1=ALU.add), r=[t_tt, t_stj[j], t_mod], w=[t_hn2[j]])
                yield

        def prep_b(blk):
            for j in range(2):
                for kc in range(8):
                    mk.op('pe', lambda kc=kc, j=j: nc.tensor.transpose(out=p_tr[:, kc, :], in_=hn2[j][:, kc * 128:(kc + 1) * 128], identity=K.identb[:]),
                          r=[t_hn2[j], K.t_const], w=[t_ptr])
                mk.op('act', lambda j=j: nc.scalar.copy(out=hT[:, :, j * 128:(j + 1) * 128], in_=p_tr[:]), r=[t_ptr], w=[t_hT])

        def up(blk, gen=None):
            for fc in range(22):
                if gen is not None and fc >= 3:
                    next(gen, None)
                pi = state["upi"] % 3
                si = state["upi"] % 2
                state["upi"] += 1
                for hh in range(2):
                    for kc in range(8):
                        mk.op('pe', lambda kc=kc, hh=hh, fc=fc, pi=pi: nc.tensor.matmul(
                            p_up[pi][:, hh, :], lhsT=wgu[:, kc, hh * DFF + fc * 128: hh * DFF + (fc + 1) * 128], rhs=hT[:, kc, :],
                            start=(kc == 0), stop=(kc == 7)), r=[t_wgu, t_hT], w=[t_pup[pi]])
                mk.op('act', lambda pi=pi, si=si: nc.scalar.activation(out=sgt[si][:], in_=p_up[pi][:, 0, :], func=AF.Silu), r=[t_pup[pi]], w=[t_sg[si]])
                mk.op('dve', lambda pi=pi, si=si, fc=fc: nc.vector.tensor_mul(out=actT[:, fc, :], in0=sgt[si][:], in1=p_up[pi][:, 1, :]),
                      r=[t_sg[si], t_pup[pi]], w=[t_act])

        def down(blk):
            for j in range(2):
                ch = blk * 2 + j
                xi = xidx(blk, j)
                for hf in range(2):
                    pd = j * 2 + hf
                    for fc in range(22):
                        mk.op('pe', lambda fc=fc, j=j, hf=hf, pd=pd: nc.tensor.matmul(
                            p_dn[pd][:], lhsT=actT[:, fc, j * 128:(j + 1) * 128], rhs=wdn[:, fc, hf * 512:(hf + 1) * 512],
                            start=(fc == 0), stop=(fc == 21)), r=[t_act, t_wdn], w=[t_pdn[pd]])
                    mk.op('dve', lambda hf=hf, pd=pd: nc.vector.tensor_mul(out=tt[:, hf * 512:(hf + 1) * 512], in0=p_dn[pd][:],
                                                                           in1=modv[:, 2, hf * 512:(hf + 1) * 512]), r=[t_pdn[pd], t_mod], w=[t_tt])
                mk.op('dve', lambda xi=xi: nc.vector.scalar_tensor_tensor(out=xc[xi][:], in0=xc[xi][:], scalar=ALPHA, in1=tt[:],
                                                                          op0=ALU.mult, op1=ALU.add), r=[t_tt, t_xc[xi]], w=[t_xc[xi]])
                ln_stats(K, xc[xi], t_xc[xi], st6b, mvb, rstdb, t_stb)
                mk.op('dve', lambda xi=xi: nc.vector.scalar_tensor_tensor(out=xc[xi][:], in0=xc[xi][:], scalar=mvb[:, 0:1], in1=modv[:, 3, :],
                                                                          op0=ALU.subtract, op1=ALU.mult), r=[t_xc[xi], t_stb, t_mod], w=[t_xc[xi]])
                mk.op('dve', lambda xi=xi: nc.vector.scalar_tensor_tensor(out=xc[xi][:], in0=xc[xi][:], scalar=rstdb[:, 0:1], in1=modv[:, 4, :],
                                                                          op0=ALU.mult, op1=ALU.add), r=[t_xc[xi], t_stb, t_mod], w=[t_xc[xi]])
                mk.dma('pool', Xout[ch * 128:(ch + 1) * 128, :], xc[xi][:], r=[t_xc[xi]], w=[t_xout], dsem=dout)

        if blks and blks[0] == 0:
            load_mod(1)
            prep_load(0)
            prep_ln(0)
            prep_b(0)
            up(0)
            down(0)
            blks = blks[1:]
        if blks:
            load_mod(0)
            prep_load(blks[0])
            if len(blks) > 1:
                prep_load(blks[1])
            prep_ln(blks[0])
            prep_b(blks[0])
            for bi_, blk in enumerate(blks):
                nxt = blks[bi_ + 1] if bi_ + 1 < len(blks) else None
                nx2 = blks[bi_ + 2] if bi_ + 2 < len(blks) else None
                gen = prep_ln_gen(nxt) if nxt is not None else None
                up(blk, gen)
                if gen is not None:
                    for _ in gen:
                        pass
                    prep_b(nxt)
                if nx2 is not None:
                    prep_load(nx2)
                down(blk)
        mk.barrier()


def phase_mixin(K, l, Xin, t_xin):
    nc, mk = K.nc, K.mk
    S = K.S
    with ExitStack() as st:
        win = sb(nc, st, "m_win", [128, 8, P_IN], BF16)
        wuq = sb(nc, st, "m_wuq", [128, 2, 576], BF16)
        wukv = sb(nc, st, "m_wukv", [128, 768], BF16)
        modv = sb(nc, st, "m_mod", [128, 2, D], F32)
        gA = sb(nc, st, "m_gA", [128, 8, 64], F32)
        gq = sb(nc, st, "m_gq", [128, 256], F32)
        gkv = sb(nc, st, "m_gkv", [128, 128], F32)
        ropeA_l = [sb(nc, st, "m_ropeA%d" % i, [128, 2, 32], F32) for i in range(2)]
        ropeC_l = [sb(nc, st, "m_ropeC%d" % i, [128, 2, 16], F32) for i in range(2)]
        t_rope_l = [Tl(), Tl()]
        xc = [sb(nc, st, "m_xc%d" % i, [128, D], F32) for i in range(2)]
        xn = sb(nc, st, "m_xn", [128, D], F32)
        hn = sb(nc, st, "m_hn", [128, D], BF16)
        hT = sb(nc, st, "m_hT", [128, 8, 128], BF16)
        ptA = sb(nc, st, "m_ptA", [128, 8, 64], F32)
        ptC = sb(nc, st, "m_ptC", [128, 416], F32)
        sq = sb(nc, st, "m_sq", [128, 512], F32)
        ss = sb(nc, st, "m_ss", [128, 16], F32)
        r1 = sb(nc, st, "m_r1", [128, 8, 32], F32)
        r2 = sb(nc, st, "m_r2", [128, 8, 32], F32)
        qkb = sb(nc, st, "m_qkb", [128, 8, 64], BF16)
        vab = sb(nc, st, "m_vab", [128, 128], BF16)
        qkT = sb(nc, st, "m_qkT", [128, 4, 128], BF16)
        pbs = sb(nc, st, "m_pbs", [128, 6, 128], F32)
        cn = sb(nc, st, "m_cn", [128, 384], BF16)
        cT = sb(nc, st, "m_cT", [128, 3, 128], BF16)
        qc = sb(nc, st, "m_qc", [128, 6, 96], F32)
        kvc = sb(nc, st, "m_kvc", [128, 6, 128], F32)
        kr = sb(nc, st, "m_kr", [128, 32], F32)
        qcb = sb(nc, st, "m_qcb", [128, 6, 96], BF16)
        kcb = sb(nc, st, "m_kcb", [128, 6, 96], BF16)
        vcb = sb(nc, st, "m_vcb", [128, 6, 64], BF16)
        qcT = sb(nc, st, "m_qcT", [128, 6, 128], BF16)
        kcT = sb(nc, st, "m_kcT", [128, 6, 128], BF16)
        st6 = sb(nc, st, "m_st6", [128, 2, 6], F32)
        mv = sb(nc, st, "m_mv", [128, 2], F32)
        rstd = sb(nc, st, "m_rstd", [128, 1], F32)
        B0 = ps(nc, st, "m_B0", [128, 8, 128], BF16)
        B1 = ps(nc, st, "m_B1", [128, 512], F32)
        B2 = ps(nc, st, "m_B2", [128, 512], F32)
        B3 = ps(nc, st, "m_B3", [128, 512], F32)
        B4 = ps(nc, st, "m_B4", [128, 4, 128], F32)
        B5 = ps(nc, st, "m_B5", [128, 4, 128], F32)
        B6 = ps(nc, st, "m_B6", [128, 8, 128], BF16)
        B7 = ps(nc, st, "m_B7", [128, 512], F32)
        tB = [Tl() for _ in range(8)]
        t_w, t_mod, t_g = Tl(), Tl(), Tl()
        t_xc = [Tl(), Tl()]
        (t_xn, t_hn, t_hT, t_ptA, t_ptC, t_sq, t_ss, t_r, t_qkb, t_vab, t_qkT, t_pbs, t_cn, t_cT, t_qc, t_kvc,
         t_kr, t_qcb, t_kcb, t_vcb, t_qcT, t_kcT, t_st) = [Tl() for _ in range(23)]
        dws, dmod, dg, dout = [mk.new_dsem() for _ in range(4)]
        drope_l = [mk.new_dsem(), mk.new_dsem()]
        dx = [mk.new_dsem(), mk.new_dsem()]

        def issue_loads(ch):
            xi = ch % 2
            mk.dma('sp', xc[xi][:], Xin[ch * 128:(ch + 1) * 128, :], r=[t_xin], w=[t_xc[xi]], dsem=dx[xi])
            if ch >= 2:
                lt = slice((ch - 2) * 128, (ch - 1) * 128)
                mk.dma('sp', ropeA_l[xi][:], K.ropeA_in[lt, :].rearrange("p (a b) -> p a b", a=2), w=[t_rope_l[xi]], dsem=drope_l[xi])
                mk.dma('sp', ropeC_l[xi][:], K.ropeC_in[lt, :].rearrange("p (a b) -> p a b", a=2), w=[t_rope_l[xi]], dsem=drope_l[xi])
        for kc in range(8):
            for two in range(2):
                mk.dma('pool', win[:, kc, 0:384].rearrange("p (h two d) -> p h two d", h=3, two=2)[:, :, two, :],
                       K.w_in[l][kc * 128:(kc + 1) * 128, two * 192:(two + 1) * 192].rearrange("p (h d) -> p h d", h=3), w=[t_w], dsem=dws)
            mk.dma('pool', win[:, kc, 384:P_IN], K.w_in[l][kc * 128:(kc + 1) * 128, 384:P_IN], w=[t_w], dsem=dws)
        for kc in range(2):
            mk.dma('pool', wuq[:, kc, :], K.w_uq[l][kc * 128:(kc + 1) * 128, :], w=[t_w], dsem=dws)
        mk.dma('pool', wukv[:], K.w_ukv[l][:, :], w=[t_w], dsem=dws)
        for h in range(6):
            mk.dma('sp', gA[:, h, :], K.a_qn[l].partition_broadcast(128), w=[t_g], dsem=dg)
        for h in range(2):
            mk.dma('sp', gA[:, 6 + h, :], K.a_kn[l].partition_broadcast(128), w=[t_g], dsem=dg)
        mk.dma('sp', gq[:], K.c_qn[l].partition_broadcast(128), w=[t_g], dsem=dg)
        mk.dma('sp', gkv[:], K.c_kvn[l].partition_broadcast(128), w=[t_g], dsem=dg)
        mk.op('pool', lambda: nc.gpsimd.tensor_scalar_mul(out=gA[:, 0:6, :], in0=gA[:, 0:6, :], scalar1=0.125), r=[t_g], w=[t_g])
        cur_g = None
        for ch in range(NCH):
            g = 1 if ch < 2 else 0
            lat = (g == 0)
            if g != cur_g:
                cur_g = g
                mrow = K.mod[l][g]
                for i in range(2):
                    mk.dma('sp', modv[:, i, :], mrow[(3 + i) * D:(4 + i) * D].partition_broadcast(128), r=[K.t_mod[l]], w=[t_mod], dsem=dmod)
                mk.op('pool', lambda: nc.gpsimd.tensor_scalar_add(out=modv[:, 1, :], in0=modv[:, 1, :], scalar1=1.0), r=[t_mod], w=[t_mod])
            xi = ch % 2
            tok = slice(ch * 128, (ch + 1) * 128)
            if ch == 0:
                issue_loads(0)
            if ch + 1 < NCH:
                issue_loads(ch + 1)
            ropeA, ropeC, t_rope = ropeA_l[xi], ropeC_l[xi], t_rope_l[xi]
            ln_stats(K, xc[xi], t_xc[xi], st6, mv, rstd, t_st)
            mk.op('dve', lambda xi=xi: nc.vector.scalar_tensor_tensor(out=xn[:], in0=xc[xi][:], scalar=mv[:, 0:1], in1=modv[:, 1, :],
                                                                      op0=ALU.subtract, op1=ALU.mult), r=[t_xc[xi], t_st, t_mod], w=[t_xn])
            mk.op('dve', lambda: nc.vector.scalar_tensor_tensor(out=hn[:], in0=xn[:], scalar=rstd[:, 0:1], in1=modv[:, 0, :],
                                                                op0=ALU.mult, op1=ALU.add), r=[t_xn, t_st, t_mod], w=[t_hn])
            for kc in range(8):
                mk.op('pe', lambda kc=kc: nc.tensor.transpose(out=B0[:, kc, :], in_=hn[:, kc * 128:(kc + 1) * 128], identity=K.identb[:]),
                      r=[t_hn, K.t_const], w=[tB[0]])
            mk.op('act', lambda: nc.scalar.copy(out=hT[:], in_=B0[:]), r=[tB[0]], w=[t_hT])
            for (Bk, tb, c0, c1) in ((B1, tB[1], 0, 512), (B3, tB[3], 512, 640), (B2, tB[2], 1408, 1824)):
                for kc in range(8):
                    mk.op('pe', lambda kc=kc, Bk=Bk, c0=c0, c1=c1: nc.tensor.matmul(Bk[:, 0:c1 - c0], lhsT=hT[:, kc, :], rhs=win[:, kc, c0:c1],
                                                                                   start=(kc == 0), stop=(kc == 7)), r=[t_hT, t_w], w=[tb])
            for c in range(6):
                Bk, tb, ci = (B4, tB[4], c) if c < 4 else (B5, tB[5], c - 4)
                for kc in range(8):
                    mk.op('pe', lambda kc=kc, Bk=Bk, ci=ci, c=c: nc.tensor.matmul(Bk[:, ci, :], lhsT=win[:, kc, 640 + c * 128:640 + (c + 1) * 128],
                                                                                 rhs=hT[:, kc, :], start=(kc == 0), stop=(kc == 7)), r=[t_hT, t_w], w=[tb])
            mk.op('act', lambda: nc.scalar.copy(out=ptA[:].rearrange("p h d -> p (h d)"), in_=B1[:]), r=[tB[1]], w=[t_ptA])
            mk.op('act', lambda: nc.scalar.copy(out=ptC[:], in_=B2[:, 0:416]), r=[tB[2]], w=[t_ptC])
            mk.op('act', lambda: nc.scalar.copy(out=vab[:], in_=B3[:, 0:128]), r=[tB[3]], w=[t_vab])
            mk.op('act', lambda: nc.scalar.copy(out=pbs[:, 0:4, :], in_=B4[:]), r=[tB[4]], w=[t_pbs])
            mk.op('act', lambda: nc.scalar.copy(out=pbs[:, 4:6, :], in_=B5[:, 0:2, :]), r=[tB[5]], w=[t_pbs])
            mk.dma('sp', S.PBT[l].rearrange("(c p) t -> p c t", p=128)[:, :, tok], pbs[:], r=[t_pbs], w=[S.t_PBT], dsem=dout)
            mk.dma('sp', S.Va[l][tok, :], vab[:], r=[t_vab], w=[S.t_A], dsem=dout)
            mk.op('dve', lambda: nc.vector.tensor_mul(out=sq[:], in0=ptA[:].rearrange("p h d -> p (h d)"), in1=ptA[:].rearrange("p h d -> p (h d)")),
                  r=[t_ptA], w=[t_sq])
            mk.op('dve', lambda: nc.vector.reduce_sum(out=ss[:, 0:8], in_=sq[:].rearrange("p (h d) -> p h d", d=64), axis=mybir.AxisListType.X),
                  r=[t_sq], w=[t_ss])
            mk.op('act', lambda: nc.scalar.activation(out=ss[:, 0:8], in_=ss[:, 0:8], func=AF.Sqrt, bias=K.eps_t[:, 0:1], scale=1.0 / 64), r=[t_ss], w=[t_ss])
            mk.op('dve', lambda: nc.vector.reciprocal(out=ss[:, 0:8], in_=ss[:, 0:8]), r=[t_ss], w=[t_ss])
            mk.op('dve', lambda: nc.vector.tensor_mul(out=ptA[:], in0=ptA[:], in1=ss[:, 0:8].unsqueeze(2).to_broadcast([128, 8, 64])), r=[t_ptA, t_ss], w=[t_ptA])
            if lat:
                mk.op('pool', lambda: nc.gpsimd.tensor_mul(out=ptA[:], in0=ptA[:], in1=gA[:]), r=[t_ptA, t_g], w=[t_ptA])
                cosb = ropeA[:, 0:1, :].to_broadcast([128, 8, 32])
                sinb = ropeA[:, 1:2, :].to_broadcast([128, 8, 32])
                mk.op('dve', lambda: nc.vector.tensor_mul(out=r1[:], in0=ptA[:, :, 0:32], in1=cosb), r=[t_ptA, t_rope], w=[t_r])
                mk.op('dve', lambda: nc.vector.tensor_mul(out=r2[:], in0=ptA[:, :, 32:64], in1=sinb), r=[t_ptA, t_rope], w=[t_r])
                mk.op('dve', lambda: nc.vector.tensor_sub(out=qkb[:, :, 0:32], in0=r1[:], in1=r2[:]), r=[t_r], w=[t_qkb])
                mk.op('dve', lambda: nc.vector.tensor_mul(out=r1[:], in0=ptA[:, :, 32:64], in1=cosb), r=[t_ptA, t_rope], w=[t_r])
                mk.op('dve', lambda: nc.vector.tensor_mul(out=r2[:], in0=ptA[:, :, 0:32], in1=sinb), r=[t_ptA, t_rope], w=[t_r])
                mk.op('dve', lambda: nc.vector.tensor_add(out=qkb[:, :, 32:64], in0=r1[:], in1=r2[:]), r=[t_r], w=[t_qkb])
            else:
                mk.op('pool', lambda: nc.gpsimd.tensor_mul(out=qkb[:], in0=ptA[:], in1=gA[:]), r=[t_ptA, t_g], w=[t_qkb])
            for pr in range(3):
                mk.op('pe', lambda pr=pr: nc.tensor.transpose(out=B6[:, pr, :], in_=qkb[:, 2 * pr:2 * pr + 2, :].rearrange("p a d -> p (a d)"),
                                                              identity=K.identb[:]), r=[t_qkb, K.t_const], w=[tB[6]])
            mk.op('pe', lambda: nc.tensor.transpose(out=B6[:, 3, :], in_=qkb[:, 6:8, :].rearrange("p a d -> p (a d)"), identity=K.identb[:]),
                  r=[t_qkb, K.t_const], w=[tB[6]])
            mk.op('act', lambda: nc.scalar.copy(out=qkT[:], in_=B6[:, 0:4, :]), r=[tB[6]], w=[t_qkT])
            for pr in range(3):
                mk.dma('sp', S.QTa[l][pr][:, tok], qkT[0:64, pr, :], r=[t_qkT], w=[S.t_A], dsem=dout)
                mk.dma('sp', S.QTa[l][pr + 3][:, tok], qkT[64:128, pr, :], r=[t_qkT], w=[S.t_A], dsem=dout)
            mk.dma('sp', S.KTa[l].rearrange("h d t -> (h d) t")[:, tok], qkT[:, 3, :], r=[t_qkT], w=[S.t_A], dsem=dout)
            mk.op('dve', lambda: nc.vector.tensor_mul(out=sq[:, 0:384], in0=ptC[:, 0:384], in1=ptC[:, 0:384]), r=[t_ptC], w=[t_sq])
            mk.op('dve', lambda: nc.vector.reduce_sum(out=ss[:, 8:9], in_=sq[:, 0:256], axis=mybir.AxisListType.X), r=[t_sq], w=[t_ss])
            mk.op('dve', lambda: nc.vector.reduce_sum(out=ss[:, 9:10], in_=sq[:, 256:384], axis=mybir.AxisListType.X), r=[t_sq], w=[t_ss])
            mk.op('act', lambda: nc.scalar.activation(out=ss[:, 8:9], in_=ss[:, 8:9], func=AF.Sqrt, bias=K.eps_t[:, 0:1], scale=1.0 / 256), r=[t_ss], w=[t_ss])
            mk.op('act', lambda: nc.scalar.activation(out=ss[:, 9:10], in_=ss[:, 9:10], func=AF.Sqrt, bias=K.eps_t[:, 0:1], scale=1.0 / 128), r=[t_ss], w=[t_ss])
            mk.op('dve', lambda: nc.vector.reciprocal(out=ss[:, 8:10], in_=ss[:, 8:10]), r=[t_ss], w=[t_ss])
            mk.op('dve', lambda: nc.vector.scalar_tensor_tensor(out=cn[:, 0:256], in0=ptC[:, 0:256], scalar=ss[:, 8:9], in1=gq[:], op0=ALU.mult, op1=ALU.mult),
                  r=[t_ptC, t_ss, t_g], w=[t_cn])
            mk.op('dve', lambda: nc.vector.scalar_tensor_tensor(out=cn[:, 256:384], in0=ptC[:, 256:384], scalar=ss[:, 9:10], in1=gkv[:], op0=ALU.mult, op1=ALU.mult),
                  r=[t_ptC, t_ss, t_g], w=[t_cn])
            for c in range(3):
                mk.op('pe', lambda c=c: nc.tensor.transpose(out=B6[:, 4 + c, :], in_=cn[:, c * 128:(c + 1) * 128], identity=K.identb[:]),
                      r=[t_cn, K.t_const], w=[tB[6]])
            mk.op('act', lambda: nc.scalar.copy(out=cT[:], in_=B6[:, 4:7, :]), r=[tB[6]], w=[t_cT])
            for kc in range(2):
                mk.op('pe', lambda kc=kc: nc.tensor.matmul(B7[:, 0:480], lhsT=cT[:, kc, :], rhs=wuq[:, kc, 0:480], start=(kc == 0), stop=(kc == 1)),
                      r=[t_cT, t_w], w=[tB[7]])
            for kc in range(2):
                mk.op('pe', lambda kc=kc: nc.tensor.matmul(B3[:, 128:224], lhsT=cT[:, kc, :], rhs=wuq[:, kc, 480:576], start=(kc == 0), stop=(kc == 1)),
                      r=[t_cT, t_w], w=[tB[3]])
            mk.op('pe', lambda: nc.tensor.matmul(B1[:], lhsT=cT[:, 2, :], rhs=wukv[:, 0:512], start=True, stop=True), r=[t_cT, t_w], w=[tB[1]])
            mk.op('pe', lambda: nc.tensor.matmul(B2[:, 0:256], lhsT=cT[:, 2, :], rhs=wukv[:, 512:768], start=True, stop=True), r=[t_cT, t_w], w=[tB[2]])
            qs = 96.0 ** -0.5
            mk.op('act', lambda: nc.scalar.mul(out=qc[:, 0:5, :].rearrange("p h d -> p (h d)"), in_=B7[:, 0:480], mul=qs), r=[tB[7]], w=[t_qc])
            mk.op('act', lambda: nc.scalar.mul(out=qc[:, 5, :], in_=B3[:, 128:224], mul=qs), r=[tB[3]], w=[t_qc])
            mk.op('act', lambda: nc.scalar.copy(out=kvc[:, 0:4, :].rearrange("p h d -> p (h d)"), in_=B1[:]), r=[tB[1]], w=[t_kvc])
            mk.op('act', lambda: nc.scalar.copy(out=kvc[:, 4:6, :].rearrange("p h d -> p (h d)"), in_=B2[:, 0:256]), r=[tB[2]], w=[t_kvc])
            mk.op('pool', lambda: nc.gpsimd.tensor_copy(out=qcb[:, :, 0:64], in_=qc[:, :, 0:64]), r=[t_qc], w=[t_qcb])
            mk.op('pool', lambda: nc.gpsimd.tensor_copy(out=kcb[:, :, 0:64], in_=kvc[:, :, 0:64]), r=[t_kvc], w=[t_kcb])
            mk.op('pool', lambda: nc.gpsimd.tensor_copy(out=vcb[:], in_=kvc[:, :, 64:128]), r=[t_kvc], w=[t_vcb])
            if lat:
                cosb = ropeC[:, 0:1, :].to_broadcast([128, 6, 16])
                sinb = ropeC[:, 1:2, :].to_broadcast([128, 6, 16])
                mk.op('dve', lambda: nc.vector.tensor_mul(out=r1[:, 0:6, 0:16], in0=qc[:, :, 64:80], in1=cosb), r=[t_qc, t_rope], w=[t_r])
                mk.op('dve', lambda: nc.vector.tensor_mul(out=r2[:, 0:6, 0:16], in0=qc[:, :, 80:96], in1=sinb), r=[t_qc, t_rope], w=[t_r])
                mk.op('dve', lambda: nc.vector.tensor_sub(out=qcb[:, :, 64:80], in0=r1[:, 0:6, 0:16], in1=r2[:, 0:6, 0:16]), r=[t_r], w=[t_qcb])
                mk.op('dve', lambda: nc.vector.tensor_mul(out=r1[:, 0:6, 0:16], in0=qc[:, :, 80:96], in1=cosb), r=[t_qc, t_rope], w=[t_r])
                mk.op('dve', lambda: nc.vector.tensor_mul(out=r2[:, 0:6, 0:16], in0=qc[:, :, 64:80], in1=sinb), r=[t_qc, t_rope], w=[t_r])
                mk.op('dve', lambda: nc.vector.tensor_add(out=qcb[:, :, 80:96], in0=r1[:, 0:6, 0:16], in1=r2[:, 0:6, 0:16]), r=[t_r], w=[t_qcb])
                mk.op('dve', lambda: nc.vector.tensor_mul(out=r1[:, 0, 0:16], in0=ptC[:, 384:400], in1=ropeC[:, 0, :]), r=[t_ptC, t_rope], w=[t_r])
                mk.op('dve', lambda: nc.vector.tensor_mul(out=r2[:, 0, 0:16], in0=ptC[:, 400:416], in1=ropeC[:, 1, :]), r=[t_ptC, t_rope], w=[t_r])
                mk.op('dve', lambda: nc.vector.tensor_sub(out=kr[:, 0:16], in0=r1[:, 0, 0:16], in1=r2[:, 0, 0:16]), r=[t_r], w=[t_kr])
                mk.op('dve', lambda: nc.vector.tensor_mul(out=r1[:, 0, 0:16], in0=ptC[:, 400:416], in1=ropeC[:, 0, :]), r=[t_ptC, t_rope], w=[t_r])
                mk.op('dve', lambda: nc.vector.tensor_mul(out=r2[:, 0, 0:16], in0=ptC[:, 384:400], in1=ropeC[:, 1, :]), r=[t_ptC, t_rope], w=[t_r])
                mk.op('dve', lambda: nc.vector.tensor_add(out=kr[:, 16:32], in0=r1[:, 0, 0:16], in1=r2[:, 0, 0:16]), r=[t_r], w=[t_kr])
            else:
                mk.op('pool', lambda: nc.gpsimd.tensor_copy(out=qcb[:, :, 64:96], in_=qc[:, :, 64:96]), r=[t_qc], w=[t_qcb])
                mk.op('pool', lambda: nc.gpsimd.tensor_copy(out=kr[:], in_=ptC[:, 384:416]), r=[t_ptC], w=[t_kr])
            mk.op('pool', lambda: nc.gpsimd.tensor_copy(out=kcb[:, :, 64:96], in_=kr[:].unsqueeze(1).to_broadcast([128, 6, 32])), r=[t_kr], w=[t_kcb])
            mk.dma('sp', S.Vc[l][tok, :], vcb[:].rearrange("p h d -> p (h d)"), r=[t_vcb], w=[S.t_C], dsem=dout)
            for h in range(6):
                mk.op('pe', lambda h=h: nc.tensor.transpose(out=B0[0:96, h, :], in_=qcb[:, h, :], identity=K.identb[:]), r=[t_qcb, K.t_const], w=[tB[0]])
            mk.op('act', lambda: nc.scalar.copy(out=qcT[0:96, :, :], in_=B0[0:96, 0:6, :]), r=[tB[0]], w=[t_qcT])
            for h in range(6):
                mk.op('pe', lambda h=h: nc.tensor.transpose(out=B6[0:96, h, :], in_=kcb[:, h, :], identity=K.identb[:]), r=[t_kcb, K.t_const], w=[tB[6]])
            mk.op('act', lambda: nc.scalar.copy(out=kcT[0:96, :, :], in_=B6[0:96, 0:6, :]), r=[tB[6]], w=[t_kcT])
            mk.dma('sp', S.QTc[l].rearrange("h d t -> d h t")[:, :, tok], qcT[0:96, :, :], r=[t_qcT], w=[S.t_C], dsem=dout)
            mk.dma('sp', S.KTc[l].rearrange("h d t -> d h t")[:, :, tok], kcT[0:96, :, :], r=[t_kcT], w=[S.t_C], dsem=dout)
        mk.barrier()


def phase_attn(K, l, need_ctx):
    nc, mk = K.nc, K.mk
    S = K.S
    heads = []
    for h in range(6):
        heads.append((64, S.KTa[l][h // 3], S.Va[l][:, (h // 3) * 64:(h // 3 + 1) * 64], S.QTa[l][h], h * 64, S.t_A))
    for h in range(6):
        heads.append((96, S.KTc[l][h], S.Vc[l][:, h * 64:(h + 1) * 64], S.QTc[l][h], 640 + h * 64, S.t_C))
    with ExitStack() as st:
        KT = [sb(nc, st, "a_KT%d" % i, [128, NT], BF16) for i in range(2)]
        QT = [sb(nc, st, "a_QT%d" % i, [128, NT], BF16) for i in range(2)]
        Vx = [sb(nc, st, "a_V%d" % i, [128, NCH, 128], BF16) for i in range(2)]
        PT = [sb(nc, st, "a_PT%d" % i, [128, 512], BF16) for i in range(4)]
        rden = sb(nc, st, "a_rden", [64, 512], F32)
        ob = [sb(nc, st, "a_ob%d" % i, [64, 512], BF16) for i in range(2)]
        ST = [ps(nc, st, "a_ST%d" % i, [128, 512], F32) for i in range(6)]
        OD = [ps(nc, st, "a_OD%d" % i, [128, 512], F32) for i in range(2)]
        t_KV = [Tl(), Tl()]
        t_rden = Tl()
        t_PT = [Tl() for _ in range(4)]
        t_ob = [Tl(), Tl()]
        t_ST = [Tl() for _ in range(6)]
        t_OD = [Tl(), Tl()]
        dl = [mk.new_dsem(), mk.new_dsem()]
        dout = mk.new_dsem()
        for i in range(2):
            mk.op('pool', lambda i=i: nc.gpsimd.memset(Vx[i][:, :, 64:128], 1.0), w=[t_KV[i]])
            mk.op('pool', lambda i=i: nc.gpsimd.memset(KT[i][:], 0.0), w=[t_KV[i]])
            mk.op('pool', lambda i=i: nc.gpsimd.memset(QT[i][:], 0.0), w=[t_KV[i]])
        si = 0
        pi = 0
        bi = 0
        for hi, (dk, KTd, Vd, QTd, row0, t_src) in enumerate(heads):
            b = hi % 2
            mk.dma('sp', KT[b][0:dk, :], KTd, r=[t_src], w=[t_KV[b]], dsem=dl[b])
            mk.dma('sp', QT[b][0:dk, :], QTd, r=[t_src], w=[t_KV[b]], dsem=dl[b])
            mk.dma('sp', Vx[b][:, :, 0:64], Vd.rearrange("(c p) d -> p c d", p=128), r=[t_src], w=[t_KV[b]], dsem=dl[b])
            blocks = [(CTX + qb * 512, 512, 0, NCH) for qb in range(8)]
            if need_ctx:
                blocks.append((0, 256, 0, 2))
            for (q0, qn, k0, k1) in blocks:
                o = bi % 2
                bi += 1
                LOOK = 3
                slots = {}

                def issue_s(kc):
                    nonlocal si, pi
                    s_ = si % 6
                    si += 1
                    p_ = pi % 4
                    pi += 1
                    slots[kc] = (s_, p_)
                    mk.op('pe', lambda kc=kc, s_=s_, b=b, dk=dk, q0=q0, qn=qn: nc.tensor.matmul(
                        ST[s_][:, 0:qn], lhsT=KT[b][:, kc * 128:(kc + 1) * 128], rhs=QT[b][:, q0:q0 + qn], start=True, stop=True),
                        r=[t_KV[b]], w=[t_ST[s_]])
                    mk.op('act', lambda s_=s_, p_=p_, qn=qn: nc.scalar.activation(out=PT[p_][:, 0:qn], in_=ST[s_][:, 0:qn], func=AF.Exp),
                          r=[t_ST[s_]], w=[t_PT[p_]])

                for kc in range(k0, min(k1, k0 + LOOK)):
                    issue_s(kc)
                for kc in range(k0, k1):
                    if kc + LOOK < k1:
                        issue_s(kc + LOOK)
                    s_, p_ = slots.pop(kc)
                    mk.op('pe', lambda kc=kc, p_=p_, o=o, b=b, qn=qn, k0=k0, k1=k1: nc.tensor.matmul(
                        OD[o][:, 0:qn], lhsT=Vx[b][:, kc, :], rhs=PT[p_][:, 0:qn], start=(kc == k0), stop=(kc == k1 - 1)),
                        r=[t_KV[b], t_PT[p_]], w=[t_OD[o]])
                mk.op('dve', lambda o=o, qn=qn: nc.vector.reciprocal(out=rden[:, 0:qn], in_=OD[o][64:128, 0:qn]), r=[t_OD[o]], w=[t_rden])
                mk.op('dve', lambda o=o, qn=qn: nc.vector.tensor_mul(out=ob[o][:, 0:qn], in0=OD[o][0:64, 0:qn], in1=rden[:, 0:qn]),
                      r=[t_OD[o], t_rden], w=[t_ob[o]])
                mk.dma('sp', S.CT[l][row0:row0 + 64, q0:q0 + qn], ob[o][:, 0:qn], r=[t_ob[o]], w=[S.t_CT], dsem=dout)
        mk.barrier()


def hyena_filter_dev(K, l, st, n, zT_d, tcol0, hT, rl1):
    nc, mk = K.nc, K.mk
    TWO_PI = 2.0 * math.pi
    with ExitStack() as fs:
        zT = sb(nc, fs, "h_zT", [33, n], F32)
        tb = sb(nc, fs, "h_tb", [128, n], F32)
        w1 = sb(nc, fs, "h_w1", [33, 64], F32)
        w23 = sb(nc, fs, "h_w23", [64, 2, 64], F32)
        w4 = sb(nc, fs, "h_w4", [64, 512], F32)
        a_all = [sb(nc, fs, "h_a%d" % i, [64, 512], F32) for i in range(4)]
        wn_all = [sb(nc, fs, "h_wn%d" % i, [128, 512], F32) for i in range(2)]
        ki_all = [sb(nc, fs, "h_ki%d" % i, [64, 512], mybir.dt.int32) for i in range(2)]
        kf_all = [sb(nc, fs, "h_kf%d" % i, [64, 512], F32) for i in range(2)]
        t_ki_all, t_kf_all = [Tl(), Tl()], [Tl(), Tl()]
        t_a_all = [Tl() for _ in range(4)]
        t_wn_all = [Tl(), Tl()]
        fb = sb(nc, fs, "h_fb", [64, 3], F32)
        l1 = sb(nc, fs, "h_l1", [128, 4], F32)
        pf_all = [ps(nc, fs, "h_pf%d" % i, [64, 512], F32) for i in range(2)]
        p4 = [ps(nc, fs, "h_p4%d" % i, [128, 512], F32) for i in range(2)]
        t_in, t_fb, t_l1 = Tl(), Tl(), Tl()
        t_pf_all = [Tl(), Tl()]
        t_p4 = [Tl(), Tl()]
        d = mk.new_dsem()
        mk.dma('sp', zT[:], zT_d, w=[t_in], dsem=d)
        mk.dma('sp', tb[:], K.hy_tdel[:, tcol0:tcol0 + n], w=[t_in], dsem=d)
        mk.dma('sp', w1[:], K.hy_w1[l], w=[t_in], dsem=d)
        mk.dma('sp', w23[:, 0, :], K.hy_w2[l], w=[t_in], dsem=d)
        mk.dma('sp', w23[:, 1, :], K.hy_w3[l], w=[t_in], dsem=d)
        mk.dma('sp', w4[:], K.hy_w4[l], w=[t_in], dsem=d)
        hs = K.hs
        mk.op('dve', lambda: nc.vector.tensor_mul(out=fb[:], in0=hs[0:64, 27:30], in1=hs[0:64, 26:27].to_broadcast([64, 3])), r=[K.t_hs], w=[t_fb])
        nb = (n + 511) // 512
        for blk in range(nb):
            c0 = blk * 512
            cn_ = min(512, n - c0)
            ai = 0
            bb = blk % 2
            a = a_all[bb * 2:bb * 2 + 2]
            t_a = t_a_all[bb * 2:bb * 2 + 2]
            ki, kf, t_ki, t_kf = ki_all[bb], kf_all[bb], t_ki_all[bb], t_kf_all[bb]
            pf = pf_all[bb * 2:bb * 2 + 2] if len(pf_all) == 4 else pf_all
            t_pf = t_pf_all[bb * 2:bb * 2 + 2] if len(pf_all) == 4 else t_pf_all
            for layer in range(3):
                pfi = layer % 2
                if layer == 0:
                    mk.op('pe', lambda pfi=pfi, c0=c0, cn_=cn_: nc.tensor.matmul(pf[pfi][:, 0:cn_], lhsT=w1[:], rhs=zT[:, c0:c0 + cn_], start=True, stop=True),
                          r=[t_in], w=[t_pf[pfi]])
                else:
                    mk.op('pe', lambda pfi=pfi, layer=layer, cn_=cn_, ai=ai: nc.tensor.matmul(pf[pfi][:, 0:cn_], lhsT=w23[:, layer - 1, :], rhs=a[ai][:, 0:cn_],
                                                                                       start=True, stop=True), r=[t_in, t_a[ai]], w=[t_pf[pfi]])
                    ai = 1 - ai
                mk.op('dve', lambda pfi=pfi, ai=ai, layer=layer, cn_=cn_: nc.vector.tensor_scalar(
                    out=a[ai][:, 0:cn_], in0=pf[pfi][:, 0:cn_], scalar1=hs[0:64, 26:27], scalar2=fb[:, layer:layer + 1], op0=ALU.mult, op1=ALU.add),
                    r=[t_pf[pfi], t_fb, K.t_hs], w=[t_a[ai]])
                mk.op('dve', lambda ai=ai, cn_=cn_: nc.vector.tensor_scalar(out=ki[:, 0:cn_], in0=a[ai][:, 0:cn_], scalar1=1.0 / TWO_PI, scalar2=None, op0=ALU.mult),
                      r=[t_a[ai]], w=[t_ki])
                mk.op('dve', lambda cn_=cn_: nc.vector.tensor_copy(out=kf[:, 0:cn_], in_=ki[:, 0:cn_]), r=[t_ki], w=[t_kf])
                mk.op('dve', lambda ai=ai, cn_=cn_: nc.vector.scalar_tensor_tensor(out=a[ai][:, 0:cn_], in0=kf[:, 0:cn_], scalar=-TWO_PI, in1=a[ai][:, 0:cn_],
                                                                                  op0=ALU.mult, op1=ALU.add), r=[t_kf, t_a[ai]], w=[t_a[ai]])
                mk.op('act', lambda ai=ai, cn_=cn_: nc.scalar.activation(out=kf[:, 0:cn_], in_=a[ai][:, 0:cn_], func=AF.Sign), r=[t_a[ai]], w=[t_kf])
                mk.op('dve', lambda ai=ai, cn_=cn_: nc.vector.scalar_tensor_tensor(out=a[ai][:, 0:cn_], in0=kf[:, 0:cn_], scalar=-math.pi, in1=a[ai][:, 0:cn_],
                                                                                  op0=ALU.mult, op1=ALU.add), r=[t_kf, t_a[ai]], w=[t_a[ai]])
                mk.op('act', lambda ai=ai, cn_=cn_: nc.scalar.activation(out=a[ai][:, 0:cn_], in_=a[ai][:, 0:cn_], func=AF.Sin, scale=-1.0),
                      r=[t_a[ai]], w=[t_a[ai]])
            for c in range(4):
                pi_ = c % 2
                mk.op('pe', lambda c=c, pi_=pi_, ai=ai, cn_=cn_: nc.tensor.matmul(p4[pi_][:, 0:cn_], lhsT=w4[:, c * 128:(c + 1) * 128], rhs=a[ai][:, 0:cn_], start=True, stop=True),
                      r=[t_in, t_a[ai]], w=[t_p4[pi_]])
                wn, t_wn = wn_all[c % 2], t_wn_all[c % 2]
                mk.op('act', lambda c=c, c0=c0, cn_=cn_, wn=wn: nc.scalar.activation(out=wn[:, 0:cn_], in_=tb[:, c0:c0 + cn_], func=AF.Exp, scale=K.negdel[:, c:c + 1]),
                      r=[t_in, K.t_const], w=[t_wn])
                mk.op('dve', lambda c=c, pi_=pi_, c0=c0, cn_=cn_, wn=wn: nc.vector.tensor_mul(out=hT[:, c, c0:c0 + cn_], in0=p4[pi_][:, 0:cn_], in1=wn[:, 0:cn_]),
                      r=[t_p4[pi_], t_wn], w=[K.t_hT])
        for c in range(4):
            lo = 0 if c < 2 else 1
            mk.op('dve', lambda c=c, lo=lo: nc.vector.tensor_reduce(out=l1[:, c:c + 1], in_=hT[:, c, lo:n], axis=mybir.AxisListType.X, op=ALU.add,
                                                                    apply_absolute_value=True), r=[K.t_hT], w=[t_l1])
        mk.op('dve', lambda: nc.vector.tensor_add(out=rl1[:], in0=l1[:, 0:2], in1=l1[:, 2:4]), r=[t_l1], w=[K.t_rl1])
        mk.op('dve', lambda: nc.vector.reciprocal(out=rl1[:], in_=rl1[:]), r=[K.t_rl1], w=[K.t_rl1])
        mk.barrier()


def hyena_conv_dev(K, l, st, n, off, hT, rl1):
    nc, mk = K.nc, K.mk
    S = K.S
    hs = K.hs
    with ExitStack() as cs:
        pp = sb(nc, cs, "h_pp", [128, n + 2], F32)
        u = sb(nc, cs, "h_u", [128, n], F32)
        x1 = sb(nc, cs, "h_x1", [128, n], F32)
        x0 = sb(nc, cs, "h_x0", [128, n], F32)
        ya = [sb(nc, cs, "h_y%d" % i, [128, n], F32) for i in range(2)]
        ob = sb(nc, cs, "h_ob", [128, n], BF16)
        t_pp, t_u, t_x1, t_x0, t_ob = Tl(), Tl(), Tl(), Tl(), Tl()
        t_y = [Tl(), Tl()]
        d = mk.new_dsem()
        dout = mk.new_dsem()
        mk.op('pool', lambda: nc.gpsimd.memset(pp[:, 0:1], 0.0), w=[t_pp])
        mk.op('pool', lambda: nc.gpsimd.memset(pp[:, n + 1:n + 2], 0.0), w=[t_pp])
        for cc in range(2):
            for which, dst, t_dst in ((0, u, t_u), (1, x1, t_x1), (2, x0, t_x0)):
                chn = which * 2 + cc
                mk.dma('sp', pp[:, 1:n + 1], S.PBT[l][chn * 128:(chn + 1) * 128, off:off + n], r=[S.t_PBT], w=[t_pp], dsem=d)
                mk.op('dve', lambda chn=chn, dst=dst: nc.vector.tensor_scalar(out=dst[:], in0=pp[:, 0:n], scalar1=hs[:, chn:chn + 1], scalar2=hs[:, 18 + chn:19 + chn],
                                                                            op0=ALU.mult, op1=ALU.add), r=[t_pp, K.t_hs], w=[t_dst])
                for k in (1, 2):
                    mk.op('dve', lambda chn=chn, dst=dst, k=k: nc.vector.scalar_tensor_tensor(out=dst[:], in0=pp[:, k:n + k], scalar=hs[:, k * 6 + chn:k * 6 + chn + 1],
                                                                                              in1=dst[:], op0=ALU.mult, op1=ALU.add), r=[t_pp, K.t_hs, t_dst], w=[t_dst])
            mk.op('dve', lambda: nc.vector.tensor_mul(out=u[:], in0=u[:], in1=x1[:]), r=[t_u, t_x1], w=[t_u])
            mk.op('pool', lambda: nc.gpsimd.memset(ya[0][:], 0.0), w=[t_y[0]])
            mk.op('pool', lambda: nc.gpsimd.memset(ya[1][:], 0.0), w=[t_y[1]])
            k = 0
            for lag in range(n):
                i = k % 2
                k += 1
                mk.op('dve', lambda lag=lag, i=i, cc=cc: nc.vector.scalar_tensor_tensor(out=ya[i][:, lag:n], in0=u[:, 0:n - lag], scalar=hT[:, cc, lag:lag + 1],
                                                                                        in1=ya[i][:, lag:n], op0=ALU.mult, op1=ALU.add), r=[t_u, K.t_hT, t_y[i]], w=[t_y[i]])
            for lag in range(1, n):
                i = k % 2
                k += 1
                mk.op('dve', lambda lag=lag, i=i, cc=cc: nc.vector.scalar_tensor_tensor(out=ya[i][:, 0:n - lag], in0=u[:, lag:n], scalar=hT[:, 2 + cc, lag:lag + 1],
                                                                                        in1=ya[i][:, 0:n - lag], op0=ALU.mult, op1=ALU.add), r=[t_u, K.t_hT, t_y[i]], w=[t_y[i]])
            mk.op('dve', lambda: nc.vector.tensor_add(out=ya[0][:], in0=ya[0][:], in1=ya[1][:]), r=[t_y[0], t_y[1]], w=[t_y[0]])
            mk.op('dve', lambda cc=cc: nc.vector.tensor_scalar_mul(out=ya[0][:], in0=ya[0][:], scalar1=rl1[:, cc:cc + 1]), r=[t_y[0], K.t_rl1], w=[t_y[0]])
            mk.op('dve', lambda cc=cc: nc.vector.scalar_tensor_tensor(out=ya[0][:], in0=u[:], scalar=hs[:, 24 + cc:25 + cc], in1=ya[0][:], op0=ALU.mult, op1=ALU.add),
                  r=[t_u, t_y[0], K.t_hs], w=[t_y[0]])
            mk.op('dve', lambda: nc.vector.tensor_mul(out=ob[:], in0=ya[0][:], in1=x0[:]), r=[t_y[0], t_x0], w=[t_ob])
            mk.dma('sp', S.CT[l][384 + cc * 128:384 + (cc + 1) * 128, off:off + n], ob[:], r=[t_ob], w=[S.t_CT], dsem=dout)
        mk.barrier()


def fft_setup(K, st):
    nc, mk = K.nc, K.mk
    F = Ctx()
    F.F1 = sb(nc, st, "x_F1", [64, 256], BF16)
    F.GB = [sb(nc, st, "x_GB%d" % i, [128, 24, 128], BF16) for i in range(2)]
    F.Us = sb(nc, st, "x_Us", [64, 128, 64], BF16)
    F.AC = sb(nc, st, "x_AC", [128, 16384], BF16)
    F.stg = sb(nc, st, "x_stg", [128, SEQ], BF16)
    F.t_stg = Tl()
    F.P1 = [ps(nc, st, "x_P1%d" % i, [128, 2, 256], F32) for i in range(2)]
    F.P2 = [ps(nc, st, "x_P2%d" % i, [128, 4, 2, 64], F32) for i in range(2)]
    F.t_tab, F.t_Us, F.t_AC, F.t_Y = Tl(), Tl(), Tl(), Tl()
    F.t_GB = [Tl(), Tl()]
    F.t_P1 = [Tl(), Tl()]
    F.t_P2 = [Tl(), Tl()]
    F.d_tab = mk.new_dsem()
    F.d_gb = [mk.new_dsem(), mk.new_dsem()]
    F.d_us = mk.new_dsem()
    F.d_x = [mk.new_dsem() for _ in range(4)]
    F.d_st = mk.new_dsem()
    F.gbi = 0
    F.xi = 0
    F.p2i = 0
    mk.dma('pool', F.F1[:], K.fft_F1[:, :], w=[F.t_tab], dsem=F.d_tab)
    for i in range(2):
        mk.op('pool', lambda i=i: nc.gpsimd.memset(F.GB[i][:], 0.0), w=[F.t_GB[i]])
    return F


def fft_alloc_work(K, F, st, extra_p2):
    nc = K.nc
    F.xs = [sb(nc, st, "x_xs%d" % i, [128, 4, 2, 64], F32) for i in range(4)]
    F.hfs = [sb(nc, st, "x_hf%d" % i, [128, 4, 2, 64], F32) for i in range(4)]
    F.tmp = [sb(nc, st, "x_tmp%d" % i, [128, 4, 64], F32) for i in range(4)]
    F.t_xs = [Tl() for _ in range(4)]
    F.t_hfs = [Tl() for _ in range(4)]
    F.t_tmp = [Tl() for _ in range(4)]
    F.P2 = F.P2[:2] + [ps(nc, st, "x_P2x%d" % i, [128, 4, 2, 64], F32) for i in range(extra_p2)]
    F.t_P2 = F.t_P2[:2] + [Tl() for _ in range(extra_p2)]


def fft_setup_inv(K, F, st):
    nc, mk = K.nc, K.mk
    F.W3 = sb(nc, st, "x_W3", [128, 512], BF16)
    F.F4 = sb(nc, st, "x_F4", [128, 64, 2, 64], BF16)
    F.Ysb = sb(nc, st, "x_Y", [128, 2, 64, 128], BF16)
    mk.dma('pool', F.W3[:], K.fft_W3[:, :], w=[F.t_tab], dsem=F.d_tab)
    mk.dma('pool', F.F4[:].rearrange("p a b c -> p (a b c)"), K.fft_F4[:, :], w=[F.t_tab], dsem=F.d_tab)


def fft_fwd(K, F, src, t_src, consumer):
    nc, mk = K.nc, K.mk
    A = F.AC[:].rearrange("p (ri k1 pr) -> p ri k1 pr", ri=2, k1=128)
    srcv = src.rearrange("c (s1 s2) -> s1 c s2", s2=64)
    for cb in range(16):
        mk.dma('sp', F.Us[:, cb * 8:(cb + 1) * 8, :], srcv[:, cb * 8:(cb + 1) * 8, :], r=[t_src], w=[F.t_Us], dsem=F.d_us)
    for pp2 in range(32):
        b = pp2 % 2
        for j in range(2):
            pr = pp2 * 2 + j
            mk.op('pe', lambda b=b, j=j, pr=pr: nc.tensor.matmul(F.P1[b][:, j, :], lhsT=F.Us[:, 2 * pr:2 * pr + 2, :].rearrange("p c s -> p (c s)"),
                                                                rhs=F.F1[:], start=True, stop=True), r=[F.t_Us, F.t_tab], w=[F.t_P1[b]])
        eng = 'act' if pp2 % 2 == 0 else 'dve'
        outv = A[:, :, :, 2 * pp2:2 * pp2 + 2].rearrange("p ri k1 pr -> p pr ri k1")
        inv = F.P1[b][:].rearrange("p pr (ri k1) -> p pr ri k1", ri=2)
        if eng == 'act':
            mk.op('act', lambda outv=outv, inv=inv: nc.scalar.copy(out=outv, in_=inv), r=[F.t_P1[b]], w=[F.t_AC])
        else:
            mk.op('dve', lambda outv=outv, inv=inv: nc.vector.tensor_copy(out=outv, in_=inv), r=[F.t_P1[b]], w=[F.t_AC])
    for kc in range(16):
        gb = F.gbi % 2
        F.gbi += 1
        if not K.gb_cached[kc]:
            srcg = K.fft_G[:, kc * 8:(kc + 1) * 8, :, :].rearrange("p a b c -> p (a b) c")
            mk.dma('pool', F.GB[gb][0:64, :, 0:64], srcg, w=[F.t_GB[gb]], dsem=F.d_gb[gb])
            mk.dma('pool', F.GB[gb][64:128, :, 64:128], srcg, w=[F.t_GB[gb]], dsem=F.d_gb[gb])
            mk.dma('sp', K.S.GBd[kc], F.GB[gb][:], r=[F.t_GB[gb]], w=[K.S.t_GBd], dsem=F.d_st)
            K.gb_cached[kc] = True
        else:
            mk.dma('sp', F.GB[gb][:], K.S.GBd[kc], r=[K.S.t_GBd], w=[F.t_GB[gb]], dsem=F.d_gb[gb])
        for half in range(2):
            pb = F.p2i % len(F.P2)
            F.p2i += 1
            for jj in range(4):
                kk = half * 4 + jj
                k1 = kc * 8 + kk
                Gre, Gim, nGim = F.GB[gb][:, kk * 3 + 0, :], F.GB[gb][:, kk * 3 + 1, :], F.GB[gb][:, kk * 3 + 2, :]
                Are, Aim = A[:, 0, k1, :], A[:, 1, k1, :]
                for (ri, l0, r0, l1, r1) in ((0, Gre, Are, nGim, Aim), (1, Gim, Are, Gre, Aim)):
                    mk.op('pe', lambda pb=pb, jj=jj, ri=ri, l0=l0, r0=r0: nc.tensor.matmul(F.P2[pb][:, jj, ri, :], lhsT=l0, rhs=r0, start=True, stop=False),
                          r=[F.t_GB[gb], F.t_AC], w=[F.t_P2[pb]])
                    mk.op('pe', lambda pb=pb, jj=jj, ri=ri, l1=l1, r1=r1: nc.tensor.matmul(F.P2[pb][:, jj, ri, :], lhsT=l1, rhs=r1, start=False, stop=True),
                          r=[F.t_GB[gb], F.t_AC], w=[F.t_P2[pb]])
            consumer(kc * 8 + half * 4, F.P2[pb], F.t_P2[pb])


def fft_inv(K, F, ysb, t_ysb):
    nc, mk = K.nc, K.mk
    C = F.AC[:].rearrange("p (ri t2 c) -> p ri t2 c", ri=2, t2=64)
    P3, tP3 = F.P1, F.t_P1
    P4 = [F.P2[i][:].rearrange("p a b c -> p (a b) c") for i in range(2)]
    tP4 = F.t_P2[0:2]
    for pp2 in range(32):
        b = pp2 % 2
        for j in range(2):
            pr = pp2 * 2 + j
            mk.op('pe', lambda b=b, j=j, pr=pr: nc.tensor.matmul(P3[b][:, j, :], lhsT=F.Ysb[:, 0, pr, :], rhs=F.W3[:, 0:256], start=True, stop=False),
                  r=[F.t_Y, F.t_tab], w=[tP3[b]])
            mk.op('pe', lambda b=b, j=j, pr=pr: nc.tensor.matmul(P3[b][:, j, :], lhsT=F.Ysb[:, 1, pr, :], rhs=F.W3[:, 256:512], start=False, stop=True),
                  r=[F.t_Y, F.t_tab], w=[tP3[b]])
        for j in range(2):
            outv = C[:, :, :, 4 * pp2 + 2 * j:4 * pp2 + 2 * j + 2].rearrange("p ri t2 cc -> p cc ri t2")
            inv = P3[b][:, j, :].rearrange("p (cc ri t2) -> p cc ri t2", cc=2, ri=2)
            if j == 0:
                mk.op('act', lambda outv=outv, inv=inv: nc.scalar.copy(out=outv, in_=inv), r=[tP3[b]], w=[F.t_AC])
            else:
                mk.op('dve', lambda outv=outv, inv=inv: nc.vector.tensor_copy(out=outv, in_=inv), r=[tP3[b]], w=[F.t_AC])
    yv = ysb[:].rearrange("p (t1 t2) -> p t1 t2", t2=64)
    for t8 in range(8):
        b = t8 % 2
        for j in range(8):
            t2 = t8 * 8 + j
            mk.op('pe', lambda b=b, j=j, t2=t2: nc.tensor.matmul(P4[b][:, j, :], lhsT=C[:, 0, t2, :], rhs=F.F4[:, t2, 0, :], start=True, stop=False),
                  r=[F.t_AC, F.t_tab], w=[tP4[b]])
            mk.op('pe', lambda b=b, j=j, t2=t2: nc.tensor.matmul(P4[b][:, j, :], lhsT=C[:, 1, t2, :], rhs=F.F4[:, t2, 1, :], start=False, stop=True),
                  r=[F.t_AC, F.t_tab], w=[tP4[b]])
        outv = yv[:, :, t8 * 8:(t8 + 1) * 8].rearrange("p t1 t2 -> p t2 t1")
        if t8 % 2 == 0:
            mk.op('act', lambda outv=outv, b=b: nc.scalar.copy(out=outv, in_=P4[b]), r=[tP4[b]], w=[t_ysb])
        else:
            mk.op('dve', lambda outv=outv, b=b: nc.vector.tensor_copy(out=outv, in_=P4[b]), r=[tP4[b]], w=[t_ysb])


def hyena_lat_fft_filter(K, l, F, hT):
    nc, mk = K.nc, K.mk
    S = K.S
    n = SEQ
    stg, t_stg = F.stg, F.t_stg
    dst = mk.new_dsem()
    with ExitStack() as ws:
        fft_alloc_work(K, F, ws, 4)
        for c in range(4):
            mk.op('pool', lambda c=c: nc.gpsimd.tensor_copy(out=stg[:], in_=hT[:, c, :]), r=[K.t_hT], w=[t_stg])
            if c >= 2:
                mk.op('pool', lambda: nc.gpsimd.memset(stg[:, 0:1], 0.0), w=[t_stg])
            mk.dma('sp', S.HTd[c * 128:(c + 1) * 128, :], stg[:], r=[t_stg], w=[S.t_HTd], dsem=dst)
        for g in range(2):
            def cons_a(k1_0, P2, tP2, g=g):
                i = F.xi % 4
                F.xi += 1
                mk.op('act', lambda: nc.scalar.copy(out=F.xs[i][:], in_=P2[:]), r=[tP2], w=[F.t_xs[i]])
                mk.dma('pool', S.XF[g][:, k1_0:k1_0 + 4, :, :], F.xs[i][:], r=[F.t_xs[i]], w=[S.t_XF], dsem=F.d_st)

            def cons_b(k1_0, P2, tP2, g=g):
                i = F.xi % 4
                F.xi += 1
                mk.dma('sp', F.hfs[i][:], S.XF[g][:, k1_0:k1_0 + 4, :, :], r=[S.t_XF], w=[F.t_hfs[i]], dsem=F.d_x[i])
                mk.op('dve', lambda: nc.vector.tensor_add(out=F.xs[i][:, :, 0, :], in0=P2[:, :, 0, :], in1=F.hfs[i][:, :, 0, :]), r=[tP2, F.t_hfs[i]], w=[F.t_xs[i]])
                mk.op('dve', lambda: nc.vector.tensor_sub(out=F.xs[i][:, :, 1, :], in0=F.hfs[i][:, :, 1, :], in1=P2[:, :, 1, :]), r=[tP2, F.t_hfs[i]], w=[F.t_xs[i]])
                mk.dma('pool', S.HF[g][:, k1_0:k1_0 + 4, :, :], F.xs[i][:], r=[F.t_xs[i]], w=[S.t_HF], dsem=F.d_st)

            fft_fwd(K, F, S.HTd[g * 128:(g + 1) * 128, :], S.t_HTd, cons_a)
            fft_fwd(K, F, S.HTd[256 + g * 128:256 + (g + 1) * 128, :], S.t_HTd, cons_b)
        mk.barrier()


def hyena_lat_fft_u(K, l, F, fs, rl1):
    nc, mk = K.nc, K.mk
    S = K.S
    hs = K.hs
    n, off = SEQ, CTX
    stg, t_stg = F.stg, F.t_stg
    dst = mk.new_dsem()
    fft_alloc_work(K, F, fs, 4)
    if True:
        pp = sb(nc, fs, "h_pp", [128, n + 2], F32)
        u = sb(nc, fs, "h_u", [128, n], F32)
        x0 = sb(nc, fs, "h_x0", [128, n], F32)
        ysb = sb(nc, fs, "h_ysb", [128, n], F32)
        t_pp, t_u, t_x0, t_ysb = Tl(), Tl(), Tl(), Tl()
        d = mk.new_dsem()
        dout = mk.new_dsem()
        mk.op('pool', lambda: nc.gpsimd.memset(pp[:, 0:1], 0.0), w=[t_pp])
        mk.op('pool', lambda: nc.gpsimd.memset(pp[:, n + 1:n + 2], 0.0), w=[t_pp])
        Yv = F.Ysb
        for cc in range(2):
            for which, dstt, t_dst in ((0, u, t_u), (1, ysb, t_ysb), (2, x0, t_x0)):
                chn = which * 2 + cc
                mk.dma('sp', pp[:, 1:n + 1], S.PBT[l][chn * 128:(chn + 1) * 128, off:off + n], r=[S.t_PBT], w=[t_pp], dsem=d)
                mk.op('dve', lambda chn=chn, dstt=dstt: nc.vector.tensor_scalar(out=dstt[:], in0=pp[:, 0:n], scalar1=hs[:, chn:chn + 1], scalar2=hs[:, 18 + chn:19 + chn],
                                                                              op0=ALU.mult, op1=ALU.add), r=[t_pp, K.t_hs], w=[t_dst])
                for k in (1, 2):
                    mk.op('dve', lambda chn=chn, dstt=dstt, k=k: nc.vector.scalar_tensor_tensor(out=dstt[:], in0=pp[:, k:n + k], scalar=hs[:, k * 6 + chn:k * 6 + chn + 1],
                                                                                                in1=dstt[:], op0=ALU.mult, op1=ALU.add), r=[t_pp, K.t_hs, t_dst], w=[t_dst])
            mk.op('dve', lambda: nc.vector.tensor_mul(out=u[:], in0=u[:], in1=ysb[:]), r=[t_u, t_ysb], w=[t_u])
            mk.op('pool', lambda: nc.gpsimd.tensor_copy(out=stg[:], in_=u[:]), r=[t_u], w=[t_stg])
            mk.dma('sp', S.UTd[cc * 128:(cc + 1) * 128, :], stg[:], r=[t_stg], w=[S.t_UTd], dsem=dst)

            def cons_c(k1_0, P2, tP2, cc=cc):
                i = F.xi % 4
                F.xi += 1
                mk.dma('sp', F.hfs[i][:], S.HF[cc][:, k1_0:k1_0 + 4, :, :], r=[S.t_HF], w=[F.t_hfs[i]], dsem=F.d_x[i])
                mk.op('act', lambda: nc.scalar.copy(out=F.xs[i][:], in_=P2[:]), r=[tP2], w=[F.t_xs[i]])
                Xre, Xim = F.xs[i][:, :, 0, :], F.xs[i][:, :, 1, :]
                Hre, Him = F.hfs[i][:, :, 0, :], F.hfs[i][:, :, 1, :]
                ta, tb, tc_, td = F.tmp
                yre = Yv[:, 0, :, k1_0:k1_0 + 4].rearrange("p pr k -> p k pr")
                yim = Yv[:, 1, :, k1_0:k1_0 + 4].rearrange("p pr k -> p k pr")
                mk.op('dve', lambda: nc.vector.tensor_mul(out=ta[:], in0=Xre, in1=Hre), r=[F.t_xs[i], F.t_hfs[i]], w=[F.t_tmp[0]])
                mk.op('dve', lambda: nc.vector.tensor_mul(out=tb[:], in0=Xim, in1=Him), r=[F.t_xs[i], F.t_hfs[i]], w=[F.t_tmp[1]])
                mk.op('dve', lambda: nc.vector.tensor_sub(out=yre, in0=ta[:], in1=tb[:]), r=[F.t_tmp[0], F.t_tmp[1]], w=[F.t_Y])
                mk.op('pool', lambda: nc.gpsimd.tensor_mul(out=tc_[:], in0=Xre, in1=Him), r=[F.t_xs[i], F.t_hfs[i]], w=[F.t_tmp[2]])
                mk.op('pool', lambda: nc.gpsimd.tensor_mul(out=td[:], in0=Xim, in1=Hre), r=[F.t_xs[i], F.t_hfs[i]], w=[F.t_tmp[3]])
                mk.op('pool', lambda: nc.gpsimd.tensor_add(out=yim, in0=tc_[:], in1=td[:]), r=[F.t_tmp[2], F.t_tmp[3]], w=[F.t_Y])

            fft_fwd(K, F, S.UTd[cc * 128:(cc + 1) * 128, :], S.t_UTd, cons_c)
            fft_inv(K, F, ysb, t_ysb)
            mk.op('dve', lambda cc=cc: nc.vector.tensor_scalar_mul(out=ysb[:], in0=ysb[:], scalar1=rl1[:, cc:cc + 1]), r=[t_ysb, K.t_rl1], w=[t_ysb])
            mk.op('dve', lambda cc=cc: nc.vector.scalar_tensor_tensor(out=ysb[:], in0=u[:], scalar=hs[:, 24 + cc:25 + cc], in1=ysb[:], op0=ALU.mult, op1=ALU.add),
                  r=[t_u, t_ysb, K.t_hs], w=[t_ysb])
            mk.op('dve', lambda: nc.vector.tensor_mul(out=stg[:], in0=ysb[:], in1=x0[:]), r=[t_ysb, t_x0], w=[t_stg])
            mk.dma('sp', S.CT[l][384 + cc * 128:384 + (cc + 1) * 128, off:off + n], stg[:], r=[t_stg], w=[S.t_CT], dsem=dout)
        mk.barrier()


def phase_hyena(K, l, need_ctx):
    nc, mk = K.nc, K.mk
    with ExitStack() as st:
        K.hs = sb(nc, st, "h_hs", [128, 32], F32)
        K.negpi = sb(nc, st, "h_negpi", [128, 1], F32)
        K.negdel = sb(nc, st, "h_negdel", [128, 4], F32)
        rl1 = sb(nc, st, "h_rl1", [128, 2], F32)
        K.t_hs, K.t_hT, K.t_rl1 = Tl(), Tl(), Tl()
        d = mk.new_dsem()
        mk.dma('sp', K.hs[:], K.hy_small[l], w=[K.t_hs], dsem=d)
        mk.dma('sp', K.negdel[:], K.hy_tdel[:, 0:4], w=[K.t_const], dsem=d)
        mk.op('dve', lambda: nc.vector.memset(K.negpi[:], -math.pi), w=[K.t_const])
        mk.op('dve', lambda: nc.vector.tensor_scalar_mul(out=K.negdel[:], in0=K.negdel[:], scalar1=-1.0), r=[K.t_const], w=[K.t_const])
        seqs = [(SEQ, CTX, K.zT_lat, 4)]
        if need_ctx:
            seqs.append((CTX, 0, K.zT_ctx, 4 + SEQ))
        for (n, off, zT_d, tcol0) in seqs:
            if n == SEQ and USE_FFT:
                with ExitStack() as s1:
                    F = fft_setup(K, s1)
                    with ExitStack() as s2:
                        hT = sb(nc, s2, "h_hT", [128, 4, n], F32)
                        hyena_filter_dev(K, l, s2, n, zT_d, tcol0, hT, rl1)
                        hyena_lat_fft_filter(K, l, F, hT)
                    fft_setup_inv(K, F, s1)
                    hyena_lat_fft_u(K, l, F, s1, rl1)
            else:
                with ExitStack() as s2:
                    hT = sb(nc, s2, "h_hT", [128, 4, n], F32)
                    hyena_filter_dev(K, l, s2, n, zT_d, tcol0, hT, rl1)
                    hyena_conv_dev(K, l, s2, n, off, hT, rl1)
        mk.barrier()


def phase_mixout(K, l, Xres, t_xres, Xout, t_xout, first_chunk=0):
    nc, mk = K.nc, K.mk
    S = K.S
    with ExitStack() as st:
        wo = sb(nc, st, "o_wo", [128, 8, D], BF16)
        modv = sb(nc, st, "o_mod", [128, 3, D], F32)
        xc = [sb(nc, st, "o_xc%d" % i, [128, D], F32) for i in range(2)]
        cT = [sb(nc, st, "o_cT%d" % i, [128, 8, 128], BF16) for i in range(2)]
        tt = sb(nc, st, "o_tt", [128, D], F32)
        st6 = sb(nc, st, "o_st6", [128, 2, 6], F32)
        mv = sb(nc, st, "o_mv", [128, 2], F32)
        rstd = sb(nc, st, "o_rstd", [128, 1], F32)
        pm = [ps(nc, st, "o_pm%d" % i, [128, 512], F32) for i in range(4)]
        t_wo, t_mod, t_tt, t_st = Tl(), Tl(), Tl(), Tl()
        t_xc = [Tl(), Tl()]
        t_cT = [Tl(), Tl()]
        t_pm = [Tl() for _ in range(4)]
        dws, dmod, dout = mk.new_dsem(), mk.new_dsem(), mk.new_dsem()
        dx = [mk.new_dsem(), mk.new_dsem()]
        for kc in range(8):
            mk.dma('pool', wo[:, kc, :], K.w_out[l][kc * 128:(kc + 1) * 128, :], w=[t_wo], dsem=dws)
        cur_g = None
        for ch in range(first_chunk, NCH):
            g = 1 if ch < 2 else 0
            if g != cur_g:
                cur_g = g
                mrow = K.mod[l][g]
                for i, src in enumerate([mrow[5 * D:6 * D], K.ln_g[l, 1], K.ln_b[l, 1]]):
                    mk.dma('sp', modv[:, i, :], src.partition_broadcast(128), r=[K.t_mod[l]], w=[t_mod], dsem=dmod)
            xi = ch % 2
            tok = slice(ch * 128, (ch + 1) * 128)

            def issue_loads(c2):
                x2 = c2 % 2
                tk = slice(c2 * 128, (c2 + 1) * 128)
                mk.dma('sp', xc[x2][:], Xres[tk, :], r=[t_xres], w=[t_xc[x2]], dsem=dx[x2])
                mk.dma('sp', cT[x2][:], S.CT[l].rearrange("(c p) t -> p c t", p=128)[:, :, tk], r=[S.t_CT], w=[t_cT[x2]], dsem=dx[x2])

            if ch == first_chunk:
                issue_loads(ch)
            if ch + 1 < NCH:
                issue_loads(ch + 1)
            for hf in range(2):
                pi = (ch % 2) * 2 + hf
                for kc in range(8):
                    mk.op('pe', lambda kc=kc, hf=hf, pi=pi, xi=xi: nc.tensor.matmul(pm[pi][:], lhsT=cT[xi][:, kc, :], rhs=wo[:, kc, hf * 512:(hf + 1) * 512],
                                                                                   start=(kc == 0), stop=(kc == 7)), r=[t_cT[xi], t_wo], w=[t_pm[pi]])
                mk.op('dve', lambda hf=hf, pi=pi: nc.vector.tensor_mul(out=tt[:, hf * 512:(hf + 1) * 512], in0=pm[pi][:], in1=modv[:, 0, hf * 512:(hf + 1) * 512]),
                      r=[t_pm[pi], t_mod], w=[t_tt])
            mk.op('dve', lambda xi=xi: nc.vector.scalar_tensor_tensor(out=xc[xi][:], in0=xc[xi][:], scalar=ALPHA, in1=tt[:], op0=ALU.mult, op1=ALU.add),
                  r=[t_tt, t_xc[xi]], w=[t_xc[xi]])
            ln_stats(K, xc[xi], t_xc[xi], st6, mv, rstd, t_st)
            mk.op('dve', lambda xi=xi: nc.vector.scalar_tensor_tensor(out=xc[xi][:], in0=xc[xi][:], scalar=mv[:, 0:1], in1=modv[:, 1, :],
                                                                      op0=ALU.subtract, op1=ALU.mult), r=[t_xc[xi], t_st, t_mod], w=[t_xc[xi]])
            mk.op('dve', lambda xi=xi: nc.vector.scalar_tensor_tensor(out=xc[xi][:], in0=xc[xi][:], scalar=rstd[:, 0:1], in1=modv[:, 2, :],
                                                                      op0=ALU.mult, op1=ALU.add), r=[t_xc[xi], t_st, t_mod], w=[t_xc[xi]])
            mk.dma('sp', Xout[tok, :], xc[xi][:], r=[t_xc[xi]], w=[t_xout], dsem=dout)
        mk.barrier()


def build(stop_after=None, dbg=()):
    nc = bass.Bass("TRN2", target_bir_lowering=False)
    K = Ctx()
    K.nc = nc
    K.mk = MK(nc)
    mk = K.mk
    DECLARED.clear()

    def din(name, shape):
        DECLARED.add(name)
        return nc.dram_tensor(name, shape, F32, kind="ExternalInput").ap()

    K.xin = din("xin", [NT, D])
    K.cc_in = din("cc", [128, 16])
    K.ada_w = din("ada_w", [DEPTH, D, NMOD * D])
    K.ada_b = din("ada_b", [DEPTH, NMOD * D])
    K.w_gu = [din("ffn1_w_gu", [DEPTH, D, 2 * DFF]), din("ffn2_w_gu", [DEPTH, D, 2 * DFF])]
    K.w_dn = [din("ffn1_w_down", [DEPTH, DFF, D]), din("ffn2_w_down", [DEPTH, DFF, D])]
    K.ln_g = din("ln_g", [DEPTH, 3, D])
    K.ln_b = din("ln_b", [DEPTH, 3, D])
    K.w_in = din("w_in", [DEPTH, D, P_IN])
    K.w_out = din("w_out", [DEPTH, D, D])
    K.a_qn = din("a_q_norm", [DEPTH, 64])
    K.a_kn = din("a_k_norm", [DEPTH, 64])
    K.c_qn = din("mla_q_norm", [DEPTH, 256])
    K.c_kvn = din("mla_kv_norm", [DEPTH, 128])
    K.w_uq = din("mla_w_uq", [DEPTH, 256, 576])
    K.w_ukv = din("mla_w_ukv", [DEPTH, 128, 768])
    K.hy_conv_w = din("hy_conv_w", [DEPTH, 3, 768])
    K.hy_conv_b = din("hy_conv_b", [DEPTH, 768])
    K.hy_w1 = din("hy_f_w1", [DEPTH, 33, 64])
    K.hy_b1 = din("hy_f_b1", [DEPTH, 64])
    K.hy_w2 = din("hy_f_w2", [DEPTH, 64, 64])
    K.hy_b2 = din("hy_f_b2", [DEPTH, 64])
    K.hy_w3 = din("hy_f_w3", [DEPTH, 64, 64])
    K.hy_b3 = din("hy_f_b3", [DEPTH, 64])
    K.hy_w4 = din("hy_f_w4", [DEPTH, 64, 512])
    K.hy_freq = din("hy_f_freq", [DEPTH, 64])
    K.hy_bias = din("hy_bias", [DEPTH, 256])
    K.ident_in = din("ident", [128, 128])
    K.hy_small = din("hy_small", [DEPTH, 128, 32])
    K.ropeA_in = din("ropeA", [SEQ, 64])
    K.ropeC_in = din("ropeC", [SEQ, 32])
    K.zT_lat = din("zT_lat", [33, SEQ])
    K.zT_ctx = din("zT_ctx", [33, CTX])
    K.hy_tdel = din("hy_tdel", [128, 4 + SEQ + CTX])
    K.fft_F1 = din("fft_F1", [64, 256])
    K.fft_G = din("fft_G", [64, 128, 3, 64])
    K.fft_W3 = din("fft_W3", [128, 512])
    K.fft_F4 = din("fft_F4", [128, 8192])
    K.out = nc.dram_tensor("out", [SEQ, D], F32, kind="ExternalOutput").ap()
    K.mod = nc.dram_tensor("mod", [DEPTH, 2, NMOD * D], F32).ap()
    K.t_mod = [Tl(), Tl()]
    XA = nc.dram_tensor("XA", [NT, D], F32).ap()
    XB = nc.dram_tensor("XB", [NT, D], F32).ap()
    XC = nc.dram_tensor("XC", [NT, D], F32).ap()
    S = Ctx()
    K.S = S
    S.QTa = nc.dram_tensor("QTa", [DEPTH, 6, 64, NT], BF16).ap()
    S.KTa = nc.dram_tensor("KTa", [DEPTH, 2, 64, NT], BF16).ap()
    S.Va = nc.dram_tensor("Va", [DEPTH, NT, 128], BF16).ap()
    S.QTc = nc.dram_tensor("QTc", [DEPTH, 6, 96, NT], BF16).ap()
    S.KTc = nc.dram_tensor("KTc", [DEPTH, 6, 96, NT], BF16).ap()
    S.Vc = nc.dram_tensor("Vc", [DEPTH, NT, 384], BF16).ap()
    S.PBT = nc.dram_tensor("PBT", [DEPTH, 768, NT], F32).ap()
    S.CT = nc.dram_tensor("CT", [DEPTH, D, NT], BF16).ap()
    S.t_A, S.t_C, S.t_PBT, S.t_CT = Tl(), Tl(), Tl(), Tl()
    S.HTd = nc.dram_tensor("HTd", [512, SEQ], BF16).ap()
    S.UTd = nc.dram_tensor("UTd", [256, SEQ], BF16).ap()
    S.XF = nc.dram_tensor("XF", [2, 128, 128, 2, 64], F32).ap()
    S.HF = nc.dram_tensor("HF", [2, 128, 128, 2, 64], F32).ap()
    S.t_HTd, S.t_UTd, S.t_XF, S.t_HF = Tl(), Tl(), Tl(), Tl()
    S.GBd = nc.dram_tensor("GBd", [16, 128, 24, 128], BF16).ap()
    S.t_GBd = Tl()
    K.gb_cached = [False] * 16
    t_xin, t_XA, t_XB, t_XC, t_out = Tl(), Tl(), Tl(), Tl(), Tl()
    dbg_outs = []

    def finish():
        dd = mk.new_dsem()
        for name in dbg:
            src = {"XA": XA, "XB": XB, "XC": XC, "QTa": S.QTa, "KTa": S.KTa, "Va": S.Va, "QTc": S.QTc, "KTc": S.KTc, "Vc": S.Vc,
                   "PBT": S.PBT, "CT": S.CT, "mod": K.mod}[name]
            o = nc.dram_tensor("dbg_" + name, list(src.shape), src.dtype, kind="ExternalOutput").ap()
            mk.dma('sp', o, src, w=[t_out], dsem=dd)
        mk.barrier()
        return nc

    with ExitStack() as gst:
        K.identb = sb(nc, gst, "identb", [128, 128], BF16)
        identf = sb(nc, gst, "identf", [128, 128], F32)
        K.eps_t = sb(nc, gst, "eps_t", [128, 1], F32)
        K.t_const = Tl()
        d0 = mk.new_dsem()
        mk.dma('sp', identf[:], K.ident_in[:, :], w=[K.t_const], dsem=d0)
        mk.op('dve', lambda: nc.vector.tensor_copy(out=K.identb[:], in_=identf[:]), r=[K.t_const], w=[K.t_const])
        mk.op('dve', lambda: nc.vector.memset(K.eps_t[:], EPS), w=[K.t_const])
        mk.barrier()

        Xcur, t_cur = K.xin, t_xin
        for l in range(DEPTH):
            last = (l == DEPTH - 1)
            phase_ada(K, l)
            mk.recycle()
            phase_ffn(K, l, 0, Xcur, t_cur, XC, t_XC, K.w_gu[0][l], K.w_dn[0][l])
            mk.recycle()
            if stop_after == "ffn1_%d" % l:
                return finish()
            phase_mixin(K, l, XC, t_XC)
            mk.recycle()
            if stop_after == "mixin_%d" % l:
                return finish()
            phase_attn(K, l, need_ctx=not last)
            mk.recycle()
            if stop_after == "attn_%d" % l:
                return finish()
            phase_hyena(K, l, need_ctx=not last)
            mk.recycle()
            if stop_after == "hyena_%d" % l:
                return finish()
            with ExitStack() as wst:
                W2 = ffn_load_weights(K, wst, K.w_gu[1][l], K.w_dn[1][l])
                phase_mixout(K, l, XC, t_XC, XB, t_XB, first_chunk=(2 if last else 0))
                if stop_after == "mixout_%d" % l:
                    return finish()
                phase_ffn(K, l, 1, XB, t_XB, XA, t_XA, K.w_gu[1][l], K.w_dn[1][l], first_chunk=(2 if last else 0), W=W2)
            mk.recycle()
            Xcur, t_cur = XA, t_XA
        dd = mk.new_dsem()
        mk.dma('sp', K.out[:, :], XA[CTX:, :], r=[t_XA], w=[t_out], dsem=dd)
        return finish()


def rope_table(rot_dim):
    rows = SEQ // 64
    row = np.repeat(np.arange(rows, dtype=np.float32), 64)
    col = np.tile(np.arange(64, dtype=np.float32), rows)
    n_freq = rot_dim // 4
    inv = (np.float32(10000.0) ** (-np.arange(n_freq, dtype=np.float32) / np.float32(n_freq))).astype(np.float32)
    ang = np.concatenate([row[:, None] * inv, col[:, None] * inv], -1).astype(np.float32)
    return np.ascontiguousarray(np.concatenate([np.cos(ang), np.sin(ang)], -1).astype(np.float32))


def fft_tables():
    N = 8192
    s1 = np.arange(64)[:, None]; k1 = np.arange(128)[None, :]
    F1 = np.exp(-2j * np.pi * s1 * k1 / 128)
    F1cat = np.concatenate([F1.real, F1.imag], 1)
    s2 = np.arange(64)[:, None, None]; k1g = np.arange(128)[None, :, None]; k2 = np.arange(64)[None, None, :]
    G = np.exp(-2j * np.pi * (s2 * k1g / 8192 + s2 * k2 / 64))
    Gc = np.stack([G.real, G.imag, -G.imag], 2)
    k2_ = np.arange(64)[:, None]; t2 = np.arange(64)[None, :]
    W3 = np.exp(2j * np.pi * k2_ * t2 / 64)
    W3a = np.zeros((128, 2, 2, 64)); W3b = np.zeros((128, 2, 2, 64))
    for cc in range(2):
        W3a[cc * 64:(cc + 1) * 64, cc, 0] = W3.real; W3a[cc * 64:(cc + 1) * 64, cc, 1] = W3.imag
        W3b[cc * 64:(cc + 1) * 64, cc, 0] = -W3.imag; W3b[cc * 64:(cc + 1) * 64, cc, 1] = W3.real
    W3cat = np.concatenate([W3a.reshape(128, 256), W3b.reshape(128, 256)], 1)
    k1_ = np.arange(128)[:, None, None]; t2_ = np.arange(64)[None, :, None]; t1 = np.arange(64)[None, None, :]
    F4 = np.exp(2j * np.pi * (t1 * k1_ / 128 + t2_ * k1_ / 8192)) / N
    F4c = np.stack([F4.real, -F4.imag], 2).reshape(128, 8192)
    f = lambda a: np.ascontiguousarray(a.astype(np.float32))
    return f(F1cat), f(Gc), f(W3cat), f(F4c)


def hyena_z(n):
    t = np.linspace(0.0, 1.0, n, dtype=np.float32)[:, None]
    w = (np.float32(2.0 * math.pi) * np.arange(n, dtype=np.float32)[:, None] / np.float32(n)).astype(np.float32)
    f = np.linspace(1e-4, 15, 16, dtype=np.float32)[None, :]
    z = np.concatenate([t, np.cos(f * w), -np.sin(f * w)], -1).astype(np.float32)
    return np.ascontiguousarray(z.T), t[:, 0]


def make_in_maps(inputs):
    x = np.asarray(inputs["x"], dtype=np.float32)
    ctx = np.asarray(inputs["ctx"], dtype=np.float32)
    c = np.asarray(inputs["c"], dtype=np.float32)
    cctx = np.asarray(inputs["c_ctx"], dtype=np.float32)
    shared = {k: np.ascontiguousarray(np.asarray(inputs[k], dtype=np.float32)) for k in inputs if k not in ("x", "ctx", "c")}
    shared["ident"] = np.eye(128, dtype=np.float32)
    shared["ropeA"] = rope_table(64)
    shared["ropeC"] = rope_table(32)
    zl, tl = hyena_z(SEQ)
    zc, tc_ = hyena_z(CTX)
    shared["zT_lat"] = zl
    shared["zT_ctx"] = zc
    deltas = np.abs(np.linspace(math.log(1e-2) / 1.5, math.log(1e-2) / 0.3, 256, dtype=np.float32))
    deltas = np.tile(deltas, 2)
    td = np.zeros((128, 4 + SEQ + CTX), np.float32)
    td[:, 0:4] = deltas.reshape(4, 128).T
    td[:, 4:4 + SEQ] = tl[None, :]
    td[:, 4 + SEQ:] = tc_[None, :]
    shared["hy_tdel"] = td
    hsm = np.zeros((DEPTH, 128, 32), np.float32)
    for l in range(DEPTH):
        cw = np.asarray(inputs["hy_conv_w"][l], np.float32)
        hsm[l, :, 0:18] = cw.reshape(3, 6, 128).transpose(2, 0, 1).reshape(128, 18)
        hsm[l, :, 18:24] = np.asarray(inputs["hy_conv_b"][l], np.float32).reshape(6, 128).T
        hsm[l, :, 24:26] = np.asarray(inputs["hy_bias"][l], np.float32).reshape(2, 128).T
        hsm[l, 0:64, 26] = np.asarray(inputs["hy_f_freq"][l], np.float32)
        hsm[l, 0:64, 27] = np.asarray(inputs["hy_f_b1"][l], np.float32)
        hsm[l, 0:64, 28] = np.asarray(inputs["hy_f_b2"][l], np.float32)
        hsm[l, 0:64, 29] = np.asarray(inputs["hy_f_b3"][l], np.float32)
    shared["hy_small"] = hsm
    shared["fft_F1"], shared["fft_G"], shared["fft_W3"], shared["fft_F4"] = fft_tables()
    maps = []
    for b in range(8):
        m = dict(shared)
        m["xin"] = np.ascontiguousarray(np.concatenate([ctx[b], x[b]], axis=0))
        cc = np.stack([c[b], cctx], axis=-1).reshape(8, 128, 2).transpose(1, 0, 2).reshape(128, 16)
        m["cc"] = np.ascontiguousarray(cc)
        maps.append(m)
    return maps


def kernel(**inputs):
    nc = build()
    maps = make_in_maps(inputs)
    maps = [{k: v for k, v in m.items() if k in DECLARED} for m in maps]
    res = run_bass_kernel_spmd(nc, maps, core_ids=list(range(8)))
    return np.stack([np.asarray(r["out"]) for r in res.results], axis=0).astype(np.float32)
```

```python
import math
from contextlib import ExitStack
import numpy as np
import concourse.bass as bass
import concourse.mybir as mybir
from concourse.bass_utils import run_bass_kernel_spmd

F32 = mybir.dt.float32
BF16 = mybir.dt.bfloat16
AF = mybir.ActivationFunctionType
ALU = mybir.AluOpType

D = 1024
SEQ = 4096
CTX = 256
NT = SEQ + CTX
NCH = NT // 128
DEPTH = 2
DFF = 2816
NMOD = 9
ALPHA = (2 * DEPTH) ** 0.25
EPS = 1e-6
P_IN = 1824
USE_FFT = True


class Tl:
    def __init__(self, name=""):
        self.name = name
        self.w = None
        self.weng = None
        self.r = {}


class MK:
    def __init__(self, nc):
        self.nc = nc
        self.eng = {'pe': nc.tensor, 'act': nc.scalar, 'dve': nc.vector, 'pool': nc.gpsimd, 'sp': nc.sync}
        self.sem = {}
        self.cnt = {}
        self.seen = {e: {} for e in self.eng}
        self._stack = ExitStack()
        for e in self.eng:
            self.sem[e] = self._stack.enter_context(nc.semaphore("s_" + e))
            self.cnt[e] = 0
        self.dsems = []
        self.free_dsems = []

    def new_dsem(self):
        if self.free_dsems:
            return self.free_dsems.pop()
        s = self._stack.enter_context(self.nc.semaphore("d%d" % len(self.dsems)))
        d = [s, 0]
        self.dsems.append(d)
        return d

    def _need(self, e, waits, dep):
        if dep is None:
            return
        sem, val = dep
        k = id(sem)
        if self.seen[e].get(k, 0) >= val:
            return
        if k not in waits or waits[k][1] < val:
            waits[k] = (sem, val)

    def _dowaits(self, e, waits):
        E = self.eng[e]
        for k, (sem, val) in waits.items():
            E.wait_ge(sem, val)
            self.seen[e][k] = val

    def op(self, e, fn, r=(), w=()):
        waits = {}
        for t in r:
            self._need(e, waits, t.w)
        for t in w:
            if t.weng != e:
                self._need(e, waits, t.w)
            for re_, dep in t.r.items():
                if re_ != e:
                    self._need(e, waits, dep)
        self._dowaits(e, waits)
        ins = fn()
        self.cnt[e] += 1
        ins.then_inc(self.sem[e], 1)
        dep = (self.sem[e], self.cnt[e])
        for t in r:
            t.r[e] = dep
        for t in w:
            t.w = dep
            t.weng = e
            t.r = {}
        return ins

    def dma(self, q, out, in_, r=(), w=(), dsem=None):
        waits = {}
        for t in r:
            self._need(q, waits, t.w)
        for t in w:
            self._need(q, waits, t.w)
            for re_, dep in t.r.items():
                self._need(q, waits, dep)
        self._dowaits(q, waits)
        ins = self.eng[q].dma_start(out=out, in_=in_)
        dsem[1] += 16
        ins.then_inc(dsem[0], 16)
        dep = (dsem[0], dsem[1])
        for t in r:
            t.r['dma%d' % id(dsem)] = dep
        for t in w:
            t.w = dep
            t.weng = 'dma'
            t.r = {}
        return ins

    def barrier(self):
        waits = {}
        for e in self.eng:
            if e != 'sp' and self.cnt[e] > 0:
                self._need('sp', waits, (self.sem[e], self.cnt[e]))
        for d in self.dsems:
            if d[1] > 0:
                self._need('sp', waits, (d[0], d[1]))
        self._dowaits('sp', waits)
        self.cnt['sp'] += 1
        self.nc.sync.sem_inc(self.sem['sp'], 1)
        for e in self.eng:
            if e != 'sp':
                self.eng[e].wait_ge(self.sem['sp'], self.cnt['sp'])
                self.seen[e][id(self.sem['sp'])] = self.cnt['sp']
                for f in self.eng:
                    if f != 'sp':
                        self.seen[e][id(self.sem[f])] = self.cnt[f]
                for d in self.dsems:
                    self.seen[e][id(d[0])] = d[1]

    def recycle(self):
        self.free_dsems = list(self.dsems)

    def close(self):
        self._stack.close()


class Ctx:
    pass


DECLARED = set()


_UID = [0]


def sb(nc, st, name, shape, dt):
    _UID[0] += 1
    return st.enter_context(nc.sbuf_tensor("sb%d_%s" % (_UID[0], name), shape, dt))


def ps(nc, st, name, shape, dt):
    _UID[0] += 1
    return st.enter_context(nc.psum_tensor("ps%d_%s" % (_UID[0], name), shape, dt))


def phase_ada(K, l):
    nc, mk = K.nc, K.mk
    with ExitStack() as st:
        cc = sb(nc, st, "ada_cc", [128, 8, 2], F32)
        sg = sb(nc, st, "ada_sg", [128, 8, 2], F32)
        wbuf = [sb(nc, st, "ada_w%d" % i, [128, 8, 512], F32) for i in range(4)]
        bia = sb(nc, st, "ada_b", [2, 9216], F32)
        mrow = sb(nc, st, "ada_m", [2, 9216], F32)
        pm = [ps(nc, st, "ada_p%d" % i, [2, 512], F32) for i in range(2)]
        t_cc, t_sg, t_b, t_m = Tl(), Tl(), Tl(), Tl()
        t_w = [Tl() for _ in range(4)]
        t_p = [Tl(), Tl()]
        d0 = mk.new_dsem()
        dw = [mk.new_dsem() for _ in range(4)]
        mk.dma('sp', cc[:], K.cc_in.rearrange("p (kc g) -> p kc g", g=2), w=[t_cc], dsem=d0)
        mk.dma('sp', bia[:], K.ada_b[l].partition_broadcast(2), w=[t_b], dsem=d0)
        mk.op('act', lambda: nc.scalar.activation(out=sg[:], in_=cc[:], func=AF.Silu), r=[t_cc], w=[t_sg])
        for nb in range(18):
            i = nb % 2
            wi = nb % 4
            mk.dma('sp', wbuf[wi][:], K.ada_w[l][:, nb * 512:(nb + 1) * 512].rearrange("(kc p) n -> p kc n", p=128),
                   w=[t_w[wi]], dsem=dw[wi])
            for kc in range(8):
                mk.op('pe', lambda kc=kc, i=i, wi=wi: nc.tensor.matmul(pm[i][:], lhsT=sg[:, kc, :], rhs=wbuf[wi][:, kc, :],
                                                                      start=(kc == 0), stop=(kc == 7)),
                      r=[t_sg, t_w[wi]], w=[t_p[i]])
            mk.op('dve', lambda nb=nb, i=i: nc.vector.tensor_add(out=mrow[:, nb * 512:(nb + 1) * 512], in0=pm[i][:],
                                                                 in1=bia[:, nb * 512:(nb + 1) * 512]),
                  r=[t_p[i], t_b], w=[t_m])
        mk.dma('sp', K.mod[l], mrow[:], r=[t_m], w=[K.t_mod[l]], dsem=d0)
        mk.barrier()


def ln_stats(K, x_ap, t_x, st6, mv, rstd, t_st):
    nc, mk = K.nc, K.mk
    for c in range(2):
        mk.op('dve', lambda c=c: nc.vector.bn_stats(out=st6[:, c, :], in_=x_ap[:, c * 512:(c + 1) * 512]), r=[t_x], w=[t_st])
    mk.op('dve', lambda: nc.vector.bn_aggr(out=mv[:], in_=st6[:]), r=[t_st], w=[t_st])
    mk.op('act', lambda: nc.scalar.activation(out=rstd[:], in_=mv[:, 1:2], func=AF.Sqrt, bias=K.eps_t[:, 0:1], scale=1.0), r=[t_st], w=[t_st])
    mk.op('dve', lambda: nc.vector.reciprocal(out=rstd[:], in_=rstd[:]), r=[t_st], w=[t_st])


def load_bcast(K, dst_ap, src_row_ap, t, dsem):
    K.mk.dma('sp', dst_ap, src_row_ap.partition_broadcast(128), w=[t], dsem=dsem)


def ffn_load_weights(K, st, w_gu, w_down):
    nc, mk = K.nc, K.mk
    W = Ctx()
    W.wgu = sb(nc, st, "f_wgu", [128, 8, 2 * DFF], BF16)
    W.wdn = sb(nc, st, "f_wdn", [128, 22, D], BF16)
    W.t_wgu, W.t_wdn = Tl(), Tl()
    dws = mk.new_dsem()
    for kc in range(8):
        for hh in range(2):
            mk.dma('pool', W.wgu[:, kc, hh * DFF:(hh + 1) * DFF], w_gu[kc * 128:(kc + 1) * 128, hh * DFF:(hh + 1) * DFF], w=[W.t_wgu], dsem=dws)
    for fc in range(22):
        mk.dma('pool', W.wdn[:, fc, :], w_down[fc * 128:(fc + 1) * 128, :], w=[W.t_wdn], dsem=dws)
    return W


def phase_ffn(K, l, which, Xin, t_xin, Xout, t_xout, w_gu, w_down, first_chunk=0, W=None):
    nc, mk = K.nc, K.mk
    mbase = 0 if which == 0 else 6
    lni = 0 if which == 0 else 2
    with ExitStack() as st:
        if W is None:
            W = ffn_load_weights(K, st, w_gu, w_down)
        wgu, wdn, t_wgu, t_wdn = W.wgu, W.wdn, W.t_wgu, W.t_wdn
        modv = sb(nc, st, "f_mod", [128, 5, D], F32)
        xc = [sb(nc, st, "f_xc%d" % i, [128, D], F32) for i in range(6)]
        hT = sb(nc, st, "f_hT", [128, 8, 256], BF16)
        actT = sb(nc, st, "f_actT", [128, 22, 256], BF16)
        tt = sb(nc, st, "f_tt", [128, D], F32)
        hn2 = [sb(nc, st, "f_hn%d" % i, [128, D], BF16) for i in range(2)]
        st6j = [sb(nc, st, "f_st6j%d" % i, [128, 2, 6], F32) for i in range(2)]
        mvj = [sb(nc, st, "f_mvj%d" % i, [128, 2], F32) for i in range(2)]
        rstdj = [sb(nc, st, "f_rstdj%d" % i, [128, 1], F32) for i in range(2)]
        sgt = [sb(nc, st, "f_sg%d" % i, [128, 256], F32) for i in range(2)]
        st6b = sb(nc, st, "f_st6b", [128, 2, 6], F32)
        mvb = sb(nc, st, "f_mvb", [128, 2], F32)
        rstdb = sb(nc, st, "f_rstdb", [128, 1], F32)
        p_tr = ps(nc, st, "f_ptr", [128, 8, 128], BF16)
        p_up = [ps(nc, st, "f_pup%d" % i, [128, 2, 256], F32) for i in range(3)]
        p_dn = [ps(nc, st, "f_pdn%d" % i, [128, 512], F32) for i in range(4)]
        t_mod = Tl()
        t_xc = [Tl() for _ in range(6)]
        t_hT, t_act, t_tt, t_stb, t_ptr = Tl(), Tl(), Tl(), Tl(), Tl()
        t_hn2 = [Tl(), Tl()]
        t_stj = [Tl(), Tl()]
        t_sg = [Tl(), Tl()]
        t_pup = [Tl() for _ in range(3)]
        t_pdn = [Tl() for _ in range(4)]
        dmod = mk.new_dsem()
        dx = [mk.new_dsem() for _ in range(6)]
        dout = mk.new_dsem()
        nblk = NCH // 2
        state = {"upi": 0}
        blks = list(range(first_chunk // 2, nblk))

        def xidx(blk, j):
            return (blk % 3) * 2 + j

        def load_mod(g):
            mrow = K.mod[l][g]
            for i, src in enumerate([mrow[(mbase + 0) * D:(mbase + 1) * D], mrow[(mbase + 1) * D:(mbase + 2) * D],
                                     mrow[(mbase + 2) * D:(mbase + 3) * D], K.ln_g[l, lni], K.ln_b[l, lni]]):
                mk.dma('sp', modv[:, i, :], src.partition_broadcast(128), r=[K.t_mod[l]], w=[t_mod], dsem=dmod)
            mk.op('pool', lambda: nc.gpsimd.tensor_scalar_add(out=modv[:, 1, :], in0=modv[:, 1, :], scalar1=1.0), r=[t_mod], w=[t_mod])
            mk.op('pool', lambda: nc.gpsimd.tensor_scalar_mul(out=modv[:, 2, :], in0=modv[:, 2, :], scalar1=0.5), r=[t_mod], w=[t_mod])

        def prep_load(blk):
            for j in range(2):
                ch = blk * 2 + j
                xi = xidx(blk, j)
                mk.dma('sp', xc[xi][:], Xin[ch * 128:(ch + 1) * 128, :], r=[t_xin], w=[t_xc[xi]], dsem=dx[xi])

        def prep_ln(blk):
            for j in range(2):
                xi = xidx(blk, j)
                ln_stats(K, xc[xi], t_xc[xi], st6j[j], mvj[j], rstdj[j], t_stj[j])
            for j in range(2):
                xi = xidx(blk, j)
                mk.op('dve', lambda xi=xi, j=j: nc.vector.scalar_tensor_tensor(out=tt[:], in0=xc[xi][:], scalar=mvj[j][:, 0:1], in1=modv[:, 1, :],
                                                                               op0=ALU.subtract, op1=ALU.mult), r=[t_xc[xi], t_stj[j], t_mod], w=[t_tt])
                mk.op('dve', lambda j=j: nc.vector.scalar_tensor_tensor(out=hn2[j][:], in0=tt[:], scalar=rstdj[j][:, 0:1], in1=modv[:, 0, :],
                                                                        op0=ALU.mult, op1=ALU.add), r=[t_tt, t_stj[j], t_mod], w=[t_hn2[j]])

        def prep_ln_gen(blk):
            for j in range(2):
                xi = xidx(blk, j)
                for c in range(2):
                    mk.op('dve', lambda c=c, xi=xi, j=j: nc.vector.bn_stats(out=st6j[j][:, c, :], in_=xc[xi][:, c * 512:(c + 1) * 512]), r=[t_xc[xi]], w=[t_stj[j]])
                    yield
                mk.op('dve', lambda j=j: nc.vector.bn_aggr(out=mvj[j][:], in_=st6j[j][:]), r=[t_stj[j]], w=[t_stj[j]])
                mk.op('act', lambda j=j: nc.scalar.activation(out=rstdj[j][:], in_=mvj[j][:, 1:2], func=AF.Sqrt, bias=K.eps_t[:, 0:1], scale=1.0), r=[t_stj[j]], w=[t_stj[j]])
                yield
                yield
                mk.op('dve', lambda j=j: nc.vector.reciprocal(out=rstdj[j][:], in_=rstdj[j][:]), r=[t_stj[j]], w=[t_stj[j]])
                yield
                mk.op('dve', lambda xi=xi, j=j: nc.vector.scalar_tensor_tensor(out=tt[:], in0=xc[xi][:], scalar=mvj[j][:, 0:1], in1=modv[:, 1, :],
                                                                               op0=ALU.subtract, op1=ALU.mult), r=[t_xc[xi], t_stj[j], t_mod], w=[t_tt])
                yield
                mk.op('dve', lambda j=j: nc.vector.scalar_tensor_tensor(out=hn2[j][:], in0=tt[:], scalar=rstdj[j][:, 0:1], in1=modv[:, 0, :],
                                                                        op0=ALU.mult, op1=ALU.add), r=[t_tt, t_stj[j], t_mod], w=[t_hn2[j]])
                yield

        def prep_b(blk):
            for j in range(2):
                for kc in range(8):
                    mk.op('pe', lambda kc=kc, j=j: nc.tensor.transpose(out=p_tr[:, kc, :], in_=hn2[j][:, kc * 128:(kc + 1) * 128], identity=K.identb[:]),
                          r=[t_hn2[j], K.t_const], w=[t_ptr])
                mk.op('act', lambda j=j: nc.scalar.copy(out=hT[:, :, j * 128:(j + 1) * 128], in_=p_tr[:]), r=[t_ptr], w=[t_hT])

        def up(blk, gen=None):
            for fc in range(22):
                if gen is not None and fc >= 3:
                    next(gen, None)
                pi = state["upi"] % 3
                si = state["upi"] % 2
                state["upi"] += 1
                for hh in range(2):
                    for kc in range(8):
                        mk.op('pe', lambda kc=kc, hh=hh, fc=fc, pi=pi: nc.tensor.matmul(
                            p_up[pi][:, hh, :], lhsT=wgu[:, kc, hh * DFF + fc * 128: hh * DFF + (fc + 1) * 128], rhs=hT[:, kc, :],
                            start=(kc == 0), stop=(kc == 7)), r=[t_wgu, t_hT], w=[t_pup[pi]])
                mk.op('act', lambda pi=pi, si=si: nc.scalar.activation(out=sgt[si][:], in_=p_up[pi][:, 0, :], func=AF.Silu), r=[t_pup[pi]], w=[t_sg[si]])
                mk.op('dve', lambda pi=pi, si=si, fc=fc: nc.vector.tensor_mul(out=actT[:, fc, :], in0=sgt[si][:], in1=p_up[pi][:, 1, :]),
                      r=[t_sg[si], t_pup[pi]], w=[t_act])

        def down(blk):
            for j in range(2):
                ch = blk * 2 + j
                xi = xidx(blk, j)
                for hf in range(2):
                    pd = j * 2 + hf
                    for fc in range(22):
                        mk.op('pe', lambda fc=fc, j=j, hf=hf, pd=pd: nc.tensor.matmul(
                            p_dn[pd][:], lhsT=actT[:, fc, j * 128:(j + 1) * 128], rhs=wdn[:, fc, hf * 512:(hf + 1) * 512],
                            start=(fc == 0), stop=(fc == 21)), r=[t_act, t_wdn], w=[t_pdn[pd]])
                    mk.op('dve', lambda hf=hf, pd=pd: nc.vector.tensor_mul(out=tt[:, hf * 512:(hf + 1) * 512], in0=p_dn[pd][:],
                                                                           in1=modv[:, 2, hf * 512:(hf + 1) * 512]), r=[t_pdn[pd], t_mod], w=[t_tt])
                mk.op('dve', lambda xi=xi: nc.vector.scalar_tensor_tensor(out=xc[xi][:], in0=xc[xi][:], scalar=ALPHA, in1=tt[:],
                                                                          op0=ALU.mult, op1=ALU.add), r=[t_tt, t_xc[xi]], w=[t_xc[xi]])
                ln_stats(K, xc[xi], t_xc[xi], st6b, mvb, rstdb, t_stb)
                mk.op('dve', lambda xi=xi: nc.vector.scalar_tensor_tensor(out=xc[xi][:], in0=xc[xi][:], scalar=mvb[:, 0:1], in1=modv[:, 3, :],
                                                                          op0=ALU.subtract, op1=ALU.mult), r=[t_xc[xi], t_stb, t_mod], w=[t_xc[xi]])
                mk.op('dve', lambda xi=xi: nc.vector.scalar_tensor_tensor(out=xc[xi][:], in0=xc[xi][:], scalar=rstdb[:, 0:1], in1=modv[:, 4, :],
                                                                          op0=ALU.mult, op1=ALU.add), r=[t_xc[xi], t_stb, t_mod], w=[t_xc[xi]])
                mk.dma('pool', Xout[ch * 128:(ch + 1) * 128, :], xc[xi][:], r=[t_xc[xi]], w=[t_xout], dsem=dout)

        if blks and blks[0] == 0:
            load_mod(1)
            prep_load(0)
            prep_ln(0)
            prep_b(0)
            up(0)
            down(0)
            blks = blks[1:]
        if blks:
            load_mod(0)
            prep_load(blks[0])
            if len(blks) > 1:
                prep_load(blks[1])
            prep_ln(blks[0])
            prep_b(blks[0])
            for bi_, blk in enumerate(blks):
                nxt = blks[bi_ + 1] if bi_ + 1 < len(blks) else None
                nx2 = blks[bi_ + 2] if bi_ + 2 < len(blks) else None
                gen = prep_ln_gen(nxt) if nxt is not None else None
                up(blk, gen)
                if gen is not None:
                    for _ in gen:
                        pass
                    prep_b(nxt)
                if nx2 is not None:
                    prep_load(nx2)
                down(blk)
        mk.barrier()


def phase_mixin(K, l, Xin, t_xin):
    nc, mk = K.nc, K.mk
    S = K.S
    with ExitStack() as st:
        win = sb(nc, st, "m_win", [128, 8, P_IN], BF16)
        wuq = sb(nc, st, "m_wuq", [128, 2, 576], BF16)
        wukv = sb(nc, st, "m_wukv", [128, 768], BF16)
        modv = sb(nc, st, "m_mod", [128, 2, D], F32)
        gA = sb(nc, st, "m_gA", [128, 8, 64], F32)
        gq = sb(nc, st, "m_gq", [128, 256], F32)
        gkv = sb(nc, st, "m_gkv", [128, 128], F32)
        ropeA_l = [sb(nc, st, "m_ropeA%d" % i, [128, 2, 32], F32) for i in range(2)]
        ropeC_l = [sb(nc, st, "m_ropeC%d" % i, [128, 2, 16], F32) for i in range(2)]
        t_rope_l = [Tl(), Tl()]
        xc = [sb(nc, st, "m_xc%d" % i, [128, D], F32) for i in range(2)]
        xn = sb(nc, st, "m_xn", [128, D], F32)
        hn = sb(nc, st, "m_hn", [128, D], BF16)
        hT = sb(nc, st, "m_hT", [128, 8, 128], BF16)
        ptA = sb(nc, st, "m_ptA", [128, 8, 64], F32)
        ptC = sb(nc, st, "m_ptC", [128, 416], F32)
        sq = sb(nc, st, "m_sq", [128, 512], F32)
        ss = sb(nc, st, "m_ss", [128, 16], F32)
        r1 = sb(nc, st, "m_r1", [128, 8, 32], F32)
        r2 = sb(nc, st, "m_r2", [128, 8, 32], F32)
        qkb = sb(nc, st, "m_qkb", [128, 8, 64], BF16)
        vab = sb(nc, st, "m_vab", [128, 128], BF16)
        qkT = sb(nc, st, "m_qkT", [128, 4, 128], BF16)
        pbs = sb(nc, st, "m_pbs", [128, 6, 128], F32)
        cn = sb(nc, st, "m_cn", [128, 384], BF16)
        cT = sb(nc, st, "m_cT", [128, 3, 128], BF16)
        qc = sb(nc, st, "m_qc", [128, 6, 96], F32)
        kvc = sb(nc, st, "m_kvc", [128, 6, 128], F32)
        kr = sb(nc, st, "m_kr", [128, 32], F32)
        qcb = sb(nc, st, "m_qcb", [128, 6, 96], BF16)
        kcb = sb(nc, st, "m_kcb", [128, 6, 96], BF16)
        vcb = sb(nc, st, "m_vcb", [128, 6, 64], BF16)
        qcT = sb(nc, st, "m_qcT", [128, 6, 128], BF16)
        kcT = sb(nc, st, "m_kcT", [128, 6, 128], BF16)
        st6 = sb(nc, st, "m_st6", [128, 2, 6], F32)
        mv = sb(nc, st, "m_mv", [128, 2], F32)
        rstd = sb(nc, st, "m_rstd", [128, 1], F32)
        B0 = ps(nc, st, "m_B0", [128, 8, 128], BF16)
        B1 = ps(nc, st, "m_B1", [128, 512], F32)
        B2 = ps(nc, st, "m_B2", [128, 512], F32)
        B3 = ps(nc, st, "m_B3", [128, 512], F32)
        B4 = ps(nc, st, "m_B4", [128, 4, 128], F32)
        B5 = ps(nc, st, "m_B5", [128, 4, 128], F32)
        B6 = ps(nc, st, "m_B6", [128, 8, 128], BF16)
        B7 = ps(nc, st, "m_B7", [128, 512], F32)
        tB = [Tl() for _ in range(8)]
        t_w, t_mod, t_g = Tl(), Tl(), Tl()
        t_xc = [Tl(), Tl()]
        (t_xn, t_hn, t_hT, t_ptA, t_ptC, t_sq, t_ss, t_r, t_qkb, t_vab, t_qkT, t_pbs, t_cn, t_cT, t_qc, t_kvc,
         t_kr, t_qcb, t_kcb, t_vcb, t_qcT, t_kcT, t_st) = [Tl() for _ in range(23)]
        dws, dmod, dg, dout = [mk.new_dsem() for _ in range(4)]
        drope_l = [mk.new_dsem(), mk.new_dsem()]
        dx = [mk.new_dsem(), mk.new_dsem()]

        def issue_loads(ch):
            xi = ch % 2
            mk.dma('sp', xc[xi][:], Xin[ch * 128:(ch + 1) * 128, :], r=[t_xin], w=[t_xc[xi]], dsem=dx[xi])
            if ch >= 2:
                lt = slice((ch - 2) * 128, (ch - 1) * 128)
                mk.dma('sp', ropeA_l[xi][:], K.ropeA_in[lt, :].rearrange("p (a b) -> p a b", a=2), w=[t_rope_l[xi]], dsem=drope_l[xi])
                mk.dma('sp', ropeC_l[xi][:], K.ropeC_in[lt, :].rearrange("p (a b) -> p a b", a=2), w=[t_rope_l[xi]], dsem=drope_l[xi])
        for kc in range(8):
            for two in range(2):
                mk.dma('pool', win[:, kc, 0:384].rearrange("p (h two d) -> p h two d", h=3, two=2)[:, :, two, :],
                       K.w_in[l][kc * 128:(kc + 1) * 128, two * 192:(two + 1) * 192].rearrange("p (h d) -> p h d", h=3), w=[t_w], dsem=dws)
            mk.dma('pool', win[:, kc, 384:P_IN], K.w_in[l][kc * 128:(kc + 1) * 128, 384:P_IN], w=[t_w], dsem=dws)
        for kc in range(2):
            mk.dma('pool', wuq[:, kc, :], K.w_uq[l][kc * 128:(kc + 1) * 128, :], w=[t_w], dsem=dws)
        mk.dma('pool', wukv[:], K.w_ukv[l][:, :], w=[t_w], dsem=dws)
        for h in range(6):
            mk.dma('sp', gA[:, h, :], K.a_qn[l].partition_broadcast(128), w=[t_g], dsem=dg)
        for h in range(2):
            mk.dma('sp', gA[:, 6 + h, :], K.a_kn[l].partition_broadcast(128), w=[t_g], dsem=dg)
        mk.dma('sp', gq[:], K.c_qn[l].partition_broadcast(128), w=[t_g], dsem=dg)
        mk.dma('sp', gkv[:], K.c_kvn[l].partition_broadcast(128), w=[t_g], dsem=dg)
        mk.op('pool', lambda: nc.gpsimd.tensor_scalar_mul(out=gA[:, 0:6, :], in0=gA[:, 0:6, :], scalar1=0.125), r=[t_g], w=[t_g])
        cur_g = None
        for ch in range(NCH):
            g = 1 if ch < 2 else 0
            lat = (g == 0)
            if g != cur_g:
                cur_g = g
                mrow = K.mod[l][g]
                for i in range(2):
                    mk.dma('sp', modv[:, i, :], mrow[(3 + i) * D:(4 + i) * D].partition_broadcast(128), r=[K.t_mod[l]], w=[t_mod], dsem=dmod)
                mk.op('pool', lambda: nc.gpsimd.tensor_scalar_add(out=modv[:, 1, :], in0=modv[:, 1, :], scalar1=1.0), r=[t_mod], w=[t_mod])
            xi = ch % 2
            tok = slice(ch * 128, (ch + 1) * 128)
            if ch == 0:
                issue_loads(0)
            if ch + 1 < NCH:
                issue_loads(ch + 1)
            ropeA, ropeC, t_rope = ropeA_l[xi], ropeC_l[xi], t_rope_l[xi]
            ln_stats(K, xc[xi], t_xc[xi], st6, mv, rstd, t_st)
            mk.op('dve', lambda xi=xi: nc.vector.scalar_tensor_tensor(out=xn[:], in0=xc[xi][:], scalar=mv[:, 0:1], in1=modv[:, 1, :],
                                                                      op0=ALU.subtract, op1=ALU.mult), r=[t_xc[xi], t_st, t_mod], w=[t_xn])
            mk.op('dve', lambda: nc.vector.scalar_tensor_tensor(out=hn[:], in0=xn[:], scalar=rstd[:, 0:1], in1=modv[:, 0, :],
                                                                op0=ALU.mult, op1=ALU.add), r=[t_xn, t_st, t_mod], w=[t_hn])
            for kc in range(8):
                mk.op('pe', lambda kc=kc: nc.tensor.transpose(out=B0[:, kc, :], in_=hn[:, kc * 128:(kc + 1) * 128], identity=K.identb[:]),
                      r=[t_hn, K.t_const], w=[tB[0]])
            mk.op('act', lambda: nc.scalar.copy(out=hT[:], in_=B0[:]), r=[tB[0]], w=[t_hT])
            for (Bk, tb, c0, c1) in ((B1, tB[1], 0, 512), (B3, tB[3], 512, 640), (B2, tB[2], 1408, 1824)):
                for kc in range(8):
                    mk.op('pe', lambda kc=kc, Bk=Bk, c0=c0, c1=c1: nc.tensor.matmul(Bk[:, 0:c1 - c0], lhsT=hT[:, kc, :], rhs=win[:, kc, c0:c1],
                                                                                   start=(kc == 0), stop=(kc == 7)), r=[t_hT, t_w], w=[tb])
            for c in range(6):
                Bk, tb, ci = (B4, tB[4], c) if c < 4 else (B5, tB[5], c - 4)
                for kc in range(8):
                    mk.op('pe', lambda kc=kc, Bk=Bk, ci=ci, c=c: nc.tensor.matmul(Bk[:, ci, :], lhsT=win[:, kc, 640 + c * 128:640 + (c + 1) * 128],
                                                                                 rhs=hT[:, kc, :], start=(kc == 0), stop=(kc == 7)), r=[t_hT, t_w], w=[tb])
            mk.op('act', lambda: nc.scalar.copy(out=ptA[:].rearrange("p h d -> p (h d)"), in_=B1[:]), r=[tB[1]], w=[t_ptA])
            mk.op('act', lambda: nc.scalar.copy(out=ptC[:], in_=B2[:, 0:416]), r=[tB[2]], w=[t_ptC])
            mk.op('act', lambda: nc.scalar.copy(out=vab[:], in_=B3[:, 0:128]), r=[tB[3]], w=[t_vab])
            mk.op('act', lambda: nc.scalar.copy(out=pbs[:, 0:4, :], in_=B4[:]), r=[tB[4]], w=[t_pbs])
            mk.op('act', lambda: nc.scalar.copy(out=pbs[:, 4:6, :], in_=B5[:, 0:2, :]), r=[tB[5]], w=[t_pbs])
            mk.dma('sp', S.PBT[l].rearrange("(c p) t -> p c t", p=128)[:, :, tok], pbs[:], r=[t_pbs], w=[S.t_PBT], dsem=dout)
            mk.dma('sp', S.Va[l][tok, :], vab[:], r=[t_vab], w=[S.t_A], dsem=dout)
            mk.op('dve', lambda: nc.vector.tensor_mul(out=sq[:], in0=ptA[:].rearrange("p h d -> p (h d)"), in1=ptA[:].rearrange("p h d -> p (h d)")),
                  r=[t_ptA], w=[t_sq])
            mk.op('dve', lambda: nc.vector.reduce_sum(out=ss[:, 0:8], in_=sq[:].rearrange("p (h d) -> p h d", d=64), axis=mybir.AxisListType.X),
                  r=[t_sq], w=[t_ss])
            mk.op('act', lambda: nc.scalar.activation(out=ss[:, 0:8], in_=ss[:, 0:8], func=AF.Sqrt, bias=K.eps_t[:, 0:1], scale=1.0 / 64), r=[t_ss], w=[t_ss])
            mk.op('dve', lambda: nc.vector.reciprocal(out=ss[:, 0:8], in_=ss[:, 0:8]), r=[t_ss], w=[t_ss])
            mk.op('dve', lambda: nc.vector.tensor_mul(out=ptA[:], in0=ptA[:], in1=ss[:, 0:8].unsqueeze(2).to_broadcast([128, 8, 64])), r=[t_ptA, t_ss], w=[t_ptA])
            if lat:
                mk.op('pool', lambda: nc.gpsimd.tensor_mul(out=ptA[:], in0=ptA[:], in1=gA[:]), r=[t_ptA, t_g], w=[t_ptA])
                cosb = ropeA[:, 0:1, :].to_broadcast([128, 8, 32])
                sinb = ropeA[:, 1:2, :].to_broadcast([128, 8, 32])
                mk.op('dve', lambda: nc.vector.tensor_mul(out=r1[:], in0=ptA[:, :, 0:32], in1=cosb), r=[t_ptA, t_rope], w=[t_r])
                mk.op('dve', lambda: nc.vector.tensor_mul(out=r2[:], in0=ptA[:, :, 32:64], in1=sinb), r=[t_ptA, t_rope], w=[t_r])
                mk.op('dve', lambda: nc.vector.tensor_sub(out=qkb[:, :, 0:32], in0=r1[:], in1=r2[:]), r=[t_r], w=[t_qkb])
                mk.op('dve', lambda: nc.vector.tensor_mul(out=r1[:], in0=ptA[:, :, 32:64], in1=cosb), r=[t_ptA, t_rope], w=[t_r])
                mk.op('dve', lambda: nc.vector.tensor_mul(out=r2[:], in0=ptA[:, :, 0:32], in1=sinb), r=[t_ptA, t_rope], w=[t_r])
                mk.op('dve', lambda: nc.vector.tensor_add(out=qkb[:, :, 32:64], in0=r1[:], in1=r2[:]), r=[t_r], w=[t_qkb])
            else:
                mk.op('pool', lambda: nc.gpsimd.tensor_mul(out=qkb[:], in0=ptA[:], in1=gA[:]), r=[t_ptA, t_g], w=[t_qkb])
            for pr in range(3):
                mk.op('pe', lambda pr=pr: nc.tensor.transpose(out=B6[:, pr, :], in_=qkb[:, 2 * pr:2 * pr + 2, :].rearrange("p a d -> p (a d)"),
                                                              identity=K.identb[:]), r=[t_qkb, K.t_const], w=[tB[6]])
            mk.op('pe', lambda: nc.tensor.transpose(out=B6[:, 3, :], in_=qkb[:, 6:8, :].rearrange("p a d -> p (a d)"), identity=K.identb[:]),
                  r=[t_qkb, K.t_const], w=[tB[6]])
            mk.op('act', lambda: nc.scalar.copy(out=qkT[:], in_=B6[:, 0:4, :]), r=[tB[6]], w=[t_qkT])
            for pr in range(3):
                mk.dma('sp', S.QTa[l][pr][:, tok], qkT[0:64, pr, :], r=[t_qkT], w=[S.t_A], dsem=dout)
                mk.dma('sp', S.QTa[l][pr + 3][:, tok], qkT[64:128, pr, :], r=[t_qkT], w=[S.t_A], dsem=dout)
            mk.dma('sp', S.KTa[l].rearrange("h d t -> (h d) t")[:, tok], qkT[:, 3, :], r=[t_qkT], w=[S.t_A], dsem=dout)
            mk.op('dve', lambda: nc.vector.tensor_mul(out=sq[:, 0:384], in0=ptC[:, 0:384], in1=ptC[:, 0:384]), r=[t_ptC], w=[t_sq])
            mk.op('dve', lambda: nc.vector.reduce_sum(out=ss[:, 8:9], in_=sq[:, 0:256], axis=mybir.AxisListType.X), r=[t_sq], w=[t_ss])
            mk.op('dve', lambda: nc.vector.reduce_sum(out=ss[:, 9:10], in_=sq[:, 256:384], axis=mybir.AxisListType.X), r=[t_sq], w=[t_ss])
            mk.op('act', lambda: nc.scalar.activation(out=ss[:, 8:9], in_=ss[:, 8:9], func=AF.Sqrt, bias=K.eps_t[:, 0:1], scale=1.0 / 256), r=[t_ss], w=[t_ss])
            mk.op('act', lambda: nc.scalar.activation(out=ss[:, 9:10], in_=ss[:, 9:10], func=AF.Sqrt, bias=K.eps_t[:, 0:1], scale=1.0 / 128), r=[t_ss], w=[t_ss])
            mk.op('dve', lambda: nc.vector.reciprocal(out=ss[:, 8:10], in_=ss[:, 8:10]), r=[t_ss], w=[t_ss])
            mk.op('dve', lambda: nc.vector.scalar_tensor_tensor(out=cn[:, 0:256], in0=ptC[:, 0:256], scalar=ss[:, 8:9], in1=gq[:], op0=ALU.mult, op1=ALU.mult),
                  r=[t_ptC, t_ss, t_g], w=[t_cn])
            mk.op('dve', lambda: nc.vector.scalar_tensor_tensor(out=cn[:, 256:384], in0=ptC[:, 256:384], scalar=ss[:, 9:10], in1=gkv[:], op0=ALU.mult, op1=ALU.mult),
                  r=[t_ptC, t_ss, t_g], w=[t_cn])
            for c in range(3):
                mk.op('pe', lambda c=c: nc.tensor.transpose(out=B6[:, 4 + c, :], in_=cn[:, c * 128:(c + 1) * 128], identity=K.identb[:]),
                      r=[t_cn, K.t_const], w=[tB[6]])
            mk.op('act', lambda: nc.scalar.copy(out=cT[:], in_=B6[:, 4:7, :]), r=[tB[6]], w=[t_cT])
            for kc in range(2):
                mk.op('pe', lambda kc=kc: nc.tensor.matmul(B7[:, 0:480], lhsT=cT[:, kc, :], rhs=wuq[:, kc, 0:480], start=(kc == 0), stop=(kc == 1)),
                      r=[t_cT, t_w], w=[tB[7]])
            for kc in range(2):
                mk.op('pe', lambda kc=kc: nc.tensor.matmul(B3[:, 128:224], lhsT=cT[:, kc, :], rhs=wuq[:, kc, 480:576], start=(kc == 0), stop=(kc == 1)),
                      r=[t_cT, t_w], w=[tB[3]])
            mk.op('pe', lambda: nc.tensor.matmul(B1[:], lhsT=cT[:, 2, :], rhs=wukv[:, 0:512], start=True, stop=True), r=[t_cT, t_w], w=[tB[1]])
            mk.op('pe', lambda: nc.tensor.matmul(B2[:, 0:256], lhsT=cT[:, 2, :], rhs=wukv[:, 512:768], start=True, stop=True), r=[t_cT, t_w], w=[tB[2]])
            qs = 96.0 ** -0.5
            mk.op('act', lambda: nc.scalar.mul(out=qc[:, 0:5, :].rearrange("p h d -> p (h d)"), in_=B7[:, 0:480], mul=qs), r=[tB[7]], w=[t_qc])
            mk.op('act', lambda: nc.scalar.mul(out=qc[:, 5, :], in_=B3[:, 128:224], mul=qs), r=[tB[3]], w=[t_qc])
            mk.op('act', lambda: nc.scalar.copy(out=kvc[:, 0:4, :].rearrange("p h d -> p (h d)"), in_=B1[:]), r=[tB[1]], w=[t_kvc])
            mk.op('act', lambda: nc.scalar.copy(out=kvc[:, 4:6, :].rearrange("p h d -> p (h d)"), in_=B2[:, 0:256]), r=[tB[2]], w=[t_kvc])
            mk.op('pool', lambda: nc.gpsimd.tensor_copy(out=qcb[:, :, 0:64], in_=qc[:, :, 0:64]), r=[t_qc], w=[t_qcb])
            mk.op('pool', lambda: nc.gpsimd.tensor_copy(out=kcb[:, :, 0:64], in_=kvc[:, :, 0:64]), r=[t_kvc], w=[t_kcb])
            mk.op('pool', lambda: nc.gpsimd.tensor_copy(out=vcb[:], in_=kvc[:, :, 64:128]), r=[t_kvc], w=[t_vcb])
            if lat:
                cosb = ropeC[:, 0:1, :].to_broadcast([128, 6, 16])
                sinb = ropeC[:, 1:2, :].to_broadcast([128, 6, 16])
                mk.op('dve', lambda: nc.vector.tensor_mul(out=r1[:, 0:6, 0:16], in0=qc[:, :, 64:80], in1=cosb), r=[t_qc, t_rope], w=[t_r])
                mk.op('dve', lambda: nc.vector.tensor_mul(out=r2[:, 0:6, 0:16], in0=qc[:, :, 80:96], in1=sinb), r=[t_qc, t_rope], w=[t_r])
                mk.op('dve', lambda: nc.vector.tensor_sub(out=qcb[:, :, 64:80], in0=r1[:, 0:6, 0:16], in1=r2[:, 0:6, 0:16]), r=[t_r], w=[t_qcb])
                mk.op('dve', lambda: nc.vector.tensor_mul(out=r1[:, 0:6, 0:16], in0=qc[:, :, 80:96], in1=cosb), r=[t_qc, t_rope], w=[t_r])
                mk.op('dve', lambda: nc.vector.tensor_mul(out=r2[:, 0:6, 0:16], in0=qc[:, :, 64:80], in1=sinb), r=[t_qc, t_rope], w=[t_r])
                mk.op('dve', lambda: nc.vector.tensor_add(out=qcb[:, :, 80:96], in0=r1[:, 0:6, 0:16], in1=r2[:, 0:6, 0:16]), r=[t_r], w=[t_qcb])
                mk.op('dve', lambda: nc.vector.tensor_mul(out=r1[:, 0, 0:16], in0=ptC[:, 384:400], in1=ropeC[:, 0, :]), r=[t_ptC, t_rope], w=[t_r])
                mk.op('dve', lambda: nc.vector.tensor_mul(out=r2[:, 0, 0:16], in0=ptC[:, 400:416], in1=ropeC[:, 1, :]), r=[t_ptC, t_rope], w=[t_r])
                mk.op('dve', lambda: nc.vector.tensor_sub(out=kr[:, 0:16], in0=r1[:, 0, 0:16], in1=r2[:, 0, 0:16]), r=[t_r], w=[t_kr])
                mk.op('dve', lambda: nc.vector.tensor_mul(out=r1[:, 0, 0:16], in0=ptC[:, 400:416], in1=ropeC[:, 0, :]), r=[t_ptC, t_rope], w=[t_r])
                mk.op('dve', lambda: nc.vector.tensor_mul(out=r2[:, 0, 0:16], in0=ptC[:, 384:400], in1=ropeC[:, 1, :]), r=[t_ptC, t_rope], w=[t_r])
                mk.op('dve', lambda: nc.vector.tensor_add(out=kr[:, 16:32], in0=r1[:, 0, 0:16], in1=r2[:, 0, 0:16]), r=[t_r], w=[t_kr])
            else:
                mk.op('pool', lambda: nc.gpsimd.tensor_copy(out=qcb[:, :, 64:96], in_=qc[:, :, 64:96]), r=[t_qc], w=[t_qcb])
                mk.op('pool', lambda: nc.gpsimd.tensor_copy(out=kr[:], in_=ptC[:, 384:416]), r=[t_ptC], w=[t_kr])
            mk.op('pool', lambda: nc.gpsimd.tensor_copy(out=kcb[:, :, 64:96], in_=kr[:].unsqueeze(1).to_broadcast([128, 6, 32])), r=[t_kr], w=[t_kcb])
            mk.dma('sp', S.Vc[l][tok, :], vcb[:].rearrange("p h d -> p (h d)"), r=[t_vcb], w=[S.t_C], dsem=dout)
            for h in range(6):
                mk.op('pe', lambda h=h: nc.tensor.transpose(out=B0[0:96, h, :], in_=qcb[:, h, :], identity=K.identb[:]), r=[t_qcb, K.t_const], w=[tB[0]])
            mk.op('act', lambda: nc.scalar.copy(out=qcT[0:96, :, :], in_=B0[0:96, 0:6, :]), r=[tB[0]], w=[t_qcT])
            for h in range(6):
                mk.op('pe', lambda h=h: nc.tensor.transpose(out=B6[0:96, h, :], in_=kcb[:, h, :], identity=K.identb[:]), r=[t_kcb, K.t_const], w=[tB[6]])
            mk.op('act', lambda: nc.scalar.copy(out=kcT[0:96, :, :], in_=B6[0:96, 0:6, :]), r=[tB[6]], w=[t_kcT])
            mk.dma('sp', S.QTc[l].rearrange("h d t -> d h t")[:, :, tok], qcT[0:96, :, :], r=[t_qcT], w=[S.t_C], dsem=dout)
            mk.dma('sp', S.KTc[l].rearrange("h d t -> d h t")[:, :, tok], kcT[0:96, :, :], r=[t_kcT], w=[S.t_C], dsem=dout)
        mk.barrier()


def phase_attn(K, l, need_ctx):
    nc, mk = K.nc, K.mk
    S = K.S
    heads = []
    for h in range(6):
        heads.append((64, S.KTa[l][h // 3], S.Va[l][:, (h // 3) * 64:(h // 3 + 1) * 64], S.QTa[l][h], h * 64, S.t_A))
    for h in range(6):
        heads.append((96, S.KTc[l][h], S.Vc[l][:, h * 64:(h + 1) * 64], S.QTc[l][h], 640 + h * 64, S.t_C))
    with ExitStack() as st:
        KT = [sb(nc, st, "a_KT%d" % i, [128, NT], BF16) for i in range(2)]
        QT = [sb(nc, st, "a_QT%d" % i, [128, NT], BF16) for i in range(2)]
        Vx = [sb(nc, st, "a_V%d" % i, [128, NCH, 128], BF16) for i in range(2)]
        PT = [sb(nc, st, "a_PT%d" % i, [128, 512], BF16) for i in range(4)]
        rden = sb(nc, st, "a_rden", [64, 512], F32)
        ob = [sb(nc, st, "a_ob%d" % i, [64, 512], BF16) for i in range(2)]
        ST = [ps(nc, st, "a_ST%d" % i, [128, 512], F32) for i in range(6)]
        OD = [ps(nc, st, "a_OD%d" % i, [128, 512], F32) for i in range(2)]
        t_KV = [Tl(), Tl()]
        t_rden = Tl()
        t_PT = [Tl() for _ in range(4)]
        t_ob = [Tl(), Tl()]
        t_ST = [Tl() for _ in range(6)]
        t_OD = [Tl(), Tl()]
        dl = [mk.new_dsem(), mk.new_dsem()]
        dout = mk.new_dsem()
        for i in range(2):
            mk.op('pool', lambda i=i: nc.gpsimd.memset(Vx[i][:, :, 64:128], 1.0), w=[t_KV[i]])
            mk.op('pool', lambda i=i: nc.gpsimd.memset(KT[i][:], 0.0), w=[t_KV[i]])
            mk.op('pool', lambda i=i: nc.gpsimd.memset(QT[i][:], 0.0), w=[t_KV[i]])
        si = 0
        pi = 0
        bi = 0
        for hi, (dk, KTd, Vd, QTd, row0, t_src) in enumerate(heads):
            b = hi % 2
            mk.dma('sp', KT[b][0:dk, :], KTd, r=[t_src], w=[t_KV[b]], dsem=dl[b])
            mk.dma('sp', QT[b][0:dk, :], QTd, r=[t_src], w=[t_KV[b]], dsem=dl[b])
            mk.dma('sp', Vx[b][:, :, 0:64], Vd.rearrange("(c p) d -> p c d", p=128), r=[t_src], w=[t_KV[b]], dsem=dl[b])
            blocks = [(CTX + qb * 512, 512, 0, NCH) for qb in range(8)]
            if need_ctx:
                blocks.append((0, 256, 0, 2))
            for (q0, qn, k0, k1) in blocks:
                o = bi % 2
                bi += 1
                LOOK = 3
                slots = {}

                def issue_s(kc):
                    nonlocal si, pi
                    s_ = si % 6
                    si += 1
                    p_ = pi % 4
                    pi += 1
                    slots[kc] = (s_, p_)
                    mk.op('pe', lambda kc=kc, s_=s_, b=b, dk=dk, q0=q0, qn=qn: nc.tensor.matmul(
                        ST[s_][:, 0:qn], lhsT=KT[b][:, kc * 128:(kc + 1) * 128], rhs=QT[b][:, q0:q0 + qn], start=True, stop=True),
                        r=[t_KV[b]], w=[t_ST[s_]])
                    mk.op('act', lambda s_=s_, p_=p_, qn=qn: nc.scalar.activation(out=PT[p_][:, 0:qn], in_=ST[s_][:, 0:qn], func=AF.Exp),
                          r=[t_ST[s_]], w=[t_PT[p_]])

                for kc in range(k0, min(k1, k0 + LOOK)):
                    issue_s(kc)
                for kc in range(k0, k1):
                    if kc + LOOK < k1:
                        issue_s(kc + LOOK)
                    s_, p_ = slots.pop(kc)
                    mk.op('pe', lambda kc=kc, p_=p_, o=o, b=b, qn=qn, k0=k0, k1=k1: nc.tensor.matmul(
                        OD[o][:, 0:qn], lhsT=Vx[b][:, kc, :], rhs=PT[p_][:, 0:qn], start=(kc == k0), stop=(kc == k1 - 1)),
                        r=[t_KV[b], t_PT[p_]], w=[t_OD[o]])
                mk.op('dve', lambda o=o, qn=qn: nc.vector.reciprocal(out=rden[:, 0:qn], in_=OD[o][64:128, 0:qn]), r=[t_OD[o]], w=[t_rden])
                mk.op('dve', lambda o=o, qn=qn: nc.vector.tensor_mul(out=ob[o][:, 0:qn], in0=OD[o][0:64, 0:qn], in1=rden[:, 0:qn]),
                      r=[t_OD[o], t_rden], w=[t_ob[o]])
                mk.dma('sp', S.CT[l][row0:row0 + 64, q0:q0 + qn], ob[o][:, 0:qn], r=[t_ob[o]], w=[S.t_CT], dsem=dout)
        mk.barrier()


def hyena_filter_dev(K, l, st, n, zT_d, tcol0, hT, rl1):
    nc, mk = K.nc, K.mk
    TWO_PI = 2.0 * math.pi
    with ExitStack() as fs:
        zT = sb(nc, fs, "h_zT", [33, n], F32)
        tb = sb(nc, fs, "h_tb", [128, n], F32)
        w1 = sb(nc, fs, "h_w1", [33, 64], F32)
        w23 = sb(nc, fs, "h_w23", [64, 2, 64], F32)
        w4 = sb(nc, fs, "h_w4", [64, 512], F32)
        a_all = [sb(nc, fs, "h_a%d" % i, [64, 512], F32) for i in range(4)]
        wn_all = [sb(nc, fs, "h_wn%d" % i, [128, 512], F32) for i in range(2)]
        ki_all = [sb(nc, fs, "h_ki%d" % i, [64, 512], mybir.dt.int32) for i in range(2)]
        kf_all = [sb(nc, fs, "h_kf%d" % i, [64, 512], F32) for i in range(2)]
        t_ki_all, t_kf_all = [Tl(), Tl()], [Tl(), Tl()]
        t_a_all = [Tl() for _ in range(4)]
        t_wn_all = [Tl(), Tl()]
        fb = sb(nc, fs, "h_fb", [64, 3], F32)
        l1 = sb(nc, fs, "h_l1", [128, 4], F32)
        pf_all = [ps(nc, fs, "h_pf%d" % i, [64, 512], F32) for i in range(2)]
        p4 = [ps(nc, fs, "h_p4%d" % i, [128, 512], F32) for i in range(2)]
        t_in, t_fb, t_l1 = Tl(), Tl(), Tl()
        t_pf_all = [Tl(), Tl()]
        t_p4 = [Tl(), Tl()]
        d = mk.new_dsem()
        mk.dma('sp', zT[:], zT_d, w=[t_in], dsem=d)
        mk.dma('sp', tb[:], K.hy_tdel[:, tcol0:tcol0 + n], w=[t_in], dsem=d)
        mk.dma('sp', w1[:], K.hy_w1[l], w=[t_in], dsem=d)
        mk.dma('sp', w23[:, 0, :], K.hy_w2[l], w=[t_in], dsem=d)
        mk.dma('sp', w23[:, 1, :], K.hy_w3[l], w=[t_in], dsem=d)
        mk.dma('sp', w4[:], K.hy_w4[l], w=[t_in], dsem=d)
        hs = K.hs
        mk.op('dve', lambda: nc.vector.tensor_mul(out=fb[:], in0=hs[0:64, 27:30], in1=hs[0:64, 26:27].to_broadcast([64, 3])), r=[K.t_hs], w=[t_fb])
        nb = (n + 511) // 512
        for blk in range(nb):
            c0 = blk * 512
            cn_ = min(512, n - c0)
            ai = 0
            bb = blk % 2
            a = a_all[bb * 2:bb * 2 + 2]
            t_a = t_a_all[bb * 2:bb * 2 + 2]
            ki, kf, t_ki, t_kf = ki_all[bb], kf_all[bb], t_ki_all[bb], t_kf_all[bb]
            pf = pf_all[bb * 2:bb * 2 + 2] if len(pf_all) == 4 else pf_all
            t_pf = t_pf_all[bb * 2:bb * 2 + 2] if len(pf_all) == 4 else t_pf_all
            for layer in range(3):
                pfi = layer % 2
                if layer == 0:
                    mk.op('pe', lambda pfi=pfi, c0=c0, cn_=cn_: nc.tensor.matmul(pf[pfi][:, 0:cn_], lhsT=w1[:], rhs=zT[:, c0:c0 + cn_], start=True, stop=True),
                          r=[t_in], w=[t_pf[pfi]])
                else:
                    mk.op('pe', lambda pfi=pfi, layer=layer, cn_=cn_, ai=ai: nc.tensor.matmul(pf[pfi][:, 0:cn_], lhsT=w23[:, layer - 1, :], rhs=a[ai][:, 0:cn_],
                                                                                       start=True, stop=True), r=[t_in, t_a[ai]], w=[t_pf[pfi]])
                    ai = 1 - ai
                mk.op('dve', lambda pfi=pfi, ai=ai, layer=layer, cn_=cn_: nc.vector.tensor_scalar(
                    out=a[ai][:, 0:cn_], in0=pf[pfi][:, 0:cn_], scalar1=hs[0:64, 26:27], scalar2=fb[:, layer:layer + 1], op0=ALU.mult, op1=ALU.add),
                    r=[t_pf[pfi], t_fb, K.t_hs], w=[t_a[ai]])
                mk.op('dve', lambda ai=ai, cn_=cn_: nc.vector.tensor_scalar(out=ki[:, 0:cn_], in0=a[ai][:, 0:cn_], scalar1=1.0 / TWO_PI, scalar2=None, op0=ALU.mult),
                      r=[t_a[ai]], w=[t_ki])
                mk.op('dve', lambda cn_=cn_: nc.vector.tensor_copy(out=kf[:, 0:cn_], in_=ki[:, 0:cn_]), r=[t_ki], w=[t_kf])
                mk.op('dve', lambda ai=ai, cn_=cn_: nc.vector.scalar_tensor_tensor(out=a[ai][:, 0:cn_], in0=kf[:, 0:cn_], scalar=-TWO_PI, in1=a[ai][:, 0:cn_],
                                                                                  op0=ALU.mult, op1=ALU.add), r=[t_kf, t_a[ai]], w=[t_a[ai]])
                mk.op('act', lambda ai=ai, cn_=cn_: nc.scalar.activation(out=kf[:, 0:cn_], in_=a[ai][:, 0:cn_], func=AF.Sign), r=[t_a[ai]], w=[t_kf])
                mk.op('dve', lambda ai=ai, cn_=cn_: nc.vector.scalar_tensor_tensor(out=a[ai][:, 0:cn_], in0=kf[:, 0:cn_], scalar=-math.pi, in1=a[ai][:, 0:cn_],
                                                                                  op0=ALU.mult, op1=ALU.add), r=[t_kf, t_a[ai]], w=[t_a[ai]])
                mk.op('act', lambda ai=ai, cn_=cn_: nc.scalar.activation(out=a[ai][:, 0:cn_], in_=a[ai][:, 0:cn_], func=AF.Sin, scale=-1.0),
                      r=[t_a[ai]], w=[t_a[ai]])
            for c in range(4):
                pi_ = c % 2
                mk.op('pe', lambda c=c, pi_=pi_, ai=ai, cn_=cn_: nc.tensor.matmul(p4[pi_][:, 0:cn_], lhsT=w4[:, c * 128:(c + 1) * 128], rhs=a[ai][:, 0:cn_], start=True, stop=True),
                      r=[t_in, t_a[ai]], w=[t_p4[pi_]])
                wn, t_wn = wn_all[c % 2], t_wn_all[c % 2]
                mk.op('act', lambda c=c, c0=c0, cn_=cn_, wn=wn: nc.scalar.activation(out=wn[:, 0:cn_], in_=tb[:, c0:c0 + cn_], func=AF.Exp, scale=K.negdel[:, c:c + 1]),
                      r=[t_in, K.t_const], w=[t_wn])
                mk.op('dve', lambda c=c, pi_=pi_, c0=c0, cn_=cn_, wn=wn: nc.vector.tensor_mul(out=hT[:, c, c0:c0 + cn_], in0=p4[pi_][:, 0:cn_], in1=wn[:, 0:cn_]),
                      r=[t_p4[pi_], t_wn], w=[K.t_hT])
        for c in range(4):
            lo = 0 if c < 2 else 1
            mk.op('dve', lambda c=c, lo=lo: nc.vector.tensor_reduce(out=l1[:, c:c + 1], in_=hT[:, c, lo:n], axis=mybir.AxisListType.X, op=ALU.add,
                                                                    apply_absolute_value=True), r=[K.t_hT], w=[t_l1])
        mk.op('dve', lambda: nc.vector.tensor_add(out=rl1[:], in0=l1[:, 0:2], in1=l1[:, 2:4]), r=[t_l1], w=[K.t_rl1])
        mk.op('dve', lambda: nc.vector.reciprocal(out=rl1[:], in_=rl1[:]), r=[K.t_rl1], w=[K.t_rl1])
        mk.barrier()


def hyena_conv_dev(K, l, st, n, off, hT, rl1):
    nc, mk = K.nc, K.mk
    S = K.S
    hs = K.hs
    with ExitStack() as cs:
        pp = sb(nc, cs, "h_pp", [128, n + 2], F32)
        u = sb(nc, cs, "h_u", [128, n], F32)
        x1 = sb(nc, cs, "h_x1", [128, n], F32)
        x0 = sb(nc, cs, "h_x0", [128, n], F32)
        ya = [sb(nc, cs, "h_y%d" % i, [128, n], F32) for i in range(2)]
        ob = sb(nc, cs, "h_ob", [128, n], BF16)
        t_pp, t_u, t_x1, t_x0, t_ob = Tl(), Tl(), Tl(), Tl(), Tl()
        t_y = [Tl(), Tl()]
        d = mk.new_dsem()
        dout = mk.new_dsem()
        mk.op('pool', lambda: nc.gpsimd.memset(pp[:, 0:1], 0.0), w=[t_pp])
        mk.op('pool', lambda: nc.gpsimd.memset(pp[:, n + 1:n + 2], 0.0), w=[t_pp])
        for cc in range(2):
            for which, dst, t_dst in ((0, u, t_u), (1, x1, t_x1), (2, x0, t_x0)):
                chn = which * 2 + cc
                mk.dma('sp', pp[:, 1:n + 1], S.PBT[l][chn * 128:(chn + 1) * 128, off:off + n], r=[S.t_PBT], w=[t_pp], dsem=d)
                mk.op('dve', lambda chn=chn, dst=dst: nc.vector.tensor_scalar(out=dst[:], in0=pp[:, 0:n], scalar1=hs[:, chn:chn + 1], scalar2=hs[:, 18 + chn:19 + chn],
                                                                            op0=ALU.mult, op1=ALU.add), r=[t_pp, K.t_hs], w=[t_dst])
                for k in (1, 2):
                    mk.op('dve', lambda chn=chn, dst=dst, k=k: nc.vector.scalar_tensor_tensor(out=dst[:], in0=pp[:, k:n + k], scalar=hs[:, k * 6 + chn:k * 6 + chn + 1],
                                                                                              in1=dst[:], op0=ALU.mult, op1=ALU.add), r=[t_pp, K.t_hs, t_dst], w=[t_dst])
            mk.op('dve', lambda: nc.vector.tensor_mul(out=u[:], in0=u[:], in1=x1[:]), r=[t_u, t_x1], w=[t_u])
            mk.op('pool', lambda: nc.gpsimd.memset(ya[0][:], 0.0), w=[t_y[0]])
            mk.op('pool', lambda: nc.gpsimd.memset(ya[1][:], 0.0), w=[t_y[1]])
            k = 0
            for lag in range(n):
                i = k % 2
                k += 1
                mk.op('dve', lambda lag=lag, i=i, cc=cc: nc.vector.scalar_tensor_tensor(out=ya[i][:, lag:n], in0=u[:, 0:n - lag], scalar=hT[:, cc, lag:lag + 1],
                                                                                        in1=ya[i][:, lag:n], op0=ALU.mult, op1=ALU.add), r=[t_u, K.t_hT, t_y[i]], w=[t_y[i]])
            for lag in range(1, n):
                i = k % 2
                k += 1
                mk.op('dve', lambda lag=lag, i=i, cc=cc: nc.vector.scalar_tensor_tensor(out=ya[i][:, 0:n - lag], in0=u[:, lag:n], scalar=hT[:, 2 + cc, lag:lag + 1],
                                                                                        in1=ya[i][:, 0:n - lag], op0=ALU.mult, op1=ALU.add), r=[t_u, K.t_hT, t_y[i]], w=[t_y[i]])
            mk.op('dve', lambda: nc.vector.tensor_add(out=ya[0][:], in0=ya[0][:], in1=ya[1][:]), r=[t_y[0], t_y[1]], w=[t_y[0]])
            mk.op('dve', lambda cc=cc: nc.vector.tensor_scalar_mul(out=ya[0][:], in0=ya[0][:], scalar1=rl1[:, cc:cc + 1]), r=[t_y[0], K.t_rl1], w=[t_y[0]])
            mk.op('dve', lambda cc=cc: nc.vector.scalar_tensor_tensor(out=ya[0][:], in0=u[:], scalar=hs[:, 24 + cc:25 + cc], in1=ya[0][:], op0=ALU.mult, op1=ALU.add),
                  r=[t_u, t_y[0], K.t_hs], w=[t_y[0]])
            mk.op('dve', lambda: nc.vector.tensor_mul(out=ob[:], in0=ya[0][:], in1=x0[:]), r=[t_y[0], t_x0], w=[t_ob])
            mk.dma('sp', S.CT[l][384 + cc * 128:384 + (cc + 1) * 128, off:off + n], ob[:], r=[t_ob], w=[S.t_CT], dsem=dout)
        mk.barrier()


def fft_setup(K, st):
    nc, mk = K.nc, K.mk
    F = Ctx()
    F.F1 = sb(nc, st, "x_F1", [64, 256], BF16)
    F.GB = [sb(nc, st, "x_GB%d" % i, [128, 24, 128], BF16) for i in range(2)]
    F.Us = sb(nc, st, "x_Us", [64, 128, 64], BF16)
    F.AC = sb(nc, st, "x_AC", [128, 16384], BF16)
    F.stg = sb(nc, st, "x_stg", [128, SEQ], BF16)
    F.t_stg = Tl()
    F.P1 = [ps(nc, st, "x_P1%d" % i, [128, 2, 256], F32) for i in range(2)]
    F.P2 = [ps(nc, st, "x_P2%d" % i, [128, 4, 2, 64], F32) for i in range(2)]
    F.t_tab, F.t_Us = Tl(), Tl()
    F.t_ACa, F.t_ACd, F.t_Yd, F.t_Yp = Tl(), Tl(), Tl(), Tl()
    F.t_GB = [Tl(), Tl()]
    F.t_P1 = [Tl(), Tl()]
    F.t_P2 = [Tl(), Tl()]
    F.d_tab = mk.new_dsem()
    F.d_gb = [mk.new_dsem(), mk.new_dsem()]
    F.d_us = mk.new_dsem()
    F.d_x = [mk.new_dsem() for _ in range(4)]
    F.d_st = mk.new_dsem()
    F.gbi = 0
    F.xi = 0
    F.p2i = 0
    mk.dma('pool', F.F1[:], K.fft_F1[:, :], w=[F.t_tab], dsem=F.d_tab)
    for i in range(2):
        mk.op('pool', lambda i=i: nc.gpsimd.memset(F.GB[i][:], 0.0), w=[F.t_GB[i]])
    return F


def fft_alloc_work(K, F, st, extra_p2):
    nc = K.nc
    F.xs = [sb(nc, st, "x_xs%d" % i, [128, 4, 2, 64], F32) for i in range(4)]
    F.hfs = [sb(nc, st, "x_hf%d" % i, [128, 4, 2, 64], F32) for i in range(4)]
    F.tmp = [sb(nc, st, "x_tmp%d" % i, [128, 4, 64], F32) for i in range(4)]
    F.t_xs = [Tl() for _ in range(4)]
    F.t_hfs = [Tl() for _ in range(4)]
    F.t_tmp = [Tl() for _ in range(4)]
    F.P2 = F.P2[:2] + [ps(nc, st, "x_P2x%d" % i, [128, 4, 2, 64], F32) for i in range(extra_p2)]
    F.t_P2 = F.t_P2[:2] + [Tl() for _ in range(extra_p2)]


def fft_setup_inv(K, F, st):
    nc, mk = K.nc, K.mk
    F.W3 = sb(nc, st, "x_W3", [128, 512], BF16)
    F.F4 = sb(nc, st, "x_F4", [128, 64, 2, 64], BF16)
    F.Ysb = sb(nc, st, "x_Y", [128, 2, 64, 128], BF16)
    mk.dma('pool', F.W3[:], K.fft_W3[:, :], w=[F.t_tab], dsem=F.d_tab)
    mk.dma('pool', F.F4[:].rearrange("p a b c -> p (a b c)"), K.fft_F4[:, :], w=[F.t_tab], dsem=F.d_tab)


def fft_fwd(K, F, src, t_src, consumer):
    nc, mk = K.nc, K.mk
    A = F.AC[:].rearrange("p (ri k1 pr) -> p ri k1 pr", ri=2, k1=128)
    srcv = src.rearrange("c (s1 s2) -> s1 c s2", s2=64)
    for cb in range(16):
        mk.dma('sp', F.Us[:, cb * 8:(cb + 1) * 8, :], srcv[:, cb * 8:(cb + 1) * 8, :], r=[t_src], w=[F.t_Us], dsem=F.d_us)
    for pp2 in range(32):
        b = pp2 % 2
        for j in range(2):
            pr = pp2 * 2 + j
            mk.op('pe', lambda b=b, j=j, pr=pr: nc.tensor.matmul(F.P1[b][:, j, :], lhsT=F.Us[:, 2 * pr:2 * pr + 2, :].rearrange("p c s -> p (c s)"),
                                                                rhs=F.F1[:], start=True, stop=True), r=[F.t_Us, F.t_tab], w=[F.t_P1[b]])
        eng = 'act' if pp2 % 2 == 0 else 'dve'
        outv = A[:, :, :, 2 * pp2:2 * pp2 + 2].rearrange("p ri k1 pr -> p pr ri k1")
        inv = F.P1[b][:].rearrange("p pr (ri k1) -> p pr ri k1", ri=2)
        if eng == 'act':
            mk.op('act', lambda outv=outv, inv=inv: nc.scalar.copy(out=outv, in_=inv), r=[F.t_P1[b]], w=[F.t_ACa])
        else:
            mk.op('dve', lambda outv=outv, inv=inv: nc.vector.tensor_copy(out=outv, in_=inv), r=[F.t_P1[b]], w=[F.t_ACd])
    for kc in range(16):
        gb = F.gbi % 2
        F.gbi += 1
        if not K.gb_cached[kc]:
            srcg = K.fft_G[:, kc * 8:(kc + 1) * 8, :, :].rearrange("p a b c -> p (a b) c")
            mk.dma('pool', F.GB[gb][0:64, :, 0:64], srcg, w=[F.t_GB[gb]], dsem=F.d_gb[gb])
            mk.dma('pool', F.GB[gb][64:128, :, 64:128], srcg, w=[F.t_GB[gb]], dsem=F.d_gb[gb])
            mk.dma('sp', K.S.GBd[kc], F.GB[gb][:], r=[F.t_GB[gb]], w=[K.S.t_GBd], dsem=F.d_st)
            K.gb_cached[kc] = True
        else:
            mk.dma('sp', F.GB[gb][:], K.S.GBd[kc], r=[K.S.t_GBd], w=[F.t_GB[gb]], dsem=F.d_gb[gb])
        for half in range(2):
            pb = F.p2i % len(F.P2)
            F.p2i += 1
            for jj in range(4):
                kk = half * 4 + jj
                k1 = kc * 8 + kk
                Gre, Gim, nGim = F.GB[gb][:, kk * 3 + 0, :], F.GB[gb][:, kk * 3 + 1, :], F.GB[gb][:, kk * 3 + 2, :]
                Are, Aim = A[:, 0, k1, :], A[:, 1, k1, :]
                for (ri, l0, r0, l1, r1) in ((0, Gre, Are, nGim, Aim), (1, Gim, Are, Gre, Aim)):
                    mk.op('pe', lambda pb=pb, jj=jj, ri=ri, l0=l0, r0=r0: nc.tensor.matmul(F.P2[pb][:, jj, ri, :], lhsT=l0, rhs=r0, start=True, stop=False),
                          r=[F.t_GB[gb], F.t_ACa, F.t_ACd], w=[F.t_P2[pb]])
                    mk.op('pe', lambda pb=pb, jj=jj, ri=ri, l1=l1, r1=r1: nc.tensor.matmul(F.P2[pb][:, jj, ri, :], lhsT=l1, rhs=r1, start=False, stop=True),
                          r=[F.t_GB[gb], F.t_ACa, F.t_ACd], w=[F.t_P2[pb]])
            consumer(kc * 8 + half * 4, F.P2[pb], F.t_P2[pb])


def fft_inv(K, F, ysb, t_ysb, t_ysb_a):
    nc, mk = K.nc, K.mk
    C = F.AC[:].rearrange("p (ri t2 c) -> p ri t2 c", ri=2, t2=64)
    P3, tP3 = F.P1, F.t_P1
    P4 = [F.P2[i][:].rearrange("p a b c -> p (a b) c") for i in range(2)]
    tP4 = F.t_P2[0:2]
    for pp2 in range(32):
        b = pp2 % 2
        for j in range(2):
            pr = pp2 * 2 + j
            mk.op('pe', lambda b=b, j=j, pr=pr: nc.tensor.matmul(P3[b][:, j, :], lhsT=F.Ysb[:, 0, pr, :], rhs=F.W3[:, 0:256], start=True, stop=False),
                  r=[F.t_Yd, F.t_Yp, F.t_tab], w=[tP3[b]])
            mk.op('pe', lambda b=b, j=j, pr=pr: nc.tensor.matmul(P3[b][:, j, :], lhsT=F.Ysb[:, 1, pr, :], rhs=F.W3[:, 256:512], start=False, stop=True),
                  r=[F.t_Yd, F.t_Yp, F.t_tab], w=[tP3[b]])
        for j in range(2):
            outv = C[:, :, :, 4 * pp2 + 2 * j:4 * pp2 + 2 * j + 2].rearrange("p ri t2 cc -> p cc ri t2")
            inv = P3[b][:, j, :].rearrange("p (cc ri t2) -> p cc ri t2", cc=2, ri=2)
            if pp2 % 2 == 0:
                mk.op('act', lambda outv=outv, inv=inv: nc.scalar.copy(out=outv, in_=inv), r=[tP3[b]], w=[F.t_ACa])
            else:
                mk.op('dve', lambda outv=outv, inv=inv: nc.vector.tensor_copy(out=outv, in_=inv), r=[tP3[b]], w=[F.t_ACd])
    yv = ysb[:].rearrange("p (t1 t2) -> p t1 t2", t2=64)
    for t8 in range(8):
        b = t8 % 2
        for j in range(8):
            t2 = t8 * 8 + j
            mk.op('pe', lambda b=b, j=j, t2=t2: nc.tensor.matmul(P4[b][:, j, :], lhsT=C[:, 0, t2, :], rhs=F.F4[:, t2, 0, :], start=True, stop=False),
                  r=[F.t_ACa, F.t_ACd, F.t_tab], w=[tP4[b]])
            mk.op('pe', lambda b=b, j=j, t2=t2: nc.tensor.matmul(P4[b][:, j, :], lhsT=C[:, 1, t2, :], rhs=F.F4[:, t2, 1, :], start=False, stop=True),
                  r=[F.t_ACa, F.t_ACd, F.t_tab], w=[tP4[b]])
        outv = yv[:, :, t8 * 8:(t8 + 1) * 8].rearrange("p t1 t2 -> p t2 t1")
        if t8 % 2 == 0:
            mk.op('act', lambda outv=outv, b=b: nc.scalar.copy(out=outv, in_=P4[b]), r=[tP4[b]], w=[t_ysb_a])
        else:
            mk.op('dve', lambda outv=outv, b=b: nc.vector.tensor_copy(out=outv, in_=P4[b]), r=[tP4[b]], w=[t_ysb])


def hyena_lat_fft_filter(K, l, F, hT):
    nc, mk = K.nc, K.mk
    S = K.S
    n = SEQ
    stg, t_stg = F.stg, F.t_stg
    dst = mk.new_dsem()
    with ExitStack() as ws:
        fft_alloc_work(K, F, ws, 4)
        for c in range(4):
            mk.op('pool', lambda c=c: nc.gpsimd.tensor_copy(out=stg[:], in_=hT[:, c, :]), r=[K.t_hT], w=[t_stg])
            if c >= 2:
                mk.op('pool', lambda: nc.gpsimd.memset(stg[:, 0:1], 0.0), w=[t_stg])
            mk.dma('sp', S.HTd[c * 128:(c + 1) * 128, :], stg[:], r=[t_stg], w=[S.t_HTd], dsem=dst)
        for g in range(2):
            def cons_a(k1_0, P2, tP2, g=g):
                i = F.xi % 4
                F.xi += 1
                mk.op('act', lambda: nc.scalar.copy(out=F.xs[i][:], in_=P2[:]), r=[tP2], w=[F.t_xs[i]])
                mk.dma('pool', S.XF[g][:, k1_0:k1_0 + 4, :, :], F.xs[i][:], r=[F.t_xs[i]], w=[S.t_XF], dsem=F.d_st)

            def cons_b(k1_0, P2, tP2, g=g):
                i = F.xi % 4
                F.xi += 1
                mk.dma('sp', F.hfs[i][:], S.XF[g][:, k1_0:k1_0 + 4, :, :], r=[S.t_XF], w=[F.t_hfs[i]], dsem=F.d_x[i])
                mk.op('dve', lambda: nc.vector.tensor_add(out=F.xs[i][:, :, 0, :], in0=P2[:, :, 0, :], in1=F.hfs[i][:, :, 0, :]), r=[tP2, F.t_hfs[i]], w=[F.t_xs[i]])
                mk.op('dve', lambda: nc.vector.tensor_sub(out=F.xs[i][:, :, 1, :], in0=F.hfs[i][:, :, 1, :], in1=P2[:, :, 1, :]), r=[tP2, F.t_hfs[i]], w=[F.t_xs[i]])
                mk.dma('pool', S.HF[g][:, k1_0:k1_0 + 4, :, :], F.xs[i][:], r=[F.t_xs[i]], w=[S.t_HF], dsem=F.d_st)

            fft_fwd(K, F, S.HTd[g * 128:(g + 1) * 128, :], S.t_HTd, cons_a)
            fft_fwd(K, F, S.HTd[256 + g * 128:256 + (g + 1) * 128, :], S.t_HTd, cons_b)
        mk.barrier()


def hyena_lat_fft_u(K, l, F, fs, rl1):
    nc, mk = K.nc, K.mk
    S = K.S
    hs = K.hs
    n, off = SEQ, CTX
    stg, t_stg = F.stg, F.t_stg
    dst = mk.new_dsem()
    fft_alloc_work(K, F, fs, 4)
    if True:
        pp = sb(nc, fs, "h_pp", [128, n + 2], F32)
        u = sb(nc, fs, "h_u", [128, n], F32)
        x0 = sb(nc, fs, "h_x0", [128, n], F32)
        ysb = sb(nc, fs, "h_ysb", [128, n], F32)
        t_pp, t_u, t_x0, t_ysb, t_ysb_a = Tl(), Tl(), Tl(), Tl(), Tl()
        d = mk.new_dsem()
        dout = mk.new_dsem()
        mk.op('pool', lambda: nc.gpsimd.memset(pp[:, 0:1], 0.0), w=[t_pp])
        mk.op('pool', lambda: nc.gpsimd.memset(pp[:, n + 1:n + 2], 0.0), w=[t_pp])
        Yv = F.Ysb
        for cc in range(2):
            for which, dstt, t_dst in ((0, u, t_u), (1, ysb, t_ysb), (2, x0, t_x0)):
                chn = which * 2 + cc
                mk.dma('sp', pp[:, 1:n + 1], S.PBT[l][chn * 128:(chn + 1) * 128, off:off + n], r=[S.t_PBT], w=[t_pp], dsem=d)
                mk.op('dve', lambda chn=chn, dstt=dstt: nc.vector.tensor_scalar(out=dstt[:], in0=pp[:, 0:n], scalar1=hs[:, chn:chn + 1], scalar2=hs[:, 18 + chn:19 + chn],
                                                                              op0=ALU.mult, op1=ALU.add), r=[t_pp, K.t_hs], w=[t_dst])
                for k in (1, 2):
                    mk.op('dve', lambda chn=chn, dstt=dstt, k=k: nc.vector.scalar_tensor_tensor(out=dstt[:], in0=pp[:, k:n + k], scalar=hs[:, k * 6 + chn:k * 6 + chn + 1],
                                                                                                in1=dstt[:], op0=ALU.mult, op1=ALU.add), r=[t_pp, K.t_hs, t_dst], w=[t_dst])
            mk.op('dve', lambda: nc.vector.tensor_mul(out=u[:], in0=u[:], in1=ysb[:]), r=[t_u, t_ysb], w=[t_u])
            mk.op('pool', lambda: nc.gpsimd.tensor_copy(out=stg[:], in_=u[:]), r=[t_u], w=[t_stg])
            mk.dma('sp', S.UTd[cc * 128:(cc + 1) * 128, :], stg[:], r=[t_stg], w=[S.t_UTd], dsem=dst)

            def cons_c(k1_0, P2, tP2, cc=cc):
                i = F.xi % 4
                F.xi += 1
                mk.dma('sp', F.hfs[i][:], S.HF[cc][:, k1_0:k1_0 + 4, :, :], r=[S.t_HF], w=[F.t_hfs[i]], dsem=F.d_x[i])
                mk.op('act', lambda: nc.scalar.copy(out=F.xs[i][:], in_=P2[:]), r=[tP2], w=[F.t_xs[i]])
                Xre, Xim = F.xs[i][:, :, 0, :], F.xs[i][:, :, 1, :]
                Hre, Him = F.hfs[i][:, :, 0, :], F.hfs[i][:, :, 1, :]
                ta, tb, tc_, td = F.tmp
                yre = Yv[:, 0, :, k1_0:k1_0 + 4].rearrange("p pr k -> p k pr")
                yim = Yv[:, 1, :, k1_0:k1_0 + 4].rearrange("p pr k -> p k pr")
                mk.op('dve', lambda: nc.vector.tensor_mul(out=ta[:], in0=Xre, in1=Hre), r=[F.t_xs[i], F.t_hfs[i]], w=[F.t_tmp[0]])
                mk.op('dve', lambda: nc.vector.tensor_mul(out=tb[:], in0=Xim, in1=Him), r=[F.t_xs[i], F.t_hfs[i]], w=[F.t_tmp[1]])
                mk.op('dve', lambda: nc.vector.tensor_sub(out=yre, in0=ta[:], in1=tb[:]), r=[F.t_tmp[0], F.t_tmp[1]], w=[F.t_Yd])
                mk.op('pool', lambda: nc.gpsimd.tensor_mul(out=tc_[:], in0=Xre, in1=Him), r=[F.t_xs[i], F.t_hfs[i]], w=[F.t_tmp[2]])
                mk.op('pool', lambda: nc.gpsimd.tensor_mul(out=td[:], in0=Xim, in1=Hre), r=[F.t_xs[i], F.t_hfs[i]], w=[F.t_tmp[3]])
                mk.op('pool', lambda: nc.gpsimd.tensor_add(out=yim, in0=tc_[:], in1=td[:]), r=[F.t_tmp[2], F.t_tmp[3]], w=[F.t_Yp])

            fft_fwd(K, F, S.UTd[cc * 128:(cc + 1) * 128, :], S.t_UTd, cons_c)
            fft_inv(K, F, ysb, t_ysb, t_ysb_a)
            mk.op('dve', lambda cc=cc: nc.vector.tensor_scalar_mul(out=ysb[:], in0=ysb[:], scalar1=rl1[:, cc:cc + 1]), r=[t_ysb, t_ysb_a, K.t_rl1], w=[t_ysb, t_ysb_a])
            mk.op('dve', lambda cc=cc: nc.vector.scalar_tensor_tensor(out=ysb[:], in0=u[:], scalar=hs[:, 24 + cc:25 + cc], in1=ysb[:], op0=ALU.mult, op1=ALU.add),
                  r=[t_u, t_ysb, K.t_hs], w=[t_ysb])
            mk.op('dve', lambda: nc.vector.tensor_mul(out=stg[:], in0=ysb[:], in1=x0[:]), r=[t_ysb, t_x0], w=[t_stg])
            mk.dma('sp', S.CT[l][384 + cc * 128:384 + (cc + 1) * 128, off:off + n], stg[:], r=[t_stg], w=[S.t_CT], dsem=dout)
        mk.barrier()


def phase_hyena(K, l, need_ctx):
    nc, mk = K.nc, K.mk
    with ExitStack() as st:
        K.hs = sb(nc, st, "h_hs", [128, 32], F32)
        K.negpi = sb(nc, st, "h_negpi", [128, 1], F32)
        K.negdel = sb(nc, st, "h_negdel", [128, 4], F32)
        rl1 = sb(nc, st, "h_rl1", [128, 2], F32)
        K.t_hs, K.t_hT, K.t_rl1 = Tl(), Tl(), Tl()
        d = mk.new_dsem()
        mk.dma('sp', K.hs[:], K.hy_small[l], w=[K.t_hs], dsem=d)
        mk.dma('sp', K.negdel[:], K.hy_tdel[:, 0:4], w=[K.t_const], dsem=d)
        mk.op('dve', lambda: nc.vector.memset(K.negpi[:], -math.pi), w=[K.t_const])
        mk.op('dve', lambda: nc.vector.tensor_scalar_mul(out=K.negdel[:], in0=K.negdel[:], scalar1=-1.0), r=[K.t_const], w=[K.t_const])
        seqs = [(SEQ, CTX, K.zT_lat, 4)]
        if need_ctx:
            seqs.append((CTX, 0, K.zT_ctx, 4 + SEQ))
        for (n, off, zT_d, tcol0) in seqs:
            if n == SEQ and USE_FFT:
                with ExitStack() as s1:
                    F = fft_setup(K, s1)
                    with ExitStack() as s2:
                        hT = sb(nc, s2, "h_hT", [128, 4, n], F32)
                        hyena_filter_dev(K, l, s2, n, zT_d, tcol0, hT, rl1)
                        hyena_lat_fft_filter(K, l, F, hT)
                    fft_setup_inv(K, F, s1)
                    hyena_lat_fft_u(K, l, F, s1, rl1)
            else:
                with ExitStack() as s2:
                    hT = sb(nc, s2, "h_hT", [128, 4, n], F32)
                    hyena_filter_dev(K, l, s2, n, zT_d, tcol0, hT, rl1)
                    hyena_conv_dev(K, l, s2, n, off, hT, rl1)
        mk.barrier()


def phase_mixout(K, l, Xres, t_xres, Xout, t_xout, first_chunk=0):
    nc, mk = K.nc, K.mk
    S = K.S
    with ExitStack() as st:
        wo = sb(nc, st, "o_wo", [128, 8, D], BF16)
        modv = sb(nc, st, "o_mod", [128, 3, D], F32)
        xc = [sb(nc, st, "o_xc%d" % i, [128, D], F32) for i in range(2)]
        cT = [sb(nc, st, "o_cT%d" % i, [128, 8, 128], BF16) for i in range(2)]
        tt = sb(nc, st, "o_tt", [128, D], F32)
        st6 = sb(nc, st, "o_st6", [128, 2, 6], F32)
        mv = sb(nc, st, "o_mv", [128, 2], F32)
        rstd = sb(nc, st, "o_rstd", [128, 1], F32)
        pm = [ps(nc, st, "o_pm%d" % i, [128, 512], F32) for i in range(4)]
        t_wo, t_mod, t_tt, t_st = Tl(), Tl(), Tl(), Tl()
        t_xc = [Tl(), Tl()]
        t_cT = [Tl(), Tl()]
        t_pm = [Tl() for _ in range(4)]
        dws, dmod, dout = mk.new_dsem(), mk.new_dsem(), mk.new_dsem()
        dx = [mk.new_dsem(), mk.new_dsem()]
        for kc in range(8):
            mk.dma('pool', wo[:, kc, :], K.w_out[l][kc * 128:(kc + 1) * 128, :], w=[t_wo], dsem=dws)
        cur_g = None
        for ch in range(first_chunk, NCH):
            g = 1 if ch < 2 else 0
            if g != cur_g:
                cur_g = g
                mrow = K.mod[l][g]
                for i, src in enumerate([mrow[5 * D:6 * D], K.ln_g[l, 1], K.ln_b[l, 1]]):
                    mk.dma('sp', modv[:, i, :], src.partition_broadcast(128), r=[K.t_mod[l]], w=[t_mod], dsem=dmod)
            xi = ch % 2
            tok = slice(ch * 128, (ch + 1) * 128)

            def issue_loads(c2):
                x2 = c2 % 2
                tk = slice(c2 * 128, (c2 + 1) * 128)
                mk.dma('sp', xc[x2][:], Xres[tk, :], r=[t_xres], w=[t_xc[x2]], dsem=dx[x2])
                mk.dma('sp', cT[x2][:], S.CT[l].rearrange("(c p) t -> p c t", p=128)[:, :, tk], r=[S.t_CT], w=[t_cT[x2]], dsem=dx[x2])

            if ch == first_chunk:
                issue_loads(ch)
            if ch + 1 < NCH:
                issue_loads(ch + 1)
            for hf in range(2):
                pi = (ch % 2) * 2 + hf
                for kc in range(8):
                    mk.op('pe', lambda kc=kc, hf=hf, pi=pi, xi=xi: nc.tensor.matmul(pm[pi][:], lhsT=cT[xi][:, kc, :], rhs=wo[:, kc, hf * 512:(hf + 1) * 512],
                                                                                   start=(kc == 0), stop=(kc == 7)), r=[t_cT[xi], t_wo], w=[t_pm[pi]])
                mk.op('dve', lambda hf=hf, pi=pi: nc.vector.tensor_mul(out=tt[:, hf * 512:(hf + 1) * 512], in0=pm[pi][:], in1=modv[:, 0, hf * 512:(hf + 1) * 512]),
                      r=[t_pm[pi], t_mod], w=[t_tt])
            mk.op('dve', lambda xi=xi: nc.vector.scalar_tensor_tensor(out=xc[xi][:], in0=xc[xi][:], scalar=ALPHA, in1=tt[:], op0=ALU.mult, op1=ALU.add),
                  r=[t_tt, t_xc[xi]], w=[t_xc[xi]])
            ln_stats(K, xc[xi], t_xc[xi], st6, mv, rstd, t_st)
            mk.op('dve', lambda xi=xi: nc.vector.scalar_tensor_tensor(out=xc[xi][:], in0=xc[xi][:], scalar=mv[:, 0:1], in1=modv[:, 1, :],
                                                                      op0=ALU.subtract, op1=ALU.mult), r=[t_xc[xi], t_st, t_mod], w=[t_xc[xi]])
            mk.op('dve', lambda xi=xi: nc.vector.scalar_tensor_tensor(out=xc[xi][:], in0=xc[xi][:], scalar=rstd[:, 0:1], in1=modv[:, 2, :],
                                                                      op0=ALU.mult, op1=ALU.add), r=[t_xc[xi], t_st, t_mod], w=[t_xc[xi]])
            mk.dma('sp', Xout[tok, :], xc[xi][:], r=[t_xc[xi]], w=[t_xout], dsem=dout)
        mk.barrier()


def build(stop_after=None, dbg=()):
    nc = bass.Bass("TRN2", target_bir_lowering=False)
    K = Ctx()
    K.nc = nc
    K.mk = MK(nc)
    mk = K.mk
    DECLARED.clear()

    def din(name, shape):
        DECLARED.add(name)
        return nc.dram_tensor(name, shape, F32, kind="ExternalInput").ap()

    K.xin = din("xin", [NT, D])
    K.cc_in = din("cc", [128, 16])
    K.ada_w = din("ada_w", [DEPTH, D, NMOD * D])
    K.ada_b = din("ada_b", [DEPTH, NMOD * D])
    K.w_gu = [din("ffn1_w_gu", [DEPTH, D, 2 * DFF]), din("ffn2_w_gu", [DEPTH, D, 2 * DFF])]
    K.w_dn = [din("ffn1_w_down", [DEPTH, DFF, D]), din("ffn2_w_down", [DEPTH, DFF, D])]
    K.ln_g = din("ln_g", [DEPTH, 3, D])
    K.ln_b = din("ln_b", [DEPTH, 3, D])
    K.w_in = din("w_in", [DEPTH, D, P_IN])
    K.w_out = din("w_out", [DEPTH, D, D])
    K.a_qn = din("a_q_norm", [DEPTH, 64])
    K.a_kn = din("a_k_norm", [DEPTH, 64])
    K.c_qn = din("mla_q_norm", [DEPTH, 256])
    K.c_kvn = din("mla_kv_norm", [DEPTH, 128])
    K.w_uq = din("mla_w_uq", [DEPTH, 256, 576])
    K.w_ukv = din("mla_w_ukv", [DEPTH, 128, 768])
    K.hy_conv_w = din("hy_conv_w", [DEPTH, 3, 768])
    K.hy_conv_b = din("hy_conv_b", [DEPTH, 768])
    K.hy_w1 = din("hy_f_w1", [DEPTH, 33, 64])
    K.hy_b1 = din("hy_f_b1", [DEPTH, 64])
    K.hy_w2 = din("hy_f_w2", [DEPTH, 64, 64])
    K.hy_b2 = din("hy_f_b2", [DEPTH, 64])
    K.hy_w3 = din("hy_f_w3", [DEPTH, 64, 64])
    K.hy_b3 = din("hy_f_b3", [DEPTH, 64])
    K.hy_w4 = din("hy_f_w4", [DEPTH, 64, 512])
    K.hy_freq = din("hy_f_freq", [DEPTH, 64])
    K.hy_bias = din("hy_bias", [DEPTH, 256])
    K.ident_in = din("ident", [128, 128])
    K.hy_small = din("hy_small", [DEPTH, 128, 32])
    K.ropeA_in = din("ropeA", [SEQ, 64])
    K.ropeC_in = din("ropeC", [SEQ, 32])
    K.zT_lat = din("zT_lat", [33, SEQ])
    K.zT_ctx = din("zT_ctx", [33, CTX])
    K.hy_tdel = din("hy_tdel", [128, 4 + SEQ + CTX])
    K.fft_F1 = din("fft_F1", [64, 256])
    K.fft_G = din("fft_G", [64, 128, 3, 64])
    K.fft_W3 = din("fft_W3", [128, 512])
    K.fft_F4 = din("fft_F4", [128, 8192])
    K.out = nc.dram_tensor("out", [SEQ, D], F32, kind="ExternalOutput").ap()
    K.mod = nc.dram_tensor("mod", [DEPTH, 2, NMOD * D], F32).ap()
    K.t_mod = [Tl(), Tl()]
    XA = nc.dram_tensor("XA", [NT, D], F32).ap()
    XB = nc.dram_tensor("XB", [NT, D], F32).ap()
    XC = nc.dram_tensor("XC", [NT, D], F32).ap()
    S = Ctx()
    K.S = S
    S.QTa = nc.dram_tensor("QTa", [DEPTH, 6, 64, NT], BF16).ap()
    S.KTa = nc.dram_tensor("KTa", [DEPTH, 2, 64, NT], BF16).ap()
    S.Va = nc.dram_tensor("Va", [DEPTH, NT, 128], BF16).ap()
    S.QTc = nc.dram_tensor("QTc", [DEPTH, 6, 96, NT], BF16).ap()
    S.KTc = nc.dram_tensor("KTc", [DEPTH, 6, 96, NT], BF16).ap()
    S.Vc = nc.dram_tensor("Vc", [DEPTH, NT, 384], BF16).ap()
    S.PBT = nc.dram_tensor("PBT", [DEPTH, 768, NT], F32).ap()
    S.CT = nc.dram_tensor("CT", [DEPTH, D, NT], BF16).ap()
    S.t_A, S.t_C, S.t_PBT, S.t_CT = Tl(), Tl(), Tl(), Tl()
    S.HTd = nc.dram_tensor("HTd", [512, SEQ], BF16).ap()
    S.UTd = nc.dram_tensor("UTd", [256, SEQ], BF16).ap()
    S.XF = nc.dram_tensor("XF", [2, 128, 128, 2, 64], F32).ap()
    S.HF = nc.dram_tensor("HF", [2, 128, 128, 2, 64], F32).ap()
    S.t_HTd, S.t_UTd, S.t_XF, S.t_HF = Tl(), Tl(), Tl(), Tl()
    S.GBd = nc.dram_tensor("GBd", [16, 128, 24, 128], BF16).ap()
    S.t_GBd = Tl()
    K.gb_cached = [False] * 16
    t_xin, t_XA, t_XB, t_XC, t_out = Tl(), Tl(), Tl(), Tl(), Tl()
    dbg_outs = []

    def finish():
        dd = mk.new_dsem()
        for name in dbg:
            src = {"XA": XA, "XB": XB, "XC": XC, "QTa": S.QTa, "KTa": S.KTa, "Va": S.Va, "QTc": S.QTc, "KTc": S.KTc, "Vc": S.Vc,
                   "PBT": S.PBT, "CT": S.CT, "mod": K.mod}[name]
            o = nc.dram_tensor("dbg_" + name, list(src.shape), src.dtype, kind="ExternalOutput").ap()
            mk.dma('sp', o, src, w=[t_out], dsem=dd)
        mk.barrier()
        return nc

    with ExitStack() as gst:
        K.identb = sb(nc, gst, "identb", [128, 128], BF16)
        identf = sb(nc, gst, "identf", [128, 128], F32)
        K.eps_t = sb(nc, gst, "eps_t", [128, 1], F32)
        K.t_const = Tl()
        d0 = mk.new_dsem()
        mk.dma('sp', identf[:], K.ident_in[:, :], w=[K.t_const], dsem=d0)
        mk.op('dve', lambda: nc.vector.tensor_copy(out=K.identb[:], in_=identf[:]), r=[K.t_const], w=[K.t_const])
        mk.op('dve', lambda: nc.vector.memset(K.eps_t[:], EPS), w=[K.t_const])
        mk.barrier()

        Xcur, t_cur = K.xin, t_xin
        for l in range(DEPTH):
            last = (l == DEPTH - 1)
            phase_ada(K, l)
            mk.recycle()
            phase_ffn(K, l, 0, Xcur, t_cur, XC, t_XC, K.w_gu[0][l], K.w_dn[0][l])
            mk.recycle()
            if stop_after == "ffn1_%d" % l:
                return finish()
            phase_mixin(K, l, XC, t_XC)
            mk.recycle()
            if stop_after == "mixin_%d" % l:
                return finish()
            phase_attn(K, l, need_ctx=not last)
            mk.recycle()
            if stop_after == "attn_%d" % l:
                return finish()
            phase_hyena(K, l, need_ctx=not last)
            mk.recycle()
            if stop_after == "hyena_%d" % l:
                return finish()
            with ExitStack() as wst:
                W2 = ffn_load_weights(K, wst, K.w_gu[1][l], K.w_dn[1][l])
                phase_mixout(K, l, XC, t_XC, XB, t_XB, first_chunk=(2 if last else 0))
                if stop_after == "mixout_%d" % l:
                    return finish()
                phase_ffn(K, l, 1, XB, t_XB, XA, t_XA, K.w_gu[1][l], K.w_dn[1][l], first_chunk=(2 if last else 0), W=W2)
            mk.recycle()
            Xcur, t_cur = XA, t_XA
        dd = mk.new_dsem()
        mk.dma('sp', K.out[:, :], XA[CTX:, :], r=[t_XA], w=[t_out], dsem=dd)
        return finish()


def rope_table(rot_dim):
    rows = SEQ // 64
    row = np.repeat(np.arange(rows, dtype=np.float32), 64)
    col = np.tile(np.arange(64, dtype=np.float32), rows)
    n_freq = rot_dim // 4
    inv = (np.float32(10000.0) ** (-np.arange(n_freq, dtype=np.float32) / np.float32(n_freq))).astype(np.float32)
    ang = np.concatenate([row[:, None] * inv, col[:, None] * inv], -1).astype(np.float32)
    return np.ascontiguousarray(np.concatenate([np.cos(ang), np.sin(ang)], -1).astype(np.float32))


def fft_tables():
    N = 8192
    s1 = np.arange(64)[:, None]; k1 = np.arange(128)[None, :]
    F1 = np.exp(-2j * np.pi * s1 * k1 / 128)
    F1cat = np.concatenate([F1.real, F1.imag], 1)
    s2 = np.arange(64)[:, None, None]; k1g = np.arange(128)[None, :, None]; k2 = np.arange(64)[None, None, :]
    G = np.exp(-2j * np.pi * (s2 * k1g / 8192 + s2 * k2 / 64))
    Gc = np.stack([G.real, G.imag, -G.imag], 2)
    k2_ = np.arange(64)[:, None]; t2 = np.arange(64)[None, :]
    W3 = np.exp(2j * np.pi * k2_ * t2 / 64)
    W3a = np.zeros((128, 2, 2, 64)); W3b = np.zeros((128, 2, 2, 64))
    for cc in range(2):
        W3a[cc * 64:(cc + 1) * 64, cc, 0] = W3.real; W3a[cc * 64:(cc + 1) * 64, cc, 1] = W3.imag
        W3b[cc * 64:(cc + 1) * 64, cc, 0] = -W3.imag; W3b[cc * 64:(cc + 1) * 64, cc, 1] = W3.real
    W3cat = np.concatenate([W3a.reshape(128, 256), W3b.reshape(128, 256)], 1)
    k1_ = np.arange(128)[:, None, None]; t2_ = np.arange(64)[None, :, None]; t1 = np.arange(64)[None, None, :]
    F4 = np.exp(2j * np.pi * (t1 * k1_ / 128 + t2_ * k1_ / 8192)) / N
    F4c = np.stack([F4.real, -F4.imag], 2).reshape(128, 8192)
    f = lambda a: np.ascontiguousarray(a.astype(np.float32))
    return f(F1cat), f(Gc), f(W3cat), f(F4c)


def hyena_z(n):
    t = np.linspace(0.0, 1.0, n, dtype=np.float32)[:, None]
    w = (np.float32(2.0 * math.pi) * np.arange(n, dtype=np.float32)[:, None] / np.float32(n)).astype(np.float32)
    f = np.linspace(1e-4, 15, 16, dtype=np.float32)[None, :]
    z = np.concatenate([t, np.cos(f * w), -np.sin(f * w)], -1).astype(np.float32)
    return np.ascontiguousarray(z.T), t[:, 0]


def make_in_maps(inputs):
    x = np.asarray(inputs["x"], dtype=np.float32)
    ctx = np.asarray(inputs["ctx"], dtype=np.float32)
    c = np.asarray(inputs["c"], dtype=np.float32)
    cctx = np.asarray(inputs["c_ctx"], dtype=np.float32)
    shared = {k: np.ascontiguousarray(np.asarray(inputs[k], dtype=np.float32)) for k in inputs if k not in ("x", "ctx", "c")}
    shared["ident"] = np.eye(128, dtype=np.float32)
    shared["ropeA"] = rope_table(64)
    shared["ropeC"] = rope_table(32)
    zl, tl = hyena_z(SEQ)
    zc, tc_ = hyena_z(CTX)
    shared["zT_lat"] = zl
    shared["zT_ctx"] = zc
    deltas = np.abs(np.linspace(math.log(1e-2) / 1.5, math.log(1e-2) / 0.3, 256, dtype=np.float32))
    deltas = np.tile(deltas, 2)
    td = np.zeros((128, 4 + SEQ + CTX), np.float32)
    td[:, 0:4] = deltas.reshape(4, 128).T
    td[:, 4:4 + SEQ] = tl[None, :]
    td[:, 4 + SEQ:] = tc_[None, :]
    shared["hy_tdel"] = td
    hsm = np.zeros((DEPTH, 128, 32), np.float32)
    for l in range(DEPTH):
        cw = np.asarray(inputs["hy_conv_w"][l], np.float32)
        hsm[l, :, 0:18] = cw.reshape(3, 6, 128).transpose(2, 0, 1).reshape(128, 18)
        hsm[l, :, 18:24] = np.asarray(inputs["hy_conv_b"][l], np.float32).reshape(6, 128).T
        hsm[l, :, 24:26] = np.asarray(inputs["hy_bias"][l], np.float32).reshape(2, 128).T
        hsm[l, 0:64, 26] = np.asarray(inputs["hy_f_freq"][l], np.float32)
        hsm[l, 0:64, 27] = np.asarray(inputs["hy_f_b1"][l], np.float32)
        hsm[l, 0:64, 28] = np.asarray(inputs["hy_f_b2"][l], np.float32)
        hsm[l, 0:64, 29] = np.asarray(inputs["hy_f_b3"][l], np.float32)
    shared["hy_small"] = hsm
    shared["fft_F1"], shared["fft_G"], shared["fft_W3"], shared["fft_F4"] = fft_tables()
    maps = []
    for b in range(8):
        m = dict(shared)
        m["xin"] = np.ascontiguousarray(np.concatenate([ctx[b], x[b]], axis=0))
        cc = np.stack([c[b], cctx], axis=-1).reshape(8, 128, 2).transpose(1, 0, 2).reshape(128, 16)
        m["cc"] = np.ascontiguousarray(cc)
        maps.append(m)
    return maps


def kernel(**inputs):
    nc = build()
    maps = make_in_maps(inputs)
    maps = [{k: v for k, v in m.items() if k in DECLARED} for m in maps]
    res = run_bass_kernel_spmd(nc, maps, core_ids=list(range(8)))
    return np.stack([np.asarray(r["out"]) for r in res.results], axis=0).astype(np.float32)
```

```python
import math
from contextlib import ExitStack
import numpy as np
import concourse.bass as bass
import concourse.mybir as mybir
from concourse.bass_utils import run_bass_kernel_spmd

F32 = mybir.dt.float32
BF16 = mybir.dt.bfloat16
AF = mybir.ActivationFunctionType
ALU = mybir.AluOpType

D = 1024
SEQ = 4096
CTX = 256
NT = SEQ + CTX
NCH = NT // 128
DEPTH = 2
DFF = 2816
NMOD = 9
ALPHA = (2 * DEPTH) ** 0.25
EPS = 1e-6
P_IN = 1824
USE_FFT = True


class Tl:
    def __init__(self, name=""):
        self.name = name
        self.w = None
        self.weng = None
        self.r = {}


class MK:
    def __init__(self, nc):
        self.nc = nc
        self.eng = {'pe': nc.tensor, 'act': nc.scalar, 'dve': nc.vector, 'pool': nc.gpsimd, 'sp': nc.sync}
        self.sem = {}
        self.cnt = {}
        self.seen = {e: {} for e in self.eng}
        self._stack = ExitStack()
        for e in self.eng:
            self.sem[e] = self._stack.enter_context(nc.semaphore("s_" + e))
            self.cnt[e] = 0
        self.dsems = []
        self.free_dsems = []

    def new_dsem(self):
        if self.free_dsems:
            return self.free_dsems.pop()
        s = self._stack.enter_context(self.nc.semaphore("d%d" % len(self.dsems)))
        d = [s, 0]
        self.dsems.append(d)
        return d

    def _need(self, e, waits, dep):
        if dep is None:
            return
        sem, val = dep
        k = id(sem)
        if self.seen[e].get(k, 0) >= val:
            return
        if k not in waits or waits[k][1] < val:
            waits[k] = (sem, val)

    def _dowaits(self, e, waits):
        E = self.eng[e]
        for k, (sem, val) in waits.items():
            E.wait_ge(sem, val)
            self.seen[e][k] = val

    def op(self, e, fn, r=(), w=()):
        waits = {}
        for t in r:
            self._need(e, waits, t.w)
        for t in w:
            if t.weng != e:
                self._need(e, waits, t.w)
            for re_, dep in t.r.items():
                if re_ != e:
                    self._need(e, waits, dep)
        self._dowaits(e, waits)
        ins = fn()
        self.cnt[e] += 1
        ins.then_inc(self.sem[e], 1)
        dep = (self.sem[e], self.cnt[e])
        for t in r:
            t.r[e] = dep
        for t in w:
            t.w = dep
            t.weng = e
            t.r = {}
        return ins

    def dma(self, q, out, in_, r=(), w=(), dsem=None):
        waits = {}
        for t in r:
            self._need(q, waits, t.w)
        for t in w:
            self._need(q, waits, t.w)
            for re_, dep in t.r.items():
                self._need(q, waits, dep)
        self._dowaits(q, waits)
        ins = self.eng[q].dma_start(out=out, in_=in_)
        dsem[1] += 16
        ins.then_inc(dsem[0], 16)
        dep = (dsem[0], dsem[1])
        for t in r:
            t.r['dma%d' % id(dsem)] = dep
        for t in w:
            t.w = dep
            t.weng = 'dma'
            t.r = {}
        return ins

    def barrier(self):
        waits = {}
        for e in self.eng:
            if e != 'sp' and self.cnt[e] > 0:
                self._need('sp', waits, (self.sem[e], self.cnt[e]))
        for d in self.dsems:
            if d[1] > 0:
                self._need('sp', waits, (d[0], d[1]))
        self._dowaits('sp', waits)
        self.cnt['sp'] += 1
        self.nc.sync.sem_inc(self.sem['sp'], 1)
        for e in self.eng:
            if e != 'sp':
                self.eng[e].wait_ge(self.sem['sp'], self.cnt['sp'])
                self.seen[e][id(self.sem['sp'])] = self.cnt['sp']
                for f in self.eng:
                    if f != 'sp':
                        self.seen[e][id(self.sem[f])] = self.cnt[f]
                for d in self.dsems:
                    self.seen[e][id(d[0])] = d[1]

    def recycle(self):
        self.free_dsems = list(self.dsems)

    def close(self):
        self._stack.close()


class Ctx:
    pass


DECLARED = set()


_UID = [0]


def sb(nc, st, name, shape, dt):
    _UID[0] += 1
    return st.enter_context(nc.sbuf_tensor("sb%d_%s" % (_UID[0], name), shape, dt))


def ps(nc, st, name, shape, dt):
    _UID[0] += 1
    return st.enter_context(nc.psum_tensor("ps%d_%s" % (_UID[0], name), shape, dt))


def phase_ada(K, l):
    nc, mk = K.nc, K.mk
    with ExitStack() as st:
        cc = sb(nc, st, "ada_cc", [128, 8, 2], F32)
        sg = sb(nc, st, "ada_sg", [128, 8, 2], F32)
        wbuf = [sb(nc, st, "ada_w%d" % i, [128, 8, 512], F32) for i in range(4)]
        bia = sb(nc, st, "ada_b", [2, 9216], F32)
        mrow = sb(nc, st, "ada_m", [2, 9216], F32)
        pm = [ps(nc, st, "ada_p%d" % i, [2, 512], F32) for i in range(2)]
        t_cc, t_sg, t_b, t_m = Tl(), Tl(), Tl(), Tl()
        t_w = [Tl() for _ in range(4)]
        t_p = [Tl(), Tl()]
        d0 = mk.new_dsem()
        dw = [mk.new_dsem() for _ in range(4)]
        mk.dma('sp', cc[:], K.cc_in.rearrange("p (kc g) -> p kc g", g=2), w=[t_cc], dsem=d0)
        mk.dma('sp', bia[:], K.ada_b[l].partition_broadcast(2), w=[t_b], dsem=d0)
        mk.op('act', lambda: nc.scalar.activation(out=sg[:], in_=cc[:], func=AF.Silu), r=[t_cc], w=[t_sg])
        for nb in range(18):
            i = nb % 2
            wi = nb % 4
            mk.dma('sp', wbuf[wi][:], K.ada_w[l][:, nb * 512:(nb + 1) * 512].rearrange("(kc p) n -> p kc n", p=128),
                   w=[t_w[wi]], dsem=dw[wi])
            for kc in range(8):
                mk.op('pe', lambda kc=kc, i=i, wi=wi: nc.tensor.matmul(pm[i][:], lhsT=sg[:, kc, :], rhs=wbuf[wi][:, kc, :],
                                                                      start=(kc == 0), stop=(kc == 7)),
                      r=[t_sg, t_w[wi]], w=[t_p[i]])
            mk.op('dve', lambda nb=nb, i=i: nc.vector.tensor_add(out=mrow[:, nb * 512:(nb + 1) * 512], in0=pm[i][:],
                                                                 in1=bia[:, nb * 512:(nb + 1) * 512]),
                  r=[t_p[i], t_b], w=[t_m])
        mk.dma('sp', K.mod[l], mrow[:], r=[t_m], w=[K.t_mod[l]], dsem=d0)
        mk.barrier()


def ln_stats(K, x_ap, t_x, st6, mv, rstd, t_st):
    nc, mk = K.nc, K.mk
    for c in range(2):
        mk.op('dve', lambda c=c: nc.vector.bn_stats(out=st6[:, c, :], in_=x_ap[:, c * 512:(c + 1) * 512]), r=[t_x], w=[t_st])
    mk.op('dve', lambda: nc.vector.bn_aggr(out=mv[:], in_=st6[:]), r=[t_st], w=[t_st])
    mk.op('act', lambda: nc.scalar.activation(out=rstd[:], in_=mv[:, 1:2], func=AF.Sqrt, bias=K.eps_t[:, 0:1], scale=1.0), r=[t_st], w=[t_st])
    mk.op('dve', lambda: nc.vector.reciprocal(out=rstd[:], in_=rstd[:]), r=[t_st], w=[t_st])


def load_bcast(K, dst_ap, src_row_ap, t, dsem):
    K.mk.dma('sp', dst_ap, src_row_ap.partition_broadcast(128), w=[t], dsem=dsem)


def ffn_load_weights(K, st, w_gu, w_down):
    nc, mk = K.nc, K.mk
    W = Ctx()
    W.wgu = sb(nc, st, "f_wgu", [128, 8, 2 * DFF], BF16)
    W.wdn = sb(nc, st, "f_wdn", [128, 22, D], BF16)
    W.t_wgu, W.t_wdn = Tl(), Tl()
    dws = mk.new_dsem()
    for kc in range(8):
        for hh in range(2):
            mk.dma('pool', W.wgu[:, kc, hh * DFF:(hh + 1) * DFF], w_gu[kc * 128:(kc + 1) * 128, hh * DFF:(hh + 1) * DFF], w=[W.t_wgu], dsem=dws)
    for fc in range(22):
        mk.dma('pool', W.wdn[:, fc, :], w_down[fc * 128:(fc + 1) * 128, :], w=[W.t_wdn], dsem=dws)
    return W


def phase_ffn(K, l, which, Xin, t_xin, Xout, t_xout, w_gu, w_down, first_chunk=0, W=None):
    nc, mk = K.nc, K.mk
    mbase = 0 if which == 0 else 6
    lni = 0 if which == 0 else 2
    with ExitStack() as st:
        if W is None:
            W = ffn_load_weights(K, st, w_gu, w_down)
        wgu, wdn, t_wgu, t_wdn = W.wgu, W.wdn, W.t_wgu, W.t_wdn
        modv = sb(nc, st, "f_mod", [128, 5, D], F32)
        xc = [sb(nc, st, "f_xc%d" % i, [128, D], F32) for i in range(6)]
        hT = sb(nc, st, "f_hT", [128, 8, 256], BF16)
        actT = sb(nc, st, "f_actT", [128, 22, 256], BF16)
        tt = sb(nc, st, "f_tt", [128, D], F32)
        hn2 = [sb(nc, st, "f_hn%d" % i, [128, D], BF16) for i in range(2)]
        st6j = [sb(nc, st, "f_st6j%d" % i, [128, 2, 6], F32) for i in range(2)]
        mvj = [sb(nc, st, "f_mvj%d" % i, [128, 2], F32) for i in range(2)]
        rstdj = [sb(nc, st, "f_rstdj%d" % i, [128, 1], F32) for i in range(2)]
        sgt = [sb(nc, st, "f_sg%d" % i, [128, 256], F32) for i in range(2)]
        st6b = sb(nc, st, "f_st6b", [128, 2, 6], F32)
        mvb = sb(nc, st, "f_mvb", [128, 2], F32)
        rstdb = sb(nc, st, "f_rstdb", [128, 1], F32)
        p_tr = ps(nc, st, "f_ptr", [128, 8, 128], BF16)
        p_up = [ps(nc, st, "f_pup%d" % i, [128, 2, 256], F32) for i in range(3)]
        p_dn = [ps(nc, st, "f_pdn%d" % i, [128, 512], F32) for i in range(4)]
        t_mod = Tl()
        t_xc = [Tl() for _ in range(6)]
        t_hT, t_act, t_tt, t_stb, t_ptr = Tl(), Tl(), Tl(), Tl(), Tl()
        t_hn2 = [Tl(), Tl()]
        t_stj = [Tl(), Tl()]
        t_sg = [Tl(), Tl()]
        t_pup = [Tl() for _ in range(3)]
        t_pdn = [Tl() for _ in range(4)]
        dmod = mk.new_dsem()
        dx = [mk.new_dsem() for _ in range(6)]
        dout = mk.new_dsem()
        nblk = NCH // 2
        state = {"upi": 0}
        blks = list(range(first_chunk // 2, nblk))

        def xidx(blk, j):
            return (blk % 3) * 2 + j

        def load_mod(g):
            mrow = K.mod[l][g]
            for i, src in enumerate([mrow[(mbase + 0) * D:(mbase + 1) * D], mrow[(mbase + 1) * D:(mbase + 2) * D],
                                     mrow[(mbase + 2) * D:(mbase + 3) * D], K.ln_g[l, lni], K.ln_b[l, lni]]):
                mk.dma('sp', modv[:, i, :], src.partition_broadcast(128), r=[K.t_mod[l]], w=[t_mod], dsem=dmod)
            mk.op('pool', lambda: nc.gpsimd.tensor_scalar_add(out=modv[:, 1, :], in0=modv[:, 1, :], scalar1=1.0), r=[t_mod], w=[t_mod])
            mk.op('pool', lambda: nc.gpsimd.tensor_scalar_mul(out=modv[:, 2, :], in0=modv[:, 2, :], scalar1=0.5), r=[t_mod], w=[t_mod])

        def prep_load(blk):
            for j in range(2):
                ch = blk * 2 + j
                xi = xidx(blk, j)
                mk.dma('sp', xc[xi][:], Xin[ch * 128:(ch + 1) * 128, :], r=[t_xin], w=[t_xc[xi]], dsem=dx[xi])

        def prep_ln(blk):
            for j in range(2):
                xi = xidx(blk, j)
                ln_stats(K, xc[xi], t_xc[xi], st6j[j], mvj[j], rstdj[j], t_stj[j])
            for j in range(2):
                xi = xidx(blk, j)
                mk.op('dve', lambda xi=xi, j=j: nc.vector.scalar_tensor_tensor(out=tt[:], in0=xc[xi][:], scalar=mvj[j][:, 0:1], in1=modv[:, 1, :],
                                                                               op0=ALU.subtract, op1=ALU.mult), r=[t_xc[xi], t_stj[j], t_mod], w=[t_tt])
                mk.op('dve', lambda j=j: nc.vector.scalar_tensor_tensor(out=hn2[j][:], in0=tt[:], scalar=rstdj[j][:, 0:1], in1=modv[:, 0, :],
                                                                        op0=ALU.mult, op1=ALU.add), r=[t_tt, t_stj[j], t_mod], w=[t_hn2[j]])

        def prep_ln_gen(blk):
            for j in range(2):
                xi = xidx(blk, j)
                for c in range(2):
                    mk.op('dve', lambda c=c, xi=xi, j=j: nc.vector.bn_stats(out=st6j[j][:, c, :], in_=xc[xi][:, c * 512:(c + 1) * 512]), r=[t_xc[xi]], w=[t_stj[j]])
                    yield
                mk.op('dve', lambda j=j: nc.vector.bn_aggr(out=mvj[j][:], in_=st6j[j][:]), r=[t_stj[j]], w=[t_stj[j]])
                mk.op('act', lambda j=j: nc.scalar.activation(out=rstdj[j][:], in_=mvj[j][:, 1:2], func=AF.Sqrt, bias=K.eps_t[:, 0:1], scale=1.0), r=[t_stj[j]], w=[t_stj[j]])
                yield
                yield
                mk.op('dve', lambda j=j: nc.vector.reciprocal(out=rstdj[j][:], in_=rstdj[j][:]), r=[t_stj[j]], w=[t_stj[j]])
                yield
                mk.op('dve', lambda xi=xi, j=j: nc.vector.scalar_tensor_tensor(out=tt[:], in0=xc[xi][:], scalar=mvj[j][:, 0:1], in1=modv[:, 1, :],
                                                                               op0=ALU.subtract, op1=ALU.mult), r=[t_xc[xi], t_stj[j], t_mod], w=[t_tt])
                yield
                mk.op('dve', lambda j=j: nc.vector.scalar_tensor_tensor(out=hn2[j][:], in0=tt[:], scalar=rstdj[j][:, 0:1], in1=modv[:, 0, :],
                                                                        op0=ALU.mult, op1=ALU.add), r=[t_tt, t_stj[j], t_mod], w=[t_hn2[j]])
                yield

        def prep_b(blk):
            for j in range(2):
                for kc in range(8):
                    mk.op('pe', lambda kc=kc, j=j: nc.tensor.transpose(out=p_tr[:, kc, :], in_=hn2[j][:, kc * 128:(kc + 1) * 128], identity=K.identb[:]),
                          r=[t_hn2[j], K.t_const], w=[t_ptr])
                mk.op('act', lambda j=j: nc.scalar.copy(out=hT[:, :, j * 128:(j + 1) * 128], in_=p_tr[:]), r=[t_ptr], w=[t_hT])

        def up(blk, gen=None):
            for fc in range(22):
                if gen is not None and fc >= 3:
                    next(gen, None)
                pi = state["upi"] % 3
                si = state["upi"] % 2
                state["upi"] += 1
                for hh in range(2):
                    for kc in range(8):
                        mk.op('pe', lambda kc=kc, hh=hh, fc=fc, pi=pi: nc.tensor.matmul(
                            p_up[pi][:, hh, :], lhsT=wgu[:, kc, hh * DFF + fc * 128: hh * DFF + (fc + 1) * 128], rhs=hT[:, kc, :],
                            start=(kc == 0), stop=(kc == 7)), r=[t_wgu, t_hT], w=[t_pup[pi]])
                mk.op('act', lambda pi=pi, si=si: nc.scalar.activation(out=sgt[si][:], in_=p_up[pi][:, 0, :], func=AF.Silu), r=[t_pup[pi]], w=[t_sg[si]])
                mk.op('dve', lambda pi=pi, si=si, fc=fc: nc.vector.tensor_mul(out=actT[:, fc, :], in0=sgt[si][:], in1=p_up[pi][:, 1, :]),
                      r=[t_sg[si], t_pup[pi]], w=[t_act])

        def down(blk):
            for j in range(2):
                ch = blk * 2 + j
                xi = xidx(blk, j)
                for hf in range(2):
                    pd = j * 2 + hf
                    for fc in range(22):
                        mk.op('pe', lambda fc=fc, j=j, hf=hf, pd=pd: nc.tensor.matmul(
                            p_dn[pd][:], lhsT=actT[:, fc, j * 128:(j + 1) * 128], rhs=wdn[:, fc, hf * 512:(hf + 1) * 512],
                            start=(fc == 0), stop=(fc == 21)), r=[t_act, t_wdn], w=[t_pdn[pd]])
                    mk.op('dve', lambda hf=hf, pd=pd: nc.vector.tensor_mul(out=tt[:, hf * 512:(hf + 1) * 512], in0=p_dn[pd][:],
                                                                           in1=modv[:, 2, hf * 512:(hf + 1) * 512]), r=[t_pdn[pd], t_mod], w=[t_tt])
                mk.op('dve', lambda xi=xi: nc.vector.scalar_tensor_tensor(out=xc[xi][:], in0=xc[xi][:], scalar=ALPHA, in1=tt[:],
                                                                          op0=ALU.mult, op1=ALU.add), r=[t_tt, t_xc[xi]], w=[t_xc[xi]])
                ln_stats(K, xc[xi], t_xc[xi], st6b, mvb, rstdb, t_stb)
                mk.op('dve', lambda xi=xi: nc.vector.scalar_tensor_tensor(out=xc[xi][:], in0=xc[xi][:], scalar=mvb[:, 0:1], in1=modv[:, 3, :],
                                                                          op0=ALU.subtract, op1=ALU.mult), r=[t_xc[xi], t_stb, t_mod], w=[t_xc[xi]])
                mk.op('dve', lambda xi=xi: nc.vector.scalar_tensor_tensor(out=xc[xi][:], in0=xc[xi][:], scalar=rstdb[:, 0:1], in1=modv[:, 4, :],
                                                                          op0=ALU.mult, op1=ALU.add), r=[t_xc[xi], t_stb, t_mod], w=[t_xc[xi]])
                mk.dma('pool', Xout[ch * 128:(ch + 1) * 128, :], xc[xi][:], r=[t_xc[xi]], w=[t_xout], dsem=dout)

        if blks and blks[0] == 0:
            load_mod(1)
            prep_load(0)
            prep_ln(0)
            prep_b(0)
            up(0)
            down(0)
            blks = blks[1:]
        if blks:
            load_mod(0)
            prep_load(blks[0])
            if len(blks) > 1:
                prep_load(blks[1])
            prep_ln(blks[0])
            prep_b(blks[0])
            for bi_, blk in enumerate(blks):
                nxt = blks[bi_ + 1] if bi_ + 1 < len(blks) else None
                nx2 = blks[bi_ + 2] if bi_ + 2 < len(blks) else None
                gen = prep_ln_gen(nxt) if nxt is not None else None
                up(blk, gen)
                if gen is not None:
                    for _ in gen:
                        pass
                    prep_b(nxt)
                if nx2 is not None:
                    prep_load(nx2)
                down(blk)
        mk.barrier()


def phase_mixin(K, l, Xin, t_xin):
    nc, mk = K.nc, K.mk
    S = K.S
    with ExitStack() as st:
        win = sb(nc, st, "m_win", [128, 8, P_IN], BF16)
        wuq = sb(nc, st, "m_wuq", [128, 2, 576], BF16)
        wukv = sb(nc, st, "m_wukv", [128, 768], BF16)
        modv = sb(nc, st, "m_mod", [128, 2, D], F32)
        gA = sb(nc, st, "m_gA", [128, 8, 64], F32)
        gq = sb(nc, st, "m_gq", [128, 256], F32)
        gkv = sb(nc, st, "m_gkv", [128, 128], F32)
        ropeA_l = [sb(nc, st, "m_ropeA%d" % i, [128, 2, 32], F32) for i in range(2)]
        ropeC_l = [sb(nc, st, "m_ropeC%d" % i, [128, 2, 16], F32) for i in range(2)]
        t_rope_l = [Tl(), Tl()]
        xc = [sb(nc, st, "m_xc%d" % i, [128, D], F32) for i in range(2)]
        xn = sb(nc, st, "m_xn", [128, D], F32)
        hn = sb(nc, st, "m_hn", [128, D], BF16)
        hT = sb(nc, st, "m_hT", [128, 8, 128], BF16)
        ptA = sb(nc, st, "m_ptA", [128, 8, 64], F32)
        ptC = sb(nc, st, "m_ptC", [128, 416], F32)
        sq = sb(nc, st, "m_sq", [128, 512], F32)
        ss = sb(nc, st, "m_ss", [128, 16], F32)
        r1 = sb(nc, st, "m_r1", [128, 8, 32], F32)
        r2 = sb(nc, st, "m_r2", [128, 8, 32], F32)
        qkb = sb(nc, st, "m_qkb", [128, 8, 64], BF16)
        vab = sb(nc, st, "m_vab", [128, 128], BF16)
        qkT = sb(nc, st, "m_qkT", [128, 4, 128], BF16)
        pbs = sb(nc, st, "m_pbs", [128, 6, 128], F32)
        cn = sb(nc, st, "m_cn", [128, 384], BF16)
        cT = sb(nc, st, "m_cT", [128, 3, 128], BF16)
        qc = sb(nc, st, "m_qc", [128, 6, 96], F32)
        kvc = sb(nc, st, "m_kvc", [128, 6, 128], F32)
        kr = sb(nc, st, "m_kr", [128, 32], F32)
        qcb = sb(nc, st, "m_qcb", [128, 6, 96], BF16)
        kcb = sb(nc, st, "m_kcb", [128, 6, 96], BF16)
        vcb = sb(nc, st, "m_vcb", [128, 6, 64], BF16)
        qcT = sb(nc, st, "m_qcT", [128, 6, 128], BF16)
        kcT = sb(nc, st, "m_kcT", [128, 6, 128], BF16)
        st6 = sb(nc, st, "m_st6", [128, 2, 6], F32)
        mv = sb(nc, st, "m_mv", [128, 2], F32)
        rstd = sb(nc, st, "m_rstd", [128, 1], F32)
        B0 = ps(nc, st, "m_B0", [128, 8, 128], BF16)
        B1 = ps(nc, st, "m_B1", [128, 512], F32)
        B2 = ps(nc, st, "m_B2", [128, 512], F32)
        B3 = ps(nc, st, "m_B3", [128, 512], F32)
        B4 = ps(nc, st, "m_B4", [128, 4, 128], F32)
        B5 = ps(nc, st, "m_B5", [128, 4, 128], F32)
        B6 = ps(nc, st, "m_B6", [128, 8, 128], BF16)
        B7 = ps(nc, st, "m_B7", [128, 512], F32)
        tB = [Tl() for _ in range(8)]
        t_w, t_mod, t_g = Tl(), Tl(), Tl()
        t_xc = [Tl(), Tl()]
        (t_xn, t_hn, t_hT, t_ptA, t_ptC, t_sq, t_ss, t_r, t_qkb, t_vab, t_qkT, t_pbs, t_cn, t_cT, t_qc, t_kvc,
         t_kr, t_qcb, t_kcb, t_vcb, t_qcT, t_kcT, t_st) = [Tl() for _ in range(23)]
        dws, dmod, dg, dout = [mk.new_dsem() for _ in range(4)]
        drope_l = [mk.new_dsem(), mk.new_dsem()]
        dx = [mk.new_dsem(), mk.new_dsem()]

        def issue_loads(ch):
            xi = ch % 2
            mk.dma('sp', xc[xi][:], Xin[ch * 128:(ch + 1) * 128, :], r=[t_xin], w=[t_xc[xi]], dsem=dx[xi])
            if ch >= 2:
                lt = slice((ch - 2) * 128, (ch - 1) * 128)
                mk.dma('sp', ropeA_l[xi][:], K.ropeA_in[lt, :].rearrange("p (a b) -> p a b", a=2), w=[t_rope_l[xi]], dsem=drope_l[xi])
                mk.dma('sp', ropeC_l[xi][:], K.ropeC_in[lt, :].rearrange("p (a b) -> p a b", a=2), w=[t_rope_l[xi]], dsem=drope_l[xi])
        for kc in range(8):
            for two in range(2):
                mk.dma('pool', win[:, kc, 0:384].rearrange("p (h two d) -> p h two d", h=3, two=2)[:, :, two, :],
                       K.w_in[l][kc * 128:(kc + 1) * 128, two * 192:(two + 1) * 192].rearrange("p (h d) -> p h d", h=3), w=[t_w], dsem=dws)
            mk.dma('pool', win[:, kc, 384:P_IN], K.w_in[l][kc * 128:(kc + 1) * 128, 384:P_IN], w=[t_w], dsem=dws)
        for kc in range(2):
            mk.dma('pool', wuq[:, kc, :], K.w_uq[l][kc * 128:(kc + 1) * 128, :], w=[t_w], dsem=dws)
        mk.dma('pool', wukv[:], K.w_ukv[l][:, :], w=[t_w], dsem=dws)
        for h in range(6):
            mk.dma('sp', gA[:, h, :], K.a_qn[l].partition_broadcast(128), w=[t_g], dsem=dg)
        for h in range(2):
            mk.dma('sp', gA[:, 6 + h, :], K.a_kn[l].partition_broadcast(128), w=[t_g], dsem=dg)
        mk.dma('sp', gq[:], K.c_qn[l].partition_broadcast(128), w=[t_g], dsem=dg)
        mk.dma('sp', gkv[:], K.c_kvn[l].partition_broadcast(128), w=[t_g], dsem=dg)
        mk.op('pool', lambda: nc.gpsimd.tensor_scalar_mul(out=gA[:, 0:6, :], in0=gA[:, 0:6, :], scalar1=0.125), r=[t_g], w=[t_g])
        cur_g = None
        for ch in range(NCH):
            g = 1 if ch < 2 else 0
            lat = (g == 0)
            if g != cur_g:
                cur_g = g
                mrow = K.mod[l][g]
                for i in range(2):
                    mk.dma('sp', modv[:, i, :], mrow[(3 + i) * D:(4 + i) * D].partition_broadcast(128), r=[K.t_mod[l]], w=[t_mod], dsem=dmod)
                mk.op('pool', lambda: nc.gpsimd.tensor_scalar_add(out=modv[:, 1, :], in0=modv[:, 1, :], scalar1=1.0), r=[t_mod], w=[t_mod])
            xi = ch % 2
            tok = slice(ch * 128, (ch + 1) * 128)
            if ch == 0:
                issue_loads(0)
            if ch + 1 < NCH:
                issue_loads(ch + 1)
            ropeA, ropeC, t_rope = ropeA_l[xi], ropeC_l[xi], t_rope_l[xi]
            ln_stats(K, xc[xi], t_xc[xi], st6, mv, rstd, t_st)
            mk.op('dve', lambda xi=xi: nc.vector.scalar_tensor_tensor(out=xn[:], in0=xc[xi][:], scalar=mv[:, 0:1], in1=modv[:, 1, :],
                                                                      op0=ALU.subtract, op1=ALU.mult), r=[t_xc[xi], t_st, t_mod], w=[t_xn])
            mk.op('dve', lambda: nc.vector.scalar_tensor_tensor(out=hn[:], in0=xn[:], scalar=rstd[:, 0:1], in1=modv[:, 0, :],
                                                                op0=ALU.mult, op1=ALU.add), r=[t_xn, t_st, t_mod], w=[t_hn])
            for kc in range(8):
                mk.op('pe', lambda kc=kc: nc.tensor.transpose(out=B0[:, kc, :], in_=hn[:, kc * 128:(kc + 1) * 128], identity=K.identb[:]),
                      r=[t_hn, K.t_const], w=[tB[0]])
            mk.op('act', lambda: nc.scalar.copy(out=hT[:], in_=B0[:]), r=[tB[0]], w=[t_hT])
            for (Bk, tb, c0, c1) in ((B1, tB[1], 0, 512), (B3, tB[3], 512, 640), (B2, tB[2], 1408, 1824)):
                for kc in range(8):
                    mk.op('pe', lambda kc=kc, Bk=Bk, c0=c0, c1=c1: nc.tensor.matmul(Bk[:, 0:c1 - c0], lhsT=hT[:, kc, :], rhs=win[:, kc, c0:c1],
                                                                                   start=(kc == 0), stop=(kc == 7)), r=[t_hT, t_w], w=[tb])
            for c in range(6):
                Bk, tb, ci = (B4, tB[4], c) if c < 4 else (B5, tB[5], c - 4)
                for kc in range(8):
                    mk.op('pe', lambda kc=kc, Bk=Bk, ci=ci, c=c: nc.tensor.matmul(Bk[:, ci, :], lhsT=win[:, kc, 640 + c * 128:640 + (c + 1) * 128],
                                                                                 rhs=hT[:, kc, :], start=(kc == 0), stop=(kc == 7)), r=[t_hT, t_w], w=[tb])
            mk.op('act', lambda: nc.scalar.copy(out=ptA[:].rearrange("p h d -> p (h d)"), in_=B1[:]), r=[tB[1]], w=[t_ptA])
            mk.op('act', lambda: nc.scalar.copy(out=ptC[:], in_=B2[:, 0:416]), r=[tB[2]], w=[t_ptC])
            mk.op('act', lambda: nc.scalar.copy(out=vab[:], in_=B3[:, 0:128]), r=[tB[3]], w=[t_vab])
            mk.op('act', lambda: nc.scalar.copy(out=pbs[:, 0:4, :], in_=B4[:]), r=[tB[4]], w=[t_pbs])
            mk.op('act', lambda: nc.scalar.copy(out=pbs[:, 4:6, :], in_=B5[:, 0:2, :]), r=[tB[5]], w=[t_pbs])
            mk.dma('sp', S.PBT[l].rearrange("(c p) t -> p c t", p=128)[:, :, tok], pbs[:], r=[t_pbs], w=[S.t_PBT], dsem=dout)
            mk.dma('sp', S.Va[l][tok, :], vab[:], r=[t_vab], w=[S.t_A], dsem=dout)
            mk.op('dve', lambda: nc.vector.tensor_mul(out=sq[:], in0=ptA[:].rearrange("p h d -> p (h d)"), in1=ptA[:].rearrange("p h d -> p (h d)")),
                  r=[t_ptA], w=[t_sq])
            mk.op('dve', lambda: nc.vector.reduce_sum(out=ss[:, 0:8], in_=sq[:].rearrange("p (h d) -> p h d", d=64), axis=mybir.AxisListType.X),
                  r=[t_sq], w=[t_ss])
            mk.op('act', lambda: nc.scalar.activation(out=ss[:, 0:8], in_=ss[:, 0:8], func=AF.Sqrt, bias=K.eps_t[:, 0:1], scale=1.0 / 64), r=[t_ss], w=[t_ss])
            mk.op('dve', lambda: nc.vector.reciprocal(out=ss[:, 0:8], in_=ss[:, 0:8]), r=[t_ss], w=[t_ss])
            mk.op('dve', lambda: nc.vector.tensor_mul(out=ptA[:], in0=ptA[:], in1=ss[:, 0:8].unsqueeze(2).to_broadcast([128, 8, 64])), r=[t_ptA, t_ss], w=[t_ptA])
            if lat:
                mk.op('pool', lambda: nc.gpsimd.tensor_mul(out=ptA[:], in0=ptA[:], in1=gA[:]), r=[t_ptA, t_g], w=[t_ptA])
                cosb = ropeA[:, 0:1, :].to_broadcast([128, 8, 32])
                sinb = ropeA[:, 1:2, :].to_broadcast([128, 8, 32])
                mk.op('dve', lambda: nc.vector.tensor_mul(out=r1[:], in0=ptA[:, :, 0:32], in1=cosb), r=[t_ptA, t_rope], w=[t_r])
                mk.op('dve', lambda: nc.vector.tensor_mul(out=r2[:], in0=ptA[:, :, 32:64], in1=sinb), r=[t_ptA, t_rope], w=[t_r])
                mk.op('dve', lambda: nc.vector.tensor_sub(out=qkb[:, :, 0:32], in0=r1[:], in1=r2[:]), r=[t_r], w=[t_qkb])
                mk.op('dve', lambda: nc.vector.tensor_mul(out=r1[:], in0=ptA[:, :, 32:64], in1=cosb), r=[t_ptA, t_rope], w=[t_r])
                mk.op('dve', lambda: nc.vector.tensor_mul(out=r2[:], in0=ptA[:, :, 0:32], in1=sinb), r=[t_ptA, t_rope], w=[t_r])
                mk.op('dve', lambda: nc.vector.tensor_add(out=qkb[:, :, 32:64], in0=r1[:], in1=r2[:]), r=[t_r], w=[t_qkb])
            else:
                mk.op('pool', lambda: nc.gpsimd.tensor_mul(out=qkb[:], in0=ptA[:], in1=gA[:]), r=[t_ptA, t_g], w=[t_qkb])
            for pr in range(3):
                mk.op('pe', lambda pr=pr: nc.tensor.transpose(out=B6[:, pr, :], in_=qkb[:, 2 * pr:2 * pr + 2, :].rearrange("p a d -> p (a d)"),
                                                              identity=K.identb[:]), r=[t_qkb, K.t_const], w=[tB[6]])
            mk.op('pe', lambda: nc.tensor.transpose(out=B6[:, 3, :], in_=qkb[:, 6:8, :].rearrange("p a d -> p (a d)"), identity=K.identb[:]),
                  r=[t_qkb, K.t_const], w=[tB[6]])
            mk.op('act', lambda: nc.scalar.copy(out=qkT[:], in_=B6[:, 0:4, :]), r=[tB[6]], w=[t_qkT])
            for pr in range(3):
                mk.dma('sp', S.QTa[l][pr][:, tok], qkT[0:64, pr, :], r=[t_qkT], w=[S.t_A], dsem=dout)
                mk.dma('sp', S.QTa[l][pr + 3][:, tok], qkT[64:128, pr, :], r=[t_qkT], w=[S.t_A], dsem=dout)
            mk.dma('sp', S.KTa[l].rearrange("h d t -> (h d) t")[:, tok], qkT[:, 3, :], r=[t_qkT], w=[S.t_A], dsem=dout)
            mk.op('dve', lambda: nc.vector.tensor_mul(out=sq[:, 0:384], in0=ptC[:, 0:384], in1=ptC[:, 0:384]), r=[t_ptC], w=[t_sq])
            mk.op('dve', lambda: nc.vector.reduce_sum(out=ss[:, 8:9], in_=sq[:, 0:256], axis=mybir.AxisListType.X), r=[t_sq], w=[t_ss])
            mk.op('dve', lambda: nc.vector.reduce_sum(out=ss[:, 9:10], in_=sq[:, 256:384], axis=mybir.AxisListType.X), r=[t_sq], w=[t_ss])
            mk.op('act', lambda: nc.scalar.activation(out=ss[:, 8:9], in_=ss[:, 8:9], func=AF.Sqrt, bias=K.eps_t[:, 0:1], scale=1.0 / 256), r=[t_ss], w=[t_ss])
            mk.op('act', lambda: nc.scalar.activation(out=ss[:, 9:10], in_=ss[:, 9:10], func=AF.Sqrt, bias=K.eps_t[:, 0:1], scale=1.0 / 128), r=[t_ss], w=[t_ss])
            mk.op('dve', lambda: nc.vector.reciprocal(out=ss[:, 8:10], in_=ss[:, 8:10]), r=[t_ss], w=[t_ss])
            mk.op('dve', lambda: nc.vector.scalar_tensor_tensor(out=cn[:, 0:256], in0=ptC[:, 0:256], scalar=ss[:, 8:9], in1=gq[:], op0=ALU.mult, op1=ALU.mult),
                  r=[t_ptC, t_ss, t_g], w=[t_cn])
            mk.op('dve', lambda: nc.vector.scalar_tensor_tensor(out=cn[:, 256:384], in0=ptC[:, 256:384], scalar=ss[:, 9:10], in1=gkv[:], op0=ALU.mult, op1=ALU.mult),
                  r=[t_ptC, t_ss, t_g], w=[t_cn])
            for c in range(3):
                mk.op('pe', lambda c=c: nc.tensor.transpose(out=B6[:, 4 + c, :], in_=cn[:, c * 128:(c + 1) * 128], identity=K.identb[:]),
                      r=[t_cn, K.t_const], w=[tB[6]])
            mk.op('act', lambda: nc.scalar.copy(out=cT[:], in_=B6[:, 4:7, :]), r=[tB[6]], w=[t_cT])
            for kc in range(2):
                mk.op('pe', lambda kc=kc: nc.tensor.matmul(B7[:, 0:480], lhsT=cT[:, kc, :], rhs=wuq[:, kc, 0:480], start=(kc == 0), stop=(kc == 1)),
                      r=[t_cT, t_w], w=[tB[7]])
            for kc in range(2):
                mk.op('pe', lambda kc=kc: nc.tensor.matmul(B3[:, 128:224], lhsT=cT[:, kc, :], rhs=wuq[:, kc, 480:576], start=(kc == 0), stop=(kc == 1)),
                      r=[t_cT, t_w], w=[tB[3]])
            mk.op('pe', lambda: nc.tensor.matmul(B1[:], lhsT=cT[:, 2, :], rhs=wukv[:, 0:512], start=True, stop=True), r=[t_cT, t_w], w=[tB[1]])
            mk.op('pe', lambda: nc.tensor.matmul(B2[:, 0:256], lhsT=cT[:, 2, :], rhs=wukv[:, 512:768], start=True, stop=True), r=[t_cT, t_w], w=[tB[2]])
            qs = 96.0 ** -0.5
            mk.op('act', lambda: nc.scalar.mul(out=qc[:, 0:5, :].rearrange("p h d -> p (h d)"), in_=B7[:, 0:480], mul=qs), r=[tB[7]], w=[t_qc])
            mk.op('act', lambda: nc.scalar.mul(out=qc[:, 5, :], in_=B3[:, 128:224], mul=qs), r=[tB[3]], w=[t_qc])
            mk.op('act', lambda: nc.scalar.copy(out=kvc[:, 0:4, :].rearrange("p h d -> p (h d)"), in_=B1[:]), r=[tB[1]], w=[t_kvc])
            mk.op('act', lambda: nc.scalar.copy(out=kvc[:, 4:6, :].rearrange("p h d -> p (h d)"), in_=B2[:, 0:256]), r=[tB[2]], w=[t_kvc])
            mk.op('pool', lambda: nc.gpsimd.tensor_copy(out=qcb[:, :, 0:64], in_=qc[:, :, 0:64]), r=[t_qc], w=[t_qcb])
            mk.op('pool', lambda: nc.gpsimd.tensor_copy(out=kcb[:, :, 0:64], in_=kvc[:, :, 0:64]), r=[t_kvc], w=[t_kcb])
            mk.op('pool', lambda: nc.gpsimd.tensor_copy(out=vcb[:], in_=kvc[:, :, 64:128]), r=[t_kvc], w=[t_vcb])
            if lat:
                cosb = ropeC[:, 0:1, :].to_broadcast([128, 6, 16])
                sinb = ropeC[:, 1:2, :].to_broadcast([128, 6, 16])
                mk.op('dve', lambda: nc.vector.tensor_mul(out=r1[:, 0:6, 0:16], in0=qc[:, :, 64:80], in1=cosb), r=[t_qc, t_rope], w=[t_r])
                mk.op('dve', lambda: nc.vector.tensor_mul(out=r2[:, 0:6, 0:16], in0=qc[:, :, 80:96], in1=sinb), r=[t_qc, t_rope], w=[t_r])
                mk.op('dve', lambda: nc.vector.tensor_sub(out=qcb[:, :, 64:80], in0=r1[:, 0:6, 0:16], in1=r2[:, 0:6, 0:16]), r=[t_r], w=[t_qcb])
                mk.op('dve', lambda: nc.vector.tensor_mul(out=r1[:, 0:6, 0:16], in0=qc[:, :, 80:96], in1=cosb), r=[t_qc, t_rope], w=[t_r])
                mk.op('dve', lambda: nc.vector.tensor_mul(out=r2[:, 0:6, 0:16], in0=qc[:, :, 64:80], in1=sinb), r=[t_qc, t_rope], w=[t_r])
                mk.op('dve', lambda: nc.vector.tensor_add(out=qcb[:, :, 80:96], in0=r1[:, 0:6, 0:16], in1=r2[:, 0:6, 0:16]), r=[t_r], w=[t_qcb])
                mk.op('dve', lambda: nc.vector.tensor_mul(out=r1[:, 0, 0:16], in0=ptC[:, 384:400], in1=ropeC[:, 0, :]), r=[t_ptC, t_rope], w=[t_r])
                mk.op('dve', lambda: nc.vector.tensor_mul(out=r2[:, 0, 0:16], in0=ptC[:, 400:416], in1=ropeC[:, 1, :]), r=[t_ptC, t_rope], w=[t_r])
                mk.op('dve', lambda: nc.vector.tensor_sub(out=kr[:, 0:16], in0=r1[:, 0, 0:16], in1=r2[:, 0, 0:16]), r=[t_r], w=[t_kr])
                mk.op('dve', lambda: nc.vector.tensor_mul(out=r1[:, 0, 0:16], in0=ptC[:, 400:416], in1=ropeC[:, 0, :]), r=[t_ptC, t_rope], w=[t_r])
                mk.op('dve', lambda: nc.vector.tensor_mul(out=r2[:, 0, 0:16], in0=ptC[:, 384:400], in1=ropeC[:, 1, :]), r=[t_ptC, t_rope], w=[t_r])
                mk.op('dve', lambda: nc.vector.tensor_add(out=kr[:, 16:32], in0=r1[:, 0, 0:16], in1=r2[:, 0, 0:16]), r=[t_r], w=[t_kr])
            else:
                mk.op('pool', lambda: nc.gpsimd.tensor_copy(out=qcb[:, :, 64:96], in_=qc[:, :, 64:96]), r=[t_qc], w=[t_qcb])
                mk.op('pool', lambda: nc.gpsimd.tensor_copy(out=kr[:], in_=ptC[:, 384:416]), r=[t_ptC], w=[t_kr])
            mk.op('pool', lambda: nc.gpsimd.tensor_copy(out=kcb[:, :, 64:96], in_=kr[:].unsqueeze(1).to_broadcast([128, 6, 32])), r=[t_kr], w=[t_kcb])
            mk.dma('sp', S.Vc[l][tok, :], vcb[:].rearrange("p h d -> p (h d)"), r=[t_vcb], w=[S.t_C], dsem=dout)
            for h in range(6):
                mk.op('pe', lambda h=h: nc.tensor.transpose(out=B0[0:96, h, :], in_=qcb[:, h, :], identity=K.identb[:]), r=[t_qcb, K.t_const], w=[tB[0]])
            mk.op('act', lambda: nc.scalar.copy(out=qcT[0:96, :, :], in_=B0[0:96, 0:6, :]), r=[tB[0]], w=[t_qcT])
            for h in range(6):
                mk.op('pe', lambda h=h: nc.tensor.transpose(out=B6[0:96, h, :], in_=kcb[:, h, :], identity=K.identb[:]), r=[t_kcb, K.t_const], w=[tB[6]])
            mk.op('act', lambda: nc.scalar.copy(out=kcT[0:96, :, :], in_=B6[0:96, 0:6, :]), r=[tB[6]], w=[t_kcT])
            mk.dma('sp', S.QTc[l].rearrange("h d t -> d h t")[:, :, tok], qcT[0:96, :, :], r=[t_qcT], w=[S.t_C], dsem=dout)
            mk.dma('sp', S.KTc[l].rearrange("h d t -> d h t")[:, :, tok], kcT[0:96, :, :], r=[t_kcT], w=[S.t_C], dsem=dout)
        mk.barrier()


def phase_attn(K, l, need_ctx):
    nc, mk = K.nc, K.mk
    S = K.S
    heads = []
    for h in range(6):
        heads.append((64, S.KTa[l][h // 3], S.Va[l][:, (h // 3) * 64:(h // 3 + 1) * 64], S.QTa[l][h], h * 64, S.t_A))
    for h in range(6):
        heads.append((96, S.KTc[l][h], S.Vc[l][:, h * 64:(h + 1) * 64], S.QTc[l][h], 640 + h * 64, S.t_C))
    with ExitStack() as st:
        KT = [sb(nc, st, "a_KT%d" % i, [128, NT], BF16) for i in range(2)]
        QT = [sb(nc, st, "a_QT%d" % i, [128, NT], BF16) for i in range(2)]
        Vx = [sb(nc, st, "a_V%d" % i, [128, NCH, 128], BF16) for i in range(2)]
        PT = [sb(nc, st, "a_PT%d" % i, [128, 2, 512], BF16) for i in range(3)]
        rden = sb(nc, st, "a_rden", [64, 512], F32)
        ob = [sb(nc, st, "a_ob%d" % i, [64, 512], BF16) for i in range(2)]
        ST = [ps(nc, st, "a_ST%d" % i, [128, 2, 512], F32) for i in range(3)]
        OD = [ps(nc, st, "a_OD%d" % i, [128, 512], F32) for i in range(2)]
        t_KV = [Tl(), Tl()]
        t_rden = Tl()
        t_PT = [Tl() for _ in range(3)]
        t_ob = [Tl(), Tl()]
        t_ST = [Tl() for _ in range(3)]
        t_OD = [Tl(), Tl()]
        dl = [mk.new_dsem(), mk.new_dsem()]
        dout = mk.new_dsem()
        for i in range(2):
            mk.op('pool', lambda i=i: nc.gpsimd.memset(Vx[i][:, :, 64:128], 1.0), w=[t_KV[i]])
            mk.op('pool', lambda i=i: nc.gpsimd.memset(KT[i][:], 0.0), w=[t_KV[i]])
            mk.op('pool', lambda i=i: nc.gpsimd.memset(QT[i][:], 0.0), w=[t_KV[i]])
        si = 0
        pi = 0
        bi = 0
        for hi, (dk, KTd, Vd, QTd, row0, t_src) in enumerate(heads):
            b = hi % 2
            mk.dma('sp', KT[b][0:dk, :], KTd, r=[t_src], w=[t_KV[b]], dsem=dl[b])
            mk.dma('sp', QT[b][0:dk, :], QTd, r=[t_src], w=[t_KV[b]], dsem=dl[b])
            mk.dma('sp', Vx[b][:, :, 0:64], Vd.rearrange("(c p) d -> p c d", p=128), r=[t_src], w=[t_KV[b]], dsem=dl[b])
            blocks = [(CTX + qb * 512, 512, 0, NCH) for qb in range(8)]
            if need_ctx:
                blocks.append((0, 256, 0, 2))
            for (q0, qn, k0, k1) in blocks:
                o = bi % 2
                bi += 1
                LOOK = 2
                slots = {}
                npair = (k1 - k0) // 2

                def issue_s(kp):
                    nonlocal si, pi
                    s_ = si % 3
                    si += 1
                    p_ = pi % 3
                    pi += 1
                    slots[kp] = (s_, p_)
                    for j in range(2):
                        kc = k0 + 2 * kp + j
                        mk.op('pe', lambda kc=kc, j=j, s_=s_, b=b, q0=q0, qn=qn: nc.tensor.matmul(
                            ST[s_][:, j, 0:qn], lhsT=KT[b][:, kc * 128:(kc + 1) * 128], rhs=QT[b][:, q0:q0 + qn], start=True, stop=True),
                            r=[t_KV[b]], w=[t_ST[s_]])
                    mk.op('act', lambda s_=s_, p_=p_, qn=qn: nc.scalar.activation(out=PT[p_][:, :, 0:qn], in_=ST[s_][:, :, 0:qn], func=AF.Exp),
                          r=[t_ST[s_]], w=[t_PT[p_]])

                for kp in range(min(npair, LOOK)):
                    issue_s(kp)
                for kp in range(npair):
                    if kp + LOOK < npair:
                        issue_s(kp + LOOK)
                    s_, p_ = slots.pop(kp)
                    for j in range(2):
                        kc = k0 + 2 * kp + j
                        mk.op('pe', lambda kc=kc, j=j, p_=p_, o=o, b=b, qn=qn, k0=k0, k1=k1: nc.tensor.matmul(
                            OD[o][:, 0:qn], lhsT=Vx[b][:, kc, :], rhs=PT[p_][:, j, 0:qn], start=(kc == k0), stop=(kc == k1 - 1)),
                            r=[t_KV[b], t_PT[p_]], w=[t_OD[o]])
                mk.op('dve', lambda o=o, qn=qn: nc.vector.reciprocal(out=rden[:, 0:qn], in_=OD[o][64:128, 0:qn]), r=[t_OD[o]], w=[t_rden])
                mk.op('dve', lambda o=o, qn=qn: nc.vector.tensor_mul(out=ob[o][:, 0:qn], in0=OD[o][0:64, 0:qn], in1=rden[:, 0:qn]),
                      r=[t_OD[o], t_rden], w=[t_ob[o]])
                mk.dma('sp', S.CT[l][row0:row0 + 64, q0:q0 + qn], ob[o][:, 0:qn], r=[t_ob[o]], w=[S.t_CT], dsem=dout)
        mk.barrier()


def hyena_filter_dev(K, l, st, n, zT_d, tcol0, hT, rl1):
    nc, mk = K.nc, K.mk
    TWO_PI = 2.0 * math.pi
    with ExitStack() as fs:
        zT = sb(nc, fs, "h_zT", [33, n], F32)
        tb = sb(nc, fs, "h_tb", [128, n], F32)
        w1 = sb(nc, fs, "h_w1", [33, 64], F32)
        w23 = sb(nc, fs, "h_w23", [64, 2, 64], F32)
        w4 = sb(nc, fs, "h_w4", [64, 512], F32)
        a_all = [sb(nc, fs, "h_a%d" % i, [64, 512], F32) for i in range(4)]
        wn_all = [sb(nc, fs, "h_wn%d" % i, [128, 512], F32) for i in range(2)]
        ki_all = [sb(nc, fs, "h_ki%d" % i, [64, 512], mybir.dt.int32) for i in range(2)]
        kf_all = [sb(nc, fs, "h_kf%d" % i, [64, 512], F32) for i in range(2)]
        t_ki_all, t_kf_all = [Tl(), Tl()], [Tl(), Tl()]
        t_a_all = [Tl() for _ in range(4)]
        t_wn_all = [Tl(), Tl()]
        fb = sb(nc, fs, "h_fb", [64, 3], F32)
        l1 = sb(nc, fs, "h_l1", [128, 4], F32)
        pf_all = [ps(nc, fs, "h_pf%d" % i, [64, 512], F32) for i in range(2)]
        p4 = [ps(nc, fs, "h_p4%d" % i, [128, 512], F32) for i in range(2)]
        t_in, t_fb, t_l1 = Tl(), Tl(), Tl()
        t_pf_all = [Tl(), Tl()]
        t_p4 = [Tl(), Tl()]
        d = mk.new_dsem()
        mk.dma('sp', zT[:], zT_d, w=[t_in], dsem=d)
        mk.dma('sp', tb[:], K.hy_tdel[:, tcol0:tcol0 + n], w=[t_in], dsem=d)
        mk.dma('sp', w1[:], K.hy_w1[l], w=[t_in], dsem=d)
        mk.dma('sp', w23[:, 0, :], K.hy_w2[l], w=[t_in], dsem=d)
        mk.dma('sp', w23[:, 1, :], K.hy_w3[l], w=[t_in], dsem=d)
        mk.dma('sp', w4[:], K.hy_w4[l], w=[t_in], dsem=d)
        hs = K.hs
        mk.op('dve', lambda: nc.vector.tensor_mul(out=fb[:], in0=hs[0:64, 27:30], in1=hs[0:64, 26:27].to_broadcast([64, 3])), r=[K.t_hs], w=[t_fb])
        nb = (n + 511) // 512
        for blk in range(nb):
            c0 = blk * 512
            cn_ = min(512, n - c0)
            ai = 0
            bb = blk % 2
            a = a_all[bb * 2:bb * 2 + 2]
            t_a = t_a_all[bb * 2:bb * 2 + 2]
            ki, kf, t_ki, t_kf = ki_all[bb], kf_all[bb], t_ki_all[bb], t_kf_all[bb]
            pf = pf_all[bb * 2:bb * 2 + 2] if len(pf_all) == 4 else pf_all
            t_pf = t_pf_all[bb * 2:bb * 2 + 2] if len(pf_all) == 4 else t_pf_all
            for layer in range(3):
                pfi = layer % 2
                if layer == 0:
                    mk.op('pe', lambda pfi=pfi, c0=c0, cn_=cn_: nc.tensor.matmul(pf[pfi][:, 0:cn_], lhsT=w1[:], rhs=zT[:, c0:c0 + cn_], start=True, stop=True),
                          r=[t_in], w=[t_pf[pfi]])
                else:
                    mk.op('pe', lambda pfi=pfi, layer=layer, cn_=cn_, ai=ai: nc.tensor.matmul(pf[pfi][:, 0:cn_], lhsT=w23[:, layer - 1, :], rhs=a[ai][:, 0:cn_],
                                                                                       start=True, stop=True), r=[t_in, t_a[ai]], w=[t_pf[pfi]])
                    ai = 1 - ai
                mk.op('dve', lambda pfi=pfi, ai=ai, layer=layer, cn_=cn_: nc.vector.tensor_scalar(
                    out=a[ai][:, 0:cn_], in0=pf[pfi][:, 0:cn_], scalar1=hs[0:64, 26:27], scalar2=fb[:, layer:layer + 1], op0=ALU.mult, op1=ALU.add),
                    r=[t_pf[pfi], t_fb, K.t_hs], w=[t_a[ai]])
                mk.op('dve', lambda ai=ai, cn_=cn_: nc.vector.tensor_scalar(out=ki[:, 0:cn_], in0=a[ai][:, 0:cn_], scalar1=1.0 / TWO_PI, scalar2=None, op0=ALU.mult),
                      r=[t_a[ai]], w=[t_ki])
                mk.op('dve', lambda cn_=cn_: nc.vector.tensor_copy(out=kf[:, 0:cn_], in_=ki[:, 0:cn_]), r=[t_ki], w=[t_kf])
                mk.op('dve', lambda ai=ai, cn_=cn_: nc.vector.scalar_tensor_tensor(out=a[ai][:, 0:cn_], in0=kf[:, 0:cn_], scalar=-TWO_PI, in1=a[ai][:, 0:cn_],
                                                                                  op0=ALU.mult, op1=ALU.add), r=[t_kf, t_a[ai]], w=[t_a[ai]])
                mk.op('act', lambda ai=ai, cn_=cn_: nc.scalar.activation(out=kf[:, 0:cn_], in_=a[ai][:, 0:cn_], func=AF.Sign), r=[t_a[ai]], w=[t_kf])
                mk.op('dve', lambda ai=ai, cn_=cn_: nc.vector.scalar_tensor_tensor(out=a[ai][:, 0:cn_], in0=kf[:, 0:cn_], scalar=-math.pi, in1=a[ai][:, 0:cn_],
                                                                                  op0=ALU.mult, op1=ALU.add), r=[t_kf, t_a[ai]], w=[t_a[ai]])
                mk.op('act', lambda ai=ai, cn_=cn_: nc.scalar.activation(out=a[ai][:, 0:cn_], in_=a[ai][:, 0:cn_], func=AF.Sin, scale=-1.0),
                      r=[t_a[ai]], w=[t_a[ai]])
            for c in range(4):
                pi_ = c % 2
                mk.op('pe', lambda c=c, pi_=pi_, ai=ai, cn_=cn_: nc.tensor.matmul(p4[pi_][:, 0:cn_], lhsT=w4[:, c * 128:(c + 1) * 128], rhs=a[ai][:, 0:cn_], start=True, stop=True),
                      r=[t_in, t_a[ai]], w=[t_p4[pi_]])
                wn, t_wn = wn_all[c % 2], t_wn_all[c % 2]
                mk.op('act', lambda c=c, c0=c0, cn_=cn_, wn=wn: nc.scalar.activation(out=wn[:, 0:cn_], in_=tb[:, c0:c0 + cn_], func=AF.Exp, scale=K.negdel[:, c:c + 1]),
                      r=[t_in, K.t_const], w=[t_wn])
                mk.op('dve', lambda c=c, pi_=pi_, c0=c0, cn_=cn_, wn=wn: nc.vector.tensor_mul(out=hT[:, c, c0:c0 + cn_], in0=p4[pi_][:, 0:cn_], in1=wn[:, 0:cn_]),
                      r=[t_p4[pi_], t_wn], w=[K.t_hT])
        for c in range(4):
            lo = 0 if c < 2 else 1
            mk.op('dve', lambda c=c, lo=lo: nc.vector.tensor_reduce(out=l1[:, c:c + 1], in_=hT[:, c, lo:n], axis=mybir.AxisListType.X, op=ALU.add,
                                                                    apply_absolute_value=True), r=[K.t_hT], w=[t_l1])
        mk.op('dve', lambda: nc.vector.tensor_add(out=rl1[:], in0=l1[:, 0:2], in1=l1[:, 2:4]), r=[t_l1], w=[K.t_rl1])
        mk.op('dve', lambda: nc.vector.reciprocal(out=rl1[:], in_=rl1[:]), r=[K.t_rl1], w=[K.t_rl1])
        mk.barrier()


def hyena_conv_dev(K, l, st, n, off, hT, rl1):
    nc, mk = K.nc, K.mk
    S = K.S
    hs = K.hs
    with ExitStack() as cs:
        pp = sb(nc, cs, "h_pp", [128, n + 2], F32)
        u = sb(nc, cs, "h_u", [128, n], F32)
        x1 = sb(nc, cs, "h_x1", [128, n], F32)
        x0 = sb(nc, cs, "h_x0", [128, n], F32)
        ya = [sb(nc, cs, "h_y%d" % i, [128, n], F32) for i in range(2)]
        ob = sb(nc, cs, "h_ob", [128, n], BF16)
        t_pp, t_u, t_x1, t_x0, t_ob = Tl(), Tl(), Tl(), Tl(), Tl()
        t_y = [Tl(), Tl()]
        d = mk.new_dsem()
        dout = mk.new_dsem()
        mk.op('pool', lambda: nc.gpsimd.memset(pp[:, 0:1], 0.0), w=[t_pp])
        mk.op('pool', lambda: nc.gpsimd.memset(pp[:, n + 1:n + 2], 0.0), w=[t_pp])
        for cc in range(2):
            for which, dst, t_dst in ((0, u, t_u), (1, x1, t_x1), (2, x0, t_x0)):
                chn = which * 2 + cc
                mk.dma('sp', pp[:, 1:n + 1], S.PBT[l][chn * 128:(chn + 1) * 128, off:off + n], r=[S.t_PBT], w=[t_pp], dsem=d)
                mk.op('dve', lambda chn=chn, dst=dst: nc.vector.tensor_scalar(out=dst[:], in0=pp[:, 0:n], scalar1=hs[:, chn:chn + 1], scalar2=hs[:, 18 + chn:19 + chn],
                                                                            op0=ALU.mult, op1=ALU.add), r=[t_pp, K.t_hs], w=[t_dst])
                for k in (1, 2):
                    mk.op('dve', lambda chn=chn, dst=dst, k=k: nc.vector.scalar_tensor_tensor(out=dst[:], in0=pp[:, k:n + k], scalar=hs[:, k * 6 + chn:k * 6 + chn + 1],
                                                                                              in1=dst[:], op0=ALU.mult, op1=ALU.add), r=[t_pp, K.t_hs, t_dst], w=[t_dst])
            mk.op('dve', lambda: nc.vector.tensor_mul(out=u[:], in0=u[:], in1=x1[:]), r=[t_u, t_x1], w=[t_u])
            mk.op('pool', lambda: nc.gpsimd.memset(ya[0][:], 0.0), w=[t_y[0]])
            mk.op('pool', lambda: nc.gpsimd.memset(ya[1][:], 0.0), w=[t_y[1]])
            k = 0
            for lag in range(n):
                i = k % 2
                k += 1
                mk.op('dve', lambda lag=lag, i=i, cc=cc: nc.vector.scalar_tensor_tensor(out=ya[i][:, lag:n], in0=u[:, 0:n - lag], scalar=hT[:, cc, lag:lag + 1],
                                                                                        in1=ya[i][:, lag:n], op0=ALU.mult, op1=ALU.add), r=[t_u, K.t_hT, t_y[i]], w=[t_y[i]])
            for lag in range(1, n):
                i = k % 2
                k += 1
                mk.op('dve', lambda lag=lag, i=i, cc=cc: nc.vector.scalar_tensor_tensor(out=ya[i][:, 0:n - lag], in0=u[:, lag:n], scalar=hT[:, 2 + cc, lag:lag + 1],
                                                                                        in1=ya[i][:, 0:n - lag], op0=ALU.mult, op1=ALU.add), r=[t_u, K.t_hT, t_y[i]], w=[t_y[i]])
            mk.op('dve', lambda: nc.vector.tensor_add(out=ya[0][:], in0=ya[0][:], in1=ya[1][:]), r=[t_y[0], t_y[1]], w=[t_y[0]])
            mk.op('dve', lambda cc=cc: nc.vector.tensor_scalar_mul(out=ya[0][:], in0=ya[0][:], scalar1=rl1[:, cc:cc + 1]), r=[t_y[0], K.t_rl1], w=[t_y[0]])
            mk.op('dve', lambda cc=cc: nc.vector.scalar_tensor_tensor(out=ya[0][:], in0=u[:], scalar=hs[:, 24 + cc:25 + cc], in1=ya[0][:], op0=ALU.mult, op1=ALU.add),
                  r=[t_u, t_y[0], K.t_hs], w=[t_y[0]])
            mk.op('dve', lambda: nc.vector.tensor_mul(out=ob[:], in0=ya[0][:], in1=x0[:]), r=[t_y[0], t_x0], w=[t_ob])
            mk.dma('sp', S.CT[l][384 + cc * 128:384 + (cc + 1) * 128, off:off + n], ob[:], r=[t_ob], w=[S.t_CT], dsem=dout)
        mk.barrier()


def fft_setup(K, st):
    nc, mk = K.nc, K.mk
    F = Ctx()
    F.F1 = sb(nc, st, "x_F1", [64, 256], BF16)
    F.GB = [sb(nc, st, "x_GB%d" % i, [128, 24, 128], BF16) for i in range(2)]
    F.Us = sb(nc, st, "x_Us", [64, 128, 64], BF16)
    F.AC = sb(nc, st, "x_AC", [128, 16384], BF16)
    F.stg = sb(nc, st, "x_stg", [128, SEQ], BF16)
    F.t_stg = Tl()
    F.P1 = [ps(nc, st, "x_P1%d" % i, [128, 2, 256], F32) for i in range(2)]
    F.P2 = [ps(nc, st, "x_P2%d" % i, [128, 4, 2, 64], F32) for i in range(2)]
    F.t_tab, F.t_Us = Tl(), Tl()
    F.t_ACa, F.t_ACd, F.t_Yd, F.t_Yp = Tl(), Tl(), Tl(), Tl()
    F.t_GB = [Tl(), Tl()]
    F.t_P1 = [Tl(), Tl()]
    F.t_P2 = [Tl(), Tl()]
    F.d_tab = mk.new_dsem()
    F.d_gb = [mk.new_dsem(), mk.new_dsem()]
    F.d_us = mk.new_dsem()
    F.d_x = [mk.new_dsem() for _ in range(4)]
    F.d_st = mk.new_dsem()
    F.gbi = 0
    F.xi = 0
    F.p2i = 0
    mk.dma('pool', F.F1[:], K.fft_F1[:, :], w=[F.t_tab], dsem=F.d_tab)
    for i in range(2):
        mk.op('pool', lambda i=i: nc.gpsimd.memset(F.GB[i][:], 0.0), w=[F.t_GB[i]])
    return F


def fft_alloc_work(K, F, st, extra_p2):
    nc = K.nc
    F.xs = [sb(nc, st, "x_xs%d" % i, [128, 4, 2, 64], F32) for i in range(4)]
    F.hfs = [sb(nc, st, "x_hf%d" % i, [128, 4, 2, 64], F32) for i in range(4)]
    F.tmp = [sb(nc, st, "x_tmp%d" % i, [128, 4, 64], F32) for i in range(4)]
    F.t_xs = [Tl() for _ in range(4)]
    F.t_hfs = [Tl() for _ in range(4)]
    F.t_tmp = [Tl() for _ in range(4)]
    F.P2 = F.P2[:2] + [ps(nc, st, "x_P2x%d" % i, [128, 4, 2, 64], F32) for i in range(extra_p2)]
    F.t_P2 = F.t_P2[:2] + [Tl() for _ in range(extra_p2)]


def fft_setup_inv(K, F, st):
    nc, mk = K.nc, K.mk
    F.W3 = sb(nc, st, "x_W3", [128, 512], BF16)
    F.F4 = sb(nc, st, "x_F4", [128, 64, 2, 64], BF16)
    F.Ysb = sb(nc, st, "x_Y", [128, 2, 64, 128], BF16)
    mk.dma('pool', F.W3[:], K.fft_W3[:, :], w=[F.t_tab], dsem=F.d_tab)
    mk.dma('pool', F.F4[:].rearrange("p a b c -> p (a b c)"), K.fft_F4[:, :], w=[F.t_tab], dsem=F.d_tab)


def fft_fwd(K, F, src, t_src, consumer):
    nc, mk = K.nc, K.mk
    A = F.AC[:].rearrange("p (ri k1 pr) -> p ri k1 pr", ri=2, k1=128)
    srcv = src.rearrange("c (s1 s2) -> s1 c s2", s2=64)
    for cb in range(16):
        mk.dma('sp', F.Us[:, cb * 8:(cb + 1) * 8, :], srcv[:, cb * 8:(cb + 1) * 8, :], r=[t_src], w=[F.t_Us], dsem=F.d_us)
    for pp2 in range(32):
        b = pp2 % 2
        for j in range(2):
            pr = pp2 * 2 + j
            mk.op('pe', lambda b=b, j=j, pr=pr: nc.tensor.matmul(F.P1[b][:, j, :], lhsT=F.Us[:, 2 * pr:2 * pr + 2, :].rearrange("p c s -> p (c s)"),
                                                                rhs=F.F1[:], start=True, stop=True), r=[F.t_Us, F.t_tab], w=[F.t_P1[b]])
        eng = 'act' if pp2 % 2 == 0 else 'dve'
        outv = A[:, :, :, 2 * pp2:2 * pp2 + 2].rearrange("p ri k1 pr -> p pr ri k1")
        inv = F.P1[b][:].rearrange("p pr (ri k1) -> p pr ri k1", ri=2)
        if eng == 'act':
            mk.op('act', lambda outv=outv, inv=inv: nc.scalar.copy(out=outv, in_=inv), r=[F.t_P1[b]], w=[F.t_ACa])
        else:
            mk.op('dve', lambda outv=outv, inv=inv: nc.vector.tensor_copy(out=outv, in_=inv), r=[F.t_P1[b]], w=[F.t_ACd])
    for kc in range(16):
        gb = F.gbi % 2
        F.gbi += 1
        if not K.gb_cached[kc]:
            srcg = K.fft_G[:, kc * 8:(kc + 1) * 8, :, :].rearrange("p a b c -> p (a b) c")
            mk.dma('pool', F.GB[gb][0:64, :, 0:64], srcg, w=[F.t_GB[gb]], dsem=F.d_gb[gb])
            mk.dma('pool', F.GB[gb][64:128, :, 64:128], srcg, w=[F.t_GB[gb]], dsem=F.d_gb[gb])
            mk.dma('sp', K.S.GBd[kc], F.GB[gb][:], r=[F.t_GB[gb]], w=[K.S.t_GBd], dsem=F.d_st)
            K.gb_cached[kc] = True
        else:
            mk.dma('sp', F.GB[gb][:], K.S.GBd[kc], r=[K.S.t_GBd], w=[F.t_GB[gb]], dsem=F.d_gb[gb])
        for half in range(2):
            pb = F.p2i % len(F.P2)
            F.p2i += 1
            for jj in range(4):
                kk = half * 4 + jj
                k1 = kc * 8 + kk
                Gre, Gim, nGim = F.GB[gb][:, kk * 3 + 0, :], F.GB[gb][:, kk * 3 + 1, :], F.GB[gb][:, kk * 3 + 2, :]
                Are, Aim = A[:, 0, k1, :], A[:, 1, k1, :]
                for (ri, l0, r0, l1, r1) in ((0, Gre, Are, nGim, Aim), (1, Gim, Are, Gre, Aim)):
                    mk.op('pe', lambda pb=pb, jj=jj, ri=ri, l0=l0, r0=r0: nc.tensor.matmul(F.P2[pb][:, jj, ri, :], lhsT=l0, rhs=r0, start=True, stop=False),
                          r=[F.t_GB[gb], F.t_ACa, F.t_ACd], w=[F.t_P2[pb]])
                    mk.op('pe', lambda pb=pb, jj=jj, ri=ri, l1=l1, r1=r1: nc.tensor.matmul(F.P2[pb][:, jj, ri, :], lhsT=l1, rhs=r1, start=False, stop=True),
                          r=[F.t_GB[gb], F.t_ACa, F.t_ACd], w=[F.t_P2[pb]])
            consumer(kc * 8 + half * 4, F.P2[pb], F.t_P2[pb])


def fft_inv(K, F, ysb, t_ysb, t_ysb_a):
    nc, mk = K.nc, K.mk
    C = F.AC[:].rearrange("p (ri t2 c) -> p ri t2 c", ri=2, t2=64)
    P3, tP3 = F.P1, F.t_P1
    P4 = [F.P2[i][:].rearrange("p a b c -> p (a b) c") for i in range(2)]
    tP4 = F.t_P2[0:2]
    for pp2 in range(32):
        b = pp2 % 2
        for j in range(2):
            pr = pp2 * 2 + j
            mk.op('pe', lambda b=b, j=j, pr=pr: nc.tensor.matmul(P3[b][:, j, :], lhsT=F.Ysb[:, 0, pr, :], rhs=F.W3[:, 0:256], start=True, stop=False),
                  r=[F.t_Yd, F.t_Yp, F.t_tab], w=[tP3[b]])
            mk.op('pe', lambda b=b, j=j, pr=pr: nc.tensor.matmul(P3[b][:, j, :], lhsT=F.Ysb[:, 1, pr, :], rhs=F.W3[:, 256:512], start=False, stop=True),
                  r=[F.t_Yd, F.t_Yp, F.t_tab], w=[tP3[b]])
        for j in range(2):
            outv = C[:, :, :, 4 * pp2 + 2 * j:4 * pp2 + 2 * j + 2].rearrange("p ri t2 cc -> p cc ri t2")
            inv = P3[b][:, j, :].rearrange("p (cc ri t2) -> p cc ri t2", cc=2, ri=2)
            if pp2 % 2 == 0:
                mk.op('act', lambda outv=outv, inv=inv: nc.scalar.copy(out=outv, in_=inv), r=[tP3[b]], w=[F.t_ACa])
            else:
                mk.op('dve', lambda outv=outv, inv=inv: nc.vector.tensor_copy(out=outv, in_=inv), r=[tP3[b]], w=[F.t_ACd])
    yv = ysb[:].rearrange("p (t1 t2) -> p t1 t2", t2=64)
    for t8 in range(8):
        b = t8 % 2
        for j in range(8):
            t2 = t8 * 8 + j
            mk.op('pe', lambda b=b, j=j, t2=t2: nc.tensor.matmul(P4[b][:, j, :], lhsT=C[:, 0, t2, :], rhs=F.F4[:, t2, 0, :], start=True, stop=False),
                  r=[F.t_ACa, F.t_ACd, F.t_tab], w=[tP4[b]])
            mk.op('pe', lambda b=b, j=j, t2=t2: nc.tensor.matmul(P4[b][:, j, :], lhsT=C[:, 1, t2, :], rhs=F.F4[:, t2, 1, :], start=False, stop=True),
                  r=[F.t_ACa, F.t_ACd, F.t_tab], w=[tP4[b]])
        outv = yv[:, :, t8 * 8:(t8 + 1) * 8].rearrange("p t1 t2 -> p t2 t1")
        if t8 % 2 == 0:
            mk.op('act', lambda outv=outv, b=b: nc.scalar.copy(out=outv, in_=P4[b]), r=[tP4[b]], w=[t_ysb_a])
        else:
            mk.op('dve', lambda outv=outv, b=b: nc.vector.tensor_copy(out=outv, in_=P4[b]), r=[tP4[b]], w=[t_ysb])


def hyena_lat_fft_filter(K, l, F, hT):
    nc, mk = K.nc, K.mk
    S = K.S
    n = SEQ
    stg, t_stg = F.stg, F.t_stg
    dst = mk.new_dsem()
    with ExitStack() as ws:
        fft_alloc_work(K, F, ws, 4)
        for c in range(4):
            mk.op('pool', lambda c=c: nc.gpsimd.tensor_copy(out=stg[:], in_=hT[:, c, :]), r=[K.t_hT], w=[t_stg])
            if c >= 2:
                mk.op('pool', lambda: nc.gpsimd.memset(stg[:, 0:1], 0.0), w=[t_stg])
            mk.dma('sp', S.HTd[c * 128:(c + 1) * 128, :], stg[:], r=[t_stg], w=[S.t_HTd], dsem=dst)
        for g in range(2):
            def cons_a(k1_0, P2, tP2, g=g):
                i = F.xi % 4
                F.xi += 1
                mk.op('act', lambda: nc.scalar.copy(out=F.xs[i][:], in_=P2[:]), r=[tP2], w=[F.t_xs[i]])
                mk.dma('pool', S.XF[g][:, k1_0:k1_0 + 4, :, :], F.xs[i][:], r=[F.t_xs[i]], w=[S.t_XF], dsem=F.d_st)

            def cons_b(k1_0, P2, tP2, g=g):
                i = F.xi % 4
                F.xi += 1
                mk.dma('sp', F.hfs[i][:], S.XF[g][:, k1_0:k1_0 + 4, :, :], r=[S.t_XF], w=[F.t_hfs[i]], dsem=F.d_x[i])
                mk.op('dve', lambda: nc.vector.tensor_add(out=F.xs[i][:, :, 0, :], in0=P2[:, :, 0, :], in1=F.hfs[i][:, :, 0, :]), r=[tP2, F.t_hfs[i]], w=[F.t_xs[i]])
                mk.op('dve', lambda: nc.vector.tensor_sub(out=F.xs[i][:, :, 1, :], in0=F.hfs[i][:, :, 1, :], in1=P2[:, :, 1, :]), r=[tP2, F.t_hfs[i]], w=[F.t_xs[i]])
                mk.dma('pool', S.HF[g][:, k1_0:k1_0 + 4, :, :], F.xs[i][:], r=[F.t_xs[i]], w=[S.t_HF], dsem=F.d_st)

            fft_fwd(K, F, S.HTd[g * 128:(g + 1) * 128, :], S.t_HTd, cons_a)
            fft_fwd(K, F, S.HTd[256 + g * 128:256 + (g + 1) * 128, :], S.t_HTd, cons_b)
        mk.barrier()


def hyena_lat_fft_u(K, l, F, fs, rl1):
    nc, mk = K.nc, K.mk
    S = K.S
    hs = K.hs
    n, off = SEQ, CTX
    stg, t_stg = F.stg, F.t_stg
    dst = mk.new_dsem()
    fft_alloc_work(K, F, fs, 4)
    if True:
        pp = sb(nc, fs, "h_pp", [128, n + 2], F32)
        u = sb(nc, fs, "h_u", [128, n], F32)
        x0 = sb(nc, fs, "h_x0", [128, n], F32)
        ysb = sb(nc, fs, "h_ysb", [128, n], F32)
        t_pp, t_u, t_x0, t_ysb, t_ysb_a = Tl(), Tl(), Tl(), Tl(), Tl()
        d = mk.new_dsem()
        dout = mk.new_dsem()
        mk.op('pool', lambda: nc.gpsimd.memset(pp[:, 0:1], 0.0), w=[t_pp])
        mk.op('pool', lambda: nc.gpsimd.memset(pp[:, n + 1:n + 2], 0.0), w=[t_pp])
        Yv = F.Ysb
        for cc in range(2):
            for which, dstt, t_dst in ((0, u, t_u), (1, ysb, t_ysb), (2, x0, t_x0)):
                chn = which * 2 + cc
                mk.dma('sp', pp[:, 1:n + 1], S.PBT[l][chn * 128:(chn + 1) * 128, off:off + n], r=[S.t_PBT], w=[t_pp], dsem=d)
                mk.op('dve', lambda chn=chn, dstt=dstt: nc.vector.tensor_scalar(out=dstt[:], in0=pp[:, 0:n], scalar1=hs[:, chn:chn + 1], scalar2=hs[:, 18 + chn:19 + chn],
                                                                              op0=ALU.mult, op1=ALU.add), r=[t_pp, K.t_hs], w=[t_dst])
                for k in (1, 2):
                    mk.op('dve', lambda chn=chn, dstt=dstt, k=k: nc.vector.scalar_tensor_tensor(out=dstt[:], in0=pp[:, k:n + k], scalar=hs[:, k * 6 + chn:k * 6 + chn + 1],
                                                                                                in1=dstt[:], op0=ALU.mult, op1=ALU.add), r=[t_pp, K.t_hs, t_dst], w=[t_dst])
            mk.op('dve', lambda: nc.vector.tensor_mul(out=u[:], in0=u[:], in1=ysb[:]), r=[t_u, t_ysb], w=[t_u])
            mk.op('pool', lambda: nc.gpsimd.tensor_copy(out=stg[:], in_=u[:]), r=[t_u], w=[t_stg])
            mk.dma('sp', S.UTd[cc * 128:(cc + 1) * 128, :], stg[:], r=[t_stg], w=[S.t_UTd], dsem=dst)

            def cons_c(k1_0, P2, tP2, cc=cc):
                i = F.xi % 4
                F.xi += 1
                mk.dma('sp', F.hfs[i][:], S.HF[cc][:, k1_0:k1_0 + 4, :, :], r=[S.t_HF], w=[F.t_hfs[i]], dsem=F.d_x[i])
                mk.op('act', lambda: nc.scalar.copy(out=F.xs[i][:], in_=P2[:]), r=[tP2], w=[F.t_xs[i]])
                Xre, Xim = F.xs[i][:, :, 0, :], F.xs[i][:, :, 1, :]
                Hre, Him = F.hfs[i][:, :, 0, :], F.hfs[i][:, :, 1, :]
                ta, tb, tc_, td = F.tmp
                yre = Yv[:, 0, :, k1_0:k1_0 + 4].rearrange("p pr k -> p k pr")
                yim = Yv[:, 1, :, k1_0:k1_0 + 4].rearrange("p pr k -> p k pr")
                mk.op('dve', lambda: nc.vector.tensor_mul(out=ta[:], in0=Xre, in1=Hre), r=[F.t_xs[i], F.t_hfs[i]], w=[F.t_tmp[0]])
                mk.op('dve', lambda: nc.vector.tensor_mul(out=tb[:], in0=Xim, in1=Him), r=[F.t_xs[i], F.t_hfs[i]], w=[F.t_tmp[1]])
                mk.op('dve', lambda: nc.vector.tensor_sub(out=yre, in0=ta[:], in1=tb[:]), r=[F.t_tmp[0], F.t_tmp[1]], w=[F.t_Yd])
                mk.op('pool', lambda: nc.gpsimd.tensor_mul(out=tc_[:], in0=Xre, in1=Him), r=[F.t_xs[i], F.t_hfs[i]], w=[F.t_tmp[2]])
                mk.op('pool', lambda: nc.gpsimd.tensor_mul(out=td[:], in0=Xim, in1=Hre), r=[F.t_xs[i], F.t_hfs[i]], w=[F.t_tmp[3]])
                mk.op('pool', lambda: nc.gpsimd.tensor_add(out=yim, in0=tc_[:], in1=td[:]), r=[F.t_tmp[2], F.t_tmp[3]], w=[F.t_Yp])

            fft_fwd(K, F, S.UTd[cc * 128:(cc + 1) * 128, :], S.t_UTd, cons_c)
            fft_inv(K, F, ysb, t_ysb, t_ysb_a)
            mk.op('dve', lambda cc=cc: nc.vector.tensor_scalar_mul(out=ysb[:], in0=ysb[:], scalar1=rl1[:, cc:cc + 1]), r=[t_ysb, t_ysb_a, K.t_rl1], w=[t_ysb, t_ysb_a])
            mk.op('dve', lambda cc=cc: nc.vector.scalar_tensor_tensor(out=ysb[:], in0=u[:], scalar=hs[:, 24 + cc:25 + cc], in1=ysb[:], op0=ALU.mult, op1=ALU.add),
                  r=[t_u, t_ysb, K.t_hs], w=[t_ysb])
            mk.op('dve', lambda: nc.vector.tensor_mul(out=stg[:], in0=ysb[:], in1=x0[:]), r=[t_ysb, t_x0], w=[t_stg])
            mk.dma('sp', S.CT[l][384 + cc * 128:384 + (cc + 1) * 128, off:off + n], stg[:], r=[t_stg], w=[S.t_CT], dsem=dout)
        mk.barrier()


def phase_hyena(K, l, need_ctx):
    nc, mk = K.nc, K.mk
    with ExitStack() as st:
        K.hs = sb(nc, st, "h_hs", [128, 32], F32)
        K.negpi = sb(nc, st, "h_negpi", [128, 1], F32)
        K.negdel = sb(nc, st, "h_negdel", [128, 4], F32)
        rl1 = sb(nc, st, "h_rl1", [128, 2], F32)
        K.t_hs, K.t_hT, K.t_rl1 = Tl(), Tl(), Tl()
        d = mk.new_dsem()
        mk.dma('sp', K.hs[:], K.hy_small[l], w=[K.t_hs], dsem=d)
        mk.dma('sp', K.negdel[:], K.hy_tdel[:, 0:4], w=[K.t_const], dsem=d)
        mk.op('dve', lambda: nc.vector.memset(K.negpi[:], -math.pi), w=[K.t_const])
        mk.op('dve', lambda: nc.vector.tensor_scalar_mul(out=K.negdel[:], in0=K.negdel[:], scalar1=-1.0), r=[K.t_const], w=[K.t_const])
        seqs = [(SEQ, CTX, K.zT_lat, 4)]
        if need_ctx:
            seqs.append((CTX, 0, K.zT_ctx, 4 + SEQ))
        for (n, off, zT_d, tcol0) in seqs:
            if n == SEQ and USE_FFT:
                with ExitStack() as s1:
                    F = fft_setup(K, s1)
                    with ExitStack() as s2:
                        hT = sb(nc, s2, "h_hT", [128, 4, n], F32)
                        hyena_filter_dev(K, l, s2, n, zT_d, tcol0, hT, rl1)
                        hyena_lat_fft_filter(K, l, F, hT)
                    fft_setup_inv(K, F, s1)
                    hyena_lat_fft_u(K, l, F, s1, rl1)
            else:
                with ExitStack() as s2:
                    hT = sb(nc, s2, "h_hT", [128, 4, n], F32)
                    hyena_filter_dev(K, l, s2, n, zT_d, tcol0, hT, rl1)
                    hyena_conv_dev(K, l, s2, n, off, hT, rl1)
        mk.barrier()


def phase_mixout(K, l, Xres, t_xres, Xout, t_xout, first_chunk=0):
    nc, mk = K.nc, K.mk
    S = K.S
    with ExitStack() as st:
        wo = sb(nc, st, "o_wo", [128, 8, D], BF16)
        modv = sb(nc, st, "o_mod", [128, 3, D], F32)
        xc = [sb(nc, st, "o_xc%d" % i, [128, D], F32) for i in range(2)]
        cT = [sb(nc, st, "o_cT%d" % i, [128, 8, 128], BF16) for i in range(2)]
        tt = sb(nc, st, "o_tt", [128, D], F32)
        st6 = sb(nc, st, "o_st6", [128, 2, 6], F32)
        mv = sb(nc, st, "o_mv", [128, 2], F32)
        rstd = sb(nc, st, "o_rstd", [128, 1], F32)
        pm = [ps(nc, st, "o_pm%d" % i, [128, 512], F32) for i in range(4)]
        t_wo, t_mod, t_tt, t_st = Tl(), Tl(), Tl(), Tl()
        t_xc = [Tl(), Tl()]
        t_cT = [Tl(), Tl()]
        t_pm = [Tl() for _ in range(4)]
        dws, dmod, dout = mk.new_dsem(), mk.new_dsem(), mk.new_dsem()
        dx = [mk.new_dsem(), mk.new_dsem()]
        for kc in range(8):
            mk.dma('pool', wo[:, kc, :], K.w_out[l][kc * 128:(kc + 1) * 128, :], w=[t_wo], dsem=dws)
        cur_g = None
        for ch in range(first_chunk, NCH):
            g = 1 if ch < 2 else 0
            if g != cur_g:
                cur_g = g
                mrow = K.mod[l][g]
                for i, src in enumerate([mrow[5 * D:6 * D], K.ln_g[l, 1], K.ln_b[l, 1]]):
                    mk.dma('sp', modv[:, i, :], src.partition_broadcast(128), r=[K.t_mod[l]], w=[t_mod], dsem=dmod)
            xi = ch % 2
            tok = slice(ch * 128, (ch + 1) * 128)

            def issue_loads(c2):
                x2 = c2 % 2
                tk = slice(c2 * 128, (c2 + 1) * 128)
                mk.dma('sp', xc[x2][:], Xres[tk, :], r=[t_xres], w=[t_xc[x2]], dsem=dx[x2])
                mk.dma('sp', cT[x2][:], S.CT[l].rearrange("(c p) t -> p c t", p=128)[:, :, tk], r=[S.t_CT], w=[t_cT[x2]], dsem=dx[x2])

            if ch == first_chunk:
                issue_loads(ch)
            if ch + 1 < NCH:
                issue_loads(ch + 1)
            for hf in range(2):
                pi = (ch % 2) * 2 + hf
                for kc in range(8):
                    mk.op('pe', lambda kc=kc, hf=hf, pi=pi, xi=xi: nc.tensor.matmul(pm[pi][:], lhsT=cT[xi][:, kc, :], rhs=wo[:, kc, hf * 512:(hf + 1) * 512],
                                                                                   start=(kc == 0), stop=(kc == 7)), r=[t_cT[xi], t_wo], w=[t_pm[pi]])
                mk.op('dve', lambda hf=hf, pi=pi: nc.vector.tensor_mul(out=tt[:, hf * 512:(hf + 1) * 512], in0=pm[pi][:], in1=modv[:, 0, hf * 512:(hf + 1) * 512]),
                      r=[t_pm[pi], t_mod], w=[t_tt])
            mk.op('dve', lambda xi=xi: nc.vector.scalar_tensor_tensor(out=xc[xi][:], in0=xc[xi][:], scalar=ALPHA, in1=tt[:], op0=ALU.mult, op1=ALU.add),
                  r=[t_tt, t_xc[xi]], w=[t_xc[xi]])
            ln_stats(K, xc[xi], t_xc[xi], st6, mv, rstd, t_st)
            mk.op('dve', lambda xi=xi: nc.vector.scalar_tensor_tensor(out=xc[xi][:], in0=xc[xi][:], scalar=mv[:, 0:1], in1=modv[:, 1, :],
                                                                      op0=ALU.subtract, op1=ALU.mult), r=[t_xc[xi], t_st, t_mod], w=[t_xc[xi]])
            mk.op('dve', lambda xi=xi: nc.vector.scalar_tensor_tensor(out=xc[xi][:], in0=xc[xi][:], scalar=rstd[:, 0:1], in1=modv[:, 2, :],
                                                                      op0=ALU.mult, op1=ALU.add), r=[t_xc[xi], t_st, t_mod], w=[t_xc[xi]])
            mk.dma('sp', Xout[tok, :], xc[xi][:], r=[t_xc[xi]], w=[t_xout], dsem=dout)
        mk.barrier()


def build(stop_after=None, dbg=()):
    nc = bass.Bass("TRN2", target_bir_lowering=False)
    K = Ctx()
    K.nc = nc
    K.mk = MK(nc)
    mk = K.mk
    DECLARED.clear()

    def din(name, shape):
        DECLARED.add(name)
        return nc.dram_tensor(name, shape, F32, kind="ExternalInput").ap()

    K.xin = din("xin", [NT, D])
    K.cc_in = din("cc", [128, 16])
    K.ada_w = din("ada_w", [DEPTH, D, NMOD * D])
    K.ada_b = din("ada_b", [DEPTH, NMOD * D])
    K.w_gu = [din("ffn1_w_gu", [DEPTH, D, 2 * DFF]), din("ffn2_w_gu", [DEPTH, D, 2 * DFF])]
    K.w_dn = [din("ffn1_w_down", [DEPTH, DFF, D]), din("ffn2_w_down", [DEPTH, DFF, D])]
    K.ln_g = din("ln_g", [DEPTH, 3, D])
    K.ln_b = din("ln_b", [DEPTH, 3, D])
    K.w_in = din("w_in", [DEPTH, D, P_IN])
    K.w_out = din("w_out", [DEPTH, D, D])
    K.a_qn = din("a_q_norm", [DEPTH, 64])
    K.a_kn = din("a_k_norm", [DEPTH, 64])
    K.c_qn = din("mla_q_norm", [DEPTH, 256])
    K.c_kvn = din("mla_kv_norm", [DEPTH, 128])
    K.w_uq = din("mla_w_uq", [DEPTH, 256, 576])
    K.w_ukv = din("mla_w_ukv", [DEPTH, 128, 768])
    K.hy_conv_w = din("hy_conv_w", [DEPTH, 3, 768])
    K.hy_conv_b = din("hy_conv_b", [DEPTH, 768])
    K.hy_w1 = din("hy_f_w1", [DEPTH, 33, 64])
    K.hy_b1 = din("hy_f_b1", [DEPTH, 64])
    K.hy_w2 = din("hy_f_w2", [DEPTH, 64, 64])
    K.hy_b2 = din("hy_f_b2", [DEPTH, 64])
    K.hy_w3 = din("hy_f_w3", [DEPTH, 64, 64])
    K.hy_b3 = din("hy_f_b3", [DEPTH, 64])
    K.hy_w4 = din("hy_f_w4", [DEPTH, 64, 512])
    K.hy_freq = din("hy_f_freq", [DEPTH, 64])
    K.hy_bias = din("hy_bias", [DEPTH, 256])
    K.ident_in = din("ident", [128, 128])
    K.hy_small = din("hy_small", [DEPTH, 128, 32])
    K.ropeA_in = din("ropeA", [SEQ, 64])
    K.ropeC_in = din("ropeC", [SEQ, 32])
    K.zT_lat = din("zT_lat", [33, SEQ])
    K.zT_ctx = din("zT_ctx", [33, CTX])
    K.hy_tdel = din("hy_tdel", [128, 4 + SEQ + CTX])
    K.fft_F1 = din("fft_F1", [64, 256])
    K.fft_G = din("fft_G", [64, 128, 3, 64])
    K.fft_W3 = din("fft_W3", [128, 512])
    K.fft_F4 = din("fft_F4", [128, 8192])
    K.out = nc.dram_tensor("out", [SEQ, D], F32, kind="ExternalOutput").ap()
    K.mod = nc.dram_tensor("mod", [DEPTH, 2, NMOD * D], F32).ap()
    K.t_mod = [Tl(), Tl()]
    XA = nc.dram_tensor("XA", [NT, D], F32).ap()
    XB = nc.dram_tensor("XB", [NT, D], F32).ap()
    XC = nc.dram_tensor("XC", [NT, D], F32).ap()
    S = Ctx()
    K.S = S
    S.QTa = nc.dram_tensor("QTa", [DEPTH, 6, 64, NT], BF16).ap()
    S.KTa = nc.dram_tensor("KTa", [DEPTH, 2, 64, NT], BF16).ap()
    S.Va = nc.dram_tensor("Va", [DEPTH, NT, 128], BF16).ap()
    S.QTc = nc.dram_tensor("QTc", [DEPTH, 6, 96, NT], BF16).ap()
    S.KTc = nc.dram_tensor("KTc", [DEPTH, 6, 96, NT], BF16).ap()
    S.Vc = nc.dram_tensor("Vc", [DEPTH, NT, 384], BF16).ap()
    S.PBT = nc.dram_tensor("PBT", [DEPTH, 768, NT], F32).ap()
    S.CT = nc.dram_tensor("CT", [DEPTH, D, NT], BF16).ap()
    S.t_A, S.t_C, S.t_PBT, S.t_CT = Tl(), Tl(), Tl(), Tl()
    S.HTd = nc.dram_tensor("HTd", [512, SEQ], BF16).ap()
    S.UTd = nc.dram_tensor("UTd", [256, SEQ], BF16).ap()
    S.XF = nc.dram_tensor("XF", [2, 128, 128, 2, 64], F32).ap()
    S.HF = nc.dram_tensor("HF", [2, 128, 128, 2, 64], F32).ap()
    S.t_HTd, S.t_UTd, S.t_XF, S.t_HF = Tl(), Tl(), Tl(), Tl()
    S.GBd = nc.dram_tensor("GBd", [16, 128, 24, 128], BF16).ap()
    S.t_GBd = Tl()
    K.gb_cached = [False] * 16
    t_xin, t_XA, t_XB, t_XC, t_out = Tl(), Tl(), Tl(), Tl(), Tl()
    dbg_outs = []

    def finish():
        dd = mk.new_dsem()
        for name in dbg:
            src = {"XA": XA, "XB": XB, "XC": XC, "QTa": S.QTa, "KTa": S.KTa, "Va": S.Va, "QTc": S.QTc, "KTc": S.KTc, "Vc": S.Vc,
                   "PBT": S.PBT, "CT": S.CT, "mod": K.mod}[name]
            o = nc.dram_tensor("dbg_" + name, list(src.shape), src.dtype, kind="ExternalOutput").ap()
            mk.dma('sp', o, src, w=[t_out], dsem=dd)
        mk.barrier()
        return nc

    with ExitStack() as gst:
        K.identb = sb(nc, gst, "identb", [128, 128], BF16)
        identf = sb(nc, gst, "identf", [128, 128], F32)
        K.eps_t = sb(nc, gst, "eps_t", [128, 1], F32)
        K.t_const = Tl()
        d0 = mk.new_dsem()
        mk.dma('sp', identf[:], K.ident_in[:, :], w=[K.t_const], dsem=d0)
        mk.op('dve', lambda: nc.vector.tensor_copy(out=K.identb[:], in_=identf[:]), r=[K.t_const], w=[K.t_const])
        mk.op('dve', lambda: nc.vector.memset(K.eps_t[:], EPS), w=[K.t_const])
        mk.barrier()

        Xcur, t_cur = K.xin, t_xin
        for l in range(DEPTH):
            last = (l == DEPTH - 1)
            phase_ada(K, l)
            mk.recycle()
            phase_ffn(K, l, 0, Xcur, t_cur, XC, t_XC, K.w_gu[0][l], K.w_dn[0][l])
            mk.recycle()
            if stop_after == "ffn1_%d" % l:
                return finish()
            phase_mixin(K, l, XC, t_XC)
            mk.recycle()
            if stop_after == "mixin_%d" % l:
                return finish()
            phase_attn(K, l, need_ctx=not last)
            mk.recycle()
            if stop_after == "attn_%d" % l:
                return finish()
            phase_hyena(K, l, need_ctx=not last)
            mk.recycle()
            if stop_after == "hyena_%d" % l:
                return finish()
            with ExitStack() as wst:
                W2 = ffn_load_weights(K, wst, K.w_gu[1][l], K.w_dn[1][l])
                phase_mixout(K, l, XC, t_XC, XB, t_XB, first_chunk=(2 if last else 0))
                if stop_after == "mixout_%d" % l:
                    return finish()
                phase_ffn(K, l, 1, XB, t_XB, XA, t_XA, K.w_gu[1][l], K.w_dn[1][l], first_chunk=(2 if last else 0), W=W2)
            mk.recycle()
            Xcur, t_cur = XA, t_XA
        dd = mk.new_dsem()
        mk.dma('sp', K.out[:, :], XA[CTX:, :], r=[t_XA], w=[t_out], dsem=dd)
        return finish()


def rope_table(rot_dim):
    rows = SEQ // 64
    row = np.repeat(np.arange(rows, dtype=np.float32), 64)
    col = np.tile(np.arange(64, dtype=np.float32), rows)
    n_freq = rot_dim // 4
    inv = (np.float32(10000.0) ** (-np.arange(n_freq, dtype=np.float32) / np.float32(n_freq))).astype(np.float32)
    ang = np.concatenate([row[:, None] * inv, col[:, None] * inv], -1).astype(np.float32)
    return np.ascontiguousarray(np.concatenate([np.cos(ang), np.sin(ang)], -1).astype(np.float32))


def fft_tables():
    N = 8192
    s1 = np.arange(64)[:, None]; k1 = np.arange(128)[None, :]
    F1 = np.exp(-2j * np.pi * s1 * k1 / 128)
    F1cat = np.concatenate([F1.real, F1.imag], 1)
    s2 = np.arange(64)[:, None, None]; k1g = np.arange(128)[None, :, None]; k2 = np.arange(64)[None, None, :]
    G = np.exp(-2j * np.pi * (s2 * k1g / 8192 + s2 * k2 / 64))
    Gc = np.stack([G.real, G.imag, -G.imag], 2)
    k2_ = np.arange(64)[:, None]; t2 = np.arange(64)[None, :]
    W3 = np.exp(2j * np.pi * k2_ * t2 / 64)
    W3a = np.zeros((128, 2, 2, 64)); W3b = np.zeros((128, 2, 2, 64))
    for cc in range(2):
        W3a[cc * 64:(cc + 1) * 64, cc, 0] = W3.real; W3a[cc * 64:(cc + 1) * 64, cc, 1] = W3.imag
        W3b[cc * 64:(cc + 1) * 64, cc, 0] = -W3.imag; W3b[cc * 64:(cc + 1) * 64, cc, 1] = W3.real
    W3cat = np.concatenate([W3a.reshape(128, 256), W3b.reshape(128, 256)], 1)
    k1_ = np.arange(128)[:, None, None]; t2_ = np.arange(64)[None, :, None]; t1 = np.arange(64)[None, None, :]
    F4 = np.exp(2j * np.pi * (t1 * k1_ / 128 + t2_ * k1_ / 8192)) / N
    F4c = np.stack([F4.real, -F4.imag], 2).reshape(128, 8192)
    f = lambda a: np.ascontiguousarray(a.astype(np.float32))
    return f(F1cat), f(Gc), f(W3cat), f(F4c)


def hyena_z(n):
    t = np.linspace(0.0, 1.0, n, dtype=np.float32)[:, None]
    w = (np.float32(2.0 * math.pi) * np.arange(n, dtype=np.float32)[:, None] / np.float32(n)).astype(np.float32)
    f = np.linspace(1e-4, 15, 16, dtype=np.float32)[None, :]
    z = np.concatenate([t, np.cos(f * w), -np.sin(f * w)], -1).astype(np.float32)
    return np.ascontiguousarray(z.T), t[:, 0]


def make_in_maps(inputs):
    x = np.asarray(inputs["x"], dtype=np.float32)
    ctx = np.asarray(inputs["ctx"], dtype=np.float32)
    c = np.asarray(inputs["c"], dtype=np.float32)
    cctx = np.asarray(inputs["c_ctx"], dtype=np.float32)
    shared = {k: np.ascontiguousarray(np.asarray(inputs[k], dtype=np.float32)) for k in inputs if k not in ("x", "ctx", "c")}
    shared["ident"] = np.eye(128, dtype=np.float32)
    shared["ropeA"] = rope_table(64)
    shared["ropeC"] = rope_table(32)
    zl, tl = hyena_z(SEQ)
    zc, tc_ = hyena_z(CTX)
    shared["zT_lat"] = zl
    shared["zT_ctx"] = zc
    deltas = np.abs(np.linspace(math.log(1e-2) / 1.5, math.log(1e-2) / 0.3, 256, dtype=np.float32))
    deltas = np.tile(deltas, 2)
    td = np.zeros((128, 4 + SEQ + CTX), np.float32)
    td[:, 0:4] = deltas.reshape(4, 128).T
    td[:, 4:4 + SEQ] = tl[None, :]
    td[:, 4 + SEQ:] = tc_[None, :]
    shared["hy_tdel"] = td
    hsm = np.zeros((DEPTH, 128, 32), np.float32)
    for l in range(DEPTH):
        cw = np.asarray(inputs["hy_conv_w"][l], np.float32)
        hsm[l, :, 0:18] = cw.reshape(3, 6, 128).transpose(2, 0, 1).reshape(128, 18)
        hsm[l, :, 18:24] = np.asarray(inputs["hy_conv_b"][l], np.float32).reshape(6, 128).T
        hsm[l, :, 24:26] = np.asarray(inputs["hy_bias"][l], np.float32).reshape(2, 128).T
        hsm[l, 0:64, 26] = np.asarray(inputs["hy_f_freq"][l], np.float32)
        hsm[l, 0:64, 27] = np.asarray(inputs["hy_f_b1"][l], np.float32)
        hsm[l, 0:64, 28] = np.asarray(inputs["hy_f_b2"][l], np.float32)
        hsm[l, 0:64, 29] = np.asarray(inputs["hy_f_b3"][l], np.float32)
    shared["hy_small"] = hsm
    shared["fft_F1"], shared["fft_G"], shared["fft_W3"], shared["fft_F4"] = fft_tables()
    maps = []
    for b in range(8):
        m = dict(shared)
        m["xin"] = np.ascontiguousarray(np.concatenate([ctx[b], x[b]], axis=0))
        cc = np.stack([c[b], cctx], axis=-1).reshape(8, 128, 2).transpose(1, 0, 2).reshape(128, 16)
        m["cc"] = np.ascontiguousarray(cc)
        maps.append(m)
    return maps


def kernel(**inputs):
    nc = build()
    maps = make_in_maps(inputs)
    maps = [{k: v for k, v in m.items() if k in DECLARED} for m in maps]
    res = run_bass_kernel_spmd(nc, maps, core_ids=list(range(8)))
    return np.stack([np.asarray(r["out"]) for r in res.results], axis=0).astype(np.float32)
```

```python
import math
from contextlib import ExitStack
import numpy as np
import concourse.bass as bass
import concourse.mybir as mybir
from concourse.bass_utils import run_bass_kernel_spmd

F32 = mybir.dt.float32
BF16 = mybir.dt.bfloat16
AF = mybir.ActivationFunctionType
ALU = mybir.AluOpType

D = 1024
SEQ = 4096
CTX = 256
NT = SEQ + CTX
NCH = NT // 128
DEPTH = 2
DFF = 2816
NMOD = 9
ALPHA = (2 * DEPTH) ** 0.25
EPS = 1e-6
P_IN = 1824
USE_FFT = True


class Tl:
    def __init__(self, name=""):
        self.name = name
        self.w = None
        self.weng = None
        self.r = {}


class MK:
    def __init__(self, nc):
        self.nc = nc
        self.eng = {'pe': nc.tensor, 'act': nc.scalar, 'dve': nc.vector, 'pool': nc.gpsimd, 'sp': nc.sync}
        self.sem = {}
        self.cnt = {}
        self.seen = {e: {} for e in self.eng}
        self._stack = ExitStack()
        for e in self.eng:
            self.sem[e] = self._stack.enter_context(nc.semaphore("s_" + e))
            self.cnt[e] = 0
        self.dsems = []
        self.free_dsems = []

    def new_dsem(self):
        if self.free_dsems:
            return self.free_dsems.pop()
        s = self._stack.enter_context(self.nc.semaphore("d%d" % len(self.dsems)))
        d = [s, 0]
        self.dsems.append(d)
        return d

    def _need(self, e, waits, dep):
        if dep is None:
            return
        sem, val = dep
        k = id(sem)
        if self.seen[e].get(k, 0) >= val:
            return
        if k not in waits or waits[k][1] < val:
            waits[k] = (sem, val)

    def _dowaits(self, e, waits):
        E = self.eng[e]
        for k, (sem, val) in waits.items():
            E.wait_ge(sem, val)
            self.seen[e][k] = val

    def op(self, e, fn, r=(), w=()):
        waits = {}
        for t in r:
            self._need(e, waits, t.w)
        for t in w:
            if t.weng != e:
                self._need(e, waits, t.w)
            for re_, dep in t.r.items():
                if re_ != e:
                    self._need(e, waits, dep)
        self._dowaits(e, waits)
        ins = fn()
        self.cnt[e] += 1
        ins.then_inc(self.sem[e], 1)
        dep = (self.sem[e], self.cnt[e])
        for t in r:
            t.r[e] = dep
        for t in w:
            t.w = dep
            t.weng = e
            t.r = {}
        return ins

    def dma(self, q, out, in_, r=(), w=(), dsem=None):
        waits = {}
        for t in r:
            self._need(q, waits, t.w)
        for t in w:
            self._need(q, waits, t.w)
            for re_, dep in t.r.items():
                self._need(q, waits, dep)
        self._dowaits(q, waits)
        ins = self.eng[q].dma_start(out=out, in_=in_)
        dsem[1] += 16
        ins.then_inc(dsem[0], 16)
        dep = (dsem[0], dsem[1])
        for t in r:
            t.r['dma%d' % id(dsem)] = dep
        for t in w:
            t.w = dep
            t.weng = 'dma'
            t.r = {}
        return ins

    def barrier(self):
        waits = {}
        for e in self.eng:
            if e != 'sp' and self.cnt[e] > 0:
                self._need('sp', waits, (self.sem[e], self.cnt[e]))
        for d in self.dsems:
            if d[1] > 0:
                self._need('sp', waits, (d[0], d[1]))
        self._dowaits('sp', waits)
        self.cnt['sp'] += 1
        self.nc.sync.sem_inc(self.sem['sp'], 1)
        for e in self.eng:
            if e != 'sp':
                self.eng[e].wait_ge(self.sem['sp'], self.cnt['sp'])
                self.seen[e][id(self.sem['sp'])] = self.cnt['sp']
                for f in self.eng:
                    if f != 'sp':
                        self.seen[e][id(self.sem[f])] = self.cnt[f]
                for d in self.dsems:
                    self.seen[e][id(d[0])] = d[1]

    def recycle(self):
        self.free_dsems = list(self.dsems)

    def close(self):
        self._stack.close()


class Ctx:
    pass


DECLARED = set()


_UID = [0]


def sb(nc, st, name, shape, dt):
    _UID[0] += 1
    return st.enter_context(nc.sbuf_tensor("sb%d_%s" % (_UID[0], name), shape, dt))


def ps(nc, st, name, shape, dt):
    _UID[0] += 1
    return st.enter_context(nc.psum_tensor("ps%d_%s" % (_UID[0], name), shape, dt))


def phase_ada(K, l):
    nc, mk = K.nc, K.mk
    with ExitStack() as st:
        cc = sb(nc, st, "ada_cc", [128, 8, 2], F32)
        sg = sb(nc, st, "ada_sg", [128, 8, 2], F32)
        wbuf = [sb(nc, st, "ada_w%d" % i, [128, 8, 512], F32) for i in range(4)]
        bia = sb(nc, st, "ada_b", [2, 9216], F32)
        mrow = sb(nc, st, "ada_m", [2, 9216], F32)
        pm = [ps(nc, st, "ada_p%d" % i, [2, 512], F32) for i in range(2)]
        t_cc, t_sg, t_b, t_m = Tl(), Tl(), Tl(), Tl()
        t_w = [Tl() for _ in range(4)]
        t_p = [Tl(), Tl()]
        d0 = mk.new_dsem()
        dw = [mk.new_dsem() for _ in range(4)]
        mk.dma('sp', cc[:], K.cc_in.rearrange("p (kc g) -> p kc g", g=2), w=[t_cc], dsem=d0)
        mk.dma('sp', bia[:], K.ada_b[l].partition_broadcast(2), w=[t_b], dsem=d0)
        mk.op('act', lambda: nc.scalar.activation(out=sg[:], in_=cc[:], func=AF.Silu), r=[t_cc], w=[t_sg])
        for nb in range(18):
            i = nb % 2
            wi = nb % 4
            mk.dma('sp', wbuf[wi][:], K.ada_w[l][:, nb * 512:(nb + 1) * 512].rearrange("(kc p) n -> p kc n", p=128),
                   w=[t_w[wi]], dsem=dw[wi])
            for kc in range(8):
                mk.op('pe', lambda kc=kc, i=i, wi=wi: nc.tensor.matmul(pm[i][:], lhsT=sg[:, kc, :], rhs=wbuf[wi][:, kc, :],
                                                                      start=(kc == 0), stop=(kc == 7)),
                      r=[t_sg, t_w[wi]], w=[t_p[i]])
            mk.op('dve', lambda nb=nb, i=i: nc.vector.tensor_add(out=mrow[:, nb * 512:(nb + 1) * 512], in0=pm[i][:],
                                                                 in1=bia[:, nb * 512:(nb + 1) * 512]),
                  r=[t_p[i], t_b], w=[t_m])
        mk.dma('sp', K.mod[l], mrow[:], r=[t_m], w=[K.t_mod[l]], dsem=d0)
        mk.barrier()


def ln_stats(K, x_ap, t_x, st6, mv, rstd, t_st):
    nc, mk = K.nc, K.mk
    for c in range(2):
        mk.op('dve', lambda c=c: nc.vector.bn_stats(out=st6[:, c, :], in_=x_ap[:, c * 512:(c + 1) * 512]), r=[t_x], w=[t_st])
    mk.op('dve', lambda: nc.vector.bn_aggr(out=mv[:], in_=st6[:]), r=[t_st], w=[t_st])
    mk.op('act', lambda: nc.scalar.activation(out=rstd[:], in_=mv[:, 1:2], func=AF.Sqrt, bias=K.eps_t[:, 0:1], scale=1.0), r=[t_st], w=[t_st])
    mk.op('dve', lambda: nc.vector.reciprocal(out=rstd[:], in_=rstd[:]), r=[t_st], w=[t_st])


def load_bcast(K, dst_ap, src_row_ap, t, dsem):
    K.mk.dma('sp', dst_ap, src_row_ap.partition_broadcast(128), w=[t], dsem=dsem)


def ffn_alloc_weights(K, st):
    nc = K.nc
    W = Ctx()
    W.wgu = sb(nc, st, "f_wgu", [128, 8, 2 * DFF], BF16)
    W.wdn = sb(nc, st, "f_wdn", [128, 22, D], BF16)
    W.t_wgu, W.t_wdn = Tl(), Tl()
    return W


def ffn_load_weights(K, st, w_gu, w_down, W=None):
    nc, mk = K.nc, K.mk
    if W is None:
        W = ffn_alloc_weights(K, st)
    dws = mk.new_dsem()
    for kc in range(8):
        for hh in range(2):
            mk.dma('pool', W.wgu[:, kc, hh * DFF:(hh + 1) * DFF], w_gu[kc * 128:(kc + 1) * 128, hh * DFF:(hh + 1) * DFF], w=[W.t_wgu], dsem=dws)
    for fc in range(22):
        mk.dma('pool', W.wdn[:, fc, :], w_down[fc * 128:(fc + 1) * 128, :], w=[W.t_wdn], dsem=dws)
    return W


def phase_ffn(K, l, which, Xin, t_xin, Xout, t_xout, w_gu, w_down, first_chunk=0, W=None):
    nc, mk = K.nc, K.mk
    mbase = 0 if which == 0 else 6
    lni = 0 if which == 0 else 2
    with ExitStack() as st:
        if W is None:
            W = ffn_load_weights(K, st, w_gu, w_down)
        wgu, wdn, t_wgu, t_wdn = W.wgu, W.wdn, W.t_wgu, W.t_wdn
        modv = sb(nc, st, "f_mod", [128, 5, D], F32)
        xc = [sb(nc, st, "f_xc%d" % i, [128, D], F32) for i in range(6)]
        hT = sb(nc, st, "f_hT", [128, 8, 256], BF16)
        actT = sb(nc, st, "f_actT", [128, 22, 256], BF16)
        tt = sb(nc, st, "f_tt", [128, D], F32)
        hn2 = [sb(nc, st, "f_hn%d" % i, [128, D], BF16) for i in range(2)]
        st6j = [sb(nc, st, "f_st6j%d" % i, [128, 2, 6], F32) for i in range(2)]
        mvj = [sb(nc, st, "f_mvj%d" % i, [128, 2], F32) for i in range(2)]
        rstdj = [sb(nc, st, "f_rstdj%d" % i, [128, 1], F32) for i in range(2)]
        sgt = [sb(nc, st, "f_sg%d" % i, [128, 256], F32) for i in range(2)]
        st6b = sb(nc, st, "f_st6b", [128, 2, 6], F32)
        mvb = sb(nc, st, "f_mvb", [128, 2], F32)
        rstdb = sb(nc, st, "f_rstdb", [128, 1], F32)
        p_tr = ps(nc, st, "f_ptr", [128, 8, 128], BF16)
        p_up = [ps(nc, st, "f_pup%d" % i, [128, 2, 256], F32) for i in range(3)]
        p_dn = [ps(nc, st, "f_pdn%d" % i, [128, 512], F32) for i in range(4)]
        t_mod = Tl()
        t_xc = [Tl() for _ in range(6)]
        t_hT, t_act, t_tt, t_stb, t_ptr = Tl(), Tl(), Tl(), Tl(), Tl()
        t_hn2 = [Tl(), Tl()]
        t_stj = [Tl(), Tl()]
        t_sg = [Tl(), Tl()]
        t_pup = [Tl() for _ in range(3)]
        t_pdn = [Tl() for _ in range(4)]
        dmod = mk.new_dsem()
        dx = [mk.new_dsem() for _ in range(6)]
        dout = mk.new_dsem()
        nblk = NCH // 2
        state = {"upi": 0}
        blks = list(range(first_chunk // 2, nblk))

        def xidx(blk, j):
            return (blk % 3) * 2 + j

        def load_mod(g):
            mrow = K.mod[l][g]
            for i, src in enumerate([mrow[(mbase + 0) * D:(mbase + 1) * D], mrow[(mbase + 1) * D:(mbase + 2) * D],
                                     mrow[(mbase + 2) * D:(mbase + 3) * D], K.ln_g[l, lni], K.ln_b[l, lni]]):
                mk.dma('sp', modv[:, i, :], src.partition_broadcast(128), r=[K.t_mod[l]], w=[t_mod], dsem=dmod)
            mk.op('pool', lambda: nc.gpsimd.tensor_scalar_add(out=modv[:, 1, :], in0=modv[:, 1, :], scalar1=1.0), r=[t_mod], w=[t_mod])
            mk.op('pool', lambda: nc.gpsimd.tensor_scalar_mul(out=modv[:, 2, :], in0=modv[:, 2, :], scalar1=0.5), r=[t_mod], w=[t_mod])

        def prep_load(blk):
            for j in range(2):
                ch = blk * 2 + j
                xi = xidx(blk, j)
                mk.dma('sp', xc[xi][:], Xin[ch * 128:(ch + 1) * 128, :], r=[t_xin], w=[t_xc[xi]], dsem=dx[xi])

        def prep_ln(blk):
            for j in range(2):
                xi = xidx(blk, j)
                ln_stats(K, xc[xi], t_xc[xi], st6j[j], mvj[j], rstdj[j], t_stj[j])
            for j in range(2):
                xi = xidx(blk, j)
                mk.op('dve', lambda xi=xi, j=j: nc.vector.scalar_tensor_tensor(out=tt[:], in0=xc[xi][:], scalar=mvj[j][:, 0:1], in1=modv[:, 1, :],
                                                                               op0=ALU.subtract, op1=ALU.mult), r=[t_xc[xi], t_stj[j], t_mod], w=[t_tt])
                mk.op('dve', lambda j=j: nc.vector.scalar_tensor_tensor(out=hn2[j][:], in0=tt[:], scalar=rstdj[j][:, 0:1], in1=modv[:, 0, :],
                                                                        op0=ALU.mult, op1=ALU.add), r=[t_tt, t_stj[j], t_mod], w=[t_hn2[j]])

        def prep_ln_gen(blk):
            for j in range(2):
                xi = xidx(blk, j)
                for c in range(2):
                    mk.op('dve', lambda c=c, xi=xi, j=j: nc.vector.bn_stats(out=st6j[j][:, c, :], in_=xc[xi][:, c * 512:(c + 1) * 512]), r=[t_xc[xi]], w=[t_stj[j]])
                    yield
                mk.op('dve', lambda j=j: nc.vector.bn_aggr(out=mvj[j][:], in_=st6j[j][:]), r=[t_stj[j]], w=[t_stj[j]])
                mk.op('act', lambda j=j: nc.scalar.activation(out=rstdj[j][:], in_=mvj[j][:, 1:2], func=AF.Sqrt, bias=K.eps_t[:, 0:1], scale=1.0), r=[t_stj[j]], w=[t_stj[j]])
                yield
                yield
                mk.op('dve', lambda j=j: nc.vector.reciprocal(out=rstdj[j][:], in_=rstdj[j][:]), r=[t_stj[j]], w=[t_stj[j]])
                yield
                mk.op('dve', lambda xi=xi, j=j: nc.vector.scalar_tensor_tensor(out=tt[:], in0=xc[xi][:], scalar=mvj[j][:, 0:1], in1=modv[:, 1, :],
                                                                               op0=ALU.subtract, op1=ALU.mult), r=[t_xc[xi], t_stj[j], t_mod], w=[t_tt])
                yield
                mk.op('dve', lambda j=j: nc.vector.scalar_tensor_tensor(out=hn2[j][:], in0=tt[:], scalar=rstdj[j][:, 0:1], in1=modv[:, 0, :],
                                                                        op0=ALU.mult, op1=ALU.add), r=[t_tt, t_stj[j], t_mod], w=[t_hn2[j]])
                yield

        def prep_b(blk):
            for j in range(2):
                for kc in range(8):
                    mk.op('pe', lambda kc=kc, j=j: nc.tensor.transpose(out=p_tr[:, kc, :], in_=hn2[j][:, kc * 128:(kc + 1) * 128], identity=K.identb[:]),
                          r=[t_hn2[j], K.t_const], w=[t_ptr])
                mk.op('act', lambda j=j: nc.scalar.copy(out=hT[:, :, j * 128:(j + 1) * 128], in_=p_tr[:]), r=[t_ptr], w=[t_hT])

        def up(blk, gen=None):
            for fc in range(22):
                if gen is not None and fc >= 3:
                    next(gen, None)
                pi = state["upi"] % 3
                si = state["upi"] % 2
                state["upi"] += 1
                for hh in range(2):
                    for kc in range(8):
                        mk.op('pe', lambda kc=kc, hh=hh, fc=fc, pi=pi: nc.tensor.matmul(
                            p_up[pi][:, hh, :], lhsT=wgu[:, kc, hh * DFF + fc * 128: hh * DFF + (fc + 1) * 128], rhs=hT[:, kc, :],
                            start=(kc == 0), stop=(kc == 7)), r=[t_wgu, t_hT], w=[t_pup[pi]])
                mk.op('act', lambda pi=pi, si=si: nc.scalar.activation(out=sgt[si][:], in_=p_up[pi][:, 0, :], func=AF.Silu), r=[t_pup[pi]], w=[t_sg[si]])
                mk.op('dve', lambda pi=pi, si=si, fc=fc: nc.vector.tensor_mul(out=actT[:, fc, :], in0=sgt[si][:], in1=p_up[pi][:, 1, :]),
                      r=[t_sg[si], t_pup[pi]], w=[t_act])

        def down(blk):
            for j in range(2):
                ch = blk * 2 + j
                xi = xidx(blk, j)
                for hf in range(2):
                    pd = j * 2 + hf
                    for fc in range(22):
                        mk.op('pe', lambda fc=fc, j=j, hf=hf, pd=pd: nc.tensor.matmul(
                            p_dn[pd][:], lhsT=actT[:, fc, j * 128:(j + 1) * 128], rhs=wdn[:, fc, hf * 512:(hf + 1) * 512],
                            start=(fc == 0), stop=(fc == 21)), r=[t_act, t_wdn], w=[t_pdn[pd]])
                    mk.op('dve', lambda hf=hf, pd=pd: nc.vector.tensor_mul(out=tt[:, hf * 512:(hf + 1) * 512], in0=p_dn[pd][:],
                                                                           in1=modv[:, 2, hf * 512:(hf + 1) * 512]), r=[t_pdn[pd], t_mod], w=[t_tt])
                mk.op('dve', lambda xi=xi: nc.vector.scalar_tensor_tensor(out=xc[xi][:], in0=xc[xi][:], scalar=ALPHA, in1=tt[:],
                                                                          op0=ALU.mult, op1=ALU.add), r=[t_tt, t_xc[xi]], w=[t_xc[xi]])
                ln_stats(K, xc[xi], t_xc[xi], st6b, mvb, rstdb, t_stb)
                mk.op('dve', lambda xi=xi: nc.vector.scalar_tensor_tensor(out=xc[xi][:], in0=xc[xi][:], scalar=mvb[:, 0:1], in1=modv[:, 3, :],
                                                                          op0=ALU.subtract, op1=ALU.mult), r=[t_xc[xi], t_stb, t_mod], w=[t_xc[xi]])
                mk.op('dve', lambda xi=xi: nc.vector.scalar_tensor_tensor(out=xc[xi][:], in0=xc[xi][:], scalar=rstdb[:, 0:1], in1=modv[:, 4, :],
                                                                          op0=ALU.mult, op1=ALU.add), r=[t_xc[xi], t_stb, t_mod], w=[t_xc[xi]])
                mk.dma('pool', Xout[ch * 128:(ch + 1) * 128, :], xc[xi][:], r=[t_xc[xi]], w=[t_xout], dsem=dout)

        if blks and blks[0] == 0:
            load_mod(1)
            prep_load(0)
            prep_ln(0)
            prep_b(0)
            up(0)
            down(0)
            blks = blks[1:]
        if blks:
            load_mod(0)
            prep_load(blks[0])
            if len(blks) > 1:
                prep_load(blks[1])
            prep_ln(blks[0])
            prep_b(blks[0])
            for bi_, blk in enumerate(blks):
                nxt = blks[bi_ + 1] if bi_ + 1 < len(blks) else None
                nx2 = blks[bi_ + 2] if bi_ + 2 < len(blks) else None
                gen = prep_ln_gen(nxt) if nxt is not None else None
                up(blk, gen)
                if gen is not None:
                    for _ in gen:
                        pass
                    prep_b(nxt)
                if nx2 is not None:
                    prep_load(nx2)
                down(blk)
        mk.barrier()


def phase_mixin(K, l, Xin, t_xin):
    nc, mk = K.nc, K.mk
    S = K.S
    with ExitStack() as st:
        win = sb(nc, st, "m_win", [128, 8, P_IN], BF16)
        wuq = sb(nc, st, "m_wuq", [128, 2, 576], BF16)
        wukv = sb(nc, st, "m_wukv", [128, 768], BF16)
        modv = sb(nc, st, "m_mod", [128, 2, D], F32)
        gA = sb(nc, st, "m_gA", [128, 8, 64], F32)
        gq = sb(nc, st, "m_gq", [128, 256], F32)
        gkv = sb(nc, st, "m_gkv", [128, 128], F32)
        ropeA_l = [sb(nc, st, "m_ropeA%d" % i, [128, 2, 32], F32) for i in range(2)]
        ropeC_l = [sb(nc, st, "m_ropeC%d" % i, [128, 2, 16], F32) for i in range(2)]
        t_rope_l = [Tl(), Tl()]
        xc = [sb(nc, st, "m_xc%d" % i, [128, D], F32) for i in range(2)]
        xn = sb(nc, st, "m_xn", [128, D], F32)
        hn = sb(nc, st, "m_hn", [128, D], BF16)
        hT = sb(nc, st, "m_hT", [128, 8, 128], BF16)
        ptA = sb(nc, st, "m_ptA", [128, 8, 64], F32)
        ptC = sb(nc, st, "m_ptC", [128, 416], F32)
        sq = sb(nc, st, "m_sq", [128, 512], F32)
        ss = sb(nc, st, "m_ss", [128, 16], F32)
        r1 = sb(nc, st, "m_r1", [128, 8, 32], F32)
        r2 = sb(nc, st, "m_r2", [128, 8, 32], F32)
        qkb = sb(nc, st, "m_qkb", [128, 8, 64], BF16)
        vab = sb(nc, st, "m_vab", [128, 128], BF16)
        qkT = sb(nc, st, "m_qkT", [128, 4, 128], BF16)
        pbs = sb(nc, st, "m_pbs", [128, 6, 128], F32)
        cn = sb(nc, st, "m_cn", [128, 384], BF16)
        cT = sb(nc, st, "m_cT", [128, 3, 128], BF16)
        qc = sb(nc, st, "m_qc", [128, 6, 96], F32)
        kvc = sb(nc, st, "m_kvc", [128, 6, 128], F32)
        kr = sb(nc, st, "m_kr", [128, 32], F32)
        qcb = sb(nc, st, "m_qcb", [128, 6, 96], BF16)
        kcb = sb(nc, st, "m_kcb", [128, 6, 96], BF16)
        vcb = sb(nc, st, "m_vcb", [128, 6, 64], BF16)
        qcT = sb(nc, st, "m_qcT", [128, 6, 128], BF16)
        kcT = sb(nc, st, "m_kcT", [128, 6, 128], BF16)
        st6 = sb(nc, st, "m_st6", [128, 2, 6], F32)
        mv = sb(nc, st, "m_mv", [128, 2], F32)
        rstd = sb(nc, st, "m_rstd", [128, 1], F32)
        B0 = ps(nc, st, "m_B0", [128, 8, 128], BF16)
        B1 = ps(nc, st, "m_B1", [128, 512], F32)
        B2 = ps(nc, st, "m_B2", [128, 512], F32)
        B3 = ps(nc, st, "m_B3", [128, 512], F32)
        B4 = ps(nc, st, "m_B4", [128, 4, 128], F32)
        B5 = ps(nc, st, "m_B5", [128, 4, 128], F32)
        B6 = ps(nc, st, "m_B6", [128, 8, 128], BF16)
        B7 = ps(nc, st, "m_B7", [128, 512], F32)
        tB = [Tl() for _ in range(8)]
        t_w, t_mod, t_g = Tl(), Tl(), Tl()
        t_xc = [Tl(), Tl()]
        (t_xn, t_hn, t_hT, t_ptA, t_ptC, t_sq, t_ss, t_r, t_qkb, t_vab, t_qkT, t_pbs, t_cn, t_cT, t_qc, t_kvc,
         t_kr, t_qcb, t_kcb, t_vcb, t_qcT, t_kcT, t_st) = [Tl() for _ in range(23)]
        dws, dmod, dg, dout = [mk.new_dsem() for _ in range(4)]
        drope_l = [mk.new_dsem(), mk.new_dsem()]
        dx = [mk.new_dsem(), mk.new_dsem()]

        def issue_loads(ch):
            xi = ch % 2
            mk.dma('sp', xc[xi][:], Xin[ch * 128:(ch + 1) * 128, :], r=[t_xin], w=[t_xc[xi]], dsem=dx[xi])
            if ch >= 2:
                lt = slice((ch - 2) * 128, (ch - 1) * 128)
                mk.dma('sp', ropeA_l[xi][:], K.ropeA_in[lt, :].rearrange("p (a b) -> p a b", a=2), w=[t_rope_l[xi]], dsem=drope_l[xi])
                mk.dma('sp', ropeC_l[xi][:], K.ropeC_in[lt, :].rearrange("p (a b) -> p a b", a=2), w=[t_rope_l[xi]], dsem=drope_l[xi])
        for kc in range(8):
            for two in range(2):
                mk.dma('pool', win[:, kc, 0:384].rearrange("p (h two d) -> p h two d", h=3, two=2)[:, :, two, :],
                       K.w_in[l][kc * 128:(kc + 1) * 128, two * 192:(two + 1) * 192].rearrange("p (h d) -> p h d", h=3), w=[t_w], dsem=dws)
            mk.dma('pool', win[:, kc, 384:P_IN], K.w_in[l][kc * 128:(kc + 1) * 128, 384:P_IN], w=[t_w], dsem=dws)
        for kc in range(2):
            mk.dma('pool', wuq[:, kc, :], K.w_uq[l][kc * 128:(kc + 1) * 128, :], w=[t_w], dsem=dws)
        mk.dma('pool', wukv[:], K.w_ukv[l][:, :], w=[t_w], dsem=dws)
        for h in range(6):
            mk.dma('sp', gA[:, h, :], K.a_qn[l].partition_broadcast(128), w=[t_g], dsem=dg)
        for h in range(2):
            mk.dma('sp', gA[:, 6 + h, :], K.a_kn[l].partition_broadcast(128), w=[t_g], dsem=dg)
        mk.dma('sp', gq[:], K.c_qn[l].partition_broadcast(128), w=[t_g], dsem=dg)
        mk.dma('sp', gkv[:], K.c_kvn[l].partition_broadcast(128), w=[t_g], dsem=dg)
        mk.op('pool', lambda: nc.gpsimd.tensor_scalar_mul(out=gA[:, 0:6, :], in0=gA[:, 0:6, :], scalar1=0.125), r=[t_g], w=[t_g])
        cur_g = None
        for ch in range(NCH):
            g = 1 if ch < 2 else 0
            lat = (g == 0)
            if g != cur_g:
                cur_g = g
                mrow = K.mod[l][g]
                for i in range(2):
                    mk.dma('sp', modv[:, i, :], mrow[(3 + i) * D:(4 + i) * D].partition_broadcast(128), r=[K.t_mod[l]], w=[t_mod], dsem=dmod)
                mk.op('pool', lambda: nc.gpsimd.tensor_scalar_add(out=modv[:, 1, :], in0=modv[:, 1, :], scalar1=1.0), r=[t_mod], w=[t_mod])
            xi = ch % 2
            tok = slice(ch * 128, (ch + 1) * 128)
            if ch == 0:
                issue_loads(0)
            if ch + 1 < NCH:
                issue_loads(ch + 1)
            ropeA, ropeC, t_rope = ropeA_l[xi], ropeC_l[xi], t_rope_l[xi]
            ln_stats(K, xc[xi], t_xc[xi], st6, mv, rstd, t_st)
            mk.op('dve', lambda xi=xi: nc.vector.scalar_tensor_tensor(out=xn[:], in0=xc[xi][:], scalar=mv[:, 0:1], in1=modv[:, 1, :],
                                                                      op0=ALU.subtract, op1=ALU.mult), r=[t_xc[xi], t_st, t_mod], w=[t_xn])
            mk.op('dve', lambda: nc.vector.scalar_tensor_tensor(out=hn[:], in0=xn[:], scalar=rstd[:, 0:1], in1=modv[:, 0, :],
                                                                op0=ALU.mult, op1=ALU.add), r=[t_xn, t_st, t_mod], w=[t_hn])
            for kc in range(8):
                mk.op('pe', lambda kc=kc: nc.tensor.transpose(out=B0[:, kc, :], in_=hn[:, kc * 128:(kc + 1) * 128], identity=K.identb[:]),
                      r=[t_hn, K.t_const], w=[tB[0]])
            mk.op('act', lambda: nc.scalar.copy(out=hT[:], in_=B0[:]), r=[tB[0]], w=[t_hT])
            for (Bk, tb, c0, c1) in ((B1, tB[1], 0, 512), (B3, tB[3], 512, 640), (B2, tB[2], 1408, 1824)):
                for kc in range(8):
                    mk.op('pe', lambda kc=kc, Bk=Bk, c0=c0, c1=c1: nc.tensor.matmul(Bk[:, 0:c1 - c0], lhsT=hT[:, kc, :], rhs=win[:, kc, c0:c1],
                                                                                   start=(kc == 0), stop=(kc == 7)), r=[t_hT, t_w], w=[tb])
            for c in range(6):
                Bk, tb, ci = (B4, tB[4], c) if c < 4 else (B5, tB[5], c - 4)
                for kc in range(8):
                    mk.op('pe', lambda kc=kc, Bk=Bk, ci=ci, c=c: nc.tensor.matmul(Bk[:, ci, :], lhsT=win[:, kc, 640 + c * 128:640 + (c + 1) * 128],
                                                                                 rhs=hT[:, kc, :], start=(kc == 0), stop=(kc == 7)), r=[t_hT, t_w], w=[tb])
            mk.op('act', lambda: nc.scalar.copy(out=ptA[:].rearrange("p h d -> p (h d)"), in_=B1[:]), r=[tB[1]], w=[t_ptA])
            mk.op('act', lambda: nc.scalar.copy(out=ptC[:], in_=B2[:, 0:416]), r=[tB[2]], w=[t_ptC])
            mk.op('act', lambda: nc.scalar.copy(out=vab[:], in_=B3[:, 0:128]), r=[tB[3]], w=[t_vab])
            mk.op('act', lambda: nc.scalar.copy(out=pbs[:, 0:4, :], in_=B4[:]), r=[tB[4]], w=[t_pbs])
            mk.op('act', lambda: nc.scalar.copy(out=pbs[:, 4:6, :], in_=B5[:, 0:2, :]), r=[tB[5]], w=[t_pbs])
            mk.dma('sp', S.PBT[l].rearrange("(c p) t -> p c t", p=128)[:, :, tok], pbs[:], r=[t_pbs], w=[S.t_PBT], dsem=dout)
            mk.dma('sp', S.Va[l][tok, :], vab[:], r=[t_vab], w=[S.t_A], dsem=dout)
            mk.op('dve', lambda: nc.vector.tensor_mul(out=sq[:], in0=ptA[:].rearrange("p h d -> p (h d)"), in1=ptA[:].rearrange("p h d -> p (h d)")),
                  r=[t_ptA], w=[t_sq])
            mk.op('dve', lambda: nc.vector.reduce_sum(out=ss[:, 0:8], in_=sq[:].rearrange("p (h d) -> p h d", d=64), axis=mybir.AxisListType.X),
                  r=[t_sq], w=[t_ss])
            mk.op('act', lambda: nc.scalar.activation(out=ss[:, 0:8], in_=ss[:, 0:8], func=AF.Sqrt, bias=K.eps_t[:, 0:1], scale=1.0 / 64), r=[t_ss], w=[t_ss])
            mk.op('dve', lambda: nc.vector.reciprocal(out=ss[:, 0:8], in_=ss[:, 0:8]), r=[t_ss], w=[t_ss])
            mk.op('dve', lambda: nc.vector.tensor_mul(out=ptA[:], in0=ptA[:], in1=ss[:, 0:8].unsqueeze(2).to_broadcast([128, 8, 64])), r=[t_ptA, t_ss], w=[t_ptA])
            if lat:
                mk.op('pool', lambda: nc.gpsimd.tensor_mul(out=ptA[:], in0=ptA[:], in1=gA[:]), r=[t_ptA, t_g], w=[t_ptA])
                cosb = ropeA[:, 0:1, :].to_broadcast([128, 8, 32])
                sinb = ropeA[:, 1:2, :].to_broadcast([128, 8, 32])
                mk.op('dve', lambda: nc.vector.tensor_mul(out=r1[:], in0=ptA[:, :, 0:32], in1=cosb), r=[t_ptA, t_rope], w=[t_r])
                mk.op('dve', lambda: nc.vector.tensor_mul(out=r2[:], in0=ptA[:, :, 32:64], in1=sinb), r=[t_ptA, t_rope], w=[t_r])
                mk.op('dve', lambda: nc.vector.tensor_sub(out=qkb[:, :, 0:32], in0=r1[:], in1=r2[:]), r=[t_r], w=[t_qkb])
                mk.op('dve', lambda: nc.vector.tensor_mul(out=r1[:], in0=ptA[:, :, 32:64], in1=cosb), r=[t_ptA, t_rope], w=[t_r])
                mk.op('dve', lambda: nc.vector.tensor_mul(out=r2[:], in0=ptA[:, :, 0:32], in1=sinb), r=[t_ptA, t_rope], w=[t_r])
                mk.op('dve', lambda: nc.vector.tensor_add(out=qkb[:, :, 32:64], in0=r1[:], in1=r2[:]), r=[t_r], w=[t_qkb])
            else:
                mk.op('pool', lambda: nc.gpsimd.tensor_mul(out=qkb[:], in0=ptA[:], in1=gA[:]), r=[t_ptA, t_g], w=[t_qkb])
            for pr in range(3):
                mk.op('pe', lambda pr=pr: nc.tensor.transpose(out=B6[:, pr, :], in_=qkb[:, 2 * pr:2 * pr + 2, :].rearrange("p a d -> p (a d)"),
                                                              identity=K.identb[:]), r=[t_qkb, K.t_const], w=[tB[6]])
            mk.op('pe', lambda: nc.tensor.transpose(out=B6[:, 3, :], in_=qkb[:, 6:8, :].rearrange("p a d -> p (a d)"), identity=K.identb[:]),
                  r=[t_qkb, K.t_const], w=[tB[6]])
            mk.op('act', lambda: nc.scalar.copy(out=qkT[:], in_=B6[:, 0:4, :]), r=[tB[6]], w=[t_qkT])
            for pr in range(3):
                mk.dma('sp', S.QTa[l][pr][:, tok], qkT[0:64, pr, :], r=[t_qkT], w=[S.t_A], dsem=dout)
                mk.dma('sp', S.QTa[l][pr + 3][:, tok], qkT[64:128, pr, :], r=[t_qkT], w=[S.t_A], dsem=dout)
            mk.dma('sp', S.KTa[l].rearrange("h d t -> (h d) t")[:, tok], qkT[:, 3, :], r=[t_qkT], w=[S.t_A], dsem=dout)
            mk.op('dve', lambda: nc.vector.tensor_mul(out=sq[:, 0:384], in0=ptC[:, 0:384], in1=ptC[:, 0:384]), r=[t_ptC], w=[t_sq])
            mk.op('dve', lambda: nc.vector.reduce_sum(out=ss[:, 8:9], in_=sq[:, 0:256], axis=mybir.AxisListType.X), r=[t_sq], w=[t_ss])
            mk.op('dve', lambda: nc.vector.reduce_sum(out=ss[:, 9:10], in_=sq[:, 256:384], axis=mybir.AxisListType.X), r=[t_sq], w=[t_ss])
            mk.op('act', lambda: nc.scalar.activation(out=ss[:, 8:9], in_=ss[:, 8:9], func=AF.Sqrt, bias=K.eps_t[:, 0:1], scale=1.0 / 256), r=[t_ss], w=[t_ss])
            mk.op('act', lambda: nc.scalar.activation(out=ss[:, 9:10], in_=ss[:, 9:10], func=AF.Sqrt, bias=K.eps_t[:, 0:1], scale=1.0 / 128), r=[t_ss], w=[t_ss])
            mk.op('dve', lambda: nc.vector.reciprocal(out=ss[:, 8:10], in_=ss[:, 8:10]), r=[t_ss], w=[t_ss])
            mk.op('dve', lambda: nc.vector.scalar_tensor_tensor(out=cn[:, 0:256], in0=ptC[:, 0:256], scalar=ss[:, 8:9], in1=gq[:], op0=ALU.mult, op1=ALU.mult),
                  r=[t_ptC, t_ss, t_g], w=[t_cn])
            mk.op('dve', lambda: nc.vector.scalar_tensor_tensor(out=cn[:, 256:384], in0=ptC[:, 256:384], scalar=ss[:, 9:10], in1=gkv[:], op0=ALU.mult, op1=ALU.mult),
                  r=[t_ptC, t_ss, t_g], w=[t_cn])
            for c in range(3):
                mk.op('pe', lambda c=c: nc.tensor.transpose(out=B6[:, 4 + c, :], in_=cn[:, c * 128:(c + 1) * 128], identity=K.identb[:]),
                      r=[t_cn, K.t_const], w=[tB[6]])
            mk.op('act', lambda: nc.scalar.copy(out=cT[:], in_=B6[:, 4:7, :]), r=[tB[6]], w=[t_cT])
            for kc in range(2):
                mk.op('pe', lambda kc=kc: nc.tensor.matmul(B7[:, 0:480], lhsT=cT[:, kc, :], rhs=wuq[:, kc, 0:480], start=(kc == 0), stop=(kc == 1)),
                      r=[t_cT, t_w], w=[tB[7]])
            for kc in range(2):
                mk.op('pe', lambda kc=kc: nc.tensor.matmul(B3[:, 128:224], lhsT=cT[:, kc, :], rhs=wuq[:, kc, 480:576], start=(kc == 0), stop=(kc == 1)),
                      r=[t_cT, t_w], w=[tB[3]])
            mk.op('pe', lambda: nc.tensor.matmul(B1[:], lhsT=cT[:, 2, :], rhs=wukv[:, 0:512], start=True, stop=True), r=[t_cT, t_w], w=[tB[1]])
            mk.op('pe', lambda: nc.tensor.matmul(B2[:, 0:256], lhsT=cT[:, 2, :], rhs=wukv[:, 512:768], start=True, stop=True), r=[t_cT, t_w], w=[tB[2]])
            qs = 96.0 ** -0.5
            mk.op('act', lambda: nc.scalar.mul(out=qc[:, 0:5, :].rearrange("p h d -> p (h d)"), in_=B7[:, 0:480], mul=qs), r=[tB[7]], w=[t_qc])
            mk.op('act', lambda: nc.scalar.mul(out=qc[:, 5, :], in_=B3[:, 128:224], mul=qs), r=[tB[3]], w=[t_qc])
            mk.op('act', lambda: nc.scalar.copy(out=kvc[:, 0:4, :].rearrange("p h d -> p (h d)"), in_=B1[:]), r=[tB[1]], w=[t_kvc])
            mk.op('act', lambda: nc.scalar.copy(out=kvc[:, 4:6, :].rearrange("p h d -> p (h d)"), in_=B2[:, 0:256]), r=[tB[2]], w=[t_kvc])
            mk.op('pool', lambda: nc.gpsimd.tensor_copy(out=qcb[:, :, 0:64], in_=qc[:, :, 0:64]), r=[t_qc], w=[t_qcb])
            mk.op('pool', lambda: nc.gpsimd.tensor_copy(out=kcb[:, :, 0:64], in_=kvc[:, :, 0:64]), r=[t_kvc], w=[t_kcb])
            mk.op('pool', lambda: nc.gpsimd.tensor_copy(out=vcb[:], in_=kvc[:, :, 64:128]), r=[t_kvc], w=[t_vcb])
            if lat:
                cosb = ropeC[:, 0:1, :].to_broadcast([128, 6, 16])
                sinb = ropeC[:, 1:2, :].to_broadcast([128, 6, 16])
                mk.op('dve', lambda: nc.vector.tensor_mul(out=r1[:, 0:6, 0:16], in0=qc[:, :, 64:80], in1=cosb), r=[t_qc, t_rope], w=[t_r])
                mk.op('dve', lambda: nc.vector.tensor_mul(out=r2[:, 0:6, 0:16], in0=qc[:, :, 80:96], in1=sinb), r=[t_qc, t_rope], w=[t_r])
                mk.op('dve', lambda: nc.vector.tensor_sub(out=qcb[:, :, 64:80], in0=r1[:, 0:6, 0:16], in1=r2[:, 0:6, 0:16]), r=[t_r], w=[t_qcb])
                mk.op('dve', lambda: nc.vector.tensor_mul(out=r1[:, 0:6, 0:16], in0=qc[:, :, 80:96], in1=cosb), r=[t_qc, t_rope], w=[t_r])
                mk.op('dve', lambda: nc.vector.tensor_mul(out=r2[:, 0:6, 0:16], in0=qc[:, :, 64:80], in1=sinb), r=[t_qc, t_rope], w=[t_r])
                mk.op('dve', lambda: nc.vector.tensor_add(out=qcb[:, :, 80:96], in0=r1[:, 0:6, 0:16], in1=r2[:, 0:6, 0:16]), r=[t_r], w=[t_qcb])
                mk.op('dve', lambda: nc.vector.tensor_mul(out=r1[:, 0, 0:16], in0=ptC[:, 384:400], in1=ropeC[:, 0, :]), r=[t_ptC, t_rope], w=[t_r])
                mk.op('dve', lambda: nc.vector.tensor_mul(out=r2[:, 0, 0:16], in0=ptC[:, 400:416], in1=ropeC[:, 1, :]), r=[t_ptC, t_rope], w=[t_r])
                mk.op('dve', lambda: nc.vector.tensor_sub(out=kr[:, 0:16], in0=r1[:, 0, 0:16], in1=r2[:, 0, 0:16]), r=[t_r], w=[t_kr])
                mk.op('dve', lambda: nc.vector.tensor_mul(out=r1[:, 0, 0:16], in0=ptC[:, 400:416], in1=ropeC[:, 0, :]), r=[t_ptC, t_rope], w=[t_r])
                mk.op('dve', lambda: nc.vector.tensor_mul(out=r2[:, 0, 0:16], in0=ptC[:, 384:400], in1=ropeC[:, 1, :]), r=[t_ptC, t_rope], w=[t_r])
                mk.op('dve', lambda: nc.vector.tensor_add(out=kr[:, 16:32], in0=r1[:, 0, 0:16], in1=r2[:, 0, 0:16]), r=[t_r], w=[t_kr])
            else:
                mk.op('pool', lambda: nc.gpsimd.tensor_copy(out=qcb[:, :, 64:96], in_=qc[:, :, 64:96]), r=[t_qc], w=[t_qcb])
                mk.op('pool', lambda: nc.gpsimd.tensor_copy(out=kr[:], in_=ptC[:, 384:416]), r=[t_ptC], w=[t_kr])
            mk.op('pool', lambda: nc.gpsimd.tensor_copy(out=kcb[:, :, 64:96], in_=kr[:].unsqueeze(1).to_broadcast([128, 6, 32])), r=[t_kr], w=[t_kcb])
            mk.dma('sp', S.Vc[l][tok, :], vcb[:].rearrange("p h d -> p (h d)"), r=[t_vcb], w=[S.t_C], dsem=dout)
            for h in range(6):
                mk.op('pe', lambda h=h: nc.tensor.transpose(out=B0[0:96, h, :], in_=qcb[:, h, :], identity=K.identb[:]), r=[t_qcb, K.t_const], w=[tB[0]])
            mk.op('act', lambda: nc.scalar.copy(out=qcT[0:96, :, :], in_=B0[0:96, 0:6, :]), r=[tB[0]], w=[t_qcT])
            for h in range(6):
                mk.op('pe', lambda h=h: nc.tensor.transpose(out=B6[0:96, h, :], in_=kcb[:, h, :], identity=K.identb[:]), r=[t_kcb, K.t_const], w=[tB[6]])
            mk.op('act', lambda: nc.scalar.copy(out=kcT[0:96, :, :], in_=B6[0:96, 0:6, :]), r=[tB[6]], w=[t_kcT])
            mk.dma('sp', S.QTc[l].rearrange("h d t -> d h t")[:, :, tok], qcT[0:96, :, :], r=[t_qcT], w=[S.t_C], dsem=dout)
            mk.dma('sp', S.KTc[l].rearrange("h d t -> d h t")[:, :, tok], kcT[0:96, :, :], r=[t_kcT], w=[S.t_C], dsem=dout)
        mk.barrier()


def phase_attn(K, l, need_ctx):
    nc, mk = K.nc, K.mk
    S = K.S
    heads = []
    for h in range(6):
        heads.append((64, S.KTa[l][h // 3], S.Va[l][:, (h // 3) * 64:(h // 3 + 1) * 64], S.QTa[l][h], h * 64, S.t_A))
    for h in range(6):
        heads.append((96, S.KTc[l][h], S.Vc[l][:, h * 64:(h + 1) * 64], S.QTc[l][h], 640 + h * 64, S.t_C))
    with ExitStack() as st:
        KT = [sb(nc, st, "a_KT%d" % i, [128, NT], BF16) for i in range(2)]
        QT = [sb(nc, st, "a_QT%d" % i, [128, NT], BF16) for i in range(2)]
        Vx = [sb(nc, st, "a_V%d" % i, [128, NCH, 128], BF16) for i in range(2)]
        PT = [sb(nc, st, "a_PT%d" % i, [128, 2, 512], BF16) for i in range(3)]
        rden = sb(nc, st, "a_rden", [64, 512], F32)
        ob = [sb(nc, st, "a_ob%d" % i, [64, 512], BF16) for i in range(2)]
        ST = [ps(nc, st, "a_ST%d" % i, [128, 2, 512], F32) for i in range(3)]
        OD = [ps(nc, st, "a_OD%d" % i, [128, 512], F32) for i in range(2)]
        t_KV = [Tl(), Tl()]
        t_rden = Tl()
        t_PT = [Tl() for _ in range(3)]
        t_ob = [Tl(), Tl()]
        t_ST = [Tl() for _ in range(3)]
        t_OD = [Tl(), Tl()]
        dl = [mk.new_dsem(), mk.new_dsem()]
        dout = mk.new_dsem()
        for i in range(2):
            mk.op('pool', lambda i=i: nc.gpsimd.memset(Vx[i][:, :, 64:128], 1.0), w=[t_KV[i]])
            mk.op('pool', lambda i=i: nc.gpsimd.memset(KT[i][:], 0.0), w=[t_KV[i]])
            mk.op('pool', lambda i=i: nc.gpsimd.memset(QT[i][:], 0.0), w=[t_KV[i]])
        si = 0
        pi = 0
        bi = 0
        def issue_head_loads(h2):
            dk2, KTd2, Vd2, QTd2, _row, t_src2 = heads[h2]
            b2 = h2 % 2
            mk.dma('sp', KT[b2][0:dk2, :], KTd2, r=[t_src2], w=[t_KV[b2]], dsem=dl[b2])
            mk.dma('sp', QT[b2][0:dk2, :], QTd2, r=[t_src2], w=[t_KV[b2]], dsem=dl[b2])
            mk.dma('sp', Vx[b2][:, :, 0:64], Vd2.rearrange("(c p) d -> p c d", p=128), r=[t_src2], w=[t_KV[b2]], dsem=dl[b2])

        issue_head_loads(0)
        for hi, (dk, KTd, Vd, QTd, row0, t_src) in enumerate(heads):
            b = hi % 2
            if hi + 1 < len(heads):
                issue_head_loads(hi + 1)
            blocks = [(CTX + qb * 512, 512, 0, NCH) for qb in range(8)]
            if need_ctx:
                blocks.append((0, 256, 0, 2))
            for (q0, qn, k0, k1) in blocks:
                o = bi % 2
                bi += 1
                LOOK = 2
                slots = {}
                npair = (k1 - k0) // 2

                def issue_s(kp):
                    nonlocal si, pi
                    s_ = si % 3
                    si += 1
                    p_ = pi % 3
                    pi += 1
                    slots[kp] = (s_, p_)
                    for j in range(2):
                        kc = k0 + 2 * kp + j
                        mk.op('pe', lambda kc=kc, j=j, s_=s_, b=b, q0=q0, qn=qn: nc.tensor.matmul(
                            ST[s_][:, j, 0:qn], lhsT=KT[b][:, kc * 128:(kc + 1) * 128], rhs=QT[b][:, q0:q0 + qn], start=True, stop=True),
                            r=[t_KV[b]], w=[t_ST[s_]])
                    mk.op('act', lambda s_=s_, p_=p_, qn=qn: nc.scalar.activation(out=PT[p_][:, :, 0:qn], in_=ST[s_][:, :, 0:qn], func=AF.Exp),
                          r=[t_ST[s_]], w=[t_PT[p_]])

                for kp in range(min(npair, LOOK)):
                    issue_s(kp)
                for kp in range(npair):
                    if kp + LOOK < npair:
                        issue_s(kp + LOOK)
                    s_, p_ = slots.pop(kp)
                    for j in range(2):
                        kc = k0 + 2 * kp + j
                        mk.op('pe', lambda kc=kc, j=j, p_=p_, o=o, b=b, qn=qn, k0=k0, k1=k1: nc.tensor.matmul(
                            OD[o][:, 0:qn], lhsT=Vx[b][:, kc, :], rhs=PT[p_][:, j, 0:qn], start=(kc == k0), stop=(kc == k1 - 1)),
                            r=[t_KV[b], t_PT[p_]], w=[t_OD[o]])
                mk.op('dve', lambda o=o, qn=qn: nc.vector.reciprocal(out=rden[:, 0:qn], in_=OD[o][64:128, 0:qn]), r=[t_OD[o]], w=[t_rden])
                mk.op('dve', lambda o=o, qn=qn: nc.vector.tensor_mul(out=ob[o][:, 0:qn], in0=OD[o][0:64, 0:qn], in1=rden[:, 0:qn]),
                      r=[t_OD[o], t_rden], w=[t_ob[o]])
                mk.dma('pool', S.CT[l][row0:row0 + 64, q0:q0 + qn], ob[o][:, 0:qn], r=[t_ob[o]], w=[S.t_CT], dsem=dout)
        mk.barrier()


def hyena_filter_dev(K, l, st, n, zT_d, tcol0, hT, rl1):
    nc, mk = K.nc, K.mk
    TWO_PI = 2.0 * math.pi
    with ExitStack() as fs:
        zT = sb(nc, fs, "h_zT", [33, n], F32)
        tb = sb(nc, fs, "h_tb", [128, n], F32)
        w1 = sb(nc, fs, "h_w1", [33, 64], F32)
        w23 = sb(nc, fs, "h_w23", [64, 2, 64], F32)
        w4 = sb(nc, fs, "h_w4", [64, 512], F32)
        a_all = [sb(nc, fs, "h_a%d" % i, [64, 512], F32) for i in range(4)]
        wn_all = [sb(nc, fs, "h_wn%d" % i, [128, 512], F32) for i in range(2)]
        ki_all = [sb(nc, fs, "h_ki%d" % i, [64, 512], mybir.dt.int32) for i in range(2)]
        kf_all = [sb(nc, fs, "h_kf%d" % i, [64, 512], F32) for i in range(2)]
        t_ki_all, t_kf_all = [Tl(), Tl()], [Tl(), Tl()]
        t_a_all = [Tl() for _ in range(4)]
        t_wn_all = [Tl(), Tl()]
        fb = sb(nc, fs, "h_fb", [64, 3], F32)
        l1 = sb(nc, fs, "h_l1", [128, 4], F32)
        pf_all = [ps(nc, fs, "h_pf%d" % i, [64, 512], F32) for i in range(2)]
        p4 = [ps(nc, fs, "h_p4%d" % i, [128, 512], F32) for i in range(2)]
        t_in, t_fb, t_l1 = Tl(), Tl(), Tl()
        t_pf_all = [Tl(), Tl()]
        t_p4 = [Tl(), Tl()]
        d = mk.new_dsem()
        mk.dma('sp', zT[:], zT_d, w=[t_in], dsem=d)
        mk.dma('sp', tb[:], K.hy_tdel[:, tcol0:tcol0 + n], w=[t_in], dsem=d)
        mk.dma('sp', w1[:], K.hy_w1[l], w=[t_in], dsem=d)
        mk.dma('sp', w23[:, 0, :], K.hy_w2[l], w=[t_in], dsem=d)
        mk.dma('sp', w23[:, 1, :], K.hy_w3[l], w=[t_in], dsem=d)
        mk.dma('sp', w4[:], K.hy_w4[l], w=[t_in], dsem=d)
        hs = K.hs
        mk.op('dve', lambda: nc.vector.tensor_mul(out=fb[:], in0=hs[0:64, 27:30], in1=hs[0:64, 26:27].to_broadcast([64, 3])), r=[K.t_hs], w=[t_fb])
        nb = (n + 511) // 512
        for blk in range(nb):
            c0 = blk * 512
            cn_ = min(512, n - c0)
            ai = 0
            bb = blk % 2
            a = a_all[bb * 2:bb * 2 + 2]
            t_a = t_a_all[bb * 2:bb * 2 + 2]
            ki, kf, t_ki, t_kf = ki_all[bb], kf_all[bb], t_ki_all[bb], t_kf_all[bb]
            pf = pf_all[bb * 2:bb * 2 + 2] if len(pf_all) == 4 else pf_all
            t_pf = t_pf_all[bb * 2:bb * 2 + 2] if len(pf_all) == 4 else t_pf_all
            for layer in range(3):
                pfi = layer % 2
                if layer == 0:
                    mk.op('pe', lambda pfi=pfi, c0=c0, cn_=cn_: nc.tensor.matmul(pf[pfi][:, 0:cn_], lhsT=w1[:], rhs=zT[:, c0:c0 + cn_], start=True, stop=True),
                          r=[t_in], w=[t_pf[pfi]])
                else:
                    mk.op('pe', lambda pfi=pfi, layer=layer, cn_=cn_, ai=ai: nc.tensor.matmul(pf[pfi][:, 0:cn_], lhsT=w23[:, layer - 1, :], rhs=a[ai][:, 0:cn_],
                                                                                       start=True, stop=True), r=[t_in, t_a[ai]], w=[t_pf[pfi]])
                    ai = 1 - ai
                mk.op('dve', lambda pfi=pfi, ai=ai, layer=layer, cn_=cn_: nc.vector.tensor_scalar(
                    out=a[ai][:, 0:cn_], in0=pf[pfi][:, 0:cn_], scalar1=hs[0:64, 26:27], scalar2=fb[:, layer:layer + 1], op0=ALU.mult, op1=ALU.add),
                    r=[t_pf[pfi], t_fb, K.t_hs], w=[t_a[ai]])
                mk.op('dve', lambda ai=ai, cn_=cn_: nc.vector.tensor_scalar(out=ki[:, 0:cn_], in0=a[ai][:, 0:cn_], scalar1=1.0 / TWO_PI, scalar2=None, op0=ALU.mult),
                      r=[t_a[ai]], w=[t_ki])
                mk.op('dve', lambda cn_=cn_: nc.vector.tensor_copy(out=kf[:, 0:cn_], in_=ki[:, 0:cn_]), r=[t_ki], w=[t_kf])
                mk.op('dve', lambda ai=ai, cn_=cn_: nc.vector.scalar_tensor_tensor(out=a[ai][:, 0:cn_], in0=kf[:, 0:cn_], scalar=-TWO_PI, in1=a[ai][:, 0:cn_],
                                                                                  op0=ALU.mult, op1=ALU.add), r=[t_kf, t_a[ai]], w=[t_a[ai]])
                mk.op('act', lambda ai=ai, cn_=cn_: nc.scalar.activation(out=kf[:, 0:cn_], in_=a[ai][:, 0:cn_], func=AF.Sign), r=[t_a[ai]], w=[t_kf])
                mk.op('dve', lambda ai=ai, cn_=cn_: nc.vector.scalar_tensor_tensor(out=a[ai][:, 0:cn_], in0=kf[:, 0:cn_], scalar=-math.pi, in1=a[ai][:, 0:cn_],
                                                                                  op0=ALU.mult, op1=ALU.add), r=[t_kf, t_a[ai]], w=[t_a[ai]])
                mk.op('act', lambda ai=ai, cn_=cn_: nc.scalar.activation(out=a[ai][:, 0:cn_], in_=a[ai][:, 0:cn_], func=AF.Sin, scale=-1.0),
                      r=[t_a[ai]], w=[t_a[ai]])
            for c in range(4):
                pi_ = c % 2
                mk.op('pe', lambda c=c, pi_=pi_, ai=ai, cn_=cn_: nc.tensor.matmul(p4[pi_][:, 0:cn_], lhsT=w4[:, c * 128:(c + 1) * 128], rhs=a[ai][:, 0:cn_], start=True, stop=True),
                      r=[t_in, t_a[ai]], w=[t_p4[pi_]])
                wn, t_wn = wn_all[c % 2], t_wn_all[c % 2]
                mk.op('act', lambda c=c, c0=c0, cn_=cn_, wn=wn: nc.scalar.activation(out=wn[:, 0:cn_], in_=tb[:, c0:c0 + cn_], func=AF.Exp, scale=K.negdel[:, c:c + 1]),
                      r=[t_in, K.t_const], w=[t_wn])
                mk.op('dve', lambda c=c, pi_=pi_, c0=c0, cn_=cn_, wn=wn: nc.vector.tensor_mul(out=hT[:, c, c0:c0 + cn_], in0=p4[pi_][:, 0:cn_], in1=wn[:, 0:cn_]),
                      r=[t_p4[pi_], t_wn], w=[K.t_hT])
        for c in range(4):
            lo = 0 if c < 2 else 1
            mk.op('dve', lambda c=c, lo=lo: nc.vector.tensor_reduce(out=l1[:, c:c + 1], in_=hT[:, c, lo:n], axis=mybir.AxisListType.X, op=ALU.add,
                                                                    apply_absolute_value=True), r=[K.t_hT], w=[t_l1])
        mk.op('dve', lambda: nc.vector.tensor_add(out=rl1[:], in0=l1[:, 0:2], in1=l1[:, 2:4]), r=[t_l1], w=[K.t_rl1])
        mk.op('dve', lambda: nc.vector.reciprocal(out=rl1[:], in_=rl1[:]), r=[K.t_rl1], w=[K.t_rl1])
        mk.barrier()


def hyena_conv_dev(K, l, st, n, off, hT, rl1):
    nc, mk = K.nc, K.mk
    S = K.S
    hs = K.hs
    with ExitStack() as cs:
        pp = sb(nc, cs, "h_pp", [128, n + 2], F32)
        u = sb(nc, cs, "h_u", [128, n], F32)
        x1 = sb(nc, cs, "h_x1", [128, n], F32)
        x0 = sb(nc, cs, "h_x0", [128, n], F32)
        ya = [sb(nc, cs, "h_y%d" % i, [128, n], F32) for i in range(2)]
        ob = sb(nc, cs, "h_ob", [128, n], BF16)
        t_pp, t_u, t_x1, t_x0, t_ob = Tl(), Tl(), Tl(), Tl(), Tl()
        t_y = [Tl(), Tl()]
        d = mk.new_dsem()
        dout = mk.new_dsem()
        mk.op('pool', lambda: nc.gpsimd.memset(pp[:, 0:1], 0.0), w=[t_pp])
        mk.op('pool', lambda: nc.gpsimd.memset(pp[:, n + 1:n + 2], 0.0), w=[t_pp])
        for cc in range(2):
            for which, dst, t_dst in ((0, u, t_u), (1, x1, t_x1), (2, x0, t_x0)):
                chn = which * 2 + cc
                mk.dma('sp', pp[:, 1:n + 1], S.PBT[l][chn * 128:(chn + 1) * 128, off:off + n], r=[S.t_PBT], w=[t_pp], dsem=d)
                mk.op('dve', lambda chn=chn, dst=dst: nc.vector.tensor_scalar(out=dst[:], in0=pp[:, 0:n], scalar1=hs[:, chn:chn + 1], scalar2=hs[:, 18 + chn:19 + chn],
                                                                            op0=ALU.mult, op1=ALU.add), r=[t_pp, K.t_hs], w=[t_dst])
                for k in (1, 2):
                    mk.op('dve', lambda chn=chn, dst=dst, k=k: nc.vector.scalar_tensor_tensor(out=dst[:], in0=pp[:, k:n + k], scalar=hs[:, k * 6 + chn:k * 6 + chn + 1],
                                                                                              in1=dst[:], op0=ALU.mult, op1=ALU.add), r=[t_pp, K.t_hs, t_dst], w=[t_dst])
            mk.op('dve', lambda: nc.vector.tensor_mul(out=u[:], in0=u[:], in1=x1[:]), r=[t_u, t_x1], w=[t_u])
            mk.op('pool', lambda: nc.gpsimd.memset(ya[0][:], 0.0), w=[t_y[0]])
            mk.op('pool', lambda: nc.gpsimd.memset(ya[1][:], 0.0), w=[t_y[1]])
            k = 0
            for lag in range(n):
                i = k % 2
                k += 1
                mk.op('dve', lambda lag=lag, i=i, cc=cc: nc.vector.scalar_tensor_tensor(out=ya[i][:, lag:n], in0=u[:, 0:n - lag], scalar=hT[:, cc, lag:lag + 1],
                                                                                        in1=ya[i][:, lag:n], op0=ALU.mult, op1=ALU.add), r=[t_u, K.t_hT, t_y[i]], w=[t_y[i]])
            for lag in range(1, n):
                i = k % 2
                k += 1
                mk.op('dve', lambda lag=lag, i=i, cc=cc: nc.vector.scalar_tensor_tensor(out=ya[i][:, 0:n - lag], in0=u[:, lag:n], scalar=hT[:, 2 + cc, lag:lag + 1],
                                                                                        in1=ya[i][:, 0:n - lag], op0=ALU.mult, op1=ALU.add), r=[t_u, K.t_hT, t_y[i]], w=[t_y[i]])
            mk.op('dve', lambda: nc.vector.tensor_add(out=ya[0][:], in0=ya[0][:], in1=ya[1][:]), r=[t_y[0], t_y[1]], w=[t_y[0]])
            mk.op('dve', lambda cc=cc: nc.vector.tensor_scalar_mul(out=ya[0][:], in0=ya[0][:], scalar1=rl1[:, cc:cc + 1]), r=[t_y[0], K.t_rl1], w=[t_y[0]])
            mk.op('dve', lambda cc=cc: nc.vector.scalar_tensor_tensor(out=ya[0][:], in0=u[:], scalar=hs[:, 24 + cc:25 + cc], in1=ya[0][:], op0=ALU.mult, op1=ALU.add),
                  r=[t_u, t_y[0], K.t_hs], w=[t_y[0]])
            mk.op('dve', lambda: nc.vector.tensor_mul(out=ob[:], in0=ya[0][:], in1=x0[:]), r=[t_y[0], t_x0], w=[t_ob])
            mk.dma('sp', S.CT[l][384 + cc * 128:384 + (cc + 1) * 128, off:off + n], ob[:], r=[t_ob], w=[S.t_CT], dsem=dout)
        mk.barrier()


def fft_setup(K, st):
    nc, mk = K.nc, K.mk
    F = Ctx()
    F.F1 = sb(nc, st, "x_F1", [64, 256], BF16)
    F.GB = [sb(nc, st, "x_GB%d" % i, [128, 24, 128], BF16) for i in range(2)]
    F.Us = sb(nc, st, "x_Us", [64, 128, 64], BF16)
    F.AC = sb(nc, st, "x_AC", [128, 16384], BF16)
    F.stg = sb(nc, st, "x_stg", [128, SEQ], BF16)
    F.t_stg = Tl()
    F.P1 = [ps(nc, st, "x_P1%d" % i, [128, 2, 256], F32) for i in range(2)]
    F.P2 = [ps(nc, st, "x_P2%d" % i, [128, 4, 2, 64], F32) for i in range(2)]
    F.t_tab, F.t_Us = Tl(), Tl()
    F.t_ACa, F.t_ACd, F.t_Yd, F.t_Yp = Tl(), Tl(), Tl(), Tl()
    F.t_GB = [Tl(), Tl()]
    F.t_P1 = [Tl(), Tl()]
    F.t_P2 = [Tl(), Tl()]
    F.d_tab = mk.new_dsem()
    F.d_gb = [mk.new_dsem(), mk.new_dsem()]
    F.d_us = mk.new_dsem()
    F.d_x = [mk.new_dsem() for _ in range(4)]
    F.d_st = mk.new_dsem()
    F.gbi = 0
    F.xi = 0
    F.p2i = 0
    mk.dma('pool', F.F1[:], K.fft_F1[:, :], w=[F.t_tab], dsem=F.d_tab)
    for i in range(2):
        mk.op('pool', lambda i=i: nc.gpsimd.memset(F.GB[i][:], 0.0), w=[F.t_GB[i]])
    return F


def fft_alloc_work(K, F, st, extra_p2):
    nc = K.nc
    F.xs = [sb(nc, st, "x_xs%d" % i, [128, 4, 2, 64], F32) for i in range(4)]
    F.hfs = [sb(nc, st, "x_hf%d" % i, [128, 4, 2, 64], F32) for i in range(4)]
    F.tmp = [sb(nc, st, "x_tmp%d" % i, [128, 4, 64], F32) for i in range(4)]
    F.t_xs = [Tl() for _ in range(4)]
    F.t_hfs = [Tl() for _ in range(4)]
    F.t_tmp = [Tl() for _ in range(4)]
    F.P2 = F.P2[:2] + [ps(nc, st, "x_P2x%d" % i, [128, 4, 2, 64], F32) for i in range(extra_p2)]
    F.t_P2 = F.t_P2[:2] + [Tl() for _ in range(extra_p2)]


def fft_setup_inv(K, F, st):
    nc, mk = K.nc, K.mk
    F.W3 = sb(nc, st, "x_W3", [128, 512], BF16)
    F.F4 = sb(nc, st, "x_F4", [128, 64, 2, 64], BF16)
    F.Ysb = sb(nc, st, "x_Y", [128, 2, 64, 128], BF16)
    mk.dma('pool', F.W3[:], K.fft_W3[:, :], w=[F.t_tab], dsem=F.d_tab)
    mk.dma('pool', F.F4[:].rearrange("p a b c -> p (a b c)"), K.fft_F4[:, :], w=[F.t_tab], dsem=F.d_tab)


def fft_fwd(K, F, src, t_src, consumer):
    nc, mk = K.nc, K.mk
    A = F.AC[:].rearrange("p (ri k1 pr) -> p ri k1 pr", ri=2, k1=128)
    srcv = src.rearrange("c (s1 s2) -> s1 c s2", s2=64)
    for cb in range(16):
        mk.dma('sp', F.Us[:, cb * 8:(cb + 1) * 8, :], srcv[:, cb * 8:(cb + 1) * 8, :], r=[t_src], w=[F.t_Us], dsem=F.d_us)
    for pp2 in range(32):
        b = pp2 % 2
        for j in range(2):
            pr = pp2 * 2 + j
            mk.op('pe', lambda b=b, j=j, pr=pr: nc.tensor.matmul(F.P1[b][:, j, :], lhsT=F.Us[:, 2 * pr:2 * pr + 2, :].rearrange("p c s -> p (c s)"),
                                                                rhs=F.F1[:], start=True, stop=True), r=[F.t_Us, F.t_tab], w=[F.t_P1[b]])
        eng = 'act' if pp2 % 2 == 0 else 'dve'
        outv = A[:, :, :, 2 * pp2:2 * pp2 + 2].rearrange("p ri k1 pr -> p pr ri k1")
        inv = F.P1[b][:].rearrange("p pr (ri k1) -> p pr ri k1", ri=2)
        if eng == 'act':
            mk.op('act', lambda outv=outv, inv=inv: nc.scalar.copy(out=outv, in_=inv), r=[F.t_P1[b]], w=[F.t_ACa])
        else:
            mk.op('dve', lambda outv=outv, inv=inv: nc.vector.tensor_copy(out=outv, in_=inv), r=[F.t_P1[b]], w=[F.t_ACd])
    for kc in range(16):
        gb = F.gbi % 2
        F.gbi += 1
        if not K.gb_cached[kc]:
            srcg = K.fft_G[:, kc * 8:(kc + 1) * 8, :, :].rearrange("p a b c -> p (a b) c")
            mk.dma('pool', F.GB[gb][0:64, :, 0:64], srcg, w=[F.t_GB[gb]], dsem=F.d_gb[gb])
            mk.dma('pool', F.GB[gb][64:128, :, 64:128], srcg, w=[F.t_GB[gb]], dsem=F.d_gb[gb])
            mk.dma('sp', K.S.GBd[kc], F.GB[gb][:], r=[F.t_GB[gb]], w=[K.S.t_GBd], dsem=F.d_st)
            K.gb_cached[kc] = True
        else:
            mk.dma('sp', F.GB[gb][:], K.S.GBd[kc], r=[K.S.t_GBd], w=[F.t_GB[gb]], dsem=F.d_gb[gb])
        for half in range(2):
            pb = F.p2i % len(F.P2)
            F.p2i += 1
            for jj in range(4):
                kk = half * 4 + jj
                k1 = kc * 8 + kk
                Gre, Gim, nGim = F.GB[gb][:, kk * 3 + 0, :], F.GB[gb][:, kk * 3 + 1, :], F.GB[gb][:, kk * 3 + 2, :]
                Are, Aim = A[:, 0, k1, :], A[:, 1, k1, :]
                for (ri, l0, r0, l1, r1) in ((0, Gre, Are, nGim, Aim), (1, Gim, Are, Gre, Aim)):
                    mk.op('pe', lambda pb=pb, jj=jj, ri=ri, l0=l0, r0=r0: nc.tensor.matmul(F.P2[pb][:, jj, ri, :], lhsT=l0, rhs=r0, start=True, stop=False),
                          r=[F.t_GB[gb], F.t_ACa, F.t_ACd], w=[F.t_P2[pb]])
                    mk.op('pe', lambda pb=pb, jj=jj, ri=ri, l1=l1, r1=r1: nc.tensor.matmul(F.P2[pb][:, jj, ri, :], lhsT=l1, rhs=r1, start=False, stop=True),
                          r=[F.t_GB[gb], F.t_ACa, F.t_ACd], w=[F.t_P2[pb]])
            consumer(kc * 8 + half * 4, F.P2[pb], F.t_P2[pb])


def fft_inv(K, F, ysb, t_ysb, t_ysb_a):
    nc, mk = K.nc, K.mk
    C = F.AC[:].rearrange("p (ri t2 c) -> p ri t2 c", ri=2, t2=64)
    P3, tP3 = F.P1, F.t_P1
    P4 = [F.P2[i][:].rearrange("p a b c -> p (a b) c") for i in range(2)]
    tP4 = F.t_P2[0:2]
    for pp2 in range(32):
        b = pp2 % 2
        for j in range(2):
            pr = pp2 * 2 + j
            mk.op('pe', lambda b=b, j=j, pr=pr: nc.tensor.matmul(P3[b][:, j, :], lhsT=F.Ysb[:, 0, pr, :], rhs=F.W3[:, 0:256], start=True, stop=False),
                  r=[F.t_Yd, F.t_Yp, F.t_tab], w=[tP3[b]])
            mk.op('pe', lambda b=b, j=j, pr=pr: nc.tensor.matmul(P3[b][:, j, :], lhsT=F.Ysb[:, 1, pr, :], rhs=F.W3[:, 256:512], start=False, stop=True),
                  r=[F.t_Yd, F.t_Yp, F.t_tab], w=[tP3[b]])
        for j in range(2):
            outv = C[:, :, :, 4 * pp2 + 2 * j:4 * pp2 + 2 * j + 2].rearrange("p ri t2 cc -> p cc ri t2")
            inv = P3[b][:, j, :].rearrange("p (cc ri t2) -> p cc ri t2", cc=2, ri=2)
            if pp2 % 2 == 0:
                mk.op('act', lambda outv=outv, inv=inv: nc.scalar.copy(out=outv, in_=inv), r=[tP3[b]], w=[F.t_ACa])
            else:
                mk.op('dve', lambda outv=outv, inv=inv: nc.vector.tensor_copy(out=outv, in_=inv), r=[tP3[b]], w=[F.t_ACd])
    yv = ysb[:].rearrange("p (t1 t2) -> p t1 t2", t2=64)
    for t8 in range(8):
        b = t8 % 2
        for j in range(8):
            t2 = t8 * 8 + j
            mk.op('pe', lambda b=b, j=j, t2=t2: nc.tensor.matmul(P4[b][:, j, :], lhsT=C[:, 0, t2, :], rhs=F.F4[:, t2, 0, :], start=True, stop=False),
                  r=[F.t_ACa, F.t_ACd, F.t_tab], w=[tP4[b]])
            mk.op('pe', lambda b=b, j=j, t2=t2: nc.tensor.matmul(P4[b][:, j, :], lhsT=C[:, 1, t2, :], rhs=F.F4[:, t2, 1, :], start=False, stop=True),
                  r=[F.t_ACa, F.t_ACd, F.t_tab], w=[tP4[b]])
        outv = yv[:, :, t8 * 8:(t8 + 1) * 8].rearrange("p t1 t2 -> p t2 t1")
        if t8 % 2 == 0:
            mk.op('act', lambda outv=outv, b=b: nc.scalar.copy(out=outv, in_=P4[b]), r=[tP4[b]], w=[t_ysb_a])
        else:
            mk.op('dve', lambda outv=outv, b=b: nc.vector.tensor_copy(out=outv, in_=P4[b]), r=[tP4[b]], w=[t_ysb])


def hyena_lat_fft_filter(K, l, F, hT):
    nc, mk = K.nc, K.mk
    S = K.S
    n = SEQ
    stg, t_stg = F.stg, F.t_stg
    dst = mk.new_dsem()
    with ExitStack() as ws:
        fft_alloc_work(K, F, ws, 4)
        for c in range(4):
            mk.op('pool', lambda c=c: nc.gpsimd.tensor_copy(out=stg[:], in_=hT[:, c, :]), r=[K.t_hT], w=[t_stg])
            if c >= 2:
                mk.op('pool', lambda: nc.gpsimd.memset(stg[:, 0:1], 0.0), w=[t_stg])
            mk.dma('sp', S.HTd[c * 128:(c + 1) * 128, :], stg[:], r=[t_stg], w=[S.t_HTd], dsem=dst)
        for g in range(2):
            def cons_a(k1_0, P2, tP2, g=g):
                i = F.xi % 4
                F.xi += 1
                mk.op('act', lambda: nc.scalar.copy(out=F.xs[i][:], in_=P2[:]), r=[tP2], w=[F.t_xs[i]])
                mk.dma('pool', S.XF[g][:, k1_0:k1_0 + 4, :, :], F.xs[i][:], r=[F.t_xs[i]], w=[S.t_XF], dsem=F.d_st)

            def cons_b(k1_0, P2, tP2, g=g):
                i = F.xi % 4
                F.xi += 1
                mk.dma('sp', F.hfs[i][:], S.XF[g][:, k1_0:k1_0 + 4, :, :], r=[S.t_XF], w=[F.t_hfs[i]], dsem=F.d_x[i])
                mk.op('dve', lambda: nc.vector.tensor_add(out=F.xs[i][:, :, 0, :], in0=P2[:, :, 0, :], in1=F.hfs[i][:, :, 0, :]), r=[tP2, F.t_hfs[i]], w=[F.t_xs[i]])
                mk.op('dve', lambda: nc.vector.tensor_sub(out=F.xs[i][:, :, 1, :], in0=F.hfs[i][:, :, 1, :], in1=P2[:, :, 1, :]), r=[tP2, F.t_hfs[i]], w=[F.t_xs[i]])
                mk.dma('pool', S.HF[g][:, k1_0:k1_0 + 4, :, :], F.xs[i][:], r=[F.t_xs[i]], w=[S.t_HF], dsem=F.d_st)

            fft_fwd(K, F, S.HTd[g * 128:(g + 1) * 128, :], S.t_HTd, cons_a)
            fft_fwd(K, F, S.HTd[256 + g * 128:256 + (g + 1) * 128, :], S.t_HTd, cons_b)
        mk.barrier()


def hyena_lat_fft_u(K, l, F, fs, rl1):
    nc, mk = K.nc, K.mk
    S = K.S
    hs = K.hs
    n, off = SEQ, CTX
    stg, t_stg = F.stg, F.t_stg
    dst = mk.new_dsem()
    fft_alloc_work(K, F, fs, 4)
    if True:
        pp = sb(nc, fs, "h_pp", [128, n + 2], F32)
        u = sb(nc, fs, "h_u", [128, n], F32)
        x0 = sb(nc, fs, "h_x0", [128, n], F32)
        ysb = sb(nc, fs, "h_ysb", [128, n], F32)
        t_pp, t_u, t_x0, t_ysb, t_ysb_a = Tl(), Tl(), Tl(), Tl(), Tl()
        d = mk.new_dsem()
        dout = mk.new_dsem()
        mk.op('pool', lambda: nc.gpsimd.memset(pp[:, 0:1], 0.0), w=[t_pp])
        mk.op('pool', lambda: nc.gpsimd.memset(pp[:, n + 1:n + 2], 0.0), w=[t_pp])
        Yv = F.Ysb
        for cc in range(2):
            for which, dstt, t_dst in ((0, u, t_u), (1, ysb, t_ysb), (2, x0, t_x0)):
                chn = which * 2 + cc
                mk.dma('sp', pp[:, 1:n + 1], S.PBT[l][chn * 128:(chn + 1) * 128, off:off + n], r=[S.t_PBT], w=[t_pp], dsem=d)
                mk.op('dve', lambda chn=chn, dstt=dstt: nc.vector.tensor_scalar(out=dstt[:], in0=pp[:, 0:n], scalar1=hs[:, chn:chn + 1], scalar2=hs[:, 18 + chn:19 + chn],
                                                                              op0=ALU.mult, op1=ALU.add), r=[t_pp, K.t_hs], w=[t_dst])
                for k in (1, 2):
                    mk.op('dve', lambda chn=chn, dstt=dstt, k=k: nc.vector.scalar_tensor_tensor(out=dstt[:], in0=pp[:, k:n + k], scalar=hs[:, k * 6 + chn:k * 6 + chn + 1],
                                                                                                in1=dstt[:], op0=ALU.mult, op1=ALU.add), r=[t_pp, K.t_hs, t_dst], w=[t_dst])
            mk.op('dve', lambda: nc.vector.tensor_mul(out=u[:], in0=u[:], in1=ysb[:]), r=[t_u, t_ysb], w=[t_u])
            mk.op('pool', lambda: nc.gpsimd.tensor_copy(out=stg[:], in_=u[:]), r=[t_u], w=[t_stg])
            mk.dma('sp', S.UTd[cc * 128:(cc + 1) * 128, :], stg[:], r=[t_stg], w=[S.t_UTd], dsem=dst)

            def cons_c(k1_0, P2, tP2, cc=cc):
                i = F.xi % 4
                F.xi += 1
                mk.dma('sp', F.hfs[i][:], S.HF[cc][:, k1_0:k1_0 + 4, :, :], r=[S.t_HF], w=[F.t_hfs[i]], dsem=F.d_x[i])
                mk.op('act', lambda: nc.scalar.copy(out=F.xs[i][:], in_=P2[:]), r=[tP2], w=[F.t_xs[i]])
                Xre, Xim = F.xs[i][:, :, 0, :], F.xs[i][:, :, 1, :]
                Hre, Him = F.hfs[i][:, :, 0, :], F.hfs[i][:, :, 1, :]
                ta, tb, tc_, td = F.tmp
                yre = Yv[:, 0, :, k1_0:k1_0 + 4].rearrange("p pr k -> p k pr")
                yim = Yv[:, 1, :, k1_0:k1_0 + 4].rearrange("p pr k -> p k pr")
                mk.op('dve', lambda: nc.vector.tensor_mul(out=ta[:], in0=Xre, in1=Hre), r=[F.t_xs[i], F.t_hfs[i]], w=[F.t_tmp[0]])
                mk.op('dve', lambda: nc.vector.tensor_mul(out=tb[:], in0=Xim, in1=Him), r=[F.t_xs[i], F.t_hfs[i]], w=[F.t_tmp[1]])
                mk.op('dve', lambda: nc.vector.tensor_sub(out=yre, in0=ta[:], in1=tb[:]), r=[F.t_tmp[0], F.t_tmp[1]], w=[F.t_Yd])
                mk.op('pool', lambda: nc.gpsimd.tensor_mul(out=tc_[:], in0=Xre, in1=Him), r=[F.t_xs[i], F.t_hfs[i]], w=[F.t_tmp[2]])
                mk.op('pool', lambda: nc.gpsimd.tensor_mul(out=td[:], in0=Xim, in1=Hre), r=[F.t_xs[i], F.t_hfs[i]], w=[F.t_tmp[3]])
                mk.op('pool', lambda: nc.gpsimd.tensor_add(out=yim, in0=tc_[:], in1=td[:]), r=[F.t_tmp[2], F.t_tmp[3]], w=[F.t_Yp])

            fft_fwd(K, F, S.UTd[cc * 128:(cc + 1) * 128, :], S.t_UTd, cons_c)
            fft_inv(K, F, ysb, t_ysb, t_ysb_a)
            mk.op('dve', lambda cc=cc: nc.vector.tensor_scalar_mul(out=ysb[:], in0=ysb[:], scalar1=rl1[:, cc:cc + 1]), r=[t_ysb, t_ysb_a, K.t_rl1], w=[t_ysb, t_ysb_a])
            mk.op('dve', lambda cc=cc: nc.vector.scalar_tensor_tensor(out=ysb[:], in0=u[:], scalar=hs[:, 24 + cc:25 + cc], in1=ysb[:], op0=ALU.mult, op1=ALU.add),
                  r=[t_u, t_ysb, K.t_hs], w=[t_ysb])
            mk.op('dve', lambda: nc.vector.tensor_mul(out=stg[:], in0=ysb[:], in1=x0[:]), r=[t_ysb, t_x0], w=[t_stg])
            mk.dma('sp', S.CT[l][384 + cc * 128:384 + (cc + 1) * 128, off:off + n], stg[:], r=[t_stg], w=[S.t_CT], dsem=dout)
        mk.barrier()


def phase_hyena(K, l, need_ctx):
    nc, mk = K.nc, K.mk
    with ExitStack() as st:
        K.hs = sb(nc, st, "h_hs", [128, 32], F32)
        K.negpi = sb(nc, st, "h_negpi", [128, 1], F32)
        K.negdel = sb(nc, st, "h_negdel", [128, 4], F32)
        rl1 = sb(nc, st, "h_rl1", [128, 2], F32)
        K.t_hs, K.t_hT, K.t_rl1 = Tl(), Tl(), Tl()
        d = mk.new_dsem()
        mk.dma('sp', K.hs[:], K.hy_small[l], w=[K.t_hs], dsem=d)
        mk.dma('sp', K.negdel[:], K.hy_tdel[:, 0:4], w=[K.t_const], dsem=d)
        mk.op('dve', lambda: nc.vector.memset(K.negpi[:], -math.pi), w=[K.t_const])
        mk.op('dve', lambda: nc.vector.tensor_scalar_mul(out=K.negdel[:], in0=K.negdel[:], scalar1=-1.0), r=[K.t_const], w=[K.t_const])
        seqs = [(SEQ, CTX, K.zT_lat, 4)]
        if need_ctx:
            seqs.append((CTX, 0, K.zT_ctx, 4 + SEQ))
        for (n, off, zT_d, tcol0) in seqs:
            if n == SEQ and USE_FFT:
                with ExitStack() as s1:
                    F = fft_setup(K, s1)
                    with ExitStack() as s2:
                        hT = sb(nc, s2, "h_hT", [128, 4, n], F32)
                        hyena_filter_dev(K, l, s2, n, zT_d, tcol0, hT, rl1)
                        hyena_lat_fft_filter(K, l, F, hT)
                    fft_setup_inv(K, F, s1)
                    hyena_lat_fft_u(K, l, F, s1, rl1)
            else:
                with ExitStack() as s2:
                    hT = sb(nc, s2, "h_hT", [128, 4, n], F32)
                    hyena_filter_dev(K, l, s2, n, zT_d, tcol0, hT, rl1)
                    hyena_conv_dev(K, l, s2, n, off, hT, rl1)
        mk.barrier()


def phase_mixout(K, l, Xres, t_xres, Xout, t_xout, first_chunk=0, after_weights=None):
    nc, mk = K.nc, K.mk
    S = K.S
    with ExitStack() as st:
        wo = sb(nc, st, "o_wo", [128, 8, D], BF16)
        modv = sb(nc, st, "o_mod", [128, 3, D], F32)
        xc = [sb(nc, st, "o_xc%d" % i, [128, D], F32) for i in range(2)]
        cT = [sb(nc, st, "o_cT%d" % i, [128, 8, 128], BF16) for i in range(2)]
        tt = sb(nc, st, "o_tt", [128, D], F32)
        st6 = sb(nc, st, "o_st6", [128, 2, 6], F32)
        mv = sb(nc, st, "o_mv", [128, 2], F32)
        rstd = sb(nc, st, "o_rstd", [128, 1], F32)
        pm = [ps(nc, st, "o_pm%d" % i, [128, 512], F32) for i in range(4)]
        t_wo, t_mod, t_tt, t_st = Tl(), Tl(), Tl(), Tl()
        t_xc = [Tl(), Tl()]
        t_cT = [Tl(), Tl()]
        t_pm = [Tl() for _ in range(4)]
        dws, dmod, dout = mk.new_dsem(), mk.new_dsem(), mk.new_dsem()
        dx = [mk.new_dsem(), mk.new_dsem()]
        for kc in range(8):
            mk.dma('pool', wo[:, kc, :], K.w_out[l][kc * 128:(kc + 1) * 128, :], w=[t_wo], dsem=dws)
        if after_weights is not None:
            after_weights()
        cur_g = None
        for ch in range(first_chunk, NCH):
            g = 1 if ch < 2 else 0
            if g != cur_g:
                cur_g = g
                mrow = K.mod[l][g]
                for i, src in enumerate([mrow[5 * D:6 * D], K.ln_g[l, 1], K.ln_b[l, 1]]):
                    mk.dma('sp', modv[:, i, :], src.partition_broadcast(128), r=[K.t_mod[l]], w=[t_mod], dsem=dmod)
            xi = ch % 2
            tok = slice(ch * 128, (ch + 1) * 128)

            def issue_loads(c2):
                x2 = c2 % 2
                tk = slice(c2 * 128, (c2 + 1) * 128)
                mk.dma('sp', xc[x2][:], Xres[tk, :], r=[t_xres], w=[t_xc[x2]], dsem=dx[x2])
                mk.dma('sp', cT[x2][:], S.CT[l].rearrange("(c p) t -> p c t", p=128)[:, :, tk], r=[S.t_CT], w=[t_cT[x2]], dsem=dx[x2])

            if ch == first_chunk:
                issue_loads(ch)
            if ch + 1 < NCH:
                issue_loads(ch + 1)
            for hf in range(2):
                pi = (ch % 2) * 2 + hf
                for kc in range(8):
                    mk.op('pe', lambda kc=kc, hf=hf, pi=pi, xi=xi: nc.tensor.matmul(pm[pi][:], lhsT=cT[xi][:, kc, :], rhs=wo[:, kc, hf * 512:(hf + 1) * 512],
                                                                                   start=(kc == 0), stop=(kc == 7)), r=[t_cT[xi], t_wo], w=[t_pm[pi]])
                mk.op('dve', lambda hf=hf, pi=pi: nc.vector.tensor_mul(out=tt[:, hf * 512:(hf + 1) * 512], in0=pm[pi][:], in1=modv[:, 0, hf * 512:(hf + 1) * 512]),
                      r=[t_pm[pi], t_mod], w=[t_tt])
            mk.op('dve', lambda xi=xi: nc.vector.scalar_tensor_tensor(out=xc[xi][:], in0=xc[xi][:], scalar=ALPHA, in1=tt[:], op0=ALU.mult, op1=ALU.add),
                  r=[t_tt, t_xc[xi]], w=[t_xc[xi]])
            ln_stats(K, xc[xi], t_xc[xi], st6, mv, rstd, t_st)
            mk.op('dve', lambda xi=xi: nc.vector.scalar_tensor_tensor(out=xc[xi][:], in0=xc[xi][:], scalar=mv[:, 0:1], in1=modv[:, 1, :],
                                                                      op0=ALU.subtract, op1=ALU.mult), r=[t_xc[xi], t_st, t_mod], w=[t_xc[xi]])
            mk.op('dve', lambda xi=xi: nc.vector.scalar_tensor_tensor(out=xc[xi][:], in0=xc[xi][:], scalar=rstd[:, 0:1], in1=modv[:, 2, :],
                                                                      op0=ALU.mult, op1=ALU.add), r=[t_xc[xi], t_st, t_mod], w=[t_xc[xi]])
            mk.dma('sp', Xout[tok, :], xc[xi][:], r=[t_xc[xi]], w=[t_xout], dsem=dout)
        mk.barrier()


def build(stop_after=None, dbg=()):
    nc = bass.Bass("TRN2", target_bir_lowering=False)
    K = Ctx()
    K.nc = nc
    K.mk = MK(nc)
    mk = K.mk
    DECLARED.clear()

    def din(name, shape):
        DECLARED.add(name)
        return nc.dram_tensor(name, shape, F32, kind="ExternalInput").ap()

    K.xin = din("xin", [NT, D])
    K.cc_in = din("cc", [128, 16])
    K.ada_w = din("ada_w", [DEPTH, D, NMOD * D])
    K.ada_b = din("ada_b", [DEPTH, NMOD * D])
    K.w_gu = [din("ffn1_w_gu", [DEPTH, D, 2 * DFF]), din("ffn2_w_gu", [DEPTH, D, 2 * DFF])]
    K.w_dn = [din("ffn1_w_down", [DEPTH, DFF, D]), din("ffn2_w_down", [DEPTH, DFF, D])]
    K.ln_g = din("ln_g", [DEPTH, 3, D])
    K.ln_b = din("ln_b", [DEPTH, 3, D])
    K.w_in = din("w_in", [DEPTH, D, P_IN])
    K.w_out = din("w_out", [DEPTH, D, D])
    K.a_qn = din("a_q_norm", [DEPTH, 64])
    K.a_kn = din("a_k_norm", [DEPTH, 64])
    K.c_qn = din("mla_q_norm", [DEPTH, 256])
    K.c_kvn = din("mla_kv_norm", [DEPTH, 128])
    K.w_uq = din("mla_w_uq", [DEPTH, 256, 576])
    K.w_ukv = din("mla_w_ukv", [DEPTH, 128, 768])
    K.hy_conv_w = din("hy_conv_w", [DEPTH, 3, 768])
    K.hy_conv_b = din("hy_conv_b", [DEPTH, 768])
    K.hy_w1 = din("hy_f_w1", [DEPTH, 33, 64])
    K.hy_b1 = din("hy_f_b1", [DEPTH, 64])
    K.hy_w2 = din("hy_f_w2", [DEPTH, 64, 64])
    K.hy_b2 = din("hy_f_b2", [DEPTH, 64])
    K.hy_w3 = din("hy_f_w3", [DEPTH, 64, 64])
    K.hy_b3 = din("hy_f_b3", [DEPTH, 64])
    K.hy_w4 = din("hy_f_w4", [DEPTH, 64, 512])
    K.hy_freq = din("hy_f_freq", [DEPTH, 64])
    K.hy_bias = din("hy_bias", [DEPTH, 256])
    K.ident_in = din("ident", [128, 128])
    K.hy_small = din("hy_small", [DEPTH, 128, 32])
    K.ropeA_in = din("ropeA", [SEQ, 64])
    K.ropeC_in = din("ropeC", [SEQ, 32])
    K.zT_lat = din("zT_lat", [33, SEQ])
    K.zT_ctx = din("zT_ctx", [33, CTX])
    K.hy_tdel = din("hy_tdel", [128, 4 + SEQ + CTX])
    K.fft_F1 = din("fft_F1", [64, 256])
    K.fft_G = din("fft_G", [64, 128, 3, 64])
    K.fft_W3 = din("fft_W3", [128, 512])
    K.fft_F4 = din("fft_F4", [128, 8192])
    K.out = nc.dram_tensor("out", [SEQ, D], F32, kind="ExternalOutput").ap()
    K.mod = nc.dram_tensor("mod", [DEPTH, 2, NMOD * D], F32).ap()
    K.t_mod = [Tl(), Tl()]
    XA = nc.dram_tensor("XA", [NT, D], F32).ap()
    XB = nc.dram_tensor("XB", [NT, D], F32).ap()
    XC = nc.dram_tensor("XC", [NT, D], F32).ap()
    S = Ctx()
    K.S = S
    S.QTa = nc.dram_tensor("QTa", [DEPTH, 6, 64, NT], BF16).ap()
    S.KTa = nc.dram_tensor("KTa", [DEPTH, 2, 64, NT], BF16).ap()
    S.Va = nc.dram_tensor("Va", [DEPTH, NT, 128], BF16).ap()
    S.QTc = nc.dram_tensor("QTc", [DEPTH, 6, 96, NT], BF16).ap()
    S.KTc = nc.dram_tensor("KTc", [DEPTH, 6, 96, NT], BF16).ap()
    S.Vc = nc.dram_tensor("Vc", [DEPTH, NT, 384], BF16).ap()
    S.PBT = nc.dram_tensor("PBT", [DEPTH, 768, NT], F32).ap()
    S.CT = nc.dram_tensor("CT", [DEPTH, D, NT], BF16).ap()
    S.t_A, S.t_C, S.t_PBT, S.t_CT = Tl(), Tl(), Tl(), Tl()
    S.HTd = nc.dram_tensor("HTd", [512, SEQ], BF16).ap()
    S.UTd = nc.dram_tensor("UTd", [256, SEQ], BF16).ap()
    S.XF = nc.dram_tensor("XF", [2, 128, 128, 2, 64], F32).ap()
    S.HF = nc.dram_tensor("HF", [2, 128, 128, 2, 64], F32).ap()
    S.t_HTd, S.t_UTd, S.t_XF, S.t_HF = Tl(), Tl(), Tl(), Tl()
    S.GBd = nc.dram_tensor("GBd", [16, 128, 24, 128], BF16).ap()
    S.t_GBd = Tl()
    K.gb_cached = [False] * 16
    t_xin, t_XA, t_XB, t_XC, t_out = Tl(), Tl(), Tl(), Tl(), Tl()
    dbg_outs = []

    def finish():
        dd = mk.new_dsem()
        for name in dbg:
            src = {"XA": XA, "XB": XB, "XC": XC, "QTa": S.QTa, "KTa": S.KTa, "Va": S.Va, "QTc": S.QTc, "KTc": S.KTc, "Vc": S.Vc,
                   "PBT": S.PBT, "CT": S.CT, "mod": K.mod}[name]
            o = nc.dram_tensor("dbg_" + name, list(src.shape), src.dtype, kind="ExternalOutput").ap()
            mk.dma('sp', o, src, w=[t_out], dsem=dd)
        mk.barrier()
        return nc

    with ExitStack() as gst:
        K.identb = sb(nc, gst, "identb", [128, 128], BF16)
        identf = sb(nc, gst, "identf", [128, 128], F32)
        K.eps_t = sb(nc, gst, "eps_t", [128, 1], F32)
        K.t_const = Tl()
        d0 = mk.new_dsem()
        mk.dma('sp', identf[:], K.ident_in[:, :], w=[K.t_const], dsem=d0)
        mk.op('dve', lambda: nc.vector.tensor_copy(out=K.identb[:], in_=identf[:]), r=[K.t_const], w=[K.t_const])
        mk.op('dve', lambda: nc.vector.memset(K.eps_t[:], EPS), w=[K.t_const])
        mk.barrier()

        Xcur, t_cur = K.xin, t_xin
        for l in range(DEPTH):
            last = (l == DEPTH - 1)
            phase_ada(K, l)
            mk.recycle()
            phase_ffn(K, l, 0, Xcur, t_cur, XC, t_XC, K.w_gu[0][l], K.w_dn[0][l])
            mk.recycle()
            if stop_after == "ffn1_%d" % l:
                return finish()
            phase_mixin(K, l, XC, t_XC)
            mk.recycle()
            if stop_after == "mixin_%d" % l:
                return finish()
            phase_attn(K, l, need_ctx=not last)
            mk.recycle()
            if stop_after == "attn_%d" % l:
                return finish()
            phase_hyena(K, l, need_ctx=not last)
            mk.recycle()
            if stop_after == "hyena_%d" % l:
                return finish()
            with ExitStack() as wst:
                W2 = ffn_alloc_weights(K, wst)
                phase_mixout(K, l, XC, t_XC, XB, t_XB, first_chunk=(2 if last else 0),
                             after_weights=lambda: ffn_load_weights(K, None, K.w_gu[1][l], K.w_dn[1][l], W=W2))
                if stop_after == "mixout_%d" % l:
                    return finish()
                phase_ffn(K, l, 1, XB, t_XB, XA, t_XA, K.w_gu[1][l], K.w_dn[1][l], first_chunk=(2 if last else 0), W=W2)
            mk.recycle()
            Xcur, t_cur = XA, t_XA
        dd = mk.new_dsem()
        mk.dma('sp', K.out[:, :], XA[CTX:, :], r=[t_XA], w=[t_out], dsem=dd)
        return finish()


def rope_table(rot_dim):
    rows = SEQ // 64
    row = np.repeat(np.arange(rows, dtype=np.float32), 64)
    col = np.tile(np.arange(64, dtype=np.float32), rows)
    n_freq = rot_dim // 4
    inv = (np.float32(10000.0) ** (-np.arange(n_freq, dtype=np.float32) / np.float32(n_freq))).astype(np.float32)
    ang = np.concatenate([row[:, None] * inv, col[:, None] * inv], -1).astype(np.float32)
    return np.ascontiguousarray(np.concatenate([np.cos(ang), np.sin(ang)], -1).astype(np.float32))


def fft_tables():
    N = 8192
    s1 = np.arange(64)[:, None]; k1 = np.arange(128)[None, :]
    F1 = np.exp(-2j * np.pi * s1 * k1 / 128)
    F1cat = np.concatenate([F1.real, F1.imag], 1)
    s2 = np.arange(64)[:, None, None]; k1g = np.arange(128)[None, :, None]; k2 = np.arange(64)[None, None, :]
    G = np.exp(-2j * np.pi * (s2 * k1g / 8192 + s2 * k2 / 64))
    Gc = np.stack([G.real, G.imag, -G.imag], 2)
    k2_ = np.arange(64)[:, None]; t2 = np.arange(64)[None, :]
    W3 = np.exp(2j * np.pi * k2_ * t2 / 64)
    W3a = np.zeros((128, 2, 2, 64)); W3b = np.zeros((128, 2, 2, 64))
    for cc in range(2):
        W3a[cc * 64:(cc + 1) * 64, cc, 0] = W3.real; W3a[cc * 64:(cc + 1) * 64, cc, 1] = W3.imag
        W3b[cc * 64:(cc + 1) * 64, cc, 0] = -W3.imag; W3b[cc * 64:(cc + 1) * 64, cc, 1] = W3.real
    W3cat = np.concatenate([W3a.reshape(128, 256), W3b.reshape(128, 256)], 1)
    k1_ = np.arange(128)[:, None, None]; t2_ = np.arange(64)[None, :, None]; t1 = np.arange(64)[None, None, :]
    F4 = np.exp(2j * np.pi * (t1 * k1_ / 128 + t2_ * k1_ / 8192)) / N
    F4c = np.stack([F4.real, -F4.imag], 2).reshape(128, 8192)
    f = lambda a: np.ascontiguousarray(a.astype(np.float32))
    return f(F1cat), f(Gc), f(W3cat), f(F4c)


def hyena_z(n):
    t = np.linspace(0.0, 1.0, n, dtype=np.float32)[:, None]
    w = (np.float32(2.0 * math.pi) * np.arange(n, dtype=np.float32)[:, None] / np.float32(n)).astype(np.float32)
    f = np.linspace(1e-4, 15, 16, dtype=np.float32)[None, :]
    z = np.concatenate([t, np.cos(f * w), -np.sin(f * w)], -1).astype(np.float32)
    return np.ascontiguousarray(z.T), t[:, 0]


def make_in_maps(inputs):
    x = np.asarray(inputs["x"], dtype=np.float32)
    ctx = np.asarray(inputs["ctx"], dtype=np.float32)
    c = np.asarray(inputs["c"], dtype=np.float32)
    cctx = np.asarray(inputs["c_ctx"], dtype=np.float32)
    shared = {k: np.ascontiguousarray(np.asarray(inputs[k], dtype=np.float32)) for k in inputs if k not in ("x", "ctx", "c")}
    shared["ident"] = np.eye(128, dtype=np.float32)
    shared["ropeA"] = rope_table(64)
    shared["ropeC"] = rope_table(32)
    zl, tl = hyena_z(SEQ)
    zc, tc_ = hyena_z(CTX)
    shared["zT_lat"] = zl
    shared["zT_ctx"] = zc
    deltas = np.abs(np.linspace(math.log(1e-2) / 1.5, math.log(1e-2) / 0.3, 256, dtype=np.float32))
    deltas = np.tile(deltas, 2)
    td = np.zeros((128, 4 + SEQ + CTX), np.float32)
    td[:, 0:4] = deltas.reshape(4, 128).T
    td[:, 4:4 + SEQ] = tl[None, :]
    td[:, 4 + SEQ:] = tc_[None, :]
    shared["hy_tdel"] = td
    hsm = np.zeros((DEPTH, 128, 32), np.float32)
    for l in range(DEPTH):
        cw = np.asarray(inputs["hy_conv_w"][l], np.float32)
        hsm[l, :, 0:18] = cw.reshape(3, 6, 128).transpose(2, 0, 1).reshape(128, 18)
        hsm[l, :, 18:24] = np.asarray(inputs["hy_conv_b"][l], np.float32).reshape(6, 128).T
        hsm[l, :, 24:26] = np.asarray(inputs["hy_bias"][l], np.float32).reshape(2, 128).T
        hsm[l, 0:64, 26] = np.asarray(inputs["hy_f_freq"][l], np.float32)
        hsm[l, 0:64, 27] = np.asarray(inputs["hy_f_b1"][l], np.float32)
        hsm[l, 0:64, 28] = np.asarray(inputs["hy_f_b2"][l], np.float32)
        hsm[l, 0:64, 29] = np.asarray(inputs["hy_f_b3"][l], np.float32)
    shared["hy_small"] = hsm
    shared["fft_F1"], shared["fft_G"], shared["fft_W3"], shared["fft_F4"] = fft_tables()
    maps = []
    for b in range(8):
        m = dict(shared)
        m["xin"] = np.ascontiguousarray(np.concatenate([ctx[b], x[b]], axis=0))
        cc = np.stack([c[b], cctx], axis=-1).reshape(8, 128, 2).transpose(1, 0, 2).reshape(128, 16)
        m["cc"] = np.ascontiguousarray(cc)
        maps.append(m)
    return maps


def kernel(**inputs):
    nc = build()
    maps = make_in_maps(inputs)
    maps = [{k: v for k, v in m.items() if k in DECLARED} for m in maps]
    res = run_bass_kernel_spmd(nc, maps, core_ids=list(range(8)))
    return np.stack([np.asarray(r["out"]) for r in res.results], axis=0).astype(np.float32)
```
